# Optimizing a Trainium2 kernel written in Bass

```python
import math
import numpy as np
import jax
import jax.numpy as jnp
from jax import lax

D_MODEL = 2048
BATCH = 8
SEQ = 4096
DEPTH = 4

HEAD_DIM = 128
D_MIX = D_MODEL
N_HEADS_TOTAL = D_MIX // HEAD_DIM
N_HEADS_B = N_HEADS_TOTAL // 4
N_HEADS_C = (3 * N_HEADS_TOTAL) // 8
N_HEADS_A = N_HEADS_TOTAL - N_HEADS_B - N_HEADS_C
CONV_WIDTH = 4
GDN_CHUNK = 64
SGU_CHUNK = 128
NSA_KV_HEADS = 2
CMP_LEN = 32
CMP_STRIDE = 16
SEL_LEN = 64
SEL_TOPK = 16
SEL_QBLOCK = 32
WINDOW = 512
WIN_BLOCK = 128
RPB_BUCKETS = 32
RPB_MAX_DIST = 128
D_FF = 4 * D_MODEL
NORM_EPS = 1e-6
NEG_BIG = -1e30
SEL_FORCE = 1e9

kernel_name = 'hybrid_gdn_sgu_nsa_trunk'


def _proj_sizes():
    da = N_HEADS_A * HEAD_DIM
    db = N_HEADS_B * HEAD_DIM
    dc = N_HEADS_C * HEAD_DIM
    dkv = NSA_KV_HEADS * HEAD_DIM
    return [da, da, da, da, N_HEADS_A, N_HEADS_A, db, db, dc, dkv, dkv, dkv, dkv, dkv, dkv, 3 * N_HEADS_C]


def _split_points():
    return [int(v) for v in np.cumsum(_proj_sizes())[:-1]]


def _rms_norm(x, gain):
    xf = x.astype(jnp.float32)
    y = xf * lax.rsqrt(jnp.mean(xf * xf, axis=-1, keepdims=True) + NORM_EPS)
    return (y * gain.astype(jnp.float32)).astype(x.dtype)


def _layer_norm(x, gain, bias):
    xf = x.astype(jnp.float32)
    mu = jnp.mean(xf, axis=-1, keepdims=True)
    var = jnp.mean(jnp.square(xf - mu), axis=-1, keepdims=True)
    y = (xf - mu) * lax.rsqrt(var + NORM_EPS) * gain.astype(jnp.float32) + bias.astype(jnp.float32)
    return y.astype(x.dtype)


def _l2norm(x):
    return x * lax.rsqrt(jnp.sum(x * x, axis=-1, keepdims=True) + NORM_EPS)


def _masked_softmax(s, mask):
    s = jnp.where(mask, s.astype(jnp.float32), NEG_BIG)
    m = jnp.max(s, axis=-1, keepdims=True)
    p = jnp.where(mask, jnp.exp(s - m), 0.0)
    return p / jnp.maximum(jnp.sum(p, axis=-1, keepdims=True), 1e-30)


def _t5_bucket(dist):
    n = jnp.maximum(dist, 0)
    max_exact = RPB_BUCKETS // 2
    log_ratio = jnp.log(jnp.maximum(n, 1).astype(jnp.float32) / max_exact) / math.log(RPB_MAX_DIST / max_exact)
    large = jnp.minimum(max_exact + (log_ratio * (RPB_BUCKETS - max_exact)).astype(jnp.int32), RPB_BUCKETS - 1)
    return jnp.where(n < max_exact, n, large)


def _rel_bias(dist, table):
    b = table[_t5_bucket(dist)]
    return jnp.moveaxis(b, -1, 0).reshape(NSA_KV_HEADS, N_HEADS_C // NSA_KV_HEADS, *dist.shape)


def _causal_short_conv(x, w):
    k = w.shape[0]
    s_ = x.shape[1]
    xp = jnp.pad(x, ((0, 0), (k - 1, 0), (0, 0)))
    y = xp[:, 0:s_] * w[0]
    for j in range(1, k):
        y = y + xp[:, j:j + s_] * w[j]
    return y


def _gated_delta_rule(q, k, v, g, beta):
    b_, h_, s_, dk = q.shape
    dv = v.shape[-1]
    c = GDN_CHUNK
    n = s_ // c
    q = q * (dk ** -0.5)
    q, k, v = (t.reshape(b_, h_, n, c, t.shape[-1]) for t in (q, k, v))
    g = jnp.cumsum(g.reshape(b_, h_, n, c), axis=-1)
    beta = beta.reshape(b_, h_, n, c)
    incl = jnp.tril(jnp.ones((c, c), dtype=bool))
    strict = jnp.tril(jnp.ones((c, c), dtype=bool), -1)
    decay = jnp.exp(jnp.where(incl, g[..., :, None] - g[..., None, :], -jnp.inf))
    k_beta = k * beta[..., None]
    v_beta = v * beta[..., None]
    lower = jnp.where(strict, jnp.einsum('bhnik,bhnjk->bhnij', k_beta, k) * decay, 0.0)
    eye = jnp.eye(c, dtype=q.dtype)
    t_inv = lax.linalg.triangular_solve(lower + eye, jnp.broadcast_to(eye, lower.shape),
                                        left_side=True, lower=True, unit_diagonal=True)
    u = t_inv @ v_beta
    w = t_inv @ (k_beta * jnp.exp(g)[..., None])
    a_intra = jnp.where(incl, jnp.einsum('bhnik,bhnjk->bhnij', q, k) * decay, 0.0)

    def step(state, xs):
        q_c, k_c, u_c, w_c, g_c, a_c = xs
        v_new = u_c - w_c @ state
        o_c = (q_c * jnp.exp(g_c)[..., None]) @ state + a_c @ v_new
        g_last = g_c[..., -1:]
        state = state * jnp.exp(g_last)[..., None] + jnp.einsum(
            'bhck,bhcv->bhkv', k_c * jnp.exp(g_last - g_c)[..., None], v_new)
        return state, o_c

    xs = tuple(jnp.moveaxis(t, 2, 0) for t in (q, k, u, w, g, a_intra))
    state0 = jnp.zeros((b_, h_, dk, dv), q.dtype)
    _, o = lax.scan(step, state0, xs)
    return jnp.moveaxis(o, 0, 2).reshape(b_, h_, s_, dv)


def _mixer_gdn(q, k, v, z, b_raw, a_raw, conv_w, a_log, dt_bias, norm_g):
    out_dtype = q.dtype
    bsz, s_, _ = q.shape
    h_, d_ = N_HEADS_A, HEAD_DIM
    qkv = jax.nn.silu(_causal_short_conv(jnp.concatenate([q, k, v], axis=-1), conv_w))
    q, k, v = jnp.split(qkv.astype(jnp.float32), 3, axis=-1)
    heads = lambda t: t.reshape(bsz, s_, h_, d_).transpose(0, 2, 1, 3)
    q, k, v = _l2norm(heads(q)), _l2norm(heads(k)), heads(v)
    beta = jax.nn.sigmoid(b_raw.astype(jnp.float32)).transpose(0, 2, 1)
    g = (-jnp.exp(a_log.astype(jnp.float32))
         * jax.nn.softplus(a_raw.astype(jnp.float32) + dt_bias.astype(jnp.float32))).transpose(0, 2, 1)
    o = _gated_delta_rule(q, k, v, g, beta).transpose(0, 2, 1, 3)
    o = _rms_norm(o, norm_g) * jax.nn.silu(z.astype(jnp.float32).reshape(bsz, s_, h_, d_))
    return o.reshape(bsz, s_, h_ * d_).astype(out_dtype)


def _mixer_sgu(u, v, ln_g, ln_b, w_s, b_s):
    bsz, s_, _ = u.shape
    n = s_ // SGU_CHUNK
    u = jax.nn.gelu(u)
    v = _layer_norm(jax.nn.gelu(v), ln_g, ln_b)
    v = v.reshape(bsz, n, SGU_CHUNK, N_HEADS_B, HEAD_DIM)
    causal = jnp.tril(jnp.ones((SGU_CHUNK, SGU_CHUNK), dtype=bool))
    w = jnp.where(causal, w_s, 0.0).astype(v.dtype)
    mixed = jnp.einsum('gts,bnsgc->bntgc', w, v) + b_s.T[None, None, :, :, None].astype(v.dtype)
    return u * mixed.reshape(bsz, s_, N_HEADS_B * HEAD_DIM)


def _mixer_nsa(q, k_cmp, v_cmp, k_slc, v_slc, k_win, v_win, gate_raw,
               q_norm_g, k_norm_g, cmp_pos, cmp_w1, cmp_w2, rel_bias):
    bsz, s_, _ = q.shape
    g_, h_, d_ = NSA_KV_HEADS, N_HEADS_C, HEAD_DIM
    r_ = h_ // g_
    pos = jnp.arange(s_, dtype=jnp.int32)
    q = _rms_norm(q.reshape(bsz, s_, g_, r_, d_), q_norm_g).transpose(0, 2, 3, 1, 4) * (d_ ** -0.5)
    kv_heads = lambda t: t.reshape(bsz, s_, g_, d_).transpose(0, 2, 1, 3)

    n_cmp = (s_ - CMP_LEN) // CMP_STRIDE + 1
    cmp_start = np.arange(n_cmp) * CMP_STRIDE
    blk_idx = cmp_start[:, None] + np.arange(CMP_LEN)[None, :]

    def compress(t, p, w1, w2):
        blocks = t[:, :, blk_idx] + p
        return jax.nn.gelu(blocks.reshape(bsz, g_, n_cmp, CMP_LEN * d_) @ w1) @ w2

    kc = _rms_norm(compress(kv_heads(k_cmp), cmp_pos[0], cmp_w1[0], cmp_w2[0]), k_norm_g)
    vc = compress(kv_heads(v_cmp), cmp_pos[1], cmp_w1[1], cmp_w2[1])
    cmp_end = jnp.asarray(cmp_start + CMP_LEN - 1, dtype=jnp.int32)
    dist_c = pos[:, None] - cmp_end[None, :]
    s_c = jnp.einsum('bgrsd,bgnd->bgrsn', q, kc) + _rel_bias(dist_c, rel_bias)
    p_c = _masked_softmax(s_c, dist_c >= 0)
    o_c = jnp.einsum('bgrsn,bgnd->bgrsd', p_c.astype(vc.dtype), vc)

    n_sel = s_ // SEL_LEN
    sel_start = np.arange(n_sel) * SEL_LEN
    overlap = (cmp_start[:, None] < sel_start[None, :] + SEL_LEN) & (cmp_start[:, None] + CMP_LEN > sel_start[None, :])
    importance = jnp.einsum('bgrsn,nj->bgsj', p_c, jnp.asarray(overlap, jnp.float32))
    cur = pos // SEL_LEN
    jsel = jnp.arange(n_sel, dtype=jnp.int32)
    causal_blk = jsel[None, :] <= cur[:, None]
    forced = (jsel[None, :] == 0) | (jsel[None, :] == cur[:, None]) | (jsel[None, :] == cur[:, None] - 1)
    score = jnp.where(forced, SEL_FORCE, jnp.where(causal_blk, importance, NEG_BIG))
    n_top = min(SEL_TOPK, n_sel)
    top_score, top_idx = lax.top_k(score, n_top)
    top_ok = top_score > 0.5 * NEG_BIG

    ks = _rms_norm(kv_heads(k_slc), k_norm_g).reshape(bsz, g_, n_sel, SEL_LEN, d_)
    vs = kv_heads(v_slc).reshape(bsz, g_, n_sel, SEL_LEN, d_)
    qb_len = SEL_QBLOCK
    n_qb = s_ // qb_len
    gather = jax.vmap(jax.vmap(lambda blocks, idx: blocks[idx]))
    table_g = rel_bias.reshape(RPB_BUCKETS, g_, r_).transpose(1, 0, 2)
    g_ar = jnp.arange(g_)[None, :, None, None, None]

    def sel_block(args):
        qb, idxb, okb, i = args
        kg = gather(ks, idxb)
        vg = gather(vs, idxb)
        qpos = i * qb_len + jnp.arange(qb_len, dtype=jnp.int32)
        kpos = idxb[..., None] * SEL_LEN + jnp.arange(SEL_LEN, dtype=jnp.int32)
        dist = qpos[None, None, :, None, None] - kpos
        mask = okb[..., None] & (dist >= 0)
        bias = table_g[g_ar, _t5_bucket(dist)].transpose(0, 1, 5, 2, 3, 4)
        s = jnp.einsum('bgrqd,bgqnld->bgrqnl', qb, kg) + bias
        p = _masked_softmax(s.reshape(bsz, g_, r_, qb_len, n_top * SEL_LEN),
                            mask.reshape(bsz, g_, 1, qb_len, n_top * SEL_LEN))
        return jnp.einsum('bgrqm,bgqmd->bgrqd', p.astype(vg.dtype),
                          vg.reshape(bsz, g_, qb_len, n_top * SEL_LEN, d_))

    xs = (jnp.moveaxis(q.reshape(bsz, g_, r_, n_qb, qb_len, d_), 3, 0),
          jnp.moveaxis(top_idx.reshape(bsz, g_, n_qb, qb_len, n_top), 2, 0),
          jnp.moveaxis(top_ok.reshape(bsz, g_, n_qb, qb_len, n_top), 2, 0),
          jnp.arange(n_qb, dtype=jnp.int32))
    o_s = lax.map(sel_block, xs)
    o_s = jnp.moveaxis(o_s, 0, 3).reshape(bsz, g_, r_, s_, d_)

    n_prev = WINDOW // WIN_BLOCK
    n_wb = s_ // WIN_BLOCK
    band_len = (n_prev + 1) * WIN_BLOCK

    def band(t):
        tp = jnp.pad(t, ((0, 0), (0, 0), (n_prev * WIN_BLOCK, 0), (0, 0))).reshape(bsz, g_, n_wb + n_prev, WIN_BLOCK, d_)
        return jnp.concatenate([tp[:, :, j:j + n_wb] for j in range(n_prev + 1)], axis=3)

    kw = band(_rms_norm(kv_heads(k_win), k_norm_g))
    vw = band(kv_heads(v_win))
    qpos_w = pos.reshape(n_wb, WIN_BLOCK)
    kpos_w = (jnp.arange(n_wb, dtype=jnp.int32)[:, None] - n_prev) * WIN_BLOCK + jnp.arange(band_len, dtype=jnp.int32)[None, :]
    dist_w = qpos_w[:, :, None] - kpos_w[:, None, :]
    mask_w = (dist_w >= 0) & (dist_w < WINDOW) & (kpos_w[:, None, :] >= 0)
    s_w = jnp.einsum('bgrnqd,bgnkd->bgrnqk', q.reshape(bsz, g_, r_, n_wb, WIN_BLOCK, d_), kw) + _rel_bias(dist_w, rel_bias)
    p_w = _masked_softmax(s_w, mask_w)
    o_w = jnp.einsum('bgrnqk,bgnkd->bgrnqd', p_w.astype(vw.dtype), vw).reshape(bsz, g_, r_, s_, d_)

    gates = jax.nn.sigmoid(gate_raw.astype(jnp.float32)).reshape(bsz, s_, 3, g_, r_)
    gates = gates.transpose(2, 0, 3, 4, 1)[..., None].astype(o_c.dtype)
    o = gates[0] * o_c + gates[1] * o_s + gates[2] * o_w
    return o.transpose(0, 3, 1, 2, 4).reshape(bsz, s_, h_ * d_)


def setup_inputs(seed: int = 0) -> dict:
    key = jax.random.key(seed)
    ks = jax.random.split(key, 21)
    f32 = jnp.float32
    nrm = lambda k, shape, scale: scale * jax.random.normal(k, shape, f32)
    gain = lambda k, shape: 1.0 + 0.02 * jax.random.normal(k, shape, f32)
    d_proj = sum(_proj_sizes())
    x = jax.random.normal(ks[0], (BATCH, SEQ, D_MODEL), f32)
    attn_norm = gain(ks[1], (DEPTH, D_MODEL))
    w_in = nrm(ks[2], (DEPTH, D_MODEL, d_proj), D_MODEL ** -0.5)
    conv_a = nrm(ks[3], (DEPTH, CONV_WIDTH, 3 * N_HEADS_A * HEAD_DIM), CONV_WIDTH ** -0.5)
    a_log = jnp.log(jax.random.uniform(ks[4], (DEPTH, N_HEADS_A), f32, 1.0, 16.0))
    dt = jnp.exp(jax.random.uniform(ks[5], (DEPTH, N_HEADS_A), f32, math.log(1e-3), math.log(1e-1)))
    dt_bias = dt + jnp.log(-jnp.expm1(-dt))
    gdn_norm = gain(ks[6], (DEPTH, HEAD_DIM))
    sgu_ln_g = gain(ks[7], (DEPTH, N_HEADS_B * HEAD_DIM))
    sgu_ln_b = nrm(ks[8], (DEPTH, N_HEADS_B * HEAD_DIM), 0.02)
    sgu_w = nrm(ks[9], (DEPTH, N_HEADS_B, SGU_CHUNK, SGU_CHUNK), SGU_CHUNK ** -0.5)
    sgu_b = gain(ks[10], (DEPTH, N_HEADS_B, SGU_CHUNK))
    nsa_q_norm = gain(ks[11], (DEPTH, HEAD_DIM))
    nsa_k_norm = gain(ks[12], (DEPTH, HEAD_DIM))
    cmp_pos = nrm(ks[13], (DEPTH, 2, CMP_LEN, HEAD_DIM), 0.02)
    cmp_w1 = nrm(ks[14], (DEPTH, 2, CMP_LEN * HEAD_DIM, HEAD_DIM), (CMP_LEN * HEAD_DIM) ** -0.5)
    cmp_w2 = nrm(ks[15], (DEPTH, 2, HEAD_DIM, HEAD_DIM), HEAD_DIM ** -0.5)
    rel_bias = nrm(ks[16], (RPB_BUCKETS, N_HEADS_C), 0.2)
    w_out = nrm(ks[17], (DEPTH, D_MIX, D_MODEL), D_MIX ** -0.5)
    mlp_norm = gain(ks[18], (DEPTH, D_MODEL))
    w_up = nrm(ks[19], (DEPTH, D_MODEL, D_FF), D_MODEL ** -0.5)
    w_down = nrm(ks[20], (DEPTH, D_FF, D_MODEL), D_FF ** -0.5)
    return {'x': x, 'attn_norm': attn_norm, 'w_in': w_in, 'conv_a': conv_a, 'a_log': a_log,
            'dt_bias': dt_bias, 'gdn_norm': gdn_norm, 'sgu_ln_g': sgu_ln_g, 'sgu_ln_b': sgu_ln_b,
            'sgu_w': sgu_w, 'sgu_b': sgu_b, 'nsa_q_norm': nsa_q_norm, 'nsa_k_norm': nsa_k_norm,
            'cmp_pos': cmp_pos, 'cmp_w1': cmp_w1, 'cmp_w2': cmp_w2, 'rel_bias': rel_bias,
            'w_out': w_out, 'mlp_norm': mlp_norm, 'w_up': w_up, 'w_down': w_down}


def reference(x, attn_norm, w_in, conv_a, a_log, dt_bias, gdn_norm, sgu_ln_g, sgu_ln_b, sgu_w, sgu_b,
              nsa_q_norm, nsa_k_norm, cmp_pos, cmp_w1, cmp_w2, rel_bias, w_out, mlp_norm, w_up, w_down):
    for l in range(DEPTH):
        h = _rms_norm(x, attn_norm[l])
        (qa, ka, va, za, ba, aa, ub, vb, qc, kcc, vcc, ksl, vsl, kwn, vwn, gc) = jnp.split(
            h @ w_in[l], _split_points(), axis=-1)
        mix = jnp.concatenate([
            _mixer_gdn(qa, ka, va, za, ba, aa, conv_a[l], a_log[l], dt_bias[l], gdn_norm[l]),
            _mixer_sgu(ub, vb, sgu_ln_g[l], sgu_ln_b[l], sgu_w[l], sgu_b[l]),
            _mixer_nsa(qc, kcc, vcc, ksl, vsl, kwn, vwn, gc, nsa_q_norm[l], nsa_k_norm[l],
                       cmp_pos[l], cmp_w1[l], cmp_w2[l], rel_bias),
        ], axis=-1)
        x = x + mix @ w_out[l]
        h = _rms_norm(x, mlp_norm[l])
        x = x + jnp.square(jax.nn.relu(h @ w_up[l])) @ w_down[l]
    return x
```

```python
from contextlib import ExitStack
import numpy as np
import concourse.bass as bass
import concourse.mybir as mybir
from concourse.bass_utils import run_bass_kernel_spmd

F32 = mybir.dt.float32
BF16 = mybir.dt.bfloat16
I32 = mybir.dt.int32
AF = mybir.ActivationFunctionType
ALU = mybir.AluOpType
AX = mybir.AxisListType

S = 4096
D = 2048
DEPTH = 4
DFF = 8192
NPROJ = 6430
NJ_IN = 51
TT = 512
NT = S // TT
EPS = 1e-6
GDN_OFF = 1
BG_CAST = True
IN_COLMAP = [(0, 0, 3072), (3084, 3072, 1024), (4108, 4096, 2304), (3072, 6400, 12), (6412, 6412, 18)]
CH_QA, CH_KA, CH_VA, CH_ZA = 0, 6, 12, 18
CH_UB, CH_VB = 24, 28
CH_QC, CH_KCC, CH_VCC, CH_KSL, CH_VSL, CH_KWN, CH_VWN = 32, 38, 40, 42, 44, 46, 48
CH_SM = 50


class KB:
    def __init__(self, nc, es):
        self.nc = nc
        self.es = es
        self.eng = {"pe": nc.tensor, "act": nc.scalar, "dve": nc.vector, "pool": nc.gpsimd, "sp": nc.sync}
        self.sem = {k: es.enter_context(nc.semaphore("s_" + k)) for k in self.eng}
        self.cnt = {k: 0 for k in self.eng}
        self.waited = {k: {} for k in self.eng}
        self.res = {}
        self.dsem = {}
        self.semname = {}

    def _deps(self, e, r, w):
        need = {}

        def add(tok, raw):
            sem, val, te = tok
            if te == e and e == "pe":
                return
            if te == e and not raw:
                return
            k = id(sem)
            if k not in need or need[k][1] < val:
                need[k] = (sem, val)

        r = list(r)
        w = list(w)
        for k in list(r):
            if k.startswith("@"):
                w.append(k)
        for k in r:
            st = self.res.get(k)
            if st and st[0]:
                add(st[0], True)
        for k in w:
            st = self.res.get(k)
            if st:
                if st[0]:
                    add(st[0], False)
                for t in st[1].values():
                    add(t, False)
        wd = self.waited[e]
        for k, (sem, val) in need.items():
            if wd.get(k, 0) < val:
                self.eng[e].wait_ge(sem, val)
                wd[k] = val

    def _upd(self, tok, r, w):
        w = list(w) + [k for k in r if k.startswith("@")]
        r = [k for k in r if not k.startswith("@")]
        for k in r:
            st = self.res.setdefault(k, [None, {}])
            st[1][id(tok[0])] = tok
        for k in w:
            self.res[k] = [tok, {}]

    def op(self, e, fn, r=(), w=()):
        self._deps(e, r, w)
        ins = fn(self.eng[e])
        self.cnt[e] += 1
        ins.then_inc(self.sem[e], 1)
        self._upd((self.sem[e], self.cnt[e], e), r, w)

    def dma(self, out, in_, key, r=(), w=(), q="sp", **kw):
        self._deps(q, r, w)
        if key not in self.dsem:
            self.dsem[key] = [self.es.enter_context(self.nc.semaphore("d%d" % len(self.dsem))), 0]
        ds = self.dsem[key]
        ds[1] += 16
        self.eng[q].dma_start(out=out, in_=in_, **kw).then_inc(ds[0], 16)
        self._upd((ds[0], ds[1], "dma"), r, w)

    def barrier(self):
        for e in self.eng:
            wd = self.waited[e]
            for e2 in self.eng:
                if e2 != e and self.cnt[e2] > wd.get(id(self.sem[e2]), 0):
                    self.eng[e].wait_ge(self.sem[e2], self.cnt[e2])
                    wd[id(self.sem[e2])] = self.cnt[e2]
            for key, (sem, val) in self.dsem.items():
                if val > wd.get(id(sem), 0):
                    self.eng[e].wait_ge(sem, val)
                    wd[id(sem)] = val
        self.res = {}


def dram_ap(handle, offset, pattern):
    return bass.AP(handle, offset, pattern)


class Prog:
    def __init__(self, n_layers=DEPTH, dbg=None, mix_in=False, enable=("gdn", "sgu", "nsa"), phases=("cast", "inproj", "mix", "ffn")):
        self.enable = enable
        self.phases = phases
        self.n_layers = n_layers
        self.dbg = dbg or ()
        self.mix_in = mix_in
        self.nc = bass.Bass("TRN2", target_bir_lowering=False)
        self.build()

    def sbt(self, name, shape, dt):
        self._uid = getattr(self, "_uid", 0) + 1
        return self.nc.sbuf_tensor("%s_%d" % (name, self._uid), shape, dt)

    def pst(self, name, shape, dt):
        self._uid = getattr(self, "_uid", 0) + 1
        return self.nc.psum_tensor("%s_%d" % (name, self._uid), shape, dt)

    def build(self):
        nc = self.nc
        L = DEPTH
        di = lambda n, s: nc.dram_tensor(n, s, F32, kind="ExternalInput")
        self.xT_in = di("xT", [D, S])
        self.attn_norm = di("attn_norm", [L, D])
        self.w_in = di("w_in", [L, D, NPROJ])
        self.conv_a = di("conv_a", [L, 4, 2304])
        self.a_log = di("a_log", [L, 6])
        self.dt_bias = di("dt_bias", [L, 6])
        self.gdn_norm = di("gdn_norm", [L, 128])
        self.sgu_ln_g = di("sgu_ln_g", [L, 512])
        self.sgu_ln_b = di("sgu_ln_b", [L, 512])
        self.sgu_w = di("sgu_w", [L, 4, 128, 128])
        self.sgu_b = di("sgu_b", [L, 4, 128])
        self.nsa_q_norm = di("nsa_q_norm", [L, 128])
        self.nsa_k_norm = di("nsa_k_norm", [L, 128])
        self.cmp_pos = di("cmp_pos", [L, 2, 32, 128])
        self.cmp_w1 = di("cmp_w1", [L, 2, 4096, 128])
        self.cmp_w2 = di("cmp_w2", [L, 2, 128, 128])
        self.rel_bias = di("rel_bias", [32, 6])
        self.w_out = di("w_out", [L, D, D])
        self.mlp_norm = di("mlp_norm", [L, D])
        self.w_up = di("w_up", [L, D, DFF])
        self.w_down = di("w_down", [L, DFF, D])
        self.cst = di("cst", [128, 1024])
        self.gains_in = di("gains_in", [128, 2 * L * 16])
        self.conv_pl = di("conv_pl", [L, 128, 72])
        self.gdnn_in = di("gdnn_in", [128, L])
        self.cst2 = di("cst2", [128, 2562])
        self.nqk_in = di("nqk_in", [128, 2 * L])
        self.cpos_pl = di("cpos_pl", [L, 2, 128, 32])
        self.eblk_in = di("eblk_in", [64, S])
        self.selc_in = di("selc_in", [3, S, 64])
        if self.mix_in:
            self.mix_dbg = di("mix_dbg", [D, S])
        self.yT = nc.dram_tensor("yT", [D, S], F32, kind="ExternalOutput")
        ds = lambda n, s, dt: nc.dram_tensor(n, s, dt, kind="Internal")
        self.xres = ds("xres", [D, S], F32)
        self.projT = ds("projT", [NJ_IN * 128, S], F32)
        self.small_tm = ds("small_tm", [S, 32], F32)
        self.mixT = ds("mixT", [D, S], BF16)
        self.wt_in = [ds("wt_in%d" % l, [NJ_IN, 128, 16, 128], BF16) for l in range(L)]
        self.wt_out = [ds("wt_out%d" % l, [16, 128, 16, 128], BF16) for l in range(L)]
        self.wt_up = [ds("wt_up%d" % l, [64, 128, 16, 128], BF16) for l in range(L)]
        self.wt_dn = [ds("wt_dn%d" % l, [16, 128, 64, 128], BF16) for l in range(L)]
        self.fvec = ds("fvec", [12, self.LF], F32)
        self.Btab = [ds("btab%d" % i, [128, self.LF], F32) for i in range(12)]
        self.fvecb = ds("fvecb", [12, self.LF], BF16)
        self.Btabb = [ds("btabb%d" % i, [128, self.LF], BF16) for i in range(12)]
        self.dbg_out = {}
        if "proj" in self.dbg:
            self.dbg_out["proj"] = nc.dram_tensor("dbg_proj", [NJ_IN * 128, S], F32, kind="ExternalOutput")
            self.dbg_out["small"] = nc.dram_tensor("dbg_small", [S, 32], F32, kind="ExternalOutput")
        if "mix" in self.dbg:
            self.dbg_out["mix"] = nc.dram_tensor("dbg_mix", [D, S], BF16, kind="ExternalOutput")

        with ExitStack() as es:
            self.kb = kb = KB(nc, es)
            sb = lambda n, s, dt: es.enter_context(self.sbt(n, s, dt))
            self.c_f = sb("c_f", [128, 1024], F32)
            self.ones_f = sb("ones_f", [128, 128], F32)
            self.gains = sb("gains", [128, 2 * L * 16], F32)
            kb.dma(self.c_f[:], self.cst.ap(), "cst", w=["cst"])
            kb.op("dve", lambda e: e.memset(self.ones_f[:], 1.0), w=["ones"])
            kb.dma(self.gains[:], self.gains_in.ap(), "g1", w=["gains"])
            self.gdnn = sb("gdnn", [128, L], F32)
            kb.dma(self.gdnn[:], self.gdnn_in.ap(), "g3", w=["gdnn"])
            self.ident = self.c_f[:, 0:128]
            self.c2 = sb("c2", [128, 2562], F32)
            self.nqk = sb("nqk", [128, 2 * L], F32)
            self.b31bc = sb("b31bc", [128, 6], F32)
            kb.dma(self.c2[:], self.cst2.ap(), "cst2", w=["cst2"])
            kb.dma(self.nqk[:], self.nqk_in.ap(), "nqk", w=["nqk"])
            kb.barrier()
            if "nsa" in self.enable and not self.mix_in:
                self.nsa_tables()
            if "cast" in self.phases:
                with self.sbt("cu_f", [128, 2, 4224], F32) as sf0, self.sbt("cu_b", [128, 2, 4224], BF16) as sbf0:
                    specs0 = self.cast_specs(0, ("in",)) if BG_CAST else [s_ for l_ in range(self.n_layers) for s_ in self.cast_specs(l_, ("in", "rest"))]
                    tk = self.cast_runner(specs0, sf0, sbf0, 2, 2, "cu")
                    while tk():
                        pass
                    kb.barrier()
            kb.barrier()
            for l in range(self.n_layers):
                if "inproj" in self.phases:
                    self.phase_inproj(l)
                kb.barrier()
                if "proj" in self.dbg and l == 0:
                    self.copy_dram(self.dbg_out["proj"], self.projT, NJ_IN * 128, S, F32)
                    kb.dma(self.dbg_out["small"].ap(), self.small_tm.ap(), "dbgs")
                    kb.barrier()
                if self.mix_in:
                    self.cast_mix_dbg()
                elif "mix" in self.phases:
                    self.phase_mixers(l)
                kb.barrier()
                if "mix" in self.dbg and l == 0:
                    kb.dma(self.dbg_out["mix"].ap(), self.mixT.ap(), "dbgm")
                    kb.barrier()
                if "ffn" in self.phases:
                    self.phase_ffn(l, last=(l == self.n_layers - 1))
                else:
                    kb.dma(self.yT.ap()[0:128, :], self.xT_in.ap()[0:128, :], "dummyy")
                kb.barrier()

    def copy_dram(self, dst, src, rows, cols, dt):
        kb = self.kb
        for r0 in range(0, rows, 1024):
            n = min(1024, rows - r0)
            kb.dma(dst.ap()[r0:r0 + n, :], src.ap()[r0:r0 + n, :], "cpd")

    def cast_mix_dbg(self):
        kb, nc = self.kb, self.nc
        with self.sbt("mdb_f", [128, S], F32) as tf, self.sbt("mdb_b", [128, S], BF16) as tb:
            for k in range(16):
                kb.dma(tf[:], self.mix_dbg.ap()[k * 128:(k + 1) * 128, :], "mdbl", w=["mdbf"])
                kb.op("dve", lambda e: e.tensor_copy(out=tb[:], in_=tf[:]), r=["mdbf"], w=["mdbb"])
                kb.dma(self.mixT.ap()[k * 128:(k + 1) * 128, :], tb[:], "mdbs", r=["mdbb"])
            kb.barrier()

    def phase_cast(self, l):
        kb, nc = self.kb, self.nc
        with self.sbt("cs_f", [128, 2, 8192], F32) as sf, self.sbt("cs_b", [128, 2, 8192], BF16) as sbf:
            self._cast_i = 0

            def unit(src_ap_list, n_src_cols, colmap, dst_fn, n_dst_cols, pad_from=None):
                i = self._cast_i
                self._cast_i += 1
                s = i % 2
                fk, bk = "csf%d" % s, "csb%d" % s
                for (o, ap, n) in src_ap_list:
                    kb.dma(sf[:, s, o:o + n], ap, "csl%d" % s, w=[fk])
                pieces = []
                for (sc, dc, n) in colmap:
                    step = (n + 3) // 4 if n >= 1536 else n
                    for a in range(0, n, step):
                        pieces.append((sc + a, dc + a, min(step, n - a)))
                engs = ["dve", "act"]
                first = True
                for pi, (sc, dc, n) in enumerate(pieces):
                    e = engs[pi % 2]
                    if e == "act":
                        f = lambda en, sc=sc, dc=dc, n=n: en.copy(out=sbf[:, s, dc:dc + n], in_=sf[:, s, sc:sc + n])
                    else:
                        f = lambda en, sc=sc, dc=dc, n=n: en.tensor_copy(out=sbf[:, s, dc:dc + n], in_=sf[:, s, sc:sc + n])
                    kb.op(e, f, r=[fk], w=[bk + "_%d" % pi])
                if pad_from is not None:
                    kb.op("pool", lambda en: en.memset(sbf[:, s, pad_from:n_dst_cols], 0.0), w=[bk + "_pad"])
                rk = [bk + "_%d" % pi for pi in range(len(pieces))] + ([bk + "_pad"] if pad_from is not None else [])
                for (dst_ap, c0, n) in dst_fn:
                    kb.dma(dst_ap, sbf[:, s, c0:c0 + n].rearrange("p (j c) -> p j c", c=128), "css%d" % s, r=rk, q="pool")

            for kc in range(16):
                src = self.w_in.ap()[l, kc * 128:(kc + 1) * 128, :]
                dst = self.wt_in[l].ap()[:, :, kc, :].rearrange("j p c -> p j c")
                unit([(0, src, NPROJ)], NPROJ, IN_COLMAP, [(dst, 0, NJ_IN * 128)], NJ_IN * 128, pad_from=NPROJ)
            for kc in range(16):
                src = self.w_up.ap()[l, kc * 128:(kc + 1) * 128, :]
                dst = self.wt_up[l].ap()[:, :, kc, :].rearrange("j p c -> p j c")
                unit([(0, src, DFF)], DFF, [(0, 0, DFF)], [(dst, 0, DFF)], DFF)
            for k4 in range(16):
                srcs = [(i * 2048, self.w_down.ap()[l, (k4 * 4 + i) * 128:(k4 * 4 + i + 1) * 128, :], 2048) for i in range(4)]
                dsts = [(self.wt_dn[l].ap()[:, :, k4 * 4 + i, :].rearrange("j p c -> p j c"), i * 2048, 2048) for i in range(4)]
                unit(srcs, 8192, [(0, 0, 8192)], dsts, 8192)
            for k4 in range(4):
                srcs = [(i * 2048, self.w_out.ap()[l, (k4 * 4 + i) * 128:(k4 * 4 + i + 1) * 128, :], 2048) for i in range(4)]
                dsts = [(self.wt_out[l].ap()[:, :, k4 * 4 + i, :].rearrange("j p c -> p j c"), i * 2048, 2048) for i in range(4)]
                unit(srcs, 8192, [(0, 0, 8192)], dsts, 8192)
            kb.barrier()


    def cast_specs(self, l, which):
        specs = []
        if "in" in which:
            for kc in range(16):
                row = self.w_in.ap()[l, kc * 128:(kc + 1) * 128, :]
                dA = self.wt_in[l].ap()[0:32, :, kc, :].rearrange("j p c -> p j c")
                dB = self.wt_in[l].ap()[32:51, :, kc, :].rearrange("j p c -> p j c")
                specs.append(([(0, row[:, 0:4108], 4108)], [(0, 0, 3072), (3084, 3072, 1024)], None, [(dA, 0, 4096)]))
                specs.append(([(0, row[:, 4108:6430], 2322), (2322, row[:, 3072:3084], 12)],
                              [(0, 0, 2304), (2322, 2304, 12), (2304, 2316, 18)], (2334, 2432), [(dB, 0, 2432)]))
        if "rest" in which:
            for kc in range(16):
                for hf in range(2):
                    src = self.w_up.ap()[l, kc * 128:(kc + 1) * 128, hf * 4096:(hf + 1) * 4096]
                    dst = self.wt_up[l].ap()[hf * 32:(hf + 1) * 32, :, kc, :].rearrange("j p c -> p j c")
                    specs.append(([(0, src, 4096)], [(0, 0, 4096)], None, [(dst, 0, 4096)]))
            for k2 in range(32):
                loads = [(i * 2048, self.w_down.ap()[l, (k2 * 2 + i) * 128:(k2 * 2 + i + 1) * 128, :], 2048) for i in range(2)]
                stores = [(self.wt_dn[l].ap()[:, :, k2 * 2 + i, :].rearrange("j p c -> p j c"), i * 2048, 2048) for i in range(2)]
                specs.append((loads, [(0, 0, 4096)], None, stores))
            for k2 in range(8):
                loads = [(i * 2048, self.w_out.ap()[l, (k2 * 2 + i) * 128:(k2 * 2 + i + 1) * 128, :], 2048) for i in range(2)]
                stores = [(self.wt_out[l].ap()[:, :, k2 * 2 + i, :].rearrange("j p c -> p j c"), i * 2048, 2048) for i in range(2)]
                specs.append((loads, [(0, 0, 4096)], None, stores))
        return specs

    def cast_runner(self, specs, sf, sbf, nf, nb, tag):
        kb = self.kb
        st = {"i": 0, "loaded": 0}

        def load(u):
            s = u % nf
            for (o, ap, n) in specs[u][0]:
                kb.dma(sf[:, s, o:o + n], ap, "%sl%d" % (tag, s), w=["%sf%d" % (tag, s)])

        def tick():
            i = st["i"]
            if i >= len(specs):
                return False
            while st["loaded"] <= min(i, len(specs) - 1) or (i == 0 and st["loaded"] <= min(nf - 1, len(specs) - 1)):
                load(st["loaded"])
                st["loaded"] += 1
            s, sb_ = i % nf, i % nb
            fk, bk = "%sf%d" % (tag, s), "%sb%d" % (tag, sb_)
            loads, convs, pad, stores = specs[i]
            keys = []
            pi = 0
            for (sc, dc, n) in convs:
                step = (n + 1) // 2 if n >= 1024 else n
                for a in range(0, n, step):
                    m = min(step, n - a)
                    e = "dve" if pi % 2 == 0 else "act"
                    k_ = "%s_%d" % (bk, pi)
                    if e == "act":
                        kb.op("act", lambda en, sc=sc, dc=dc, a=a, m=m: en.copy(out=sbf[:, sb_, dc + a:dc + a + m], in_=sf[:, s, sc + a:sc + a + m]), r=[fk], w=[k_])
                    else:
                        kb.op("dve", lambda en, sc=sc, dc=dc, a=a, m=m: en.tensor_copy(out=sbf[:, sb_, dc + a:dc + a + m], in_=sf[:, s, sc + a:sc + a + m]), r=[fk], w=[k_])
                    keys.append(k_)
                    pi += 1
            if pad is not None:
                kb.op("pool", lambda en: en.memset(sbf[:, sb_, pad[0]:pad[1]], 0.0), w=[bk + "_pad"])
                keys.append(bk + "_pad")
            allk = ["%s_%d" % (bk, q) for q in range(8)] + [bk + "_pad"]
            for (dst_ap, c0, n) in stores:
                kb.dma(dst_ap, sbf[:, sb_, c0:c0 + n].rearrange("p (j c) -> p j c", c=128), "%ss%d" % (tag, sb_), r=keys, w=[], q="pool")
            for k_ in allk:
                if k_ not in keys:
                    stt = kb.res.setdefault(k_, [None, {}])
                    for kk in keys[:1]:
                        stt[1].update(kb.res[kk][1])
            st["i"] += 1
            if st["loaded"] < len(specs) and st["loaded"] <= i + nf:
                load(st["loaded"])
                st["loaded"] += 1
            return True

        return tick

    def rmsnorm_tile(self, xt, ht, gain_col0, sq, rstd, ps_ss, tag, xkeys):
        kb = self.kb
        htag = tag
        tag = tag[0]
        for kc in range(16):
            s = kc % 2
            kb.op("act", lambda e, kc=kc, s=s: e.activation(out=sq[:, s, :], in_=xt[:, kc, :], func=AF.Square),
                  r=[xkeys[kc]], w=[tag + "sq%d" % s])
            kb.op("pe", lambda e, kc=kc, s=s: e.matmul(ps_ss[:], self.ones_f[:], sq[:, s, :], start=(kc == 0), stop=(kc == 15)),
                  r=[tag + "sq%d" % s, "ones"], w=[tag + "ss"])
        kb.op("act", lambda e: e.activation(out=rstd[:], in_=ps_ss[:], func=AF.Ln, bias=self.eps_t[:, 0:1], scale=1.0 / D),
              r=[tag + "ss", "eps"], w=[tag + "rstd"])
        kb.op("act", lambda e: e.activation(out=rstd[:], in_=rstd[:], func=AF.Exp, scale=-0.5), r=[tag + "rstd"], w=[tag + "rstd"])
        for kc in range(16):
            kb.op("dve", lambda e, kc=kc: e.scalar_tensor_tensor(out=ht[:, kc, :], in0=xt[:, kc, :],
                                                              scalar=self.gains[:, gain_col0 + kc:gain_col0 + kc + 1],
                                                              in1=rstd[:], op0=ALU.mult, op1=ALU.mult),
                  r=[xkeys[kc], tag + "rstd", "gains"], w=[htag + "h%d" % kc])

    def phase_inproj(self, l):
        kb, nc = self.kb, self.nc
        xsrc = self.xT_in if l == 0 else self.xres
        with ExitStack() as es:
            sb = lambda n, s, dt: es.enter_context(self.sbt(n, s, dt))
            xt = sb("a_x", [128, 16, TT], F32)
            ht2 = sb("a_h", [128, 2, 16, TT], BF16)
            sq = sb("a_sq", [128, 2, TT], F32)
            rstd = sb("a_rstd", [128, TT], F32)
            self.eps_t = sb("a_eps", [128, 1], F32)
            NW = 4
            wt = sb("a_w", [128, NW, 16, 128], BF16)
            NO = 4
            ot = sb("a_o", [128, NO, TT], F32)
            osm = sb("a_osm", [128, 4, 32], F32)
            ps_ss = es.enter_context(self.pst("a_pss", [128, TT], F32))
            ps_o = [es.enter_context(self.pst("a_po%d" % i, [128, TT], F32)) for i in range(4)]
            ps_sm = es.enter_context(self.pst("a_psm", [128, 4, 32], F32))
            kb.op("pool", lambda e: e.memset(self.eps_t[:], EPS), w=["eps"])
            ui = 0

            def prep_tile(t):
                t0 = t * TT
                kb.dma(xt[:], xsrc.ap()[:, t0:t0 + TT].rearrange("(k p) t -> p k t", p=128), "ax", w=["ax"])
                if l == 0:
                    kb.dma(self.xres.ap()[:, t0:t0 + TT].rearrange("(k p) t -> p k t", p=128), xt[:], "axs", r=["ax"], q="pool")
                self.rmsnorm_tile(xt, ht2[:, t % 2], l * 16, sq, rstd, ps_ss, "a%d" % (t % 2), ["ax"] * 16)

            prep_tile(0)
            for t in range(NT):
                t0 = t * TT
                ht = ht2[:, t % 2]
                HK = ["a%dh%d" % (t % 2, k) for k in range(16)]
                for j in range(NJ_IN):
                    if j == 6 and t + 1 < NT:
                        prep_tile(t + 1)
                    ws = ui % NW
                    pb = ui % 4
                    os_ = ui % NO
                    ui += 1
                    kb.dma(wt[:, ws], self.wt_in[l].ap()[j], "aw%d" % ws, w=["aw%d" % ws])
                    M = 128 if j < CH_SM else 30
                    for kc in range(16):
                        kb.op("pe", lambda e, kc=kc, ws=ws, pb=pb, M=M: e.matmul(ps_o[pb][0:M, :], wt[:, ws, kc, 0:M], ht[:, kc, :],
                                                                                 start=(kc == 0), stop=(kc == 15)),
                              r=["aw%d" % ws, HK[kc]], w=["apo%d" % pb])
                    ev = "act" if ui % 2 == 0 else "dve"
                    if ev == "act":
                        kb.op("act", lambda e, pb=pb, os_=os_, M=M: e.copy(out=ot[0:M, os_, :], in_=ps_o[pb][0:M, :]),
                              r=["apo%d" % pb], w=["ao%d" % os_])
                    else:
                        kb.op("dve", lambda e, pb=pb, os_=os_, M=M: e.tensor_copy(out=ot[0:M, os_, :], in_=ps_o[pb][0:M, :]),
                              r=["apo%d" % pb], w=["ao%d" % os_])
                    kb.dma(self.projT.ap()[j * 128:j * 128 + M, t0:t0 + TT], ot[0:M, os_, :], "aos%d" % os_, r=["ao%d" % os_], q="pool")
                    if j == CH_SM:
                        for q4 in range(4):
                            for kc in range(16):
                                kb.op("pe", lambda e, kc=kc, ws=ws, q4=q4: e.matmul(ps_sm[:, q4, 0:30], ht[:, kc, q4 * 128:(q4 + 1) * 128],
                                                                                   wt[:, ws, kc, 0:30], start=(kc == 0), stop=(kc == 15)),
                                      r=["aw%d" % ws, HK[kc]], w=["apsm"])
                        kb.op("dve", lambda e: e.tensor_copy(out=osm[:, :, 0:30], in_=ps_sm[:, :, 0:30]), r=["apsm"], w=["aosm"])
                        kb.dma(self.small_tm.ap()[t0:t0 + TT, 0:30].rearrange("(q p) c -> p q c", p=128), osm[:, :, 0:30], "aosm", r=["aosm"], q="pool")
            kb.barrier()

    def phase_ffn(self, l, last):
        kb, nc = self.kb, self.nc
        with ExitStack() as es:
            sb = lambda n, s, dt: es.enter_context(self.sbt(n, s, dt))
            xt = sb("f_x", [128, 16, TT], F32)
            ht = sb("f_h", [128, 16, TT], BF16)
            at = sb("f_a", [128, 64, TT], BF16)
            sq = sb("f_sq", [128, 2, TT], F32)
            rstd = sb("f_rstd", [128, TT], F32)
            rl = sb("f_rl", [128, 2, TT], F32)
            self.eps_t = sb("f_eps", [128, 1], F32)
            NW = 3
            wt = sb("f_w", [128, NW, 16, 128], BF16)
            wd = sb("f_wd", [128, 2, 64, 128], BF16)
            ps_ss = es.enter_context(self.pst("f_pss", [128, TT], F32))
            ps_o = [es.enter_context(self.pst("f_po%d" % i, [128, TT], F32)) for i in range(4)]
            kb.op("pool", lambda e: e.memset(self.eps_t[:], EPS), w=["eps"])
            HK = ["fh%d" % k for k in range(16)]
            XK = ["fx%d" % k for k in range(16)]
            ui = 0
            for t in range(NT):
                t0 = t * TT
                kb.dma(ht[:], self.mixT.ap()[:, t0:t0 + TT].rearrange("(k p) t -> p k t", p=128), "fm", w=HK)
                for j in range(16):
                    ws, pb = ui % NW, ui % 4
                    ui += 1
                    kb.dma(wt[:, ws], self.wt_out[l].ap()[j], "fw%d" % ws, w=["fw%d" % ws])
                    kb.dma(xt[:, j, :], self.xres.ap()[j * 128:(j + 1) * 128, t0:t0 + TT], "fxl%d" % (j % 8), w=[XK[j]])
                    for kc in range(16):
                        kb.op("pe", lambda e, kc=kc, ws=ws, pb=pb: e.matmul(ps_o[pb][:], wt[:, ws, kc, :], ht[:, kc, :],
                                                                           start=(kc == 0), stop=(kc == 15)),
                              r=["fw%d" % ws, HK[kc]], w=["fpo%d" % pb])
                    kb.op("dve", lambda e, j=j, pb=pb: e.tensor_tensor(out=xt[:, j, :], in0=xt[:, j, :], in1=ps_o[pb][:], op=ALU.add),
                          r=["fpo%d" % pb, XK[j]], w=[XK[j]])
                self.rmsnorm_tile(xt, ht, DEPTH * 16 + l * 16, sq, rstd, ps_ss, "f", XK)
                for j in range(64):
                    ws, pb = ui % NW, ui % 4
                    ui += 1
                    kb.dma(wt[:, ws], self.wt_up[l].ap()[j], "fw%d" % ws, w=["fw%d" % ws])
                    for kc in range(16):
                        kb.op("pe", lambda e, kc=kc, ws=ws, pb=pb: e.matmul(ps_o[pb][:], wt[:, ws, kc, :], ht[:, kc, :],
                                                                           start=(kc == 0), stop=(kc == 15)),
                              r=["fw%d" % ws, HK[kc]], w=["fpo%d" % pb])
                    s2 = j % 2
                    kb.op("act", lambda e, pb=pb, s2=s2: e.activation(out=rl[:, s2, :], in_=ps_o[pb][:], func=AF.Relu),
                          r=["fpo%d" % pb], w=["frl%d" % s2])
                    kb.op("dve", lambda e, j=j, s2=s2: e.tensor_tensor(out=at[:, j, :], in0=rl[:, s2, :], in1=rl[:, s2, :], op=ALU.mult),
                          r=["frl%d" % s2], w=["fa%d" % j])
                for n in range(16):
                    s2, pb = n % 2, ui % 4
                    ui += 1
                    kb.dma(wd[:, s2], self.wt_dn[l].ap()[n], "fwd%d" % s2, w=["fwd%d" % s2])
                    for f in range(64):
                        kb.op("pe", lambda e, f=f, s2=s2, pb=pb: e.matmul(ps_o[pb][:], wd[:, s2, f, :], at[:, f, :],
                                                                          start=(f == 0), stop=(f == 63)),
                              r=["fwd%d" % s2, "fa%d" % f], w=["fpo%d" % pb])
                    kb.op("dve", lambda e, n=n, pb=pb: e.tensor_tensor(out=xt[:, n, :], in0=xt[:, n, :], in1=ps_o[pb][:], op=ALU.add),
                          r=["fpo%d" % pb, XK[n]], w=[XK[n]])
                    dst = self.yT if last else self.xres
                    kb.dma(dst.ap()[n * 128:(n + 1) * 128, t0:t0 + TT], xt[:, n, :], "fxs%d" % (n % 4), r=[XK[n]], q="pool")
            kb.barrier()

    def phase_mixers(self, l):
        if "gdn" in self.enable:
            self.phase_gdn(l)
            self.kb.barrier()
        if "sgu" in self.enable:
            self.phase_sgu(l)
            self.kb.barrier()
        if "nsa" in self.enable:
            self.phase_nsa(l)
            self.kb.barrier()

    def gelu(self, e_name, out, in_, r, w):
        self.kb.op("act", lambda e: e.activation(out=out, in_=in_, func=AF.Gelu_apprx_tanh), r=r, w=w)

    def bcast_row(self, handle, offset, n):
        return bass.AP(handle, offset, [[0, 128], [1, n]])

    def phase_sgu(self, l):
        kb, nc = self.kb, self.nc
        with ExitStack() as es:
            sb = lambda n, s, dt: es.enter_context(self.sbt(n, s, dt))
            wnat = sb("s_wn", [128, 4, 128], F32)
            wTm = sb("s_wT", [128, 4, 128], F32)
            brow = sb("s_br", [1, 512], F32)
            lng = sb("s_lng", [128, 512], F32)
            lnb = sb("s_lnb", [128, 512], F32)
            ut = sb("s_u", [128, 2, 4, TT], F32)
            vt = sb("s_v", [128, 2, 4, TT], F32)
            vn = sb("s_vn", [128, 2, 512], F32)
            st6 = sb("s_st", [128, 6], F32)
            mv = sb("s_mv", [128, 2], F32)
            rs = sb("s_rs", [128, 1], F32)
            eps_t = sb("s_eps", [128, 1], F32)
            obf = sb("s_o", [128, 2, 4, TT], BF16)
            ps_tr = es.enter_context(self.pst("s_ptr", [128, 512], F32))
            ps_m = es.enter_context(self.pst("s_pm", [128, 4, 128], F32))
            maskT = self.c_f[:, 512:640]
            kb.op("pool", lambda e: e.memset(eps_t[:], EPS), w=["seps"])
            kb.dma(wnat[:], self.sgu_w.ap()[l].rearrange("g t s -> t g s"), "swn", w=["swn"])
            kb.dma(brow[:], self.sgu_b.ap()[l:l + 1].rearrange("a g t -> a (g t)"), "sbr", w=["sbr"])
            kb.dma(lng[:], self.bcast_row(self.sgu_ln_g, l * 512, 512), "slg", w=["slg"])
            kb.dma(lnb[:], self.bcast_row(self.sgu_ln_b, l * 512, 512), "slb", w=["slb"])
            for g in range(4):
                kb.op("pe", lambda e, g=g: e.transpose(ps_m[:, g, :], wnat[:, g, :], self.ident), r=["swn", "cst"], w=["spm"])
            kb.op("dve", lambda e: e.tensor_tensor(out=wTm[:], in0=ps_m[:], in1=maskT.unsqueeze(1).to_broadcast([128, 4, 128]), op=ALU.mult),
                  r=["spm", "cst"], w=["swT"])
            for t in range(NT):
                t0 = t * TT
                s = t % 2
                kb.dma(ut[:, s], self.projT.ap()[CH_UB * 128:(CH_UB + 4) * 128, t0:t0 + TT].rearrange("(g p) t -> p g t", p=128), "su%d" % s, w=["su%d" % s])
                kb.dma(vt[:, s], self.projT.ap()[CH_VB * 128:(CH_VB + 4) * 128, t0:t0 + TT].rearrange("(g p) t -> p g t", p=128), "sv%d" % s, w=["sv%d" % s])
                for g in range(4):
                    self.gelu("act", ut[:, s, g, :], ut[:, s, g, :], ["su%d" % s], ["su%d" % s])
                    self.gelu("act", vt[:, s, g, :], vt[:, s, g, :], ["sv%d" % s], ["sv%d" % s])
                for c4 in range(4):
                    cs = slice(c4 * 128, (c4 + 1) * 128)
                    v2 = c4 % 2
                    for g in range(4):
                        kb.op("pe", lambda e, g=g, cs=cs: e.transpose(ps_tr[:, g * 128:(g + 1) * 128], vt[:, s, g, cs], self.ident),
                              r=["sv%d" % s, "cst"], w=["sptr"])
                    kb.op("dve", lambda e: e.bn_stats(out=st6[:], in_=ps_tr[:]), r=["sptr"], w=["sst"])
                    kb.op("dve", lambda e: e.bn_aggr(out=mv[:], in_=st6[:]), r=["sst"], w=["smv"])
                    kb.op("act", lambda e: e.activation(out=rs[:], in_=mv[:, 1:2], func=AF.Ln, bias=eps_t[:, 0:1], scale=1.0),
                          r=["smv", "seps"], w=["srs"])
                    kb.op("act", lambda e: e.activation(out=rs[:], in_=rs[:], func=AF.Exp, scale=-0.5), r=["srs"], w=["srs"])
                    kb.op("dve", lambda e, v2=v2: e.tensor_scalar(out=vn[:, v2, :], in0=ps_tr[:], scalar1=mv[:, 0:1], scalar2=rs[:, 0:1],
                                                                   op0=ALU.subtract, op1=ALU.mult), r=["sptr", "smv", "srs"], w=["svn%d" % v2])
                    kb.op("dve", lambda e, v2=v2: e.tensor_tensor(out=vn[:, v2, :], in0=vn[:, v2, :], in1=lng[:], op=ALU.mult),
                          r=["svn%d" % v2, "slg"], w=["svn%d" % v2])
                    kb.op("dve", lambda e, v2=v2: e.tensor_tensor(out=vn[:, v2, :], in0=vn[:, v2, :], in1=lnb[:], op=ALU.add),
                          r=["svn%d" % v2, "slb"], w=["svn%d" % v2])
                    for g in range(4):
                        kb.op("pe", lambda e, g=g, v2=v2: e.matmul(ps_m[:, g, :], vn[:, v2, g * 128:(g + 1) * 128], wTm[:, g, :], start=True, stop=False),
                              r=["svn%d" % v2, "swT"], w=["spm"])
                        kb.op("pe", lambda e, g=g: e.matmul(ps_m[:, g, :], self.ones_f[0:1, :], brow[0:1, g * 128:(g + 1) * 128], start=False, stop=True),
                              r=["sbr", "ones"], w=["spm"])
                    kb.op("dve", lambda e, cs=cs: e.tensor_tensor(out=obf[:, s, :, cs], in0=ut[:, s, :, cs], in1=ps_m[:], op=ALU.mult),
                          r=["spm", "su%d" % s], w=["so%d" % s])
                kb.dma(self.mixT.ap()[768:1280, t0:t0 + TT].rearrange("(g p) t -> p g t", p=128), obf[:, s], "sos%d" % s, r=["so%d" % s])
            kb.barrier()

    def phase_gdn(self, l):
        kb, nc = self.kb, self.nc
        H = 6
        with ExitStack() as es:
            sb = lambda n, s, dt: es.enter_context(self.sbt(n, s, dt))
            PS = es.enter_context(self.pst("g_ps", [128, 4096], F32))
            slot = lambda i: PS[:, i * 128:(i + 1) * 128]
            P1 = lambda h: slot(4 * h)
            P2 = lambda h: slot(4 * h + 1)
            P3 = lambda h: slot(4 * h + 2)
            P4 = lambda h: slot(4 * h + 3)
            ps_ss = PS[:, 24 * 128:28 * 128]
            ps_sm = PS[:, 28 * 128:28 * 128 + 6]
            convw = sb("g_cw", [128, 72], F32)
            dtb = sb("g_dtb", [128, 6], F32)
            nega = sb("g_na", [128, 6], F32)
            eps_t = sb("g_eps", [128, 1], F32)
            sm = sb("g_sm", [128, 4, 30], F32)
            beta = sb("g_beta", [128, 4, 6], F32)
            nbeta = sb("g_nbeta", [128, 4, 6], F32)
            gtm = sb("g_g", [128, 4, 6], F32)
            xin = sb("g_xin", [128, 2, 3, 515], F32)
            QKV = sb("g_qkv", [128, 6, 3, 512], F32)
            sq = sb("g_sq", [128, 512], F32)
            rn = sb("g_rn", [128, 512], F32)
            sz = sb("g_sz", [128, 6, 512], F32)
            obf = sb("g_obf", [128, 6, 512], BF16)
            Sst = sb("g_S", [128, 6, 128], F32)
            names = ["Ktm", "Vb", "rb", "ebc", "E", "Es", "Ei", "NT", "Aqk", "N", "AqkT", "R", "Pa", "Pb", "PTa", "PTb",
                     "Um0", "Um1", "Kw", "WT", "QgT", "Kpp", "Vnew", "Osb", "On"]
            A = {n: sb("g_" + n, [128, 6, 128], F32) for n in names}
            A["tE"] = A["E"]
            bg_tick = None
            if BG_CAST and "cast" in self.phases:
                bsf = sb("g_csf", [128, 2, 4224], F32)
                bsb = sb("g_csb", [128, 1, 4224], BF16)
                bspecs = self.cast_specs(l, ("rest",)) + (self.cast_specs(l + 1, ("in",)) if l + 1 < self.n_layers else [])
                bg_tick = self.cast_runner(bspecs, bsf, bsb, 2, 1, "cg")
            gc = sb("g_gc", [128, 6], F32)
            s1 = sb("g_s1", [128, 6], F32)
            dl = sb("g_dl", [128, 6], F32)
            egl = sb("g_egl", [128, 6, 2], F32)
            ssq = sb("g_ssq", [128, 6], F32)
            rm = sb("g_rm", [128, 4], F32)
            triBD = self.c_f[:, 128:256]
            m_incl = self.c_f[:, 256:384]
            m_strict = self.c_f[:, 384:512]
            GK = lambda n, h: "%s%d" % (n, 3 * (h // 3))
            K = lambda n, h: ("@gb%d" % h) if n in ("P1", "P2", "P3", "P4") else "g%s%d" % (n, h)
            kb.op("pool", lambda e: e.memset(eps_t[:], EPS), w=["geps"])
            kb.op("pool", lambda e: e.memset(rm[:], 0.0), w=["grm"])
            kb.op("pool", lambda e: e.memset(rm[0:64, 0:1], 1.0), w=["grm"])
            kb.op("pool", lambda e: e.memset(rm[64:128, 1:2], 1.0), w=["grm"])
            kb.op("pool", lambda e: e.memset(rm[0:64, 2:3], -1.0), w=["grm"])
            kb.op("pool", lambda e: e.memset(rm[64:128, 3:4], -1.0), w=["grm"])
            kb.op("pool", lambda e: e.memset(Sst[:], 0.0), w=[K("S", h) for h in range(H)])
            kb.dma(convw[:], self.conv_pl.ap()[l], "gcw", w=["gcw"])
            kb.dma(dtb[:], self.bcast_row(self.dt_bias, l * 6, 6), "gdtb", w=["gdtb"])
            kb.dma(nega[:], self.bcast_row(self.a_log, l * 6, 6), "gna", w=["gna"])
            kb.op("act", lambda e: e.activation(out=nega[:], in_=nega[:], func=AF.Exp), r=["gna"], w=["gna"])
            kb.op("dve", lambda e: e.tensor_scalar(out=nega[:], in0=nega[:], scalar1=-1.0, scalar2=None, op0=ALU.mult), r=["gna"], w=["gna"])
            for b in range(NT):
                t0 = b * TT
                kb.dma(sm[:], self.small_tm.ap()[t0:t0 + TT, 0:30].rearrange("(q p) c -> p q c", p=128), "gsm", w=["gsm"])
                kb.dma(sz[:], self.projT.ap()[CH_ZA * 128:(CH_ZA + 6) * 128, t0:t0 + TT].rearrange("(h p) t -> p h t", p=128), "gsz", w=["gsz"])
                kb.op("act", lambda e: e.activation(out=sz[:], in_=sz[:], func=AF.Silu), r=["gsz"], w=["gsz"])
                kb.op("act", lambda e: e.activation(out=beta[:], in_=sm[:, :, 0:6], func=AF.Sigmoid), r=["gsm"], w=["gbeta"])
                kb.op("dve", lambda e: e.tensor_scalar(out=nbeta[:], in0=beta[:], scalar1=-1.0, scalar2=None, op0=ALU.mult), r=["gbeta"], w=["gnbeta"])
                kb.op("dve", lambda e: e.tensor_tensor(out=gtm[:], in0=sm[:, :, 6:12], in1=dtb[:].unsqueeze(1).to_broadcast([128, 4, 6]), op=ALU.add),
                      r=["gsm", "gdtb"], w=["gg"])
                kb.op("act", lambda e: e.activation(out=gtm[:], in_=gtm[:], func=AF.Exp), r=["gg"], w=["gg"])
                kb.op("act", lambda e: e.activation(out=gtm[:], in_=gtm[:], func=AF.Ln, bias=1.0, scale=1.0), r=["gg"], w=["gg"])
                kb.op("dve", lambda e: e.tensor_tensor(out=gtm[:], in0=gtm[:], in1=nega[:].unsqueeze(1).to_broadcast([128, 4, 6]), op=ALU.mult),
                      r=["gg", "gna"], w=["gg"])
                for h in range(H):
                    xs = h % 2
                    for qi, ch in enumerate((CH_QA, CH_KA, CH_VA)):
                        row = (ch + h) * 128
                        if b == 0:
                            kb.op("pool", lambda e, qi=qi: e.memset(xin[:, xs, qi, 0:3], 0.0), w=["gx%d" % xs])
                            kb.dma(xin[:, xs, qi, 3:515], self.projT.ap()[row:row + 128, 0:TT], "gx%d" % xs, w=["gx%d" % xs])
                        else:
                            kb.dma(xin[:, xs, qi, :], self.projT.ap()[row:row + 128, t0 - 3:t0 + TT], "gx%d" % xs, w=["gx%d" % xs])
                    for qi, ch in enumerate((CH_QA, CH_KA, CH_VA)):
                        cw = lambda j: convw[:, (ch + h) * 4 + j:(ch + h) * 4 + j + 1]
                        dst = QKV[:, h, qi, :]
                        kb.op("dve", lambda e, qi=qi, cw=cw, dst=dst: e.tensor_scalar(out=dst, in0=xin[:, xs, qi, 0:512], scalar1=cw(0), scalar2=None, op0=ALU.mult),
                              r=["gx%d" % xs, "gcw"], w=[K("qkv%d" % qi, h)])
                        for j in (1, 2, 3):
                            kb.op("dve", lambda e, qi=qi, cw=cw, dst=dst, j=j: e.scalar_tensor_tensor(out=dst, in0=xin[:, xs, qi, j:j + 512], scalar=cw(j), in1=dst,
                                                                                              op0=ALU.mult, op1=ALU.add),
                                  r=["gx%d" % xs, "gcw", K("qkv%d" % qi, h)], w=[K("qkv%d" % qi, h)])
                        kb.op("act", lambda e, dst=dst: e.activation(out=dst, in_=dst, func=AF.Silu), r=[K("qkv%d" % qi, h)], w=[K("qkv%d" % qi, h)])
                        if qi < 2:
                            kb.op("act", lambda e, dst=dst: e.activation(out=sq[:], in_=dst, func=AF.Square), r=[K("qkv%d" % qi, h)], w=["gsq"])
                            kb.op("pe", lambda e: e.matmul(ps_ss, self.ones_f[:], sq[:], start=True, stop=True), r=["gsq", "ones"], w=["@gpss"])
                            kb.op("act", lambda e: e.activation(out=rn[:], in_=ps_ss, func=AF.Ln, bias=eps_t[:, 0:1], scale=1.0), r=["@gpss", "geps"], w=["grn"])
                            kb.op("act", lambda e: e.activation(out=rn[:], in_=rn[:], func=AF.Exp, scale=-0.5), r=["grn"], w=["grn"])
                            sc = (128.0 ** -0.5) if qi == 0 else 1.0
                            kb.op("dve", lambda e, dst=dst, sc=sc: e.scalar_tensor_tensor(out=dst, in0=dst, scalar=sc, in1=rn[:], op0=ALU.mult, op1=ALU.mult),
                                  r=["grn", K("qkv%d" % qi, h)], w=[K("qkv%d" % qi, h)])
                def mk_stages(pr):
                    stages = []
                    lev = 0; Pc = PTc = Pn = PTn = None; c = 0; r0 = 0
                    cs = slice(pr * 128, (pr + 1) * 128)
                    Qt = lambda h: QKV[:, h, 0, cs]
                    Kt = lambda h: QKV[:, h, 1, cs]
                    Vt = lambda h: QKV[:, h, 2, cs]
                    a = lambda n, h: A[n][:, h, :]
                    def _pre(heads):
                        h0, n = heads[0], len(heads)
                        psm = PS[:, 28 * 128 + h0:28 * 128 + h0 + n]
                        kb.op("pe", lambda e: e.matmul(psm, triBD, gtm[:, pr, h0:h0 + n], start=True, stop=True), r=["gg", "cst"], w=["@gpsm"])
                        kb.op("dve", lambda e: e.tensor_copy(out=gc[:, h0:h0 + n], in_=psm), r=["@gpsm"], w=["ggc%d" % h0])
                        kb.op("act", lambda e: e.activation(out=s1[:, h0:h0 + n], in_=gc[:, h0:h0 + n], func=AF.Exp), r=["ggc%d" % h0], w=["gs1%d" % h0])
                        kb.op("dve", lambda e: e.tensor_tensor(out=s1[:, h0:h0 + n], in0=s1[:, h0:h0 + n], in1=beta[:, pr, h0:h0 + n], op=ALU.mult), r=["gs1%d" % h0, "gbeta"], w=["gs1%d" % h0])
                    stages.append(_pre)
                    def _st(heads, lev=lev, Pc=Pc, PTc=PTc, Pn=Pn, PTn=PTn, c=c, r0=r0):
                        for h in heads:
                            kb.op("pe", lambda e, h=h: e.transpose(P1(h), Kt(h), self.ident), r=[K("qkv1", h), "cst"], w=[K("P1", h)])
                            kb.op("pe", lambda e, h=h: e.transpose(P2(h), Vt(h), self.ident), r=[K("qkv2", h), "cst"], w=[K("P2", h)])
                            kb.op("dve", lambda e, h=h: e.tensor_scalar(out=a("rb", h), in0=triBD, scalar1=gtm[:, pr, h:h + 1], scalar2=None, op0=ALU.mult),
                                  r=["gg", "cst"], w=[K("rb", h)])
                            kb.op("pe", lambda e, h=h: e.matmul(P3(h), self.ones_f[:], a("rb", h), start=True, stop=True), r=[K("rb", h), "ones"], w=[K("P3", h)])
                            kb.op("act", lambda e, h=h: e.copy(out=a("Ktm", h), in_=P1(h)), r=[K("P1", h)], w=[K("Ktm", h)])
                            kb.op("dve", lambda e, h=h: e.tensor_scalar(out=a("Vb", h), in0=P2(h), scalar1=beta[:, pr, h:h + 1], scalar2=None, op0=ALU.mult),
                                  r=[K("P2", h), "gbeta"], w=[K("Vb", h)])
                    stages.append(_st)
                    def _st(heads, lev=lev, Pc=Pc, PTc=PTc, Pn=Pn, PTn=PTn, c=c, r0=r0):
                        for h in heads:
                            kb.op("dve", lambda e, h=h: e.tensor_scalar(out=a("tE", h), in0=P3(h), scalar1=gc[:, h:h + 1], scalar2=0.0, op0=ALU.subtract, op1=ALU.max),
                                  r=[K("P3", h), GK("ggc", h)], w=[K("E", h)])
                            kb.op("act", lambda e, h=h: e.activation(out=a("E", h), in_=a("tE", h), func=AF.Exp, scale=-1.0), r=[K("E", h)], w=[K("E", h)])
                            kb.op("act", lambda e, h=h: e.activation(out=a("ebc", h), in_=P3(h), func=AF.Exp), r=[K("P3", h)], w=[K("ebc", h)])
                            kb.op("act", lambda e, h=h: e.activation(out=egl[:, h, :], in_=PS[:, (4 * h + 2) * 128 + 63:(4 * h + 2) * 128 + 128:64], func=AF.Exp),
                                  r=[K("P3", h)], w=[K("egl", h)])
                            kb.op("dve", lambda e, h=h: e.tensor_tensor(out=dl[0:64, h:h + 1], in0=PS[0:64, (4 * h + 2) * 128 + 63:(4 * h + 2) * 128 + 64], in1=gc[0:64, h:h + 1], op=ALU.subtract),
                                  r=[K("P3", h), GK("ggc", h)], w=[K("dl", h)])
                            kb.op("dve", lambda e, h=h: e.tensor_tensor(out=dl[64:128, h:h + 1], in0=PS[64:128, (4 * h + 2) * 128 + 127:(4 * h + 2) * 128 + 128], in1=gc[64:128, h:h + 1], op=ALU.subtract),
                                  r=[K("P3", h), GK("ggc", h)], w=[K("dl", h)])
                            kb.op("act", lambda e, h=h: e.activation(out=dl[:, h:h + 1], in_=dl[:, h:h + 1], func=AF.Exp), r=[K("dl", h)], w=[K("dl", h)])
                            kb.op("dve", lambda e, h=h: e.tensor_tensor(out=a("Es", h), in0=a("E", h), in1=m_strict, op=ALU.mult), r=[K("E", h), "cst"], w=[K("Es", h)])
                            kb.op("dve", lambda e, h=h: e.tensor_tensor(out=a("Ei", h), in0=a("E", h), in1=m_incl, op=ALU.mult), r=[K("E", h), "cst"], w=[K("Ei", h)])
                            kb.op("pe", lambda e, h=h: e.matmul(P1(h), Kt(h), Kt(h), start=True, stop=True), r=[K("qkv1", h)], w=[K("P1", h)])
                            kb.op("pe", lambda e, h=h: e.matmul(P2(h), Qt(h), Kt(h), start=True, stop=True), r=[K("qkv0", h), K("qkv1", h)], w=[K("P2", h)])
                            kb.op("dve", lambda e, h=h: e.scalar_tensor_tensor(out=a("NT", h), in0=P1(h), scalar=nbeta[:, pr, h:h + 1], in1=a("Es", h), op0=ALU.mult, op1=ALU.mult),
                                  r=[K("P1", h), "gnbeta", K("Es", h)], w=[K("NT", h)])
                            kb.op("dve", lambda e, h=h: e.tensor_tensor(out=a("Aqk", h), in0=P2(h), in1=a("Ei", h), op=ALU.mult), r=[K("P2", h), K("Ei", h)], w=[K("Aqk", h)])
                            kb.op("act", lambda e, h=h: e.activation(out=a("Kw", h), in_=a("Ktm", h), func=AF.Identity, scale=s1[:, h:h + 1]),
                                  r=[K("Ktm", h), GK("gs1", h)], w=[K("Kw", h)])
                            kb.op("act", lambda e, h=h: e.activation(out=a("Kpp", h), in_=a("Ktm", h), func=AF.Identity, scale=dl[:, h:h + 1]),
                                  r=[K("Ktm", h), K("dl", h)], w=[K("Kpp", h)])
                            kb.op("dve", lambda e, h=h: e.tensor_tensor(out=a("QgT", h), in0=Qt(h), in1=a("ebc", h), op=ALU.mult), r=[K("qkv0", h), K("ebc", h)], w=[K("QgT", h)])
                    stages.append(_st)
                    def _st(heads, lev=lev, Pc=Pc, PTc=PTc, Pn=Pn, PTn=PTn, c=c, r0=r0):
                        for h in heads:
                            kb.op("pe", lambda e, h=h: e.transpose(P3(h), a("NT", h), self.ident), r=[K("NT", h), "cst"], w=[K("P3", h)])
                            kb.op("pe", lambda e, h=h: e.transpose(P4(h), a("Aqk", h), self.ident), r=[K("Aqk", h), "cst"], w=[K("P4", h)])
                            kb.op("act", lambda e, h=h: e.copy(out=a("N", h), in_=P3(h)), r=[K("P3", h)], w=[K("N", h)])
                            kb.op("dve", lambda e, h=h: e.tensor_tensor(out=a("R", h), in0=P3(h), in1=self.ident, op=ALU.add), r=[K("P3", h), "cst"], w=[K("R", h)])
                            kb.op("act", lambda e, h=h: e.copy(out=a("AqkT", h), in_=P4(h)), r=[K("P4", h)], w=[K("AqkT", h)])
                    stages.append(_st)
                    _n0 = len(stages)
                    Pc, PTc = "N", "NT"
                    for lev in range(1, 6):
                        Pn, PTn = ("Pa", "PTa") if lev % 2 == 1 else ("Pb", "PTb")
                        def _st(heads, lev=lev, Pc=Pc, PTc=PTc, Pn=Pn, PTn=PTn, c=c, r0=r0):
                            for h in heads:
                                if lev < 5:
                                    kb.op("pe", lambda e, h=h, Pc=Pc, PTc=PTc: e.matmul(P1(h), a(PTc, h), a(Pc, h), start=True, stop=True),
                                          r=[K(Pc, h), K(PTc, h)], w=[K("P1", h)])
                                kb.op("pe", lambda e, h=h, Pc=Pc, PTc=PTc: e.matmul(P2(h), a(Pc, h), a(PTc, h), start=True, stop=True),
                                      r=[K(Pc, h), K(PTc, h)], w=[K("P2", h)])
                        stages.append(_st)
                        def _st(heads, lev=lev, Pc=Pc, PTc=PTc, Pn=Pn, PTn=PTn, c=c, r0=r0):
                            for h in heads:
                                if lev < 5:
                                    kb.op("act", lambda e, h=h, Pn=Pn: e.copy(out=a(Pn, h), in_=P1(h)), r=[K("P1", h)], w=[K(Pn, h)])
                                kb.op("dve", lambda e, h=h, PTn=PTn: e.tensor_copy(out=a(PTn, h), in_=P2(h)), r=[K("P2", h)], w=[K(PTn, h)])
                        stages.append(_st)
                        def _st(heads, lev=lev, Pc=Pc, PTc=PTc, Pn=Pn, PTn=PTn, c=c, r0=r0):
                            for h in heads:
                                kb.op("pe", lambda e, h=h, PTn=PTn: e.matmul(P3(h), a(PTn, h), a("R", h), start=True, stop=True),
                                      r=[K(PTn, h), K("R", h)], w=[K("P3", h)])
                        stages.append(_st)
                        def _st(heads, lev=lev, Pc=Pc, PTc=PTc, Pn=Pn, PTn=PTn, c=c, r0=r0):
                            for h in heads:
                                kb.op("dve", lambda e, h=h: e.tensor_tensor(out=a("R", h), in0=a("R", h), in1=P3(h), op=ALU.add), r=[K("P3", h), K("R", h)], w=[K("R", h)])
                        stages.append(_st)
                        Pc, PTc = Pn, PTn
                    _lv = stages[_n0:]
                    del stages[_n0:]
                    _sq = [_lv[4 * k] for k in range(5)]; _ev = [_lv[4 * k + 1] for k in range(5)]
                    _rm = [_lv[4 * k + 2] for k in range(5)]; _ra = [_lv[4 * k + 3] for k in range(5)]
                    stages += [_sq[0], _ev[0]]
                    for _k in range(1, 5):
                        stages += [_sq[_k], _rm[_k - 1], _ev[_k], _ra[_k - 1]]
                    stages += [_rm[4], _ra[4]]
                    def _st(heads, lev=lev, Pc=Pc, PTc=PTc, Pn=Pn, PTn=PTn, c=c, r0=r0):
                        for h in heads:
                            kb.op("pe", lambda e, h=h: e.matmul(P1(h), a("R", h), a("Vb", h), start=True, stop=True), r=[K("R", h), K("Vb", h)], w=[K("P1", h)])
                            kb.op("pe", lambda e, h=h: e.matmul(P2(h), a("Kw", h), a("R", h), start=True, stop=True), r=[K("R", h), K("Kw", h)], w=[K("P2", h)])
                            kb.op("act", lambda e, h=h: e.activation(out=a("Um0", h), in_=P1(h), func=AF.Identity, scale=rm[:, 0:1]), r=[K("P1", h), "grm"], w=[K("Um0", h)])
                            kb.op("dve", lambda e, h=h: e.tensor_scalar(out=a("Um1", h), in0=P1(h), scalar1=rm[:, 1:2], scalar2=None, op0=ALU.mult), r=[K("P1", h), "grm"], w=[K("Um1", h)])
                            kb.op("act", lambda e, h=h: e.copy(out=a("WT", h), in_=P2(h)), r=[K("P2", h)], w=[K("WT", h)])
                    stages.append(_st)
                    for c in range(2):
                        r0 = 64 * c
                        def _st(heads, lev=lev, Pc=Pc, PTc=PTc, Pn=Pn, PTn=PTn, c=c, r0=r0):
                            for h in heads:
                                kb.op("pe", lambda e, h=h: e.matmul(P3(h), a("WT", h), Sst[:, h, :], start=True, stop=True), r=[K("WT", h), K("S", h)], w=[K("P3", h)])
                        stages.append(_st)
                        def _st(heads, lev=lev, Pc=Pc, PTc=PTc, Pn=Pn, PTn=PTn, c=c, r0=r0):
                            for h in heads:
                                kb.op("dve", lambda e, h=h, c=c: e.scalar_tensor_tensor(out=a("Vnew", h), in0=P3(h), scalar=rm[:, 2 + c:3 + c], in1=a("Um%d" % c, h),
                                                                                        op0=ALU.mult, op1=ALU.add), r=[K("P3", h), K("Um%d" % c, h), "grm"], w=[K("Vnew", h)])
                        stages.append(_st)
                        def _st(heads, lev=lev, Pc=Pc, PTc=PTc, Pn=Pn, PTn=PTn, c=c, r0=r0):
                            for h in heads:
                                kb.op("pe", lambda e, h=h: e.matmul(P4(h), a("QgT", h), Sst[:, h, :], start=True, stop=False), r=[K("QgT", h), K("S", h)], w=[K("P4", h)])
                                kb.op("pe", lambda e, h=h: e.matmul(P4(h), a("AqkT", h), a("Vnew", h), start=False, stop=True), r=[K("AqkT", h), K("Vnew", h)], w=[K("P4", h)])
                                kb.op("pe", lambda e, h=h: e.matmul(P1(h), a("Kpp", h), a("Vnew", h), start=True, stop=True), r=[K("Kpp", h), K("Vnew", h)], w=[K("P1", h)])
                        stages.append(_st)
                        def _st(heads, lev=lev, Pc=Pc, PTc=PTc, Pn=Pn, PTn=PTn, c=c, r0=r0):
                            for h in heads:
                                kb.op("act", lambda e, h=h, r0=r0: e.copy(out=A["Osb"][r0:r0 + 64, h, :], in_=PS[r0:r0 + 64, (4 * h + 3) * 128:(4 * h + 4) * 128]),
                                      r=[K("P4", h)], w=[K("Osb%d" % c, h)])
                                kb.op("dve", lambda e, h=h, c=c: e.scalar_tensor_tensor(out=Sst[:, h, :], in0=Sst[:, h, :], scalar=egl[:, h, c:c + 1], in1=P1(h),
                                                                                        op0=ALU.mult, op1=ALU.add), r=[K("P1", h), K("egl", h), K("S", h)], w=[K("S", h)])
                        stages.append(_st)
                    def _st(heads, lev=lev, Pc=Pc, PTc=PTc, Pn=Pn, PTn=PTn, c=c, r0=r0):
                        for h in heads:
                            kb.op("act", lambda e, h=h: e.activation(out=a("On", h), in_=a("Osb", h), func=AF.Square, accum_out=ssq[:, h:h + 1]),
                                  r=[K("Osb0", h), K("Osb1", h)], w=[K("On", h), K("ssq", h)])
                            kb.op("act", lambda e, h=h: e.activation(out=ssq[:, h:h + 1], in_=ssq[:, h:h + 1], func=AF.Ln, bias=eps_t[:, 0:1], scale=1.0 / 128),
                                  r=[K("ssq", h), "geps"], w=[K("ssq", h)])
                            kb.op("act", lambda e, h=h: e.activation(out=ssq[:, h:h + 1], in_=ssq[:, h:h + 1], func=AF.Exp, scale=-0.5), r=[K("ssq", h)], w=[K("ssq", h)])
                            kb.op("dve", lambda e, h=h: e.tensor_scalar(out=a("On", h), in0=a("Osb", h), scalar1=ssq[:, h:h + 1], scalar2=None, op0=ALU.mult),
                                  r=[K("ssq", h), K("Osb0", h), K("Osb1", h), K("On", h)], w=[K("On", h)])
                            kb.op("pe", lambda e, h=h: e.transpose(P2(h), a("On", h), self.ident), r=[K("On", h), "cst"], w=[K("P2", h)])
                            kb.op("dve", lambda e, h=h: e.scalar_tensor_tensor(out=obf[:, h, cs], in0=P2(h), scalar=self.gdnn[:, l:l + 1], in1=sz[:, h, cs],
                                                                               op0=ALU.mult, op1=ALU.mult), r=[K("P2", h), "gsz", "gdnn"], w=["gobf%d" % h])
                    stages.append(_st)
                    return stages
                seq = []
                for pr in range(4):
                    seq += mk_stages(pr)
                GA, GB, off = [0, 1, 2], [3, 4, 5], GDN_OFF
                for step in range(len(seq) + off):
                    if step < len(seq):
                        seq[step](GA)
                    if 0 <= step - off < len(seq):
                        seq[step - off](GB)
                    if bg_tick is not None and step % 10 == 5:
                        bg_tick()
                kb.dma(self.mixT.ap()[0:768, t0:t0 + TT].rearrange("(h p) t -> p h t", p=128), obf[:], "gos", r=["gobf%d" % h for h in range(6)])
            if bg_tick is not None:
                while bg_tick():
                    pass
            kb.barrier()


    LF = 8448
    OFF = 4224

    def nsa_tables(self):
        kb, nc = self.kb, self.nc
        LF, OFF = self.LF, self.OFF
        with ExitStack() as es:
            sb = lambda n, s, dt: es.enter_context(self.sbt(n, s, dt))
            Fv = sb("t_fv", [6, LF], F32)
            Fvb = sb("t_fvb", [6, LF], BF16)
            relb = sb("t_rb", [32, 6], F32)
            t31 = sb("t_31", [6, 1], F32)
            ps = es.enter_context(self.pst("t_ps", [128, 512], F32))
            OH = self.c2[0:32, 0:128]
            kb.dma(relb[:], self.rel_bias.ap(), "trb", w=["trb"])
            kb.dma(t31[:], self.rel_bias.ap()[31:32, :].rearrange("a h -> h a"), "t31", w=["t31"], allow_slow_non_contiguous=True)
            kb.dma(self.b31bc[:], self.bcast_row(self.rel_bias, 31 * 6, 6), "tb31", w=["b31"])
            kb.op("pool", lambda e: e.memset(Fv[:, 0:OFF], 0.0), w=["tfv0"])
            kb.op("pe", lambda e: e.matmul(ps[0:6, 0:128], relb[:], OH, start=True, stop=True), r=["trb", "cst2"], w=["@tps"])
            kb.op("act", lambda e: e.activation(out=Fv[:, OFF:OFF + 128], in_=ps[0:6, 0:128], func=AF.Exp), r=["@tps"], w=["tfv1"])
            kb.op("act", lambda e: e.activation(out=Fv[:, OFF + 128:LF], in_=Fv[:, 0:LF - OFF - 128], func=AF.Exp, bias=t31[:, 0:1], scale=0.0),
                  r=["tfv0", "t31"], w=["tfv2"])
            kb.dma(self.fvec.ap()[0:6, :], Fv[:], "tfs", r=["tfv0", "tfv1", "tfv2"])
            kb.op("dve", lambda e: e.tensor_copy(out=Fvb[:], in_=Fv[:]), r=["tfv0", "tfv1", "tfv2"], w=["tfvb"])
            kb.dma(self.fvecb.ap()[0:6, :], Fvb[:], "tfsb", r=["tfvb"])
            kb.op("pool", lambda e: e.memset(Fv[:, OFF + 512:LF], 0.0), r=["tfv2"], w=["tfv2"])
            kb.dma(self.fvec.ap()[6:12, :], Fv[:], "tfs", r=["tfv0", "tfv1", "tfv2"], w=["fvec"])
            kb.op("dve", lambda e: e.tensor_copy(out=Fvb[:], in_=Fv[:]), r=["tfv0", "tfv1", "tfv2"], w=["tfvb"])
            kb.dma(self.fvecb.ap()[6:12, :], Fvb[:], "tfsb", r=["tfvb"])
            kb.barrier()
            for i in range(12):
                kb.dma(self.Btab[i].ap(), bass.AP(self.fvec, i * LF, [[0, 128], [1, LF]]), "tbr")
                kb.dma(self.Btabb[i].ap(), bass.AP(self.fvecb, i * LF, [[0, 128], [1, LF]]), "tbrb")
            kb.barrier()

    def mtile(self, hh, rs, delta, bf=False):
        LF, OFF = self.LF, self.OFF
        return bass.AP((self.Btabb if bf else self.Btab)[hh], delta + OFF, [[LF - rs, 128], [1, TT]])

    def phase_nsa(self, l):
        kb, nc = self.kb, self.nc
        H = 6
        with ExitStack() as es:
            sb = lambda n, s, dt: es.enter_context(self.sbt(n, s, dt))
            pst = lambda n: es.enter_context(self.pst(n, [128, TT], F32))
            ps_s = [pst("n_ps0"), pst("n_ps1")]
            ps_o, ps_d, ps_i, ps_g, ps_x = pst("n_po"), pst("n_pd"), pst("n_pi"), pst("n_pg"), pst("n_px")
            kT = {"s": sb("n_ksT", [128, 2, S], BF16), "w": sb("n_kwT", [128, 2, S], BF16)}
            vtm = {"s": sb("n_vs", [128, 2, 32, 128], BF16), "w": sb("n_vw", [128, 2, 32, 128], BF16)}
            kcT = sb("n_kcT", [128, 2, 256], F32)
            vc = sb("n_vc", [128, 2, 2, 128], F32)
            eps_t = sb("n_eps", [128, 1], F32)
            ones_b = sb("n_1b", [128, 128], BF16)
            eblk = sb("n_eblk", [64, S], BF16)
            gq = sb("n_gq", [128, 1], F32)
            kb.op("pool", lambda e: e.memset(eps_t[:], EPS), w=["neps"])
            kb.op("pool", lambda e: e.memset(ones_b[:], 1.0), w=["n1b"])
            kb.op("pool", lambda e: e.memset(kcT[:], 0.0), w=["nkcT0", "nkcT1"])
            kb.op("pool", lambda e: e.memset(vc[:], 0.0), w=["nvc0", "nvc1"])
            kb.op("dve", lambda e: e.tensor_scalar(out=gq[:], in0=self.nqk[:, l:l + 1], scalar1=128.0 ** -0.5, scalar2=None, op0=ALU.mult), r=["nqk"], w=["ngq"])
            gk = self.nqk[:, DEPTH + l:DEPTH + l + 1]
            with ExitStack() as es2:
                sb2 = lambda n, s, dt: es2.enter_context(self.sbt(n, s, dt))
                stg = sb2("n_stg", [128, 2, S], F32)
                w1 = sb2("n_w1", [128, 32, 128], F32)
                w2 = sb2("n_w2", [128, 128], F32)
                posT = sb2("n_pos", [128, 32], F32)
                c1 = sb2("n_c1", [128, 1], F32)
                g1T = sb2("n_g1T", [128, 256], F32)
                sq = sb2("n_sq", [128, TT], F32)
                rstd = sb2("n_rstd", [128, TT], F32)
                ebf = sb2("n_ebf", [64, S], F32)
                kb.dma(ebf[:], self.eblk_in.ap(), "neb", w=["nebf"])
                kb.op("dve", lambda e: e.tensor_copy(out=eblk[:], in_=ebf[:]), r=["nebf"], w=["neblk"])
                si = 0
                for br, chk, chv in (("c", CH_KCC, CH_VCC), ("s", CH_KSL, CH_VSL), ("w", CH_KWN, CH_VWN)):
                    for g in range(2):
                        for kv, ch in ((0, chk), (1, chv)):
                            s_ = si % 2
                            si += 1
                            sk = "nstg%d" % s_
                            kb.dma(stg[:, s_, :], self.projT.ap()[(ch + g) * 128:(ch + g + 1) * 128, :], sk, w=[sk])
                            if br == "c":
                                kb.dma(w1[:], self.cmp_w1.ap()[l, kv].rearrange("(a d) j -> d a j", d=128), "nw1", w=["nw1"])
                                kb.dma(w2[:], self.cmp_w2.ap()[l, kv], "nw2", w=["nw2"])
                                kb.dma(posT[:], self.cpos_pl.ap()[l, kv], "npos", w=["npos"])
                                for a in range(32):
                                    kb.op("pe", lambda e, a=a: e.matmul(ps_x[:, 0:1], w1[:, a, :], posT[:, a:a + 1], start=(a == 0), stop=(a == 31)),
                                          r=["nw1", "npos"], w=["@npx"])
                                kb.op("dve", lambda e: e.tensor_copy(out=c1[:], in_=ps_x[:, 0:1]), r=["@npx"], w=["nc1"])
                                for a in range(32):
                                    kb.op("pe", lambda e, a=a, s_=s_: e.matmul(ps_o[:, 0:255], w1[:, a, :], stg[:, s_, a:a + 16 * 254 + 1:16], start=(a == 0), stop=(a == 31)),
                                          r=["nw1", sk], w=["@npo"])
                                kb.op("act", lambda e: e.activation(out=g1T[:, 0:255], in_=ps_o[:, 0:255], func=AF.Gelu_apprx_tanh, bias=c1[:, 0:1], scale=1.0),
                                      r=["@npo", "nc1"], w=["ng1T"])
                                if kv == 0:
                                    kb.op("pe", lambda e: e.matmul(ps_d[:, 0:255], w2[:], g1T[:, 0:255], start=True, stop=True), r=["nw2", "ng1T"], w=["@npd"])
                                    kb.op("act", lambda e: e.activation(out=sq[:, 0:255], in_=ps_d[:, 0:255], func=AF.Square), r=["@npd"], w=["nsq"])
                                    kb.op("pe", lambda e: e.matmul(ps_i[:, 0:255], self.ones_f[:], sq[:, 0:255], start=True, stop=True), r=["nsq", "ones"], w=["@npi"])
                                    kb.op("act", lambda e: e.activation(out=rstd[:, 0:255], in_=ps_i[:, 0:255], func=AF.Ln, bias=eps_t[:, 0:1], scale=1.0 / 128),
                                          r=["@npi", "neps"], w=["nrstd"])
                                    kb.op("act", lambda e: e.activation(out=rstd[:, 0:255], in_=rstd[:, 0:255], func=AF.Exp, scale=-0.5), r=["nrstd"], w=["nrstd"])
                                    kb.op("dve", lambda e, g=g: e.scalar_tensor_tensor(out=kcT[:, g, 0:255], in0=ps_d[:, 0:255], scalar=gk, in1=rstd[:, 0:255],
                                                                                      op0=ALU.mult, op1=ALU.mult), r=["@npd", "nrstd", "nqk"], w=["nkcT%d" % g])
                                else:
                                    for c, nn in ((0, 128), (1, 127)):
                                        kb.op("pe", lambda e, c=c, nn=nn: e.matmul(ps_d[0:nn, c * 128:(c + 1) * 128], g1T[:, c * 128:c * 128 + nn], w2[:], start=True, stop=True),
                                              r=["nw2", "ng1T"], w=["@npd"])
                                        kb.op("dve", lambda e, c=c, nn=nn, g=g: e.tensor_copy(out=vc[0:nn, g, c, :], in_=ps_d[0:nn, c * 128:(c + 1) * 128]),
                                              r=["@npd"], w=["nvc%d" % g])
                            elif kv == 0:
                                for t in range(NT):
                                    ts_ = slice(t * TT, (t + 1) * TT)
                                    kb.op("act", lambda e, ts_=ts_, s_=s_: e.activation(out=sq[:], in_=stg[:, s_, ts_], func=AF.Square), r=[sk], w=["nsq"])
                                    kb.op("pe", lambda e: e.matmul(ps_x[:], self.ones_f[:], sq[:], start=True, stop=True), r=["nsq", "ones"], w=["@npx"])
                                    kb.op("act", lambda e: e.activation(out=rstd[:], in_=ps_x[:], func=AF.Ln, bias=eps_t[:, 0:1], scale=1.0 / 128), r=["@npx", "neps"], w=["nrstd"])
                                    kb.op("act", lambda e: e.activation(out=rstd[:], in_=rstd[:], func=AF.Exp, scale=-0.5), r=["nrstd"], w=["nrstd"])
                                    kb.op("dve", lambda e, ts_=ts_, s_=s_, br=br, g=g: e.scalar_tensor_tensor(out=kT[br][:, g, ts_], in0=stg[:, s_, ts_], scalar=gk, in1=rstd[:],
                                                                                                      op0=ALU.mult, op1=ALU.mult), r=[sk, "nrstd", "nqk"], w=["nkT" + br])
                            else:
                                for kt in range(32):
                                    q4 = kt % 4
                                    kb.op("pe", lambda e, kt=kt, q4=q4, s_=s_: e.transpose(ps_i[:, q4 * 128:(q4 + 1) * 128], stg[:, s_, kt * 128:(kt + 1) * 128], self.ident),
                                          r=[sk, "cst"], w=["@npi"])
                                    if q4 == 3:
                                        kb.op("act", lambda e, kt=kt, br=br, g=g: e.copy(out=vtm[br][:, g, kt - 3:kt + 1, :], in_=ps_i[:].rearrange("p (a c) -> p a c", c=128)),
                                              r=["@npi"], w=["nv" + br])
                kb.barrier()
            qraw = sb("n_qraw", [128, 6, TT], F32)
            qn = sb("n_qn", [128, 6, TT], F32)
            qb = sb("n_qb", [128, 6, TT], BF16)
            acc = sb("n_acc", [128, 6, TT], F32)
            obf = sb("n_obf", [128, 6, TT], BF16)
            ec = sb("n_ec", [128, 2, 2, TT], F32)
            e32 = sb("n_e32", [128, 2, TT], F32)
            ebt = sb("n_eb", [128, 3, TT], BF16)
            Mt = sb("n_M", [128, 3, TT], F32)
            Mtb = sb("n_Mb", [128, 3, TT], BF16)
            negT = sb("n_negT", [64, 2, TT], BF16)
            selc = sb("n_selc", [128, 3, 4, 64], F32)
            sig = sb("n_sig", [30, TT], F32)
            impg = sb("n_imp", [128, 2, 4, 64], F32)
            imp2 = sb("n_imp2", [128, 4, 64], F32)
            rden = sb("n_rden", [128, 4, 1], F32)
            score = sb("n_score", [128, 64], F32)
            work = sb("n_work", [128, 64], F32)
            mx8 = sb("n_mx8", [128, 16], F32)
            nsel = sb("n_nsel", [128, 64], F32)
            wgt = sb("n_wgt", [128, 2, TT], F32)
            sqd = sb("n_sq2", [128, 2, TT], F32)
            rstdd = sb("n_rstd2", [128, 2, TT], F32)
            OvE = self.c2[:, 128:258].rearrange("p (c j) -> p c j", c=2)
            Sel = self.c2[0:30, 258:258 + 18 * 128].rearrange("p (k c) -> p k c", k=18)
            cnt = {"e": 0, "m": 0, "s": 0, "mb": 0}

            def combine(h, gate_k, first):
                kb.op("pe", lambda e: e.matmul(ps_g[:], Sel[:, gate_k, :], sig[:], start=True, stop=True), r=["nsig", "cst2"], w=["@npg"])
                kb.op("dve", lambda e: e.tensor_scalar(out=wgt[:, 0, :], in0=ps_d[:], scalar1=1e-18, scalar2=None, op0=ALU.max), r=["@npd"], w=["nwgt0"])
                kb.op("act", lambda e: e.activation(out=wgt[:, 0, :], in_=wgt[:, 0, :], func=AF.Ln), r=["nwgt0"], w=["nwgt0"])
                kb.op("act", lambda e: e.activation(out=wgt[:, 0, :], in_=wgt[:, 0, :], func=AF.Exp, scale=-1.0), r=["nwgt0"], w=["nwgt0"])
                kb.op("dve", lambda e: e.tensor_tensor(out=wgt[:, 0, :], in0=wgt[:, 0, :], in1=ps_g[:], op=ALU.mult), r=["nwgt0", "@npg"], w=["nwgt0"])
                if first:
                    kb.op("dve", lambda e: e.tensor_tensor(out=acc[:, h, :], in0=ps_o[:], in1=wgt[:, 0, :], op=ALU.mult), r=["@npo", "nwgt0"], w=["nacc%d" % h])
                else:
                    kb.op("dve", lambda e: e.tensor_tensor(out=wgt[:, 1, :], in0=ps_o[:], in1=wgt[:, 0, :], op=ALU.mult), r=["@npo", "nwgt0"], w=["nwgt1"])
                    kb.op("dve", lambda e: e.tensor_tensor(out=acc[:, h, :], in0=acc[:, h, :], in1=wgt[:, 1, :], op=ALU.add), r=["nwgt1", "nacc%d" % h], w=["nacc%d" % h])

            for t in range(NT):
                q0 = t * TT
                kb.dma(qraw[:], self.projT.ap()[CH_QC * 128:(CH_QC + 6) * 128, q0:q0 + TT].rearrange("(h p) t -> p h t", p=128), "nq", w=["nqraw"])
                kb.dma(sig[:], self.projT.ap()[CH_SM * 128:CH_SM * 128 + 30, q0:q0 + TT], "nsg", w=["nsig"])
                kb.op("act", lambda e: e.activation(out=sig[:], in_=sig[:], func=AF.Sigmoid), r=["nsig"], w=["nsig"])
                for a_ in range(3):
                    kb.dma(selc[:, a_], self.selc_in.ap()[a_, q0:q0 + TT, :].rearrange("(s p) j -> p s j", p=128), "nselc", w=["nselc"])
                for h in range(H):
                    p2 = h % 2
                    kb.op("act", lambda e, h=h, p2=p2: e.activation(out=sqd[:, p2, :], in_=qraw[:, h, :], func=AF.Square), r=["nqraw"], w=["nsq2_%d" % p2])
                    kb.op("pe", lambda e, p2=p2: e.matmul(ps_x[:], self.ones_f[:], sqd[:, p2, :], start=True, stop=True), r=["nsq2_%d" % p2, "ones"], w=["@npx"])
                    kb.op("act", lambda e, p2=p2: e.activation(out=rstdd[:, p2, :], in_=ps_x[:], func=AF.Ln, bias=eps_t[:, 0:1], scale=1.0 / 128), r=["@npx", "neps"], w=["nrstd2_%d" % p2])
                    kb.op("act", lambda e, p2=p2: e.activation(out=rstdd[:, p2, :], in_=rstdd[:, p2, :], func=AF.Exp, scale=-0.5), r=["nrstd2_%d" % p2], w=["nrstd2_%d" % p2])
                    kb.op("dve", lambda e, h=h, p2=p2: e.scalar_tensor_tensor(out=qn[:, h, :], in0=qraw[:, h, :], scalar=gq[:, 0:1], in1=rstdd[:, p2, :], op0=ALU.mult, op1=ALU.mult),
                          r=["nqraw", "nrstd2_%d" % p2, "ngq"], w=["nqn%d" % h])
                    kb.op("act", lambda e, h=h: e.copy(out=qb[:, h, :], in_=qn[:, h, :]), r=["nqn%d" % h], w=["nqb%d" % h])
                def cmp_front(h):
                    g = h // 3
                    ep = h % 2
                    for c in range(2):
                        pb = cnt["s"] % 2
                        cnt["s"] += 1
                        ms = cnt["m"] % 3
                        cnt["m"] += 1
                        kb.dma(Mt[:, ms, :], self.mtile(h, 16, q0 - 16 * (c * 128) - 31), "nM%d" % ms, w=["nM%d" % ms])
                        kb.op("pe", lambda e, c=c, pb=pb: e.matmul(ps_s[pb][:], kcT[:, g, c * 128:(c + 1) * 128], qn[:, h, :], start=True, stop=True),
                              r=["nkcT%d" % g, "nqn%d" % h], w=["@nps%d" % pb])
                        kb.op("act", lambda e, c=c, pb=pb: e.activation(out=ec[:, ep, c, :], in_=ps_s[pb][:], func=AF.Exp), r=["@nps%d" % pb], w=["nec%d_%d" % (ep, c)])
                        kb.op("dve", lambda e, c=c, ms=ms: e.tensor_tensor(out=ec[:, ep, c, :], in0=ec[:, ep, c, :], in1=Mt[:, ms, :], op=ALU.mult),
                              r=["nec%d_%d" % (ep, c), "nM%d" % ms], w=["nec%d_%d" % (ep, c)])

                def cmp_rest(h):
                    g = h // 3
                    r_ = h % 3
                    ep = h % 2
                    EK = ["nec%d_0" % ep, "nec%d_1" % ep]
                    for c in range(2):
                        kb.op("pe", lambda e, c=c: e.matmul(ps_o[:], vc[:, g, c, :], ec[:, ep, c, :], start=(c == 0), stop=(c == 1)), r=["nvc%d" % g, EK[c]], w=["@npo"])
                    for c in range(2):
                        kb.op("pe", lambda e, c=c: e.matmul(ps_d[:], self.ones_f[:], ec[:, ep, c, :], start=(c == 0), stop=(c == 1)), r=["ones", EK[c]], w=["@npd"])
                    for qs in range(4):
                        for c in range(2):
                            kb.op("pe", lambda e, c=c, qs=qs: e.matmul(ps_i[:, qs * 65:(qs + 1) * 65], ec[:, ep, c, qs * 128:(qs + 1) * 128], OvE[:, c, :], start=(c == 0), stop=(c == 1)),
                                  r=EK + ["cst2"], w=["@npi"])
                    pi3 = ps_i[:, 0:260].rearrange("p (s j) -> p s j", j=65)
                    kb.op("dve", lambda e: e.tensor_scalar(out=rden[:], in0=pi3[:, :, 64:65], scalar1=1e-30, scalar2=None, op0=ALU.max), r=["@npi"], w=["nrden"])
                    kb.op("dve", lambda e: e.reciprocal(out=rden[:], in_=rden[:]), r=["nrden"], w=["nrden"])
                    if r_ == 0:
                        kb.op("dve", lambda e: e.tensor_tensor(out=impg[:, g], in0=pi3[:, :, 0:64], in1=rden[:].to_broadcast([128, 4, 64]), op=ALU.mult),
                              r=["@npi", "nrden"], w=["nimp%d" % g])
                    else:
                        kb.op("dve", lambda e: e.tensor_tensor(out=imp2[:], in0=pi3[:, :, 0:64], in1=rden[:].to_broadcast([128, 4, 64]), op=ALU.mult),
                              r=["@npi", "nrden"], w=["nimp2"])
                        kb.op("dve", lambda e: e.tensor_tensor(out=impg[:, g], in0=impg[:, g], in1=imp2[:], op=ALU.add), r=["nimp%d" % g, "nimp2"], w=["nimp%d" % g])
                    combine(h, 0 * 6 + h, True)

                cmp_front(0)
                for h in range(H):
                    if h + 1 < H:
                        cmp_front(h + 1)
                    cmp_rest(h)
                for g in range(2):
                    for qs in range(4):
                        kb.op("dve", lambda e, qs=qs: e.tensor_tensor(out=score[:], in0=impg[:, g, qs, :], in1=selc[:, 0, qs, :], op=ALU.mult), r=["nimp%d" % g, "nselc"], w=["nscore"])
                        kb.op("dve", lambda e, qs=qs: e.tensor_tensor(out=score[:], in0=score[:], in1=selc[:, 1, qs, :], op=ALU.add), r=["nscore", "nselc"], w=["nscore"])
                        kb.op("dve", lambda e: e.max(out=mx8[:, 0:8], in_=score[:]), r=["nscore"], w=["nmx8"])
                        kb.op("dve", lambda e: e.match_replace(out=work[:], in_to_replace=mx8[:, 0:8], in_values=score[:], imm_value=-3.0e38), r=["nscore", "nmx8"], w=["nwork"])
                        kb.op("dve", lambda e: e.max(out=mx8[:, 8:16], in_=work[:]), r=["nwork"], w=["nmx8b"])
                        kb.op("dve", lambda e, qs=qs: e.scalar_tensor_tensor(out=nsel[:], in0=score[:], scalar=mx8[:, 15:16], in1=selc[:, 2, qs, :], op0=ALU.is_ge, op1=ALU.mult),
                              r=["nscore", "nmx8b", "nselc"], w=["nnsel"])
                        kb.op("dve", lambda e: e.tensor_scalar(out=nsel[:], in0=nsel[:], scalar1=-1.0, scalar2=30000.0, op0=ALU.add, op1=ALU.mult), r=["nnsel"], w=["nnsel"])
                        kb.op("pe", lambda e: e.transpose(ps_x[0:64, 0:128], nsel[:], self.ident), r=["nnsel", "cst"], w=["@npx"])
                        kb.op("act", lambda e, qs=qs: e.copy(out=negT[:, g, qs * 128:(qs + 1) * 128], in_=ps_x[0:64, 0:128]), r=["@npx"], w=["nnegT%d" % g])
                for g in range(2):
                    tiles = []
                    for r_ in range(3):
                        h = g * 3 + r_
                        for br in ("s", "w"):
                            if br == "s":
                                kts = list(range(0, (q0 + TT) // 128))
                            else:
                                kts = list(range(max(0, (q0 - 512) // 128), (q0 + TT) // 128))
                            for i, kt in enumerate(kts):
                                tiles.append(dict(h=h, br=br, kt=kt, i=i, n=len(kts), delta=q0 - kt * 128))
                    for ti, tl in enumerate(tiles):
                        tl["pb"] = ti % 2
                        tl["es"] = ti % 3
                        tl["e2"] = ti % 2
                        tl["band"] = (tl["br"] == "w") or tl["delta"] < 256

                    def emitS(tl):
                        h, br, kt, pb = tl["h"], tl["br"], tl["kt"], tl["pb"]
                        if tl["band"]:
                            ms = cnt["mb"] % 3
                            cnt["mb"] += 1
                            tl["ms"] = ms
                            kb.dma(Mtb[:, ms, :], self.mtile(h + (6 if br == "w" else 0), 1, tl["delta"], bf=True), "nMb%d" % ms, w=["nMb%d" % ms])
                        kb.op("pe", lambda e: e.matmul(ps_s[pb][:], kT[br][:, g, kt * 128:(kt + 1) * 128], qb[:, h, :], start=True, stop=(br == "w")),
                              r=["nkT" + br, "nqb%d" % h], w=["@nps%d" % pb])
                        if br == "s":
                            kb.op("pe", lambda e: e.matmul(ps_s[pb][:], eblk[:, kt * 128:(kt + 1) * 128], negT[:, g, :], start=False, stop=True),
                                  r=["neblk", "nnegT%d" % g], w=["@nps%d" % pb])

                    def emitE(tl):
                        h, br, kt, pb, es_, e2 = tl["h"], tl["br"], tl["kt"], tl["pb"], tl["es"], tl["e2"]
                        if tl["band"]:
                            ms = tl["ms"]
                            kb.op("act", lambda e: e.activation(out=e32[:, e2, :], in_=ps_s[pb][:], func=AF.Exp), r=["@nps%d" % pb], w=["ne32%d" % e2])
                            kb.op("dve", lambda e: e.tensor_tensor(out=ebt[:, es_, :], in0=e32[:, e2, :], in1=Mtb[:, ms, :], op=ALU.mult),
                                  r=["ne32%d" % e2, "nMb%d" % ms], w=["neb%d" % es_])
                        else:
                            kb.op("act", lambda e: e.activation(out=ebt[:, es_, :], in_=ps_s[pb][:], func=AF.Exp, bias=self.b31bc[:, h:h + 1], scale=1.0),
                                  r=["@nps%d" % pb, "b31"], w=["neb%d" % es_])

                    def emitPV(tl):
                        h, br, kt, es_, i, n = tl["h"], tl["br"], tl["kt"], tl["es"], tl["i"], tl["n"]
                        kb.op("pe", lambda e: e.matmul(ps_o[:], vtm[br][:, g, kt, :], ebt[:, es_, :], start=(i == 0), stop=(i == n - 1)),
                              r=["nv" + br, "neb%d" % es_], w=["@npo"])
                        kb.op("pe", lambda e: e.matmul(ps_d[:], ones_b[:], ebt[:, es_, :], start=(i == 0), stop=(i == n - 1)),
                              r=["n1b", "neb%d" % es_], w=["@npd"])
                        if i == n - 1:
                            combine(h, (1 if br == "s" else 2) * 6 + h, False)

                    emitS(tiles[0])
                    for ti, tl in enumerate(tiles):
                        if ti + 1 < len(tiles):
                            emitS(tiles[ti + 1])
                        emitE(tl)
                        emitPV(tl)
                for h in range(H):
                    kb.op("act", lambda e, h=h: e.copy(out=obf[:, h, :], in_=acc[:, h, :]), r=["nacc%d" % h], w=["nobf"])
                kb.dma(self.mixT.ap()[1280:2048, q0:q0 + TT].rearrange("(h p) t -> p h t", p=128), obf[:], "nos", r=["nobf"])
            kb.barrier()


def make_consts():
    c = np.zeros((128, 1024), np.float32)
    c[:, 0:128] = np.eye(128, dtype=np.float32)
    i = np.arange(128)
    same = (i[:, None] // 64) == (i[None, :] // 64)
    c[:, 128:256] = (same & (i[:, None] <= i[None, :])).astype(np.float32)
    c[:, 256:384] = (same & (i[:, None] >= i[None, :])).astype(np.float32)
    c[:, 384:512] = (same & (i[:, None] > i[None, :])).astype(np.float32)
    c[:, 512:640] = (i[:, None] <= i[None, :]).astype(np.float32)
    return c


def t5_bucket_np(n):
    n = np.maximum(n, 0)
    max_exact = 16
    lr = np.log(np.maximum(n, 1).astype(np.float32) / max_exact) / np.float32(np.log(128 / max_exact))
    large = np.minimum(max_exact + (lr * 16).astype(np.int32), 31)
    return np.where(n < max_exact, n, large)


def make_consts2():
    c = np.zeros((128, 2562), np.float32)
    d = np.arange(128)
    b = t5_bucket_np(d)
    c[b, d] = 1.0
    n = np.arange(256)
    j = np.arange(64)
    ov = ((n[:, None] * 16 < j[None, :] * 64 + 64) & (n[:, None] * 16 + 32 > j[None, :] * 64)).astype(np.float32)
    ov[255] = 0
    ove = np.zeros((256, 65), np.float32)
    ove[:, :64] = ov
    ove[:255, 64] = 1.0
    c[:, 128:258] = ove.reshape(2, 128, 65).transpose(1, 0, 2).reshape(128, 130)
    sel = np.zeros((30, 18, 128), np.float32)
    for k in range(18):
        sel[12 + k, k, :] = 1.0
    c[0:30, 258:258 + 18 * 128] = sel.reshape(30, -1)
    key = np.arange(S)
    eblk = (key[None, :] // 64 == np.arange(64)[:, None]).astype(np.float32)
    q = np.arange(S)
    cur = q // 64
    causal = j[None, :] <= cur[:, None]
    forced = (j[None, :] == 0) | (j[None, :] == cur[:, None]) | (j[None, :] == cur[:, None] - 1)
    a1 = (causal & ~forced).astype(np.float32)
    a2 = np.where(forced, np.float32(1e9), np.where(causal, np.float32(0), np.float32(-1e30))).astype(np.float32)
    selc = np.stack([a1, a2, causal.astype(np.float32)], axis=0)
    return c, eblk, np.ascontiguousarray(selc)


def kernel(**inputs):
    prog = Prog()
    return run_prog(prog, inputs)


def run_prog(prog, inputs, extra=None, cores=8):
    x = np.asarray(inputs["x"], np.float32)
    cst = make_consts()
    pl = lambda a: np.asarray(a, np.float32).reshape(DEPTH, 16, 128).transpose(2, 0, 1).reshape(128, DEPTH * 16)
    gains = np.ascontiguousarray(np.concatenate([pl(inputs["attn_norm"]), pl(inputs["mlp_norm"])], axis=1))
    conv_pl = np.ascontiguousarray(np.asarray(inputs["conv_a"], np.float32).reshape(DEPTH, 4, 18, 128).transpose(0, 3, 2, 1).reshape(DEPTH, 128, 72))
    gdnn = np.ascontiguousarray(np.asarray(inputs["gdn_norm"], np.float32).T)
    cst2, eblk, selc = make_consts2()
    nqk = np.ascontiguousarray(np.concatenate([np.asarray(inputs["nsa_q_norm"], np.float32).T, np.asarray(inputs["nsa_k_norm"], np.float32).T], axis=1))
    cpos_pl = np.ascontiguousarray(np.asarray(inputs["cmp_pos"], np.float32).transpose(0, 1, 3, 2))
    in_maps = []
    for c in range(cores):
        m = {k: np.ascontiguousarray(np.asarray(v, np.float32)) for k, v in inputs.items() if k != "x"}
        m["xT"] = np.ascontiguousarray(x[c].T)
        m["cst"] = cst
        m["gains_in"] = gains
        m["conv_pl"] = conv_pl
        m["cst2"] = cst2
        m["eblk_in"] = eblk
        m["selc_in"] = selc
        m["nqk_in"] = nqk
        m["cpos_pl"] = cpos_pl
        m["gdnn_in"] = gdnn
        if extra:
            m.update(extra)
        in_maps.append(m)
    res = run_bass_kernel_spmd(prog.nc, in_maps, core_ids=list(range(cores)))
    prog.last_results = res.results
    out = np.stack([np.ascontiguousarray(r["yT"].T) for r in res.results], axis=0)
    return out.astype(np.float32)
```

```python
from contextlib import ExitStack
import numpy as np
import concourse.bass as bass
import concourse.mybir as mybir
from concourse.bass_utils import run_bass_kernel_spmd

F32 = mybir.dt.float32
BF16 = mybir.dt.bfloat16
I32 = mybir.dt.int32
AF = mybir.ActivationFunctionType
ALU = mybir.AluOpType
AX = mybir.AxisListType

S = 4096
D = 2048
DEPTH = 4
DFF = 8192
NPROJ = 6430
NJ_IN = 51
TT = 512
NT = S // TT
EPS = 1e-6
GDN_OFF = 1
BG_CAST = True
IN_COLMAP = [(0, 0, 3072), (3084, 3072, 1024), (4108, 4096, 2304), (3072, 6400, 12), (6412, 6412, 18)]
CH_QA, CH_KA, CH_VA, CH_ZA = 0, 6, 12, 18
CH_UB, CH_VB = 24, 28
CH_QC, CH_KCC, CH_VCC, CH_KSL, CH_VSL, CH_KWN, CH_VWN = 32, 38, 40, 42, 44, 46, 48
CH_SM = 50


class KB:
    def __init__(self, nc, es):
        self.nc = nc
        self.es = es
        self.eng = {"pe": nc.tensor, "act": nc.scalar, "dve": nc.vector, "pool": nc.gpsimd, "sp": nc.sync}
        self.sem = {k: es.enter_context(nc.semaphore("s_" + k)) for k in self.eng}
        self.cnt = {k: 0 for k in self.eng}
        self.waited = {k: {} for k in self.eng}
        self.res = {}
        self.dsem = {}
        self.semname = {}

    def _deps(self, e, r, w):
        need = {}

        def add(tok, raw):
            sem, val, te = tok
            if te == e and e == "pe":
                return
            if te == e and not raw:
                return
            k = id(sem)
            if k not in need or need[k][1] < val:
                need[k] = (sem, val)

        r = list(r)
        w = list(w)
        for k in list(r):
            if k.startswith("@"):
                w.append(k)
        for k in r:
            st = self.res.get(k)
            if st and st[0]:
                add(st[0], True)
        for k in w:
            st = self.res.get(k)
            if st:
                if st[0]:
                    add(st[0], False)
                for t in st[1].values():
                    add(t, False)
        wd = self.waited[e]
        for k, (sem, val) in need.items():
            if wd.get(k, 0) < val:
                self.eng[e].wait_ge(sem, val)
                wd[k] = val

    def _upd(self, tok, r, w):
        w = list(w) + [k for k in r if k.startswith("@")]
        r = [k for k in r if not k.startswith("@")]
        for k in r:
            st = self.res.setdefault(k, [None, {}])
            st[1][id(tok[0])] = tok
        for k in w:
            self.res[k] = [tok, {}]

    def op(self, e, fn, r=(), w=()):
        self._deps(e, r, w)
        ins = fn(self.eng[e])
        self.cnt[e] += 1
        ins.then_inc(self.sem[e], 1)
        self._upd((self.sem[e], self.cnt[e], e), r, w)

    def dma(self, out, in_, key, r=(), w=(), q="sp", **kw):
        self._deps(q, r, w)
        if key not in self.dsem:
            self.dsem[key] = [self.es.enter_context(self.nc.semaphore("d%d" % len(self.dsem))), 0]
        ds = self.dsem[key]
        ds[1] += 16
        self.eng[q].dma_start(out=out, in_=in_, **kw).then_inc(ds[0], 16)
        self._upd((ds[0], ds[1], "dma"), r, w)

    def barrier(self):
        for e in self.eng:
            wd = self.waited[e]
            for e2 in self.eng:
                if e2 != e and self.cnt[e2] > wd.get(id(self.sem[e2]), 0):
                    self.eng[e].wait_ge(self.sem[e2], self.cnt[e2])
                    wd[id(self.sem[e2])] = self.cnt[e2]
            for key, (sem, val) in self.dsem.items():
                if val > wd.get(id(sem), 0):
                    self.eng[e].wait_ge(sem, val)
                    wd[id(sem)] = val
        self.res = {}


def dram_ap(handle, offset, pattern):
    return bass.AP(handle, offset, pattern)


class Prog:
    def __init__(self, n_layers=DEPTH, dbg=None, mix_in=False, enable=("gdn", "sgu", "nsa"), phases=("cast", "inproj", "mix", "ffn")):
        self.enable = enable
        self.phases = phases
        self.n_layers = n_layers
        self.dbg = dbg or ()
        self.mix_in = mix_in
        self.nc = bass.Bass("TRN2", target_bir_lowering=False)
        self.build()

    def sbt(self, name, shape, dt):
        self._uid = getattr(self, "_uid", 0) + 1
        return self.nc.sbuf_tensor("%s_%d" % (name, self._uid), shape, dt)

    def pst(self, name, shape, dt):
        self._uid = getattr(self, "_uid", 0) + 1
        return self.nc.psum_tensor("%s_%d" % (name, self._uid), shape, dt)

    def build(self):
        nc = self.nc
        L = DEPTH
        di = lambda n, s: nc.dram_tensor(n, s, F32, kind="ExternalInput")
        self.xT_in = di("xT", [D, S])
        self.attn_norm = di("attn_norm", [L, D])
        self.w_in = di("w_in", [L, D, NPROJ])
        self.conv_a = di("conv_a", [L, 4, 2304])
        self.a_log = di("a_log", [L, 6])
        self.dt_bias = di("dt_bias", [L, 6])
        self.gdn_norm = di("gdn_norm", [L, 128])
        self.sgu_ln_g = di("sgu_ln_g", [L, 512])
        self.sgu_ln_b = di("sgu_ln_b", [L, 512])
        self.sgu_w = di("sgu_w", [L, 4, 128, 128])
        self.sgu_b = di("sgu_b", [L, 4, 128])
        self.nsa_q_norm = di("nsa_q_norm", [L, 128])
        self.nsa_k_norm = di("nsa_k_norm", [L, 128])
        self.cmp_pos = di("cmp_pos", [L, 2, 32, 128])
        self.cmp_w1 = di("cmp_w1", [L, 2, 4096, 128])
        self.cmp_w2 = di("cmp_w2", [L, 2, 128, 128])
        self.rel_bias = di("rel_bias", [32, 6])
        self.w_out = di("w_out", [L, D, D])
        self.mlp_norm = di("mlp_norm", [L, D])
        self.w_up = di("w_up", [L, D, DFF])
        self.w_down = di("w_down", [L, DFF, D])
        self.cst = di("cst", [128, 1024])
        self.gains_in = di("gains_in", [128, 2 * L * 16])
        self.conv_pl = di("conv_pl", [L, 128, 72])
        self.gdnn_in = di("gdnn_in", [128, L])
        self.cst2 = di("cst2", [128, 2562])
        self.nqk_in = di("nqk_in", [128, 2 * L])
        self.cpos_pl = di("cpos_pl", [L, 2, 128, 32])
        self.eblk_in = di("eblk_in", [64, S])
        self.selc_in = di("selc_in", [3, S, 64])
        if self.mix_in:
            self.mix_dbg = di("mix_dbg", [D, S])
        self.yT = nc.dram_tensor("yT", [D, S], F32, kind="ExternalOutput")
        ds = lambda n, s, dt: nc.dram_tensor(n, s, dt, kind="Internal")
        self.xres = ds("xres", [D, S], F32)
        self.projT = ds("projT", [NJ_IN * 128, S], F32)
        self.small_tm = ds("small_tm", [S, 32], F32)
        self.mixT = ds("mixT", [D, S], BF16)
        self.wt_in = [ds("wt_in%d" % l, [NJ_IN, 128, 16, 128], BF16) for l in range(L)]
        self.wt_out = [ds("wt_out%d" % l, [16, 128, 16, 128], BF16) for l in range(L)]
        self.wt_up = [ds("wt_up%d" % l, [64, 128, 16, 128], BF16) for l in range(L)]
        self.wt_dn = [ds("wt_dn%d" % l, [16, 128, 64, 128], BF16) for l in range(L)]
        self.fvec = ds("fvec", [12, self.LF], F32)
        self.Btab = [ds("btab%d" % i, [128, self.LF], F32) for i in range(12)]
        self.fvecb = ds("fvecb", [12, self.LF], BF16)
        self.Btabb = [ds("btabb%d" % i, [128, self.LF], BF16) for i in range(12)]
        self.dbg_out = {}
        if "proj" in self.dbg:
            self.dbg_out["proj"] = nc.dram_tensor("dbg_proj", [NJ_IN * 128, S], F32, kind="ExternalOutput")
            self.dbg_out["small"] = nc.dram_tensor("dbg_small", [S, 32], F32, kind="ExternalOutput")
        if "mix" in self.dbg:
            self.dbg_out["mix"] = nc.dram_tensor("dbg_mix", [D, S], BF16, kind="ExternalOutput")

        with ExitStack() as es:
            self.kb = kb = KB(nc, es)
            sb = lambda n, s, dt: es.enter_context(self.sbt(n, s, dt))
            self.c_f = sb("c_f", [128, 1024], F32)
            self.ones_f = sb("ones_f", [128, 128], F32)
            self.gains = sb("gains", [128, 2 * L * 16], F32)
            kb.dma(self.c_f[:], self.cst.ap(), "cst", w=["cst"])
            kb.op("dve", lambda e: e.memset(self.ones_f[:], 1.0), w=["ones"])
            self.ones_b = sb("ones_b", [128, 128], BF16)
            kb.op("dve", lambda e: e.memset(self.ones_b[:], 1.0), w=["onesb"])
            kb.dma(self.gains[:], self.gains_in.ap(), "g1", w=["gains"])
            self.gdnn = sb("gdnn", [128, L], F32)
            kb.dma(self.gdnn[:], self.gdnn_in.ap(), "g3", w=["gdnn"])
            self.ident = self.c_f[:, 0:128]
            self.c2 = sb("c2", [128, 2562], F32)
            self.nqk = sb("nqk", [128, 2 * L], F32)
            self.b31bc = sb("b31bc", [128, 6], F32)
            kb.dma(self.c2[:], self.cst2.ap(), "cst2", w=["cst2"])
            kb.dma(self.nqk[:], self.nqk_in.ap(), "nqk", w=["nqk"])
            kb.barrier()
            if "nsa" in self.enable and not self.mix_in:
                self.nsa_tables()
            if "cast" in self.phases:
                with self.sbt("cu_f", [128, 2, 4224], F32) as sf0, self.sbt("cu_b", [128, 2, 4224], BF16) as sbf0:
                    specs0 = self.cast_specs(0, ("in",)) if BG_CAST else [s_ for l_ in range(self.n_layers) for s_ in self.cast_specs(l_, ("in", "rest"))]
                    tk = self.cast_runner(specs0, sf0, sbf0, 2, 2, "cu")
                    while tk():
                        pass
                    kb.barrier()
            kb.barrier()
            for l in range(self.n_layers):
                if "inproj" in self.phases:
                    self.phase_inproj(l)
                kb.barrier()
                if "proj" in self.dbg and l == 0:
                    self.copy_dram(self.dbg_out["proj"], self.projT, NJ_IN * 128, S, F32)
                    kb.dma(self.dbg_out["small"].ap(), self.small_tm.ap(), "dbgs")
                    kb.barrier()
                if self.mix_in:
                    self.cast_mix_dbg()
                elif "mix" in self.phases:
                    self.phase_mixers(l)
                kb.barrier()
                if "mix" in self.dbg and l == 0:
                    kb.dma(self.dbg_out["mix"].ap(), self.mixT.ap(), "dbgm")
                    kb.barrier()
                if "ffn" in self.phases:
                    self.phase_ffn(l, last=(l == self.n_layers - 1))
                else:
                    kb.dma(self.yT.ap()[0:128, :], self.xT_in.ap()[0:128, :], "dummyy")
                kb.barrier()

    def copy_dram(self, dst, src, rows, cols, dt):
        kb = self.kb
        for r0 in range(0, rows, 1024):
            n = min(1024, rows - r0)
            kb.dma(dst.ap()[r0:r0 + n, :], src.ap()[r0:r0 + n, :], "cpd")

    def cast_mix_dbg(self):
        kb, nc = self.kb, self.nc
        with self.sbt("mdb_f", [128, S], F32) as tf, self.sbt("mdb_b", [128, S], BF16) as tb:
            for k in range(16):
                kb.dma(tf[:], self.mix_dbg.ap()[k * 128:(k + 1) * 128, :], "mdbl", w=["mdbf"])
                kb.op("dve", lambda e: e.tensor_copy(out=tb[:], in_=tf[:]), r=["mdbf"], w=["mdbb"])
                kb.dma(self.mixT.ap()[k * 128:(k + 1) * 128, :], tb[:], "mdbs", r=["mdbb"])
            kb.barrier()

    def phase_cast(self, l):
        kb, nc = self.kb, self.nc
        with self.sbt("cs_f", [128, 2, 8192], F32) as sf, self.sbt("cs_b", [128, 2, 8192], BF16) as sbf:
            self._cast_i = 0

            def unit(src_ap_list, n_src_cols, colmap, dst_fn, n_dst_cols, pad_from=None):
                i = self._cast_i
                self._cast_i += 1
                s = i % 2
                fk, bk = "csf%d" % s, "csb%d" % s
                for (o, ap, n) in src_ap_list:
                    kb.dma(sf[:, s, o:o + n], ap, "csl%d" % s, w=[fk])
                pieces = []
                for (sc, dc, n) in colmap:
                    step = (n + 3) // 4 if n >= 1536 else n
                    for a in range(0, n, step):
                        pieces.append((sc + a, dc + a, min(step, n - a)))
                engs = ["dve", "act"]
                first = True
                for pi, (sc, dc, n) in enumerate(pieces):
                    e = engs[pi % 2]
                    if e == "act":
                        f = lambda en, sc=sc, dc=dc, n=n: en.copy(out=sbf[:, s, dc:dc + n], in_=sf[:, s, sc:sc + n])
                    else:
                        f = lambda en, sc=sc, dc=dc, n=n: en.tensor_copy(out=sbf[:, s, dc:dc + n], in_=sf[:, s, sc:sc + n])
                    kb.op(e, f, r=[fk], w=[bk + "_%d" % pi])
                if pad_from is not None:
                    kb.op("pool", lambda en: en.memset(sbf[:, s, pad_from:n_dst_cols], 0.0), w=[bk + "_pad"])
                rk = [bk + "_%d" % pi for pi in range(len(pieces))] + ([bk + "_pad"] if pad_from is not None else [])
                for (dst_ap, c0, n) in dst_fn:
                    kb.dma(dst_ap, sbf[:, s, c0:c0 + n].rearrange("p (j c) -> p j c", c=128), "css%d" % s, r=rk, q="pool")

            for kc in range(16):
                src = self.w_in.ap()[l, kc * 128:(kc + 1) * 128, :]
                dst = self.wt_in[l].ap()[:, :, kc, :].rearrange("j p c -> p j c")
                unit([(0, src, NPROJ)], NPROJ, IN_COLMAP, [(dst, 0, NJ_IN * 128)], NJ_IN * 128, pad_from=NPROJ)
            for kc in range(16):
                src = self.w_up.ap()[l, kc * 128:(kc + 1) * 128, :]
                dst = self.wt_up[l].ap()[:, :, kc, :].rearrange("j p c -> p j c")
                unit([(0, src, DFF)], DFF, [(0, 0, DFF)], [(dst, 0, DFF)], DFF)
            for k4 in range(16):
                srcs = [(i * 2048, self.w_down.ap()[l, (k4 * 4 + i) * 128:(k4 * 4 + i + 1) * 128, :], 2048) for i in range(4)]
                dsts = [(self.wt_dn[l].ap()[:, :, k4 * 4 + i, :].rearrange("j p c -> p j c"), i * 2048, 2048) for i in range(4)]
                unit(srcs, 8192, [(0, 0, 8192)], dsts, 8192)
            for k4 in range(4):
                srcs = [(i * 2048, self.w_out.ap()[l, (k4 * 4 + i) * 128:(k4 * 4 + i + 1) * 128, :], 2048) for i in range(4)]
                dsts = [(self.wt_out[l].ap()[:, :, k4 * 4 + i, :].rearrange("j p c -> p j c"), i * 2048, 2048) for i in range(4)]
                unit(srcs, 8192, [(0, 0, 8192)], dsts, 8192)
            kb.barrier()


    def cast_specs(self, l, which):
        specs = []
        if "in" in which:
            for kc in range(16):
                row = self.w_in.ap()[l, kc * 128:(kc + 1) * 128, :]
                dA = self.wt_in[l].ap()[0:32, :, kc, :].rearrange("j p c -> p j c")
                dB = self.wt_in[l].ap()[32:51, :, kc, :].rearrange("j p c -> p j c")
                specs.append(([(0, row[:, 0:4108], 4108)], [(0, 0, 3072), (3084, 3072, 1024)], None, [(dA, 0, 4096)]))
                specs.append(([(0, row[:, 4108:6430], 2322), (2322, row[:, 3072:3084], 12)],
                              [(0, 0, 2304), (2322, 2304, 12), (2304, 2316, 18)], (2334, 2432), [(dB, 0, 2432)]))
        if "rest" in which:
            for kc in range(16):
                for hf in range(2):
                    src = self.w_up.ap()[l, kc * 128:(kc + 1) * 128, hf * 4096:(hf + 1) * 4096]
                    dst = self.wt_up[l].ap()[hf * 32:(hf + 1) * 32, :, kc, :].rearrange("j p c -> p j c")
                    specs.append(([(0, src, 4096)], [(0, 0, 4096)], None, [(dst, 0, 4096)]))
            for k2 in range(32):
                loads = [(i * 2048, self.w_down.ap()[l, (k2 * 2 + i) * 128:(k2 * 2 + i + 1) * 128, :], 2048) for i in range(2)]
                stores = [(self.wt_dn[l].ap()[:, :, k2 * 2 + i, :].rearrange("j p c -> p j c"), i * 2048, 2048) for i in range(2)]
                specs.append((loads, [(0, 0, 4096)], None, stores))
            for k2 in range(8):
                loads = [(i * 2048, self.w_out.ap()[l, (k2 * 2 + i) * 128:(k2 * 2 + i + 1) * 128, :], 2048) for i in range(2)]
                stores = [(self.wt_out[l].ap()[:, :, k2 * 2 + i, :].rearrange("j p c -> p j c"), i * 2048, 2048) for i in range(2)]
                specs.append((loads, [(0, 0, 4096)], None, stores))
        return specs

    def cast_runner(self, specs, sf, sbf, nf, nb, tag):
        kb = self.kb
        st = {"i": 0, "loaded": 0}

        def load(u):
            s = u % nf
            for (o, ap, n) in specs[u][0]:
                kb.dma(sf[:, s, o:o + n], ap, "%sl%d" % (tag, s), w=["%sf%d" % (tag, s)])

        def tick():
            i = st["i"]
            if i >= len(specs):
                return False
            while st["loaded"] <= min(i, len(specs) - 1) or (i == 0 and st["loaded"] <= min(nf - 1, len(specs) - 1)):
                load(st["loaded"])
                st["loaded"] += 1
            s, sb_ = i % nf, i % nb
            fk, bk = "%sf%d" % (tag, s), "%sb%d" % (tag, sb_)
            loads, convs, pad, stores = specs[i]
            keys = []
            pi = 0
            for (sc, dc, n) in convs:
                step = (n + 1) // 2 if n >= 1024 else n
                for a in range(0, n, step):
                    m = min(step, n - a)
                    e = "dve" if pi % 2 == 0 else "act"
                    k_ = "%s_%d" % (bk, pi)
                    if e == "act":
                        kb.op("act", lambda en, sc=sc, dc=dc, a=a, m=m: en.copy(out=sbf[:, sb_, dc + a:dc + a + m], in_=sf[:, s, sc + a:sc + a + m]), r=[fk], w=[k_])
                    else:
                        kb.op("dve", lambda en, sc=sc, dc=dc, a=a, m=m: en.tensor_copy(out=sbf[:, sb_, dc + a:dc + a + m], in_=sf[:, s, sc + a:sc + a + m]), r=[fk], w=[k_])
                    keys.append(k_)
                    pi += 1
            if pad is not None:
                kb.op("pool", lambda en: en.memset(sbf[:, sb_, pad[0]:pad[1]], 0.0), w=[bk + "_pad"])
                keys.append(bk + "_pad")
            allk = ["%s_%d" % (bk, q) for q in range(8)] + [bk + "_pad"]
            for (dst_ap, c0, n) in stores:
                kb.dma(dst_ap, sbf[:, sb_, c0:c0 + n].rearrange("p (j c) -> p j c", c=128), "%ss%d" % (tag, sb_), r=keys, w=[], q="pool")
            for k_ in allk:
                if k_ not in keys:
                    stt = kb.res.setdefault(k_, [None, {}])
                    for kk in keys[:1]:
                        stt[1].update(kb.res[kk][1])
            st["i"] += 1
            if st["loaded"] < len(specs) and st["loaded"] <= i + nf:
                load(st["loaded"])
                st["loaded"] += 1
            return True

        return tick

    def rmsnorm_tile(self, xt, ht, gain_col0, sq, rstd, ps_ss, tag, xkeys):
        kb = self.kb
        htag = tag
        tag = tag[0]
        for kc in range(16):
            s = kc % 2
            kb.op("act", lambda e, kc=kc, s=s: e.activation(out=sq[:, s, :], in_=xt[:, kc, :], func=AF.Square),
                  r=[xkeys[kc]], w=[tag + "sq%d" % s])
            kb.op("pe", lambda e, kc=kc, s=s: e.matmul(ps_ss[:], self.ones_b[:], sq[:, s, :], start=(kc == 0), stop=(kc == 15)),
                  r=[tag + "sq%d" % s, "onesb"], w=[tag + "ss"])
        kb.op("act", lambda e: e.activation(out=rstd[:], in_=ps_ss[:], func=AF.Ln, bias=self.eps_t[:, 0:1], scale=1.0 / D),
              r=[tag + "ss", "eps"], w=[tag + "rstd"])
        kb.op("act", lambda e: e.activation(out=rstd[:], in_=rstd[:], func=AF.Exp, scale=-0.5), r=[tag + "rstd"], w=[tag + "rstd"])
        for kc in range(16):
            kb.op("dve", lambda e, kc=kc: e.scalar_tensor_tensor(out=ht[:, kc, :], in0=xt[:, kc, :],
                                                              scalar=self.gains[:, gain_col0 + kc:gain_col0 + kc + 1],
                                                              in1=rstd[:], op0=ALU.mult, op1=ALU.mult),
                  r=[xkeys[kc], tag + "rstd", "gains"], w=[htag + "h%d" % kc])

    def phase_inproj(self, l):
        kb, nc = self.kb, self.nc
        xsrc = self.xT_in if l == 0 else self.xres
        with ExitStack() as es:
            sb = lambda n, s, dt: es.enter_context(self.sbt(n, s, dt))
            xt = sb("a_x", [128, 16, TT], F32)
            ht2 = sb("a_h", [128, 2, 16, TT], BF16)
            sq = sb("a_sq", [128, 2, TT], BF16)
            rstd = sb("a_rstd", [128, TT], F32)
            self.eps_t = sb("a_eps", [128, 1], F32)
            NW = 4
            wt = sb("a_w", [128, NW, 16, 128], BF16)
            NO = 4
            ot = sb("a_o", [128, NO, TT], F32)
            osm = sb("a_osm", [128, 4, 32], F32)
            ps_ss = es.enter_context(self.pst("a_pss", [128, TT], F32))
            ps_o = [es.enter_context(self.pst("a_po%d" % i, [128, TT], F32)) for i in range(4)]
            ps_sm = es.enter_context(self.pst("a_psm", [128, 4, 32], F32))
            kb.op("pool", lambda e: e.memset(self.eps_t[:], EPS), w=["eps"])
            ui = 0

            def prep_tile(t):
                t0 = t * TT
                kb.dma(xt[:], xsrc.ap()[:, t0:t0 + TT].rearrange("(k p) t -> p k t", p=128), "ax", w=["ax"])
                if l == 0:
                    kb.dma(self.xres.ap()[:, t0:t0 + TT].rearrange("(k p) t -> p k t", p=128), xt[:], "axs", r=["ax"], q="pool")
                self.rmsnorm_tile(xt, ht2[:, t % 2], l * 16, sq, rstd, ps_ss, "a%d" % (t % 2), ["ax"] * 16)

            prep_tile(0)
            for t in range(NT):
                t0 = t * TT
                ht = ht2[:, t % 2]
                HK = ["a%dh%d" % (t % 2, k) for k in range(16)]
                for j in range(NJ_IN):
                    if j == 6 and t + 1 < NT:
                        prep_tile(t + 1)
                    ws = ui % NW
                    pb = ui % 4
                    os_ = ui % NO
                    ui += 1
                    kb.dma(wt[:, ws], self.wt_in[l].ap()[j], "aw%d" % ws, w=["aw%d" % ws])
                    M = 128 if j < CH_SM else 30
                    for kc in range(16):
                        kb.op("pe", lambda e, kc=kc, ws=ws, pb=pb, M=M: e.matmul(ps_o[pb][0:M, :], wt[:, ws, kc, 0:M], ht[:, kc, :],
                                                                                 start=(kc == 0), stop=(kc == 15)),
                              r=["aw%d" % ws, HK[kc]], w=["apo%d" % pb])
                    ev = "act" if ui % 2 == 0 else "dve"
                    if ev == "act":
                        kb.op("act", lambda e, pb=pb, os_=os_, M=M: e.copy(out=ot[0:M, os_, :], in_=ps_o[pb][0:M, :]),
                              r=["apo%d" % pb], w=["ao%d" % os_])
                    else:
                        kb.op("dve", lambda e, pb=pb, os_=os_, M=M: e.tensor_copy(out=ot[0:M, os_, :], in_=ps_o[pb][0:M, :]),
                              r=["apo%d" % pb], w=["ao%d" % os_])
                    kb.dma(self.projT.ap()[j * 128:j * 128 + M, t0:t0 + TT], ot[0:M, os_, :], "aos%d" % os_, r=["ao%d" % os_], q="pool")
                    if j == CH_SM:
                        for q4 in range(4):
                            for kc in range(16):
                                kb.op("pe", lambda e, kc=kc, ws=ws, q4=q4: e.matmul(ps_sm[:, q4, 0:30], ht[:, kc, q4 * 128:(q4 + 1) * 128],
                                                                                   wt[:, ws, kc, 0:30], start=(kc == 0), stop=(kc == 15)),
                                      r=["aw%d" % ws, HK[kc]], w=["apsm"])
                        kb.op("dve", lambda e: e.tensor_copy(out=osm[:, :, 0:30], in_=ps_sm[:, :, 0:30]), r=["apsm"], w=["aosm"])
                        kb.dma(self.small_tm.ap()[t0:t0 + TT, 0:30].rearrange("(q p) c -> p q c", p=128), osm[:, :, 0:30], "aosm", r=["aosm"], q="pool")
            kb.barrier()

    def phase_ffn(self, l, last):
        kb, nc = self.kb, self.nc
        with ExitStack() as es:
            sb = lambda n, s, dt: es.enter_context(self.sbt(n, s, dt))
            xt = sb("f_x", [128, 16, TT], F32)
            ht = sb("f_h", [128, 16, TT], BF16)
            at = sb("f_a", [128, 64, TT], BF16)
            sq = sb("f_sq", [128, 2, TT], BF16)
            rstd = sb("f_rstd", [128, TT], F32)
            rl = sb("f_rl", [128, 2, TT], F32)
            self.eps_t = sb("f_eps", [128, 1], F32)
            NW = 3
            wt = sb("f_w", [128, NW, 16, 128], BF16)
            wd = sb("f_wd", [128, 2, 64, 128], BF16)
            ps_ss = es.enter_context(self.pst("f_pss", [128, TT], F32))
            ps_o = [es.enter_context(self.pst("f_po%d" % i, [128, TT], F32)) for i in range(4)]
            kb.op("pool", lambda e: e.memset(self.eps_t[:], EPS), w=["eps"])
            HK = ["fh%d" % k for k in range(16)]
            XK = ["fx%d" % k for k in range(16)]
            ui = 0
            for t in range(NT):
                t0 = t * TT
                kb.dma(ht[:], self.mixT.ap()[:, t0:t0 + TT].rearrange("(k p) t -> p k t", p=128), "fm", w=HK)
                for j in range(16):
                    ws, pb = ui % NW, ui % 4
                    ui += 1
                    kb.dma(wt[:, ws], self.wt_out[l].ap()[j], "fw%d" % ws, w=["fw%d" % ws])
                    kb.dma(xt[:, j, :], self.xres.ap()[j * 128:(j + 1) * 128, t0:t0 + TT], "fxl%d" % (j % 8), w=[XK[j]])
                    for kc in range(16):
                        kb.op("pe", lambda e, kc=kc, ws=ws, pb=pb: e.matmul(ps_o[pb][:], wt[:, ws, kc, :], ht[:, kc, :],
                                                                           start=(kc == 0), stop=(kc == 15)),
                              r=["fw%d" % ws, HK[kc]], w=["fpo%d" % pb])
                    kb.op("dve", lambda e, j=j, pb=pb: e.tensor_tensor(out=xt[:, j, :], in0=xt[:, j, :], in1=ps_o[pb][:], op=ALU.add),
                          r=["fpo%d" % pb, XK[j]], w=[XK[j]])
                self.rmsnorm_tile(xt, ht, DEPTH * 16 + l * 16, sq, rstd, ps_ss, "f", XK)
                for j in range(64):
                    ws, pb = ui % NW, ui % 4
                    ui += 1
                    kb.dma(wt[:, ws], self.wt_up[l].ap()[j], "fw%d" % ws, w=["fw%d" % ws])
                    for kc in range(16):
                        kb.op("pe", lambda e, kc=kc, ws=ws, pb=pb: e.matmul(ps_o[pb][:], wt[:, ws, kc, :], ht[:, kc, :],
                                                                           start=(kc == 0), stop=(kc == 15)),
                              r=["fw%d" % ws, HK[kc]], w=["fpo%d" % pb])
                    s2 = j % 2
                    kb.op("act", lambda e, pb=pb, s2=s2: e.activation(out=rl[:, s2, :], in_=ps_o[pb][:], func=AF.Relu),
                          r=["fpo%d" % pb], w=["frl%d" % s2])
                    kb.op("dve", lambda e, j=j, s2=s2: e.tensor_tensor(out=at[:, j, :], in0=rl[:, s2, :], in1=rl[:, s2, :], op=ALU.mult),
                          r=["frl%d" % s2], w=["fa%d" % j])
                for n in range(16):
                    s2, pb = n % 2, ui % 4
                    ui += 1
                    kb.dma(wd[:, s2], self.wt_dn[l].ap()[n], "fwd%d" % s2, w=["fwd%d" % s2])
                    for f in range(64):
                        kb.op("pe", lambda e, f=f, s2=s2, pb=pb: e.matmul(ps_o[pb][:], wd[:, s2, f, :], at[:, f, :],
                                                                          start=(f == 0), stop=(f == 63)),
                              r=["fwd%d" % s2, "fa%d" % f], w=["fpo%d" % pb])
                    kb.op("dve", lambda e, n=n, pb=pb: e.tensor_tensor(out=xt[:, n, :], in0=xt[:, n, :], in1=ps_o[pb][:], op=ALU.add),
                          r=["fpo%d" % pb, XK[n]], w=[XK[n]])
                    dst = self.yT if last else self.xres
                    kb.dma(dst.ap()[n * 128:(n + 1) * 128, t0:t0 + TT], xt[:, n, :], "fxs%d" % (n % 4), r=[XK[n]], q="pool")
            kb.barrier()

    def phase_mixers(self, l):
        if "gdn" in self.enable:
            self.phase_gdn(l)
            self.kb.barrier()
        if "sgu" in self.enable:
            self.phase_sgu(l)
            self.kb.barrier()
        if "nsa" in self.enable:
            self.phase_nsa(l)
            self.kb.barrier()

    def gelu(self, e_name, out, in_, r, w):
        self.kb.op("act", lambda e: e.activation(out=out, in_=in_, func=AF.Gelu_apprx_tanh), r=r, w=w)

    def bcast_row(self, handle, offset, n):
        return bass.AP(handle, offset, [[0, 128], [1, n]])

    def phase_sgu(self, l):
        kb, nc = self.kb, self.nc
        with ExitStack() as es:
            sb = lambda n, s, dt: es.enter_context(self.sbt(n, s, dt))
            wnat = sb("s_wn", [128, 4, 128], F32)
            wTm = sb("s_wT", [128, 4, 128], F32)
            brow = sb("s_br", [1, 512], F32)
            lng = sb("s_lng", [128, 512], F32)
            lnb = sb("s_lnb", [128, 512], F32)
            ut = sb("s_u", [128, 2, 4, TT], F32)
            vt = sb("s_v", [128, 2, 4, TT], F32)
            vn = sb("s_vn", [128, 2, 512], F32)
            st6 = sb("s_st", [128, 6], F32)
            mv = sb("s_mv", [128, 2], F32)
            rs = sb("s_rs", [128, 1], F32)
            eps_t = sb("s_eps", [128, 1], F32)
            obf = sb("s_o", [128, 2, 4, TT], BF16)
            ps_tr = es.enter_context(self.pst("s_ptr", [128, 512], F32))
            ps_m = es.enter_context(self.pst("s_pm", [128, 4, 128], F32))
            maskT = self.c_f[:, 512:640]
            kb.op("pool", lambda e: e.memset(eps_t[:], EPS), w=["seps"])
            kb.dma(wnat[:], self.sgu_w.ap()[l].rearrange("g t s -> t g s"), "swn", w=["swn"])
            kb.dma(brow[:], self.sgu_b.ap()[l:l + 1].rearrange("a g t -> a (g t)"), "sbr", w=["sbr"])
            kb.dma(lng[:], self.bcast_row(self.sgu_ln_g, l * 512, 512), "slg", w=["slg"])
            kb.dma(lnb[:], self.bcast_row(self.sgu_ln_b, l * 512, 512), "slb", w=["slb"])
            for g in range(4):
                kb.op("pe", lambda e, g=g: e.transpose(ps_m[:, g, :], wnat[:, g, :], self.ident), r=["swn", "cst"], w=["spm"])
            kb.op("dve", lambda e: e.tensor_tensor(out=wTm[:], in0=ps_m[:], in1=maskT.unsqueeze(1).to_broadcast([128, 4, 128]), op=ALU.mult),
                  r=["spm", "cst"], w=["swT"])
            for t in range(NT):
                t0 = t * TT
                s = t % 2
                kb.dma(ut[:, s], self.projT.ap()[CH_UB * 128:(CH_UB + 4) * 128, t0:t0 + TT].rearrange("(g p) t -> p g t", p=128), "su%d" % s, w=["su%d" % s])
                kb.dma(vt[:, s], self.projT.ap()[CH_VB * 128:(CH_VB + 4) * 128, t0:t0 + TT].rearrange("(g p) t -> p g t", p=128), "sv%d" % s, w=["sv%d" % s])
                for g in range(4):
                    self.gelu("act", ut[:, s, g, :], ut[:, s, g, :], ["su%d" % s], ["su%d" % s])
                    self.gelu("act", vt[:, s, g, :], vt[:, s, g, :], ["sv%d" % s], ["sv%d" % s])
                for c4 in range(4):
                    cs = slice(c4 * 128, (c4 + 1) * 128)
                    v2 = c4 % 2
                    for g in range(4):
                        kb.op("pe", lambda e, g=g, cs=cs: e.transpose(ps_tr[:, g * 128:(g + 1) * 128], vt[:, s, g, cs], self.ident),
                              r=["sv%d" % s, "cst"], w=["sptr"])
                    kb.op("dve", lambda e: e.bn_stats(out=st6[:], in_=ps_tr[:]), r=["sptr"], w=["sst"])
                    kb.op("dve", lambda e: e.bn_aggr(out=mv[:], in_=st6[:]), r=["sst"], w=["smv"])
                    kb.op("act", lambda e: e.activation(out=rs[:], in_=mv[:, 1:2], func=AF.Ln, bias=eps_t[:, 0:1], scale=1.0),
                          r=["smv", "seps"], w=["srs"])
                    kb.op("act", lambda e: e.activation(out=rs[:], in_=rs[:], func=AF.Exp, scale=-0.5), r=["srs"], w=["srs"])
                    kb.op("dve", lambda e, v2=v2: e.tensor_scalar(out=vn[:, v2, :], in0=ps_tr[:], scalar1=mv[:, 0:1], scalar2=rs[:, 0:1],
                                                                   op0=ALU.subtract, op1=ALU.mult), r=["sptr", "smv", "srs"], w=["svn%d" % v2])
                    kb.op("dve", lambda e, v2=v2: e.tensor_tensor(out=vn[:, v2, :], in0=vn[:, v2, :], in1=lng[:], op=ALU.mult),
                          r=["svn%d" % v2, "slg"], w=["svn%d" % v2])
                    kb.op("dve", lambda e, v2=v2: e.tensor_tensor(out=vn[:, v2, :], in0=vn[:, v2, :], in1=lnb[:], op=ALU.add),
                          r=["svn%d" % v2, "slb"], w=["svn%d" % v2])
                    for g in range(4):
                        kb.op("pe", lambda e, g=g, v2=v2: e.matmul(ps_m[:, g, :], vn[:, v2, g * 128:(g + 1) * 128], wTm[:, g, :], start=True, stop=False),
                              r=["svn%d" % v2, "swT"], w=["spm"])
                        kb.op("pe", lambda e, g=g: e.matmul(ps_m[:, g, :], self.ones_f[0:1, :], brow[0:1, g * 128:(g + 1) * 128], start=False, stop=True),
                              r=["sbr", "ones"], w=["spm"])
                    kb.op("dve", lambda e, cs=cs: e.tensor_tensor(out=obf[:, s, :, cs], in0=ut[:, s, :, cs], in1=ps_m[:], op=ALU.mult),
                          r=["spm", "su%d" % s], w=["so%d" % s])
                kb.dma(self.mixT.ap()[768:1280, t0:t0 + TT].rearrange("(g p) t -> p g t", p=128), obf[:, s], "sos%d" % s, r=["so%d" % s])
            kb.barrier()

    def phase_gdn(self, l):
        kb, nc = self.kb, self.nc
        H = 6
        with ExitStack() as es:
            sb = lambda n, s, dt: es.enter_context(self.sbt(n, s, dt))
            PS = es.enter_context(self.pst("g_ps", [128, 4096], F32))
            slot = lambda i: PS[:, i * 128:(i + 1) * 128]
            P1 = lambda h: slot(4 * h)
            P2 = lambda h: slot(4 * h + 1)
            P3 = lambda h: slot(4 * h + 2)
            P4 = lambda h: slot(4 * h + 3)
            ps_ss = PS[:, 24 * 128:28 * 128]
            ps_sm = PS[:, 28 * 128:28 * 128 + 6]
            convw = sb("g_cw", [128, 72], F32)
            dtb = sb("g_dtb", [128, 6], F32)
            nega = sb("g_na", [128, 6], F32)
            eps_t = sb("g_eps", [128, 1], F32)
            sm = sb("g_sm", [128, 4, 30], F32)
            beta = sb("g_beta", [128, 4, 6], F32)
            nbeta = sb("g_nbeta", [128, 4, 6], F32)
            gtm = sb("g_g", [128, 4, 6], F32)
            xin = sb("g_xin", [128, 2, 3, 515], F32)
            QKV = sb("g_qkv", [128, 6, 3, 512], F32)
            sq = sb("g_sq", [128, 512], BF16)
            rn = sb("g_rn", [128, 512], F32)
            sz = sb("g_sz", [128, 6, 512], F32)
            obf = sb("g_obf", [128, 6, 512], BF16)
            Sst = sb("g_S", [128, 6, 128], F32)
            names = ["Ktm", "Vb", "rb", "ebc", "E", "Es", "Ei", "NT", "Aqk", "N", "AqkT", "R", "Pa", "Pb", "PTa", "PTb",
                     "Um0", "Um1", "Kw", "WT", "QgT", "Kpp", "Vnew", "Osb", "On"]
            A = {n: sb("g_" + n, [128, 6, 128], F32) for n in names}
            A["tE"] = A["E"]
            bg_tick = None
            if BG_CAST and "cast" in self.phases:
                bsf = sb("g_csf", [128, 2, 4224], F32)
                bsb = sb("g_csb", [128, 1, 4224], BF16)
                bspecs = self.cast_specs(l, ("rest",)) + (self.cast_specs(l + 1, ("in",)) if l + 1 < self.n_layers else [])
                bg_tick = self.cast_runner(bspecs, bsf, bsb, 2, 1, "cg")
            gc = sb("g_gc", [128, 6], F32)
            s1 = sb("g_s1", [128, 6], F32)
            dl = sb("g_dl", [128, 6], F32)
            egl = sb("g_egl", [128, 6, 2], F32)
            ssq = sb("g_ssq", [128, 6], F32)
            rm = sb("g_rm", [128, 4], F32)
            triBD = self.c_f[:, 128:256]
            m_incl = self.c_f[:, 256:384]
            m_strict = self.c_f[:, 384:512]
            GK = lambda n, h: "%s%d" % (n, 3 * (h // 3))
            K = lambda n, h: ("@gb%d" % h) if n in ("P1", "P2", "P3", "P4") else "g%s%d" % (n, h)
            kb.op("pool", lambda e: e.memset(eps_t[:], EPS), w=["geps"])
            kb.op("pool", lambda e: e.memset(rm[:], 0.0), w=["grm"])
            kb.op("pool", lambda e: e.memset(rm[0:64, 0:1], 1.0), w=["grm"])
            kb.op("pool", lambda e: e.memset(rm[64:128, 1:2], 1.0), w=["grm"])
            kb.op("pool", lambda e: e.memset(rm[0:64, 2:3], -1.0), w=["grm"])
            kb.op("pool", lambda e: e.memset(rm[64:128, 3:4], -1.0), w=["grm"])
            kb.op("pool", lambda e: e.memset(Sst[:], 0.0), w=[K("S", h) for h in range(H)])
            kb.dma(convw[:], self.conv_pl.ap()[l], "gcw", w=["gcw"])
            kb.dma(dtb[:], self.bcast_row(self.dt_bias, l * 6, 6), "gdtb", w=["gdtb"])
            kb.dma(nega[:], self.bcast_row(self.a_log, l * 6, 6), "gna", w=["gna"])
            kb.op("act", lambda e: e.activation(out=nega[:], in_=nega[:], func=AF.Exp), r=["gna"], w=["gna"])
            kb.op("dve", lambda e: e.tensor_scalar(out=nega[:], in0=nega[:], scalar1=-1.0, scalar2=None, op0=ALU.mult), r=["gna"], w=["gna"])
            for b in range(NT):
                t0 = b * TT
                kb.dma(sm[:], self.small_tm.ap()[t0:t0 + TT, 0:30].rearrange("(q p) c -> p q c", p=128), "gsm", w=["gsm"])
                kb.dma(sz[:], self.projT.ap()[CH_ZA * 128:(CH_ZA + 6) * 128, t0:t0 + TT].rearrange("(h p) t -> p h t", p=128), "gsz", w=["gsz"])
                kb.op("act", lambda e: e.activation(out=sz[:], in_=sz[:], func=AF.Silu), r=["gsz"], w=["gsz"])
                kb.op("act", lambda e: e.activation(out=beta[:], in_=sm[:, :, 0:6], func=AF.Sigmoid), r=["gsm"], w=["gbeta"])
                kb.op("dve", lambda e: e.tensor_scalar(out=nbeta[:], in0=beta[:], scalar1=-1.0, scalar2=None, op0=ALU.mult), r=["gbeta"], w=["gnbeta"])
                kb.op("dve", lambda e: e.tensor_tensor(out=gtm[:], in0=sm[:, :, 6:12], in1=dtb[:].unsqueeze(1).to_broadcast([128, 4, 6]), op=ALU.add),
                      r=["gsm", "gdtb"], w=["gg"])
                kb.op("act", lambda e: e.activation(out=gtm[:], in_=gtm[:], func=AF.Exp), r=["gg"], w=["gg"])
                kb.op("act", lambda e: e.activation(out=gtm[:], in_=gtm[:], func=AF.Ln, bias=1.0, scale=1.0), r=["gg"], w=["gg"])
                kb.op("dve", lambda e: e.tensor_tensor(out=gtm[:], in0=gtm[:], in1=nega[:].unsqueeze(1).to_broadcast([128, 4, 6]), op=ALU.mult),
                      r=["gg", "gna"], w=["gg"])
                for h in range(H):
                    xs = h % 2
                    for qi, ch in enumerate((CH_QA, CH_KA, CH_VA)):
                        row = (ch + h) * 128
                        if b == 0:
                            kb.op("pool", lambda e, qi=qi: e.memset(xin[:, xs, qi, 0:3], 0.0), w=["gx%d" % xs])
                            kb.dma(xin[:, xs, qi, 3:515], self.projT.ap()[row:row + 128, 0:TT], "gx%d" % xs, w=["gx%d" % xs])
                        else:
                            kb.dma(xin[:, xs, qi, :], self.projT.ap()[row:row + 128, t0 - 3:t0 + TT], "gx%d" % xs, w=["gx%d" % xs])
                    for qi, ch in enumerate((CH_QA, CH_KA, CH_VA)):
                        cw = lambda j: convw[:, (ch + h) * 4 + j:(ch + h) * 4 + j + 1]
                        dst = QKV[:, h, qi, :]
                        kb.op("dve", lambda e, qi=qi, cw=cw, dst=dst: e.tensor_scalar(out=dst, in0=xin[:, xs, qi, 0:512], scalar1=cw(0), scalar2=None, op0=ALU.mult),
                              r=["gx%d" % xs, "gcw"], w=[K("qkv%d" % qi, h)])
                        for j in (1, 2, 3):
                            kb.op("dve", lambda e, qi=qi, cw=cw, dst=dst, j=j: e.scalar_tensor_tensor(out=dst, in0=xin[:, xs, qi, j:j + 512], scalar=cw(j), in1=dst,
                                                                                              op0=ALU.mult, op1=ALU.add),
                                  r=["gx%d" % xs, "gcw", K("qkv%d" % qi, h)], w=[K("qkv%d" % qi, h)])
                        kb.op("act", lambda e, dst=dst: e.activation(out=dst, in_=dst, func=AF.Silu), r=[K("qkv%d" % qi, h)], w=[K("qkv%d" % qi, h)])
                        if qi < 2:
                            kb.op("act", lambda e, dst=dst: e.activation(out=sq[:], in_=dst, func=AF.Square), r=[K("qkv%d" % qi, h)], w=["gsq"])
                            kb.op("pe", lambda e: e.matmul(ps_ss, self.ones_b[:], sq[:], start=True, stop=True), r=["gsq", "onesb"], w=["@gpss"])
                            kb.op("act", lambda e: e.activation(out=rn[:], in_=ps_ss, func=AF.Ln, bias=eps_t[:, 0:1], scale=1.0), r=["@gpss", "geps"], w=["grn"])
                            kb.op("act", lambda e: e.activation(out=rn[:], in_=rn[:], func=AF.Exp, scale=-0.5), r=["grn"], w=["grn"])
                            sc = (128.0 ** -0.5) if qi == 0 else 1.0
                            kb.op("dve", lambda e, dst=dst, sc=sc: e.scalar_tensor_tensor(out=dst, in0=dst, scalar=sc, in1=rn[:], op0=ALU.mult, op1=ALU.mult),
                                  r=["grn", K("qkv%d" % qi, h)], w=[K("qkv%d" % qi, h)])
                def mk_stages(pr):
                    stages = []
                    lev = 0; Pc = PTc = Pn = PTn = None; c = 0; r0 = 0
                    cs = slice(pr * 128, (pr + 1) * 128)
                    Qt = lambda h: QKV[:, h, 0, cs]
                    Kt = lambda h: QKV[:, h, 1, cs]
                    Vt = lambda h: QKV[:, h, 2, cs]
                    a = lambda n, h: A[n][:, h, :]
                    def _pre(heads):
                        h0, n = heads[0], len(heads)
                        psm = PS[:, 28 * 128 + h0:28 * 128 + h0 + n]
                        kb.op("pe", lambda e: e.matmul(psm, triBD, gtm[:, pr, h0:h0 + n], start=True, stop=True), r=["gg", "cst"], w=["@gpsm"])
                        kb.op("dve", lambda e: e.tensor_copy(out=gc[:, h0:h0 + n], in_=psm), r=["@gpsm"], w=["ggc%d" % h0])
                        kb.op("act", lambda e: e.activation(out=s1[:, h0:h0 + n], in_=gc[:, h0:h0 + n], func=AF.Exp), r=["ggc%d" % h0], w=["gs1%d" % h0])
                        kb.op("dve", lambda e: e.tensor_tensor(out=s1[:, h0:h0 + n], in0=s1[:, h0:h0 + n], in1=beta[:, pr, h0:h0 + n], op=ALU.mult), r=["gs1%d" % h0, "gbeta"], w=["gs1%d" % h0])
                    stages.append(_pre)
                    def _st(heads, lev=lev, Pc=Pc, PTc=PTc, Pn=Pn, PTn=PTn, c=c, r0=r0):
                        for h in heads:
                            kb.op("pe", lambda e, h=h: e.transpose(P1(h), Kt(h), self.ident), r=[K("qkv1", h), "cst"], w=[K("P1", h)])
                            kb.op("pe", lambda e, h=h: e.transpose(P2(h), Vt(h), self.ident), r=[K("qkv2", h), "cst"], w=[K("P2", h)])
                            kb.op("dve", lambda e, h=h: e.tensor_scalar(out=a("rb", h), in0=triBD, scalar1=gtm[:, pr, h:h + 1], scalar2=None, op0=ALU.mult),
                                  r=["gg", "cst"], w=[K("rb", h)])
                            kb.op("pe", lambda e, h=h: e.matmul(P3(h), self.ones_f[:], a("rb", h), start=True, stop=True), r=[K("rb", h), "ones"], w=[K("P3", h)])
                            kb.op("act", lambda e, h=h: e.copy(out=a("Ktm", h), in_=P1(h)), r=[K("P1", h)], w=[K("Ktm", h)])
                            kb.op("dve", lambda e, h=h: e.tensor_scalar(out=a("Vb", h), in0=P2(h), scalar1=beta[:, pr, h:h + 1], scalar2=None, op0=ALU.mult),
                                  r=[K("P2", h), "gbeta"], w=[K("Vb", h)])
                    stages.append(_st)
                    def _st(heads, lev=lev, Pc=Pc, PTc=PTc, Pn=Pn, PTn=PTn, c=c, r0=r0):
                        for h in heads:
                            kb.op("dve", lambda e, h=h: e.tensor_scalar(out=a("tE", h), in0=P3(h), scalar1=gc[:, h:h + 1], scalar2=0.0, op0=ALU.subtract, op1=ALU.max),
                                  r=[K("P3", h), GK("ggc", h)], w=[K("E", h)])
                            kb.op("act", lambda e, h=h: e.activation(out=a("E", h), in_=a("tE", h), func=AF.Exp, scale=-1.0), r=[K("E", h)], w=[K("E", h)])
                            kb.op("act", lambda e, h=h: e.activation(out=a("ebc", h), in_=P3(h), func=AF.Exp), r=[K("P3", h)], w=[K("ebc", h)])
                            kb.op("act", lambda e, h=h: e.activation(out=egl[:, h, :], in_=PS[:, (4 * h + 2) * 128 + 63:(4 * h + 2) * 128 + 128:64], func=AF.Exp),
                                  r=[K("P3", h)], w=[K("egl", h)])
                            kb.op("dve", lambda e, h=h: e.tensor_tensor(out=dl[0:64, h:h + 1], in0=PS[0:64, (4 * h + 2) * 128 + 63:(4 * h + 2) * 128 + 64], in1=gc[0:64, h:h + 1], op=ALU.subtract),
                                  r=[K("P3", h), GK("ggc", h)], w=[K("dl", h)])
                            kb.op("dve", lambda e, h=h: e.tensor_tensor(out=dl[64:128, h:h + 1], in0=PS[64:128, (4 * h + 2) * 128 + 127:(4 * h + 2) * 128 + 128], in1=gc[64:128, h:h + 1], op=ALU.subtract),
                                  r=[K("P3", h), GK("ggc", h)], w=[K("dl", h)])
                            kb.op("act", lambda e, h=h: e.activation(out=dl[:, h:h + 1], in_=dl[:, h:h + 1], func=AF.Exp), r=[K("dl", h)], w=[K("dl", h)])
                            kb.op("dve", lambda e, h=h: e.tensor_tensor(out=a("Es", h), in0=a("E", h), in1=m_strict, op=ALU.mult), r=[K("E", h), "cst"], w=[K("Es", h)])
                            kb.op("dve", lambda e, h=h: e.tensor_tensor(out=a("Ei", h), in0=a("E", h), in1=m_incl, op=ALU.mult), r=[K("E", h), "cst"], w=[K("Ei", h)])
                            kb.op("pe", lambda e, h=h: e.matmul(P1(h), Kt(h), Kt(h), start=True, stop=True), r=[K("qkv1", h)], w=[K("P1", h)])
                            kb.op("pe", lambda e, h=h: e.matmul(P2(h), Qt(h), Kt(h), start=True, stop=True), r=[K("qkv0", h), K("qkv1", h)], w=[K("P2", h)])
                            kb.op("dve", lambda e, h=h: e.scalar_tensor_tensor(out=a("NT", h), in0=P1(h), scalar=nbeta[:, pr, h:h + 1], in1=a("Es", h), op0=ALU.mult, op1=ALU.mult),
                                  r=[K("P1", h), "gnbeta", K("Es", h)], w=[K("NT", h)])
                            kb.op("dve", lambda e, h=h: e.tensor_tensor(out=a("Aqk", h), in0=P2(h), in1=a("Ei", h), op=ALU.mult), r=[K("P2", h), K("Ei", h)], w=[K("Aqk", h)])
                            kb.op("act", lambda e, h=h: e.activation(out=a("Kw", h), in_=a("Ktm", h), func=AF.Identity, scale=s1[:, h:h + 1]),
                                  r=[K("Ktm", h), GK("gs1", h)], w=[K("Kw", h)])
                            kb.op("act", lambda e, h=h: e.activation(out=a("Kpp", h), in_=a("Ktm", h), func=AF.Identity, scale=dl[:, h:h + 1]),
                                  r=[K("Ktm", h), K("dl", h)], w=[K("Kpp", h)])
                            kb.op("dve", lambda e, h=h: e.tensor_tensor(out=a("QgT", h), in0=Qt(h), in1=a("ebc", h), op=ALU.mult), r=[K("qkv0", h), K("ebc", h)], w=[K("QgT", h)])
                    stages.append(_st)
                    def _st(heads, lev=lev, Pc=Pc, PTc=PTc, Pn=Pn, PTn=PTn, c=c, r0=r0):
                        for h in heads:
                            kb.op("pe", lambda e, h=h: e.transpose(P3(h), a("NT", h), self.ident), r=[K("NT", h), "cst"], w=[K("P3", h)])
                            kb.op("pe", lambda e, h=h: e.transpose(P4(h), a("Aqk", h), self.ident), r=[K("Aqk", h), "cst"], w=[K("P4", h)])
                            kb.op("act", lambda e, h=h: e.copy(out=a("N", h), in_=P3(h)), r=[K("P3", h)], w=[K("N", h)])
                            kb.op("dve", lambda e, h=h: e.tensor_tensor(out=a("R", h), in0=P3(h), in1=self.ident, op=ALU.add), r=[K("P3", h), "cst"], w=[K("R", h)])
                            kb.op("act", lambda e, h=h: e.copy(out=a("AqkT", h), in_=P4(h)), r=[K("P4", h)], w=[K("AqkT", h)])
                    stages.append(_st)
                    _n0 = len(stages)
                    Pc, PTc = "N", "NT"
                    for lev in range(1, 6):
                        Pn, PTn = ("Pa", "PTa") if lev % 2 == 1 else ("Pb", "PTb")
                        def _st(heads, lev=lev, Pc=Pc, PTc=PTc, Pn=Pn, PTn=PTn, c=c, r0=r0):
                            for h in heads:
                                if lev < 5:
                                    kb.op("pe", lambda e, h=h, Pc=Pc, PTc=PTc: e.matmul(P1(h), a(PTc, h), a(Pc, h), start=True, stop=True),
                                          r=[K(Pc, h), K(PTc, h)], w=[K("P1", h)])
                                kb.op("pe", lambda e, h=h, Pc=Pc, PTc=PTc: e.matmul(P2(h), a(Pc, h), a(PTc, h), start=True, stop=True),
                                      r=[K(Pc, h), K(PTc, h)], w=[K("P2", h)])
                        stages.append(_st)
                        def _st(heads, lev=lev, Pc=Pc, PTc=PTc, Pn=Pn, PTn=PTn, c=c, r0=r0):
                            for h in heads:
                                if lev < 5:
                                    kb.op("act", lambda e, h=h, Pn=Pn: e.copy(out=a(Pn, h), in_=P1(h)), r=[K("P1", h)], w=[K(Pn, h)])
                                kb.op("dve", lambda e, h=h, PTn=PTn: e.tensor_copy(out=a(PTn, h), in_=P2(h)), r=[K("P2", h)], w=[K(PTn, h)])
                        stages.append(_st)
                        def _st(heads, lev=lev, Pc=Pc, PTc=PTc, Pn=Pn, PTn=PTn, c=c, r0=r0):
                            for h in heads:
                                kb.op("pe", lambda e, h=h, PTn=PTn: e.matmul(P3(h), a(PTn, h), a("R", h), start=True, stop=True),
                                      r=[K(PTn, h), K("R", h)], w=[K("P3", h)])
                        stages.append(_st)
                        def _st(heads, lev=lev, Pc=Pc, PTc=PTc, Pn=Pn, PTn=PTn, c=c, r0=r0):
                            for h in heads:
                                kb.op("dve", lambda e, h=h: e.tensor_tensor(out=a("R", h), in0=a("R", h), in1=P3(h), op=ALU.add), r=[K("P3", h), K("R", h)], w=[K("R", h)])
                        stages.append(_st)
                        Pc, PTc = Pn, PTn
                    _lv = stages[_n0:]
                    del stages[_n0:]
                    _sq = [_lv[4 * k] for k in range(5)]; _ev = [_lv[4 * k + 1] for k in range(5)]
                    _rm = [_lv[4 * k + 2] for k in range(5)]; _ra = [_lv[4 * k + 3] for k in range(5)]
                    stages += [_sq[0], _ev[0]]
                    for _k in range(1, 5):
                        stages += [_sq[_k], _rm[_k - 1], _ev[_k], _ra[_k - 1]]
                    stages += [_rm[4], _ra[4]]
                    def _st(heads, lev=lev, Pc=Pc, PTc=PTc, Pn=Pn, PTn=PTn, c=c, r0=r0):
                        for h in heads:
                            kb.op("pe", lambda e, h=h: e.matmul(P1(h), a("R", h), a("Vb", h), start=True, stop=True), r=[K("R", h), K("Vb", h)], w=[K("P1", h)])
                            kb.op("pe", lambda e, h=h: e.matmul(P2(h), a("Kw", h), a("R", h), start=True, stop=True), r=[K("R", h), K("Kw", h)], w=[K("P2", h)])
                            kb.op("act", lambda e, h=h: e.activation(out=a("Um0", h), in_=P1(h), func=AF.Identity, scale=rm[:, 0:1]), r=[K("P1", h), "grm"], w=[K("Um0", h)])
                            kb.op("dve", lambda e, h=h: e.tensor_scalar(out=a("Um1", h), in0=P1(h), scalar1=rm[:, 1:2], scalar2=None, op0=ALU.mult), r=[K("P1", h), "grm"], w=[K("Um1", h)])
                            kb.op("act", lambda e, h=h: e.copy(out=a("WT", h), in_=P2(h)), r=[K("P2", h)], w=[K("WT", h)])
                    stages.append(_st)
                    for c in range(2):
                        r0 = 64 * c
                        def _st(heads, lev=lev, Pc=Pc, PTc=PTc, Pn=Pn, PTn=PTn, c=c, r0=r0):
                            for h in heads:
                                kb.op("pe", lambda e, h=h: e.matmul(P3(h), a("WT", h), Sst[:, h, :], start=True, stop=True), r=[K("WT", h), K("S", h)], w=[K("P3", h)])
                        stages.append(_st)
                        def _st(heads, lev=lev, Pc=Pc, PTc=PTc, Pn=Pn, PTn=PTn, c=c, r0=r0):
                            for h in heads:
                                kb.op("dve", lambda e, h=h, c=c: e.scalar_tensor_tensor(out=a("Vnew", h), in0=P3(h), scalar=rm[:, 2 + c:3 + c], in1=a("Um%d" % c, h),
                                                                                        op0=ALU.mult, op1=ALU.add), r=[K("P3", h), K("Um%d" % c, h), "grm"], w=[K("Vnew", h)])
                        stages.append(_st)
                        def _st(heads, lev=lev, Pc=Pc, PTc=PTc, Pn=Pn, PTn=PTn, c=c, r0=r0):
                            for h in heads:
                                kb.op("pe", lambda e, h=h: e.matmul(P4(h), a("QgT", h), Sst[:, h, :], start=True, stop=False), r=[K("QgT", h), K("S", h)], w=[K("P4", h)])
                                kb.op("pe", lambda e, h=h: e.matmul(P4(h), a("AqkT", h), a("Vnew", h), start=False, stop=True), r=[K("AqkT", h), K("Vnew", h)], w=[K("P4", h)])
                                kb.op("pe", lambda e, h=h: e.matmul(P1(h), a("Kpp", h), a("Vnew", h), start=True, stop=True), r=[K("Kpp", h), K("Vnew", h)], w=[K("P1", h)])
                        stages.append(_st)
                        def _st(heads, lev=lev, Pc=Pc, PTc=PTc, Pn=Pn, PTn=PTn, c=c, r0=r0):
                            for h in heads:
                                kb.op("act", lambda e, h=h, r0=r0: e.copy(out=A["Osb"][r0:r0 + 64, h, :], in_=PS[r0:r0 + 64, (4 * h + 3) * 128:(4 * h + 4) * 128]),
                                      r=[K("P4", h)], w=[K("Osb%d" % c, h)])
                                kb.op("dve", lambda e, h=h, c=c: e.scalar_tensor_tensor(out=Sst[:, h, :], in0=Sst[:, h, :], scalar=egl[:, h, c:c + 1], in1=P1(h),
                                                                                        op0=ALU.mult, op1=ALU.add), r=[K("P1", h), K("egl", h), K("S", h)], w=[K("S", h)])
                        stages.append(_st)
                    def _st(heads, lev=lev, Pc=Pc, PTc=PTc, Pn=Pn, PTn=PTn, c=c, r0=r0):
                        for h in heads:
                            kb.op("act", lambda e, h=h: e.activation(out=a("On", h), in_=a("Osb", h), func=AF.Square, accum_out=ssq[:, h:h + 1]),
                                  r=[K("Osb0", h), K("Osb1", h)], w=[K("On", h), K("ssq", h)])
                            kb.op("act", lambda e, h=h: e.activation(out=ssq[:, h:h + 1], in_=ssq[:, h:h + 1], func=AF.Ln, bias=eps_t[:, 0:1], scale=1.0 / 128),
                                  r=[K("ssq", h), "geps"], w=[K("ssq", h)])
                            kb.op("act", lambda e, h=h: e.activation(out=ssq[:, h:h + 1], in_=ssq[:, h:h + 1], func=AF.Exp, scale=-0.5), r=[K("ssq", h)], w=[K("ssq", h)])
                            kb.op("dve", lambda e, h=h: e.tensor_scalar(out=a("On", h), in0=a("Osb", h), scalar1=ssq[:, h:h + 1], scalar2=None, op0=ALU.mult),
                                  r=[K("ssq", h), K("Osb0", h), K("Osb1", h), K("On", h)], w=[K("On", h)])
                            kb.op("pe", lambda e, h=h: e.transpose(P2(h), a("On", h), self.ident), r=[K("On", h), "cst"], w=[K("P2", h)])
                            kb.op("dve", lambda e, h=h: e.scalar_tensor_tensor(out=obf[:, h, cs], in0=P2(h), scalar=self.gdnn[:, l:l + 1], in1=sz[:, h, cs],
                                                                               op0=ALU.mult, op1=ALU.mult), r=[K("P2", h), "gsz", "gdnn"], w=["gobf%d" % h])
                    stages.append(_st)
                    return stages
                seq = []
                for pr in range(4):
                    seq += mk_stages(pr)
                GA, GB, off = [0, 1, 2], [3, 4, 5], GDN_OFF
                for step in range(len(seq) + off):
                    if step < len(seq):
                        seq[step](GA)
                    if 0 <= step - off < len(seq):
                        seq[step - off](GB)
                    if bg_tick is not None and step % 10 == 5:
                        bg_tick()
                kb.dma(self.mixT.ap()[0:768, t0:t0 + TT].rearrange("(h p) t -> p h t", p=128), obf[:], "gos", r=["gobf%d" % h for h in range(6)])
            if bg_tick is not None:
                while bg_tick():
                    pass
            kb.barrier()


    LF = 8448
    OFF = 4224

    def nsa_tables(self):
        kb, nc = self.kb, self.nc
        LF, OFF = self.LF, self.OFF
        with ExitStack() as es:
            sb = lambda n, s, dt: es.enter_context(self.sbt(n, s, dt))
            Fv = sb("t_fv", [6, LF], F32)
            Fvb = sb("t_fvb", [6, LF], BF16)
            relb = sb("t_rb", [32, 6], F32)
            t31 = sb("t_31", [6, 1], F32)
            ps = es.enter_context(self.pst("t_ps", [128, 512], F32))
            OH = self.c2[0:32, 0:128]
            kb.dma(relb[:], self.rel_bias.ap(), "trb", w=["trb"])
            kb.dma(t31[:], self.rel_bias.ap()[31:32, :].rearrange("a h -> h a"), "t31", w=["t31"], allow_slow_non_contiguous=True)
            kb.dma(self.b31bc[:], self.bcast_row(self.rel_bias, 31 * 6, 6), "tb31", w=["b31"])
            kb.op("pool", lambda e: e.memset(Fv[:, 0:OFF], 0.0), w=["tfv0"])
            kb.op("pe", lambda e: e.matmul(ps[0:6, 0:128], relb[:], OH, start=True, stop=True), r=["trb", "cst2"], w=["@tps"])
            kb.op("act", lambda e: e.activation(out=Fv[:, OFF:OFF + 128], in_=ps[0:6, 0:128], func=AF.Exp), r=["@tps"], w=["tfv1"])
            kb.op("act", lambda e: e.activation(out=Fv[:, OFF + 128:LF], in_=Fv[:, 0:LF - OFF - 128], func=AF.Exp, bias=t31[:, 0:1], scale=0.0),
                  r=["tfv0", "t31"], w=["tfv2"])
            kb.dma(self.fvec.ap()[0:6, :], Fv[:], "tfs", r=["tfv0", "tfv1", "tfv2"])
            kb.op("dve", lambda e: e.tensor_copy(out=Fvb[:], in_=Fv[:]), r=["tfv0", "tfv1", "tfv2"], w=["tfvb"])
            kb.dma(self.fvecb.ap()[0:6, :], Fvb[:], "tfsb", r=["tfvb"])
            kb.op("pool", lambda e: e.memset(Fv[:, OFF + 512:LF], 0.0), r=["tfv2"], w=["tfv2"])
            kb.dma(self.fvec.ap()[6:12, :], Fv[:], "tfs", r=["tfv0", "tfv1", "tfv2"], w=["fvec"])
            kb.op("dve", lambda e: e.tensor_copy(out=Fvb[:], in_=Fv[:]), r=["tfv0", "tfv1", "tfv2"], w=["tfvb"])
            kb.dma(self.fvecb.ap()[6:12, :], Fvb[:], "tfsb", r=["tfvb"])
            kb.barrier()
            for i in range(12):
                kb.dma(self.Btab[i].ap(), bass.AP(self.fvec, i * LF, [[0, 128], [1, LF]]), "tbr")
                kb.dma(self.Btabb[i].ap(), bass.AP(self.fvecb, i * LF, [[0, 128], [1, LF]]), "tbrb")
            kb.barrier()

    def mtile(self, hh, rs, delta, bf=False):
        LF, OFF = self.LF, self.OFF
        return bass.AP((self.Btabb if bf else self.Btab)[hh], delta + OFF, [[LF - rs, 128], [1, TT]])

    def phase_nsa(self, l):
        kb, nc = self.kb, self.nc
        H = 6
        with ExitStack() as es:
            sb = lambda n, s, dt: es.enter_context(self.sbt(n, s, dt))
            pst = lambda n: es.enter_context(self.pst(n, [128, TT], F32))
            ps_s = [pst("n_ps0"), pst("n_ps1")]
            ps_o, ps_d, ps_i, ps_g, ps_x = pst("n_po"), pst("n_pd"), pst("n_pi"), pst("n_pg"), pst("n_px")
            kT = {"s": sb("n_ksT", [128, 2, S], BF16), "w": sb("n_kwT", [128, 2, S], BF16)}
            vtm = {"s": sb("n_vs", [128, 2, 32, 128], BF16), "w": sb("n_vw", [128, 2, 32, 128], BF16)}
            kcT = sb("n_kcT", [128, 2, 256], F32)
            vc = sb("n_vc", [128, 2, 2, 128], F32)
            eps_t = sb("n_eps", [128, 1], F32)
            ones_b = sb("n_1b", [128, 128], BF16)
            eblk = sb("n_eblk", [64, S], BF16)
            gq = sb("n_gq", [128, 1], F32)
            kb.op("pool", lambda e: e.memset(eps_t[:], EPS), w=["neps"])
            kb.op("pool", lambda e: e.memset(ones_b[:], 1.0), w=["n1b"])
            kb.op("pool", lambda e: e.memset(kcT[:], 0.0), w=["nkcT0", "nkcT1"])
            kb.op("pool", lambda e: e.memset(vc[:], 0.0), w=["nvc0", "nvc1"])
            kb.op("dve", lambda e: e.tensor_scalar(out=gq[:], in0=self.nqk[:, l:l + 1], scalar1=128.0 ** -0.5, scalar2=None, op0=ALU.mult), r=["nqk"], w=["ngq"])
            gk = self.nqk[:, DEPTH + l:DEPTH + l + 1]
            with ExitStack() as es2:
                sb2 = lambda n, s, dt: es2.enter_context(self.sbt(n, s, dt))
                stg = sb2("n_stg", [128, 2, S], F32)
                w1 = sb2("n_w1", [128, 32, 128], F32)
                w2 = sb2("n_w2", [128, 128], F32)
                posT = sb2("n_pos", [128, 32], F32)
                c1 = sb2("n_c1", [128, 1], F32)
                g1T = sb2("n_g1T", [128, 256], F32)
                sq = sb2("n_sq", [128, TT], F32)
                rstd = sb2("n_rstd", [128, TT], F32)
                ebf = sb2("n_ebf", [64, S], F32)
                kb.dma(ebf[:], self.eblk_in.ap(), "neb", w=["nebf"])
                kb.op("dve", lambda e: e.tensor_copy(out=eblk[:], in_=ebf[:]), r=["nebf"], w=["neblk"])
                si = 0
                for br, chk, chv in (("c", CH_KCC, CH_VCC), ("s", CH_KSL, CH_VSL), ("w", CH_KWN, CH_VWN)):
                    for g in range(2):
                        for kv, ch in ((0, chk), (1, chv)):
                            s_ = si % 2
                            si += 1
                            sk = "nstg%d" % s_
                            kb.dma(stg[:, s_, :], self.projT.ap()[(ch + g) * 128:(ch + g + 1) * 128, :], sk, w=[sk])
                            if br == "c":
                                kb.dma(w1[:], self.cmp_w1.ap()[l, kv].rearrange("(a d) j -> d a j", d=128), "nw1", w=["nw1"])
                                kb.dma(w2[:], self.cmp_w2.ap()[l, kv], "nw2", w=["nw2"])
                                kb.dma(posT[:], self.cpos_pl.ap()[l, kv], "npos", w=["npos"])
                                for a in range(32):
                                    kb.op("pe", lambda e, a=a: e.matmul(ps_x[:, 0:1], w1[:, a, :], posT[:, a:a + 1], start=(a == 0), stop=(a == 31)),
                                          r=["nw1", "npos"], w=["@npx"])
                                kb.op("dve", lambda e: e.tensor_copy(out=c1[:], in_=ps_x[:, 0:1]), r=["@npx"], w=["nc1"])
                                for a in range(32):
                                    kb.op("pe", lambda e, a=a, s_=s_: e.matmul(ps_o[:, 0:255], w1[:, a, :], stg[:, s_, a:a + 16 * 254 + 1:16], start=(a == 0), stop=(a == 31)),
                                          r=["nw1", sk], w=["@npo"])
                                kb.op("act", lambda e: e.activation(out=g1T[:, 0:255], in_=ps_o[:, 0:255], func=AF.Gelu_apprx_tanh, bias=c1[:, 0:1], scale=1.0),
                                      r=["@npo", "nc1"], w=["ng1T"])
                                if kv == 0:
                                    kb.op("pe", lambda e: e.matmul(ps_d[:, 0:255], w2[:], g1T[:, 0:255], start=True, stop=True), r=["nw2", "ng1T"], w=["@npd"])
                                    kb.op("act", lambda e: e.activation(out=sq[:, 0:255], in_=ps_d[:, 0:255], func=AF.Square), r=["@npd"], w=["nsq"])
                                    kb.op("pe", lambda e: e.matmul(ps_i[:, 0:255], self.ones_f[:], sq[:, 0:255], start=True, stop=True), r=["nsq", "ones"], w=["@npi"])
                                    kb.op("act", lambda e: e.activation(out=rstd[:, 0:255], in_=ps_i[:, 0:255], func=AF.Ln, bias=eps_t[:, 0:1], scale=1.0 / 128),
                                          r=["@npi", "neps"], w=["nrstd"])
                                    kb.op("act", lambda e: e.activation(out=rstd[:, 0:255], in_=rstd[:, 0:255], func=AF.Exp, scale=-0.5), r=["nrstd"], w=["nrstd"])
                                    kb.op("dve", lambda e, g=g: e.scalar_tensor_tensor(out=kcT[:, g, 0:255], in0=ps_d[:, 0:255], scalar=gk, in1=rstd[:, 0:255],
                                                                                      op0=ALU.mult, op1=ALU.mult), r=["@npd", "nrstd", "nqk"], w=["nkcT%d" % g])
                                else:
                                    for c, nn in ((0, 128), (1, 127)):
                                        kb.op("pe", lambda e, c=c, nn=nn: e.matmul(ps_d[0:nn, c * 128:(c + 1) * 128], g1T[:, c * 128:c * 128 + nn], w2[:], start=True, stop=True),
                                              r=["nw2", "ng1T"], w=["@npd"])
                                        kb.op("dve", lambda e, c=c, nn=nn, g=g: e.tensor_copy(out=vc[0:nn, g, c, :], in_=ps_d[0:nn, c * 128:(c + 1) * 128]),
                                              r=["@npd"], w=["nvc%d" % g])
                            elif kv == 0:
                                for t in range(NT):
                                    ts_ = slice(t * TT, (t + 1) * TT)
                                    kb.op("act", lambda e, ts_=ts_, s_=s_: e.activation(out=sq[:], in_=stg[:, s_, ts_], func=AF.Square), r=[sk], w=["nsq"])
                                    kb.op("pe", lambda e: e.matmul(ps_x[:], self.ones_f[:], sq[:], start=True, stop=True), r=["nsq", "ones"], w=["@npx"])
                                    kb.op("act", lambda e: e.activation(out=rstd[:], in_=ps_x[:], func=AF.Ln, bias=eps_t[:, 0:1], scale=1.0 / 128), r=["@npx", "neps"], w=["nrstd"])
                                    kb.op("act", lambda e: e.activation(out=rstd[:], in_=rstd[:], func=AF.Exp, scale=-0.5), r=["nrstd"], w=["nrstd"])
                                    kb.op("dve", lambda e, ts_=ts_, s_=s_, br=br, g=g: e.scalar_tensor_tensor(out=kT[br][:, g, ts_], in0=stg[:, s_, ts_], scalar=gk, in1=rstd[:],
                                                                                                      op0=ALU.mult, op1=ALU.mult), r=[sk, "nrstd", "nqk"], w=["nkT" + br])
                            else:
                                for kt in range(32):
                                    q4 = kt % 4
                                    kb.op("pe", lambda e, kt=kt, q4=q4, s_=s_: e.transpose(ps_i[:, q4 * 128:(q4 + 1) * 128], stg[:, s_, kt * 128:(kt + 1) * 128], self.ident),
                                          r=[sk, "cst"], w=["@npi"])
                                    if q4 == 3:
                                        kb.op("act", lambda e, kt=kt, br=br, g=g: e.copy(out=vtm[br][:, g, kt - 3:kt + 1, :], in_=ps_i[:].rearrange("p (a c) -> p a c", c=128)),
                                              r=["@npi"], w=["nv" + br])
                kb.barrier()
            qraw = sb("n_qraw", [128, 6, TT], F32)
            qn = sb("n_qn", [128, 6, TT], F32)
            qb = sb("n_qb", [128, 6, TT], BF16)
            acc = sb("n_acc", [128, 6, TT], F32)
            obf = sb("n_obf", [128, 6, TT], BF16)
            ec = sb("n_ec", [128, 2, 2, TT], F32)
            e32 = sb("n_e32", [128, 2, TT], F32)
            ebt = sb("n_eb", [128, 3, TT], BF16)
            Mt = sb("n_M", [128, 3, TT], F32)
            Mtb = sb("n_Mb", [128, 3, TT], BF16)
            negT = sb("n_negT", [64, 2, TT], BF16)
            selc = sb("n_selc", [128, 3, 4, 64], F32)
            sig = sb("n_sig", [30, TT], F32)
            impg = sb("n_imp", [128, 2, 4, 64], F32)
            imp2 = sb("n_imp2", [128, 4, 64], F32)
            rden = sb("n_rden", [128, 4, 1], F32)
            score = sb("n_score", [128, 64], F32)
            work = sb("n_work", [128, 64], F32)
            mx8 = sb("n_mx8", [128, 16], F32)
            nsel = sb("n_nsel", [128, 64], F32)
            wgt = sb("n_wgt", [128, 2, TT], F32)
            sqd = sb("n_sq2", [128, 2, TT], BF16)
            rstdd = sb("n_rstd2", [128, 2, TT], F32)
            OvE = self.c2[:, 128:258].rearrange("p (c j) -> p c j", c=2)
            Sel = self.c2[0:30, 258:258 + 18 * 128].rearrange("p (k c) -> p k c", k=18)
            cnt = {"e": 0, "m": 0, "s": 0, "mb": 0}

            def combine(h, gate_k, first):
                kb.op("pe", lambda e: e.matmul(ps_g[:], Sel[:, gate_k, :], sig[:], start=True, stop=True), r=["nsig", "cst2"], w=["@npg"])
                kb.op("dve", lambda e: e.tensor_scalar(out=wgt[:, 0, :], in0=ps_d[:], scalar1=1e-18, scalar2=None, op0=ALU.max), r=["@npd"], w=["nwgt0"])
                kb.op("act", lambda e: e.activation(out=wgt[:, 0, :], in_=wgt[:, 0, :], func=AF.Ln), r=["nwgt0"], w=["nwgt0"])
                kb.op("act", lambda e: e.activation(out=wgt[:, 0, :], in_=wgt[:, 0, :], func=AF.Exp, scale=-1.0), r=["nwgt0"], w=["nwgt0"])
                kb.op("dve", lambda e: e.tensor_tensor(out=wgt[:, 0, :], in0=wgt[:, 0, :], in1=ps_g[:], op=ALU.mult), r=["nwgt0", "@npg"], w=["nwgt0"])
                if first:
                    kb.op("dve", lambda e: e.tensor_tensor(out=acc[:, h, :], in0=ps_o[:], in1=wgt[:, 0, :], op=ALU.mult), r=["@npo", "nwgt0"], w=["nacc%d" % h])
                else:
                    kb.op("dve", lambda e: e.tensor_tensor(out=wgt[:, 1, :], in0=ps_o[:], in1=wgt[:, 0, :], op=ALU.mult), r=["@npo", "nwgt0"], w=["nwgt1"])
                    kb.op("dve", lambda e: e.tensor_tensor(out=acc[:, h, :], in0=acc[:, h, :], in1=wgt[:, 1, :], op=ALU.add), r=["nwgt1", "nacc%d" % h], w=["nacc%d" % h])

            for t in range(NT):
                q0 = t * TT
                kb.dma(qraw[:], self.projT.ap()[CH_QC * 128:(CH_QC + 6) * 128, q0:q0 + TT].rearrange("(h p) t -> p h t", p=128), "nq", w=["nqraw"])
                kb.dma(sig[:], self.projT.ap()[CH_SM * 128:CH_SM * 128 + 30, q0:q0 + TT], "nsg", w=["nsig"])
                kb.op("act", lambda e: e.activation(out=sig[:], in_=sig[:], func=AF.Sigmoid), r=["nsig"], w=["nsig"])
                for a_ in range(3):
                    kb.dma(selc[:, a_], self.selc_in.ap()[a_, q0:q0 + TT, :].rearrange("(s p) j -> p s j", p=128), "nselc", w=["nselc"])
                for h in range(H):
                    p2 = h % 2
                    kb.op("act", lambda e, h=h, p2=p2: e.activation(out=sqd[:, p2, :], in_=qraw[:, h, :], func=AF.Square), r=["nqraw"], w=["nsq2_%d" % p2])
                    kb.op("pe", lambda e, p2=p2: e.matmul(ps_x[:], self.ones_b[:], sqd[:, p2, :], start=True, stop=True), r=["nsq2_%d" % p2, "onesb"], w=["@npx"])
                    kb.op("act", lambda e, p2=p2: e.activation(out=rstdd[:, p2, :], in_=ps_x[:], func=AF.Ln, bias=eps_t[:, 0:1], scale=1.0 / 128), r=["@npx", "neps"], w=["nrstd2_%d" % p2])
                    kb.op("act", lambda e, p2=p2: e.activation(out=rstdd[:, p2, :], in_=rstdd[:, p2, :], func=AF.Exp, scale=-0.5), r=["nrstd2_%d" % p2], w=["nrstd2_%d" % p2])
                    kb.op("dve", lambda e, h=h, p2=p2: e.scalar_tensor_tensor(out=qn[:, h, :], in0=qraw[:, h, :], scalar=gq[:, 0:1], in1=rstdd[:, p2, :], op0=ALU.mult, op1=ALU.mult),
                          r=["nqraw", "nrstd2_%d" % p2, "ngq"], w=["nqn%d" % h])
                    kb.op("act", lambda e, h=h: e.copy(out=qb[:, h, :], in_=qn[:, h, :]), r=["nqn%d" % h], w=["nqb%d" % h])
                def cmp_front(h):
                    g = h // 3
                    ep = h % 2
                    for c in range(2):
                        pb = cnt["s"] % 2
                        cnt["s"] += 1
                        ms = cnt["m"] % 3
                        cnt["m"] += 1
                        kb.dma(Mt[:, ms, :], self.mtile(h, 16, q0 - 16 * (c * 128) - 31), "nM%d" % ms, w=["nM%d" % ms])
                        kb.op("pe", lambda e, c=c, pb=pb: e.matmul(ps_s[pb][:], kcT[:, g, c * 128:(c + 1) * 128], qn[:, h, :], start=True, stop=True),
                              r=["nkcT%d" % g, "nqn%d" % h], w=["@nps%d" % pb])
                        kb.op("act", lambda e, c=c, pb=pb: e.activation(out=ec[:, ep, c, :], in_=ps_s[pb][:], func=AF.Exp), r=["@nps%d" % pb], w=["nec%d_%d" % (ep, c)])
                        kb.op("dve", lambda e, c=c, ms=ms: e.tensor_tensor(out=ec[:, ep, c, :], in0=ec[:, ep, c, :], in1=Mt[:, ms, :], op=ALU.mult),
                              r=["nec%d_%d" % (ep, c), "nM%d" % ms], w=["nec%d_%d" % (ep, c)])

                def cmp_rest(h):
                    g = h // 3
                    r_ = h % 3
                    ep = h % 2
                    EK = ["nec%d_0" % ep, "nec%d_1" % ep]
                    for c in range(2):
                        kb.op("pe", lambda e, c=c: e.matmul(ps_o[:], vc[:, g, c, :], ec[:, ep, c, :], start=(c == 0), stop=(c == 1)), r=["nvc%d" % g, EK[c]], w=["@npo"])
                    for c in range(2):
                        kb.op("pe", lambda e, c=c: e.matmul(ps_d[:], self.ones_f[:], ec[:, ep, c, :], start=(c == 0), stop=(c == 1)), r=["ones", EK[c]], w=["@npd"])
                    for qs in range(4):
                        for c in range(2):
                            kb.op("pe", lambda e, c=c, qs=qs: e.matmul(ps_i[:, qs * 65:(qs + 1) * 65], ec[:, ep, c, qs * 128:(qs + 1) * 128], OvE[:, c, :], start=(c == 0), stop=(c == 1)),
                                  r=EK + ["cst2"], w=["@npi"])
                    pi3 = ps_i[:, 0:260].rearrange("p (s j) -> p s j", j=65)
                    kb.op("dve", lambda e: e.tensor_scalar(out=rden[:], in0=pi3[:, :, 64:65], scalar1=1e-30, scalar2=None, op0=ALU.max), r=["@npi"], w=["nrden"])
                    kb.op("dve", lambda e: e.reciprocal(out=rden[:], in_=rden[:]), r=["nrden"], w=["nrden"])
                    if r_ == 0:
                        kb.op("dve", lambda e: e.tensor_tensor(out=impg[:, g], in0=pi3[:, :, 0:64], in1=rden[:].to_broadcast([128, 4, 64]), op=ALU.mult),
                              r=["@npi", "nrden"], w=["nimp%d" % g])
                    else:
                        kb.op("dve", lambda e: e.tensor_tensor(out=imp2[:], in0=pi3[:, :, 0:64], in1=rden[:].to_broadcast([128, 4, 64]), op=ALU.mult),
                              r=["@npi", "nrden"], w=["nimp2"])
                        kb.op("dve", lambda e: e.tensor_tensor(out=impg[:, g], in0=impg[:, g], in1=imp2[:], op=ALU.add), r=["nimp%d" % g, "nimp2"], w=["nimp%d" % g])
                    combine(h, 0 * 6 + h, True)

                cmp_front(0)
                for h in range(H):
                    if h + 1 < H:
                        cmp_front(h + 1)
                    cmp_rest(h)
                for g in range(2):
                    for qs in range(4):
                        kb.op("dve", lambda e, qs=qs: e.tensor_tensor(out=score[:], in0=impg[:, g, qs, :], in1=selc[:, 0, qs, :], op=ALU.mult), r=["nimp%d" % g, "nselc"], w=["nscore"])
                        kb.op("dve", lambda e, qs=qs: e.tensor_tensor(out=score[:], in0=score[:], in1=selc[:, 1, qs, :], op=ALU.add), r=["nscore", "nselc"], w=["nscore"])
                        kb.op("dve", lambda e: e.max(out=mx8[:, 0:8], in_=score[:]), r=["nscore"], w=["nmx8"])
                        kb.op("dve", lambda e: e.match_replace(out=work[:], in_to_replace=mx8[:, 0:8], in_values=score[:], imm_value=-3.0e38), r=["nscore", "nmx8"], w=["nwork"])
                        kb.op("dve", lambda e: e.max(out=mx8[:, 8:16], in_=work[:]), r=["nwork"], w=["nmx8b"])
                        kb.op("dve", lambda e, qs=qs: e.scalar_tensor_tensor(out=nsel[:], in0=score[:], scalar=mx8[:, 15:16], in1=selc[:, 2, qs, :], op0=ALU.is_ge, op1=ALU.mult),
                              r=["nscore", "nmx8b", "nselc"], w=["nnsel"])
                        kb.op("dve", lambda e: e.tensor_scalar(out=nsel[:], in0=nsel[:], scalar1=-1.0, scalar2=30000.0, op0=ALU.add, op1=ALU.mult), r=["nnsel"], w=["nnsel"])
                        kb.op("pe", lambda e: e.transpose(ps_x[0:64, 0:128], nsel[:], self.ident), r=["nnsel", "cst"], w=["@npx"])
                        kb.op("act", lambda e, qs=qs: e.copy(out=negT[:, g, qs * 128:(qs + 1) * 128], in_=ps_x[0:64, 0:128]), r=["@npx"], w=["nnegT%d" % g])
                for g in range(2):
                    tiles = []
                    for r_ in range(3):
                        h = g * 3 + r_
                        for br in ("s", "w"):
                            if br == "s":
                                kts = list(range(0, (q0 + TT) // 128))
                            else:
                                kts = list(range(max(0, (q0 - 512) // 128), (q0 + TT) // 128))
                            for i, kt in enumerate(kts):
                                tiles.append(dict(h=h, br=br, kt=kt, i=i, n=len(kts), delta=q0 - kt * 128))
                    for ti, tl in enumerate(tiles):
                        tl["pb"] = ti % 2
                        tl["es"] = ti % 3
                        tl["e2"] = ti % 2
                        tl["band"] = (tl["br"] == "w") or tl["delta"] < 256

                    def emitS(tl):
                        h, br, kt, pb = tl["h"], tl["br"], tl["kt"], tl["pb"]
                        if tl["band"]:
                            ms = cnt["mb"] % 3
                            cnt["mb"] += 1
                            tl["ms"] = ms
                            kb.dma(Mtb[:, ms, :], self.mtile(h + (6 if br == "w" else 0), 1, tl["delta"], bf=True), "nMb%d" % ms, w=["nMb%d" % ms])
                        kb.op("pe", lambda e: e.matmul(ps_s[pb][:], kT[br][:, g, kt * 128:(kt + 1) * 128], qb[:, h, :], start=True, stop=(br == "w")),
                              r=["nkT" + br, "nqb%d" % h], w=["@nps%d" % pb])
                        if br == "s":
                            kb.op("pe", lambda e: e.matmul(ps_s[pb][:], eblk[:, kt * 128:(kt + 1) * 128], negT[:, g, :], start=False, stop=True),
                                  r=["neblk", "nnegT%d" % g], w=["@nps%d" % pb])

                    def emitE(tl):
                        h, br, kt, pb, es_, e2 = tl["h"], tl["br"], tl["kt"], tl["pb"], tl["es"], tl["e2"]
                        if tl["band"]:
                            ms = tl["ms"]
                            kb.op("act", lambda e: e.activation(out=e32[:, e2, :], in_=ps_s[pb][:], func=AF.Exp), r=["@nps%d" % pb], w=["ne32%d" % e2])
                            kb.op("dve", lambda e: e.tensor_tensor(out=ebt[:, es_, :], in0=e32[:, e2, :], in1=Mtb[:, ms, :], op=ALU.mult),
                                  r=["ne32%d" % e2, "nMb%d" % ms], w=["neb%d" % es_])
                        else:
                            kb.op("act", lambda e: e.activation(out=ebt[:, es_, :], in_=ps_s[pb][:], func=AF.Exp, bias=self.b31bc[:, h:h + 1], scale=1.0),
                                  r=["@nps%d" % pb, "b31"], w=["neb%d" % es_])

                    def emitPV(tl):
                        h, br, kt, es_, i, n = tl["h"], tl["br"], tl["kt"], tl["es"], tl["i"], tl["n"]
                        kb.op("pe", lambda e: e.matmul(ps_o[:], vtm[br][:, g, kt, :], ebt[:, es_, :], start=(i == 0), stop=(i == n - 1)),
                              r=["nv" + br, "neb%d" % es_], w=["@npo"])
                        kb.op("pe", lambda e: e.matmul(ps_d[:], ones_b[:], ebt[:, es_, :], start=(i == 0), stop=(i == n - 1)),
                              r=["n1b", "neb%d" % es_], w=["@npd"])
                        if i == n - 1:
                            combine(h, (1 if br == "s" else 2) * 6 + h, False)

                    emitS(tiles[0])
                    for ti, tl in enumerate(tiles):
                        if ti + 1 < len(tiles):
                            emitS(tiles[ti + 1])
                        emitE(tl)
                        emitPV(tl)
                for h in range(H):
                    kb.op("act", lambda e, h=h: e.copy(out=obf[:, h, :], in_=acc[:, h, :]), r=["nacc%d" % h], w=["nobf"])
                kb.dma(self.mixT.ap()[1280:2048, q0:q0 + TT].rearrange("(h p) t -> p h t", p=128), obf[:], "nos", r=["nobf"])
            kb.barrier()


def make_consts():
    c = np.zeros((128, 1024), np.float32)
    c[:, 0:128] = np.eye(128, dtype=np.float32)
    i = np.arange(128)
    same = (i[:, None] // 64) == (i[None, :] // 64)
    c[:, 128:256] = (same & (i[:, None] <= i[None, :])).astype(np.float32)
    c[:, 256:384] = (same & (i[:, None] >= i[None, :])).astype(np.float32)
    c[:, 384:512] = (same & (i[:, None] > i[None, :])).astype(np.float32)
    c[:, 512:640] = (i[:, None] <= i[None, :]).astype(np.float32)
    return c


def t5_bucket_np(n):
    n = np.maximum(n, 0)
    max_exact = 16
    lr = np.log(np.maximum(n, 1).astype(np.float32) / max_exact) / np.float32(np.log(128 / max_exact))
    large = np.minimum(max_exact + (lr * 16).astype(np.int32), 31)
    return np.where(n < max_exact, n, large)


def make_consts2():
    c = np.zeros((128, 2562), np.float32)
    d = np.arange(128)
    b = t5_bucket_np(d)
    c[b, d] = 1.0
    n = np.arange(256)
    j = np.arange(64)
    ov = ((n[:, None] * 16 < j[None, :] * 64 + 64) & (n[:, None] * 16 + 32 > j[None, :] * 64)).astype(np.float32)
    ov[255] = 0
    ove = np.zeros((256, 65), np.float32)
    ove[:, :64] = ov
    ove[:255, 64] = 1.0
    c[:, 128:258] = ove.reshape(2, 128, 65).transpose(1, 0, 2).reshape(128, 130)
    sel = np.zeros((30, 18, 128), np.float32)
    for k in range(18):
        sel[12 + k, k, :] = 1.0
    c[0:30, 258:258 + 18 * 128] = sel.reshape(30, -1)
    key = np.arange(S)
    eblk = (key[None, :] // 64 == np.arange(64)[:, None]).astype(np.float32)
    q = np.arange(S)
    cur = q // 64
    causal = j[None, :] <= cur[:, None]
    forced = (j[None, :] == 0) | (j[None, :] == cur[:, None]) | (j[None, :] == cur[:, None] - 1)
    a1 = (causal & ~forced).astype(np.float32)
    a2 = np.where(forced, np.float32(1e9), np.where(causal, np.float32(0), np.float32(-1e30))).astype(np.float32)
    selc = np.stack([a1, a2, causal.astype(np.float32)], axis=0)
    return c, eblk, np.ascontiguousarray(selc)


def kernel(**inputs):
    prog = Prog()
    return run_prog(prog, inputs)


def run_prog(prog, inputs, extra=None, cores=8):
    x = np.asarray(inputs["x"], np.float32)
    cst = make_consts()
    pl = lambda a: np.asarray(a, np.float32).reshape(DEPTH, 16, 128).transpose(2, 0, 1).reshape(128, DEPTH * 16)
    gains = np.ascontiguousarray(np.concatenate([pl(inputs["attn_norm"]), pl(inputs["mlp_norm"])], axis=1))
    conv_pl = np.ascontiguousarray(np.asarray(inputs["conv_a"], np.float32).reshape(DEPTH, 4, 18, 128).transpose(0, 3, 2, 1).reshape(DEPTH, 128, 72))
    gdnn = np.ascontiguousarray(np.asarray(inputs["gdn_norm"], np.float32).T)
    cst2, eblk, selc = make_consts2()
    nqk = np.ascontiguousarray(np.concatenate([np.asarray(inputs["nsa_q_norm"], np.float32).T, np.asarray(inputs["nsa_k_norm"], np.float32).T], axis=1))
    cpos_pl = np.ascontiguousarray(np.asarray(inputs["cmp_pos"], np.float32).transpose(0, 1, 3, 2))
    in_maps = []
    for c in range(cores):
        m = {k: np.ascontiguousarray(np.asarray(v, np.float32)) for k, v in inputs.items() if k != "x"}
        m["xT"] = np.ascontiguousarray(x[c].T)
        m["cst"] = cst
        m["gains_in"] = gains
        m["conv_pl"] = conv_pl
        m["cst2"] = cst2
        m["eblk_in"] = eblk
        m["selc_in"] = selc
        m["nqk_in"] = nqk
        m["cpos_pl"] = cpos_pl
        m["gdnn_in"] = gdnn
        if extra:
            m.update(extra)
        in_maps.append(m)
    res = run_bass_kernel_spmd(prog.nc, in_maps, core_ids=list(range(cores)))
    prog.last_results = res.results
    out = np.stack([np.ascontiguousarray(r["yT"].T) for r in res.results], axis=0)
    return out.astype(np.float32)
```

```python
from contextlib import ExitStack
import numpy as np
import concourse.bass as bass
import concourse.mybir as mybir
from concourse.bass_utils import run_bass_kernel_spmd

F32 = mybir.dt.float32
BF16 = mybir.dt.bfloat16
I32 = mybir.dt.int32
AF = mybir.ActivationFunctionType
ALU = mybir.AluOpType
AX = mybir.AxisListType

S = 4096
D = 2048
DEPTH = 4
DFF = 8192
NPROJ = 6430
NJ_IN = 51
TT = 512
NT = S // TT
EPS = 1e-6
GDN_OFF = 1
BG_CAST = True
IN_COLMAP = [(0, 0, 3072), (3084, 3072, 1024), (4108, 4096, 2304), (3072, 6400, 12), (6412, 6412, 18)]
CH_QA, CH_KA, CH_VA, CH_ZA = 0, 6, 12, 18
CH_UB, CH_VB = 24, 28
CH_QC, CH_KCC, CH_VCC, CH_KSL, CH_VSL, CH_KWN, CH_VWN = 32, 38, 40, 42, 44, 46, 48
CH_SM = 50


class KB:
    def __init__(self, nc, es):
        self.nc = nc
        self.es = es
        self.eng = {"pe": nc.tensor, "act": nc.scalar, "dve": nc.vector, "pool": nc.gpsimd, "sp": nc.sync}
        self.sem = {k: es.enter_context(nc.semaphore("s_" + k)) for k in self.eng}
        self.cnt = {k: 0 for k in self.eng}
        self.waited = {k: {} for k in self.eng}
        self.res = {}
        self.dsem = {}
        self.semname = {}

    def _deps(self, e, r, w):
        need = {}

        def add(tok, raw):
            sem, val, te = tok
            if te == e and e == "pe":
                return
            if te == e and not raw:
                return
            k = id(sem)
            if k not in need or need[k][1] < val:
                need[k] = (sem, val)

        r = list(r)
        w = list(w)
        for k in list(r):
            if k.startswith("@"):
                w.append(k)
        for k in r:
            st = self.res.get(k)
            if st and st[0]:
                add(st[0], True)
        for k in w:
            st = self.res.get(k)
            if st:
                if st[0]:
                    add(st[0], False)
                for t in st[1].values():
                    add(t, False)
        wd = self.waited[e]
        for k, (sem, val) in need.items():
            if wd.get(k, 0) < val:
                self.eng[e].wait_ge(sem, val)
                wd[k] = val

    def _upd(self, tok, r, w):
        w = list(w) + [k for k in r if k.startswith("@")]
        r = [k for k in r if not k.startswith("@")]
        for k in r:
            st = self.res.setdefault(k, [None, {}])
            st[1][id(tok[0])] = tok
        for k in w:
            self.res[k] = [tok, {}]

    def op(self, e, fn, r=(), w=()):
        self._deps(e, r, w)
        ins = fn(self.eng[e])
        self.cnt[e] += 1
        ins.then_inc(self.sem[e], 1)
        self._upd((self.sem[e], self.cnt[e], e), r, w)

    def dma(self, out, in_, key, r=(), w=(), q="sp", **kw):
        self._deps(q, r, w)
        if key not in self.dsem:
            self.dsem[key] = [self.es.enter_context(self.nc.semaphore("d%d" % len(self.dsem))), 0]
        ds = self.dsem[key]
        ds[1] += 16
        self.eng[q].dma_start(out=out, in_=in_, **kw).then_inc(ds[0], 16)
        self._upd((ds[0], ds[1], "dma"), r, w)

    def barrier(self):
        for e in self.eng:
            wd = self.waited[e]
            for e2 in self.eng:
                if e2 != e and self.cnt[e2] > wd.get(id(self.sem[e2]), 0):
                    self.eng[e].wait_ge(self.sem[e2], self.cnt[e2])
                    wd[id(self.sem[e2])] = self.cnt[e2]
            for key, (sem, val) in self.dsem.items():
                if val > wd.get(id(sem), 0):
                    self.eng[e].wait_ge(sem, val)
                    wd[id(sem)] = val
        self.res = {}


def dram_ap(handle, offset, pattern):
    return bass.AP(handle, offset, pattern)


class Prog:
    def __init__(self, n_layers=DEPTH, dbg=None, mix_in=False, enable=("gdn", "sgu", "nsa"), phases=("cast", "inproj", "mix", "ffn")):
        self.enable = enable
        self.phases = phases
        self.n_layers = n_layers
        self.dbg = dbg or ()
        self.mix_in = mix_in
        self.nc = bass.Bass("TRN2", target_bir_lowering=False)
        self.build()

    def sbt(self, name, shape, dt):
        self._uid = getattr(self, "_uid", 0) + 1
        return self.nc.sbuf_tensor("%s_%d" % (name, self._uid), shape, dt)

    def pst(self, name, shape, dt):
        self._uid = getattr(self, "_uid", 0) + 1
        return self.nc.psum_tensor("%s_%d" % (name, self._uid), shape, dt)

    def build(self):
        nc = self.nc
        L = DEPTH
        di = lambda n, s: nc.dram_tensor(n, s, F32, kind="ExternalInput")
        self.xT_in = di("xT", [D, S])
        self.attn_norm = di("attn_norm", [L, D])
        self.w_in = di("w_in", [L, D, NPROJ])
        self.conv_a = di("conv_a", [L, 4, 2304])
        self.a_log = di("a_log", [L, 6])
        self.dt_bias = di("dt_bias", [L, 6])
        self.gdn_norm = di("gdn_norm", [L, 128])
        self.sgu_ln_g = di("sgu_ln_g", [L, 512])
        self.sgu_ln_b = di("sgu_ln_b", [L, 512])
        self.sgu_w = di("sgu_w", [L, 4, 128, 128])
        self.sgu_b = di("sgu_b", [L, 4, 128])
        self.nsa_q_norm = di("nsa_q_norm", [L, 128])
        self.nsa_k_norm = di("nsa_k_norm", [L, 128])
        self.cmp_pos = di("cmp_pos", [L, 2, 32, 128])
        self.cmp_w1 = di("cmp_w1", [L, 2, 4096, 128])
        self.cmp_w2 = di("cmp_w2", [L, 2, 128, 128])
        self.rel_bias = di("rel_bias", [32, 6])
        self.w_out = di("w_out", [L, D, D])
        self.mlp_norm = di("mlp_norm", [L, D])
        self.w_up = di("w_up", [L, D, DFF])
        self.w_down = di("w_down", [L, DFF, D])
        self.cst = di("cst", [128, 1024])
        self.gains_in = di("gains_in", [128, 2 * L * 16])
        self.conv_pl = di("conv_pl", [L, 128, 72])
        self.gdnn_in = di("gdnn_in", [128, L])
        self.cst2 = di("cst2", [128, 2562])
        self.nqk_in = di("nqk_in", [128, 2 * L])
        self.cpos_pl = di("cpos_pl", [L, 2, 128, 32])
        self.eblk_in = di("eblk_in", [64, S])
        self.selc_in = di("selc_in", [3, S, 64])
        if self.mix_in:
            self.mix_dbg = di("mix_dbg", [D, S])
        self.yT = nc.dram_tensor("yT", [D, S], F32, kind="ExternalOutput")
        ds = lambda n, s, dt: nc.dram_tensor(n, s, dt, kind="Internal")
        self.xres = ds("xres", [D, S], F32)
        self.projT = ds("projT", [NJ_IN * 128, S], F32)
        self.small_tm = ds("small_tm", [S, 32], F32)
        self.mixT = ds("mixT", [D, S], BF16)
        self.wt_in = [ds("wt_in%d" % l, [NJ_IN, 128, 16, 128], BF16) for l in range(L)]
        self.wt_out = [ds("wt_out%d" % l, [16, 128, 16, 128], BF16) for l in range(L)]
        self.wt_up = [ds("wt_up%d" % l, [64, 128, 16, 128], BF16) for l in range(L)]
        self.wt_dn = [ds("wt_dn%d" % l, [16, 128, 64, 128], BF16) for l in range(L)]
        self.fvec = ds("fvec", [12, self.LF], F32)
        self.Btab = [ds("btab%d" % i, [128, self.LF], F32) for i in range(12)]
        self.fvecb = ds("fvecb", [12, self.LF], BF16)
        self.Btabb = [ds("btabb%d" % i, [128, self.LF], BF16) for i in range(12)]
        self.dbg_out = {}
        if "proj" in self.dbg:
            self.dbg_out["proj"] = nc.dram_tensor("dbg_proj", [NJ_IN * 128, S], F32, kind="ExternalOutput")
            self.dbg_out["small"] = nc.dram_tensor("dbg_small", [S, 32], F32, kind="ExternalOutput")
        if "mix" in self.dbg:
            self.dbg_out["mix"] = nc.dram_tensor("dbg_mix", [D, S], BF16, kind="ExternalOutput")

        with ExitStack() as es:
            self.kb = kb = KB(nc, es)
            sb = lambda n, s, dt: es.enter_context(self.sbt(n, s, dt))
            self.c_f = sb("c_f", [128, 1024], F32)
            self.ones_f = sb("ones_f", [128, 128], F32)
            self.gains = sb("gains", [128, 2 * L * 16], F32)
            kb.dma(self.c_f[:], self.cst.ap(), "cst", w=["cst"])
            kb.op("dve", lambda e: e.memset(self.ones_f[:], 1.0), w=["ones"])
            self.ones_b = sb("ones_b", [128, 128], BF16)
            kb.op("dve", lambda e: e.memset(self.ones_b[:], 1.0), w=["onesb"])
            kb.dma(self.gains[:], self.gains_in.ap(), "g1", w=["gains"])
            self.gdnn = sb("gdnn", [128, L], F32)
            kb.dma(self.gdnn[:], self.gdnn_in.ap(), "g3", w=["gdnn"])
            self.ident = self.c_f[:, 0:128]
            self.c2 = sb("c2", [128, 2562], F32)
            self.nqk = sb("nqk", [128, 2 * L], F32)
            self.b31bc = sb("b31bc", [128, 6], F32)
            kb.dma(self.c2[:], self.cst2.ap(), "cst2", w=["cst2"])
            kb.dma(self.nqk[:], self.nqk_in.ap(), "nqk", w=["nqk"])
            kb.barrier()
            if "nsa" in self.enable and not self.mix_in:
                self.nsa_tables()
            if "cast" in self.phases:
                with self.sbt("cu_f", [128, 2, 4224], F32) as sf0, self.sbt("cu_b", [128, 2, 4224], BF16) as sbf0:
                    specs0 = self.cast_specs(0, ("in",)) if BG_CAST else [s_ for l_ in range(self.n_layers) for s_ in self.cast_specs(l_, ("in", "rest"))]
                    tk = self.cast_runner(specs0, sf0, sbf0, 2, 2, "cu")
                    while tk():
                        pass
                    kb.barrier()
            kb.barrier()
            for l in range(self.n_layers):
                if "inproj" in self.phases:
                    self.phase_inproj(l)
                kb.barrier()
                if "proj" in self.dbg and l == 0:
                    self.copy_dram(self.dbg_out["proj"], self.projT, NJ_IN * 128, S, F32)
                    kb.dma(self.dbg_out["small"].ap(), self.small_tm.ap(), "dbgs")
                    kb.barrier()
                if self.mix_in:
                    self.cast_mix_dbg()
                elif "mix" in self.phases:
                    self.phase_mixers(l)
                kb.barrier()
                if "mix" in self.dbg and l == 0:
                    kb.dma(self.dbg_out["mix"].ap(), self.mixT.ap(), "dbgm")
                    kb.barrier()
                if "ffn" in self.phases:
                    self.phase_ffn(l, last=(l == self.n_layers - 1))
                else:
                    kb.dma(self.yT.ap()[0:128, :], self.xT_in.ap()[0:128, :], "dummyy")
                kb.barrier()

    def copy_dram(self, dst, src, rows, cols, dt):
        kb = self.kb
        for r0 in range(0, rows, 1024):
            n = min(1024, rows - r0)
            kb.dma(dst.ap()[r0:r0 + n, :], src.ap()[r0:r0 + n, :], "cpd")

    def cast_mix_dbg(self):
        kb, nc = self.kb, self.nc
        with self.sbt("mdb_f", [128, S], F32) as tf, self.sbt("mdb_b", [128, S], BF16) as tb:
            for k in range(16):
                kb.dma(tf[:], self.mix_dbg.ap()[k * 128:(k + 1) * 128, :], "mdbl", w=["mdbf"])
                kb.op("dve", lambda e: e.tensor_copy(out=tb[:], in_=tf[:]), r=["mdbf"], w=["mdbb"])
                kb.dma(self.mixT.ap()[k * 128:(k + 1) * 128, :], tb[:], "mdbs", r=["mdbb"])
            kb.barrier()

    def phase_cast(self, l):
        kb, nc = self.kb, self.nc
        with self.sbt("cs_f", [128, 2, 8192], F32) as sf, self.sbt("cs_b", [128, 2, 8192], BF16) as sbf:
            self._cast_i = 0

            def unit(src_ap_list, n_src_cols, colmap, dst_fn, n_dst_cols, pad_from=None):
                i = self._cast_i
                self._cast_i += 1
                s = i % 2
                fk, bk = "csf%d" % s, "csb%d" % s
                for (o, ap, n) in src_ap_list:
                    kb.dma(sf[:, s, o:o + n], ap, "csl%d" % s, w=[fk])
                pieces = []
                for (sc, dc, n) in colmap:
                    step = (n + 3) // 4 if n >= 1536 else n
                    for a in range(0, n, step):
                        pieces.append((sc + a, dc + a, min(step, n - a)))
                engs = ["dve", "act"]
                first = True
                for pi, (sc, dc, n) in enumerate(pieces):
                    e = engs[pi % 2]
                    if e == "act":
                        f = lambda en, sc=sc, dc=dc, n=n: en.copy(out=sbf[:, s, dc:dc + n], in_=sf[:, s, sc:sc + n])
                    else:
                        f = lambda en, sc=sc, dc=dc, n=n: en.tensor_copy(out=sbf[:, s, dc:dc + n], in_=sf[:, s, sc:sc + n])
                    kb.op(e, f, r=[fk], w=[bk + "_%d" % pi])
                if pad_from is not None:
                    kb.op("pool", lambda en: en.memset(sbf[:, s, pad_from:n_dst_cols], 0.0), w=[bk + "_pad"])
                rk = [bk + "_%d" % pi for pi in range(len(pieces))] + ([bk + "_pad"] if pad_from is not None else [])
                for (dst_ap, c0, n) in dst_fn:
                    kb.dma(dst_ap, sbf[:, s, c0:c0 + n].rearrange("p (j c) -> p j c", c=128), "css%d" % s, r=rk, q="pool")

            for kc in range(16):
                src = self.w_in.ap()[l, kc * 128:(kc + 1) * 128, :]
                dst = self.wt_in[l].ap()[:, :, kc, :].rearrange("j p c -> p j c")
                unit([(0, src, NPROJ)], NPROJ, IN_COLMAP, [(dst, 0, NJ_IN * 128)], NJ_IN * 128, pad_from=NPROJ)
            for kc in range(16):
                src = self.w_up.ap()[l, kc * 128:(kc + 1) * 128, :]
                dst = self.wt_up[l].ap()[:, :, kc, :].rearrange("j p c -> p j c")
                unit([(0, src, DFF)], DFF, [(0, 0, DFF)], [(dst, 0, DFF)], DFF)
            for k4 in range(16):
                srcs = [(i * 2048, self.w_down.ap()[l, (k4 * 4 + i) * 128:(k4 * 4 + i + 1) * 128, :], 2048) for i in range(4)]
                dsts = [(self.wt_dn[l].ap()[:, :, k4 * 4 + i, :].rearrange("j p c -> p j c"), i * 2048, 2048) for i in range(4)]
                unit(srcs, 8192, [(0, 0, 8192)], dsts, 8192)
            for k4 in range(4):
                srcs = [(i * 2048, self.w_out.ap()[l, (k4 * 4 + i) * 128:(k4 * 4 + i + 1) * 128, :], 2048) for i in range(4)]
                dsts = [(self.wt_out[l].ap()[:, :, k4 * 4 + i, :].rearrange("j p c -> p j c"), i * 2048, 2048) for i in range(4)]
                unit(srcs, 8192, [(0, 0, 8192)], dsts, 8192)
            kb.barrier()


    def cast_specs(self, l, which):
        specs = []
        if "in" in which:
            for kc in range(16):
                row = self.w_in.ap()[l, kc * 128:(kc + 1) * 128, :]
                dA = self.wt_in[l].ap()[0:32, :, kc, :].rearrange("j p c -> p j c")
                dB = self.wt_in[l].ap()[32:51, :, kc, :].rearrange("j p c -> p j c")
                specs.append(([(0, row[:, 0:4108], 4108)], [(0, 0, 3072), (3084, 3072, 1024)], None, [(dA, 0, 4096)]))
                specs.append(([(0, row[:, 4108:6430], 2322), (2322, row[:, 3072:3084], 12)],
                              [(0, 0, 2304), (2322, 2304, 12), (2304, 2316, 18)], (2334, 2432), [(dB, 0, 2432)]))
        if "rest" in which:
            for kc in range(16):
                for hf in range(2):
                    src = self.w_up.ap()[l, kc * 128:(kc + 1) * 128, hf * 4096:(hf + 1) * 4096]
                    dst = self.wt_up[l].ap()[hf * 32:(hf + 1) * 32, :, kc, :].rearrange("j p c -> p j c")
                    specs.append(([(0, src, 4096)], [(0, 0, 4096)], None, [(dst, 0, 4096)]))
            for k2 in range(32):
                loads = [(i * 2048, self.w_down.ap()[l, (k2 * 2 + i) * 128:(k2 * 2 + i + 1) * 128, :], 2048) for i in range(2)]
                stores = [(self.wt_dn[l].ap()[:, :, k2 * 2 + i, :].rearrange("j p c -> p j c"), i * 2048, 2048) for i in range(2)]
                specs.append((loads, [(0, 0, 4096)], None, stores))
            for k2 in range(8):
                loads = [(i * 2048, self.w_out.ap()[l, (k2 * 2 + i) * 128:(k2 * 2 + i + 1) * 128, :], 2048) for i in range(2)]
                stores = [(self.wt_out[l].ap()[:, :, k2 * 2 + i, :].rearrange("j p c -> p j c"), i * 2048, 2048) for i in range(2)]
                specs.append((loads, [(0, 0, 4096)], None, stores))
        return specs

    def cast_runner(self, specs, sf, sbf, nf, nb, tag):
        kb = self.kb
        st = {"i": 0, "loaded": 0}

        def load(u):
            s = u % nf
            for (o, ap, n) in specs[u][0]:
                kb.dma(sf[:, s, o:o + n], ap, "%sl%d" % (tag, s), w=["%sf%d" % (tag, s)])

        def tick():
            i = st["i"]
            if i >= len(specs):
                return False
            while st["loaded"] <= min(i, len(specs) - 1) or (i == 0 and st["loaded"] <= min(nf - 1, len(specs) - 1)):
                load(st["loaded"])
                st["loaded"] += 1
            s, sb_ = i % nf, i % nb
            fk, bk = "%sf%d" % (tag, s), "%sb%d" % (tag, sb_)
            loads, convs, pad, stores = specs[i]
            keys = []
            pi = 0
            for (sc, dc, n) in convs:
                step = (n + 1) // 2 if n >= 1024 else n
                for a in range(0, n, step):
                    m = min(step, n - a)
                    e = "dve" if pi % 2 == 0 else "act"
                    k_ = "%s_%d" % (bk, pi)
                    if e == "act":
                        kb.op("act", lambda en, sc=sc, dc=dc, a=a, m=m: en.copy(out=sbf[:, sb_, dc + a:dc + a + m], in_=sf[:, s, sc + a:sc + a + m]), r=[fk], w=[k_])
                    else:
                        kb.op("dve", lambda en, sc=sc, dc=dc, a=a, m=m: en.tensor_copy(out=sbf[:, sb_, dc + a:dc + a + m], in_=sf[:, s, sc + a:sc + a + m]), r=[fk], w=[k_])
                    keys.append(k_)
                    pi += 1
            if pad is not None:
                kb.op("pool", lambda en: en.memset(sbf[:, sb_, pad[0]:pad[1]], 0.0), w=[bk + "_pad"])
                keys.append(bk + "_pad")
            allk = ["%s_%d" % (bk, q) for q in range(8)] + [bk + "_pad"]
            for (dst_ap, c0, n) in stores:
                kb.dma(dst_ap, sbf[:, sb_, c0:c0 + n].rearrange("p (j c) -> p j c", c=128), "%ss%d" % (tag, sb_), r=keys, w=[], q="pool")
            for k_ in allk:
                if k_ not in keys:
                    stt = kb.res.setdefault(k_, [None, {}])
                    for kk in keys[:1]:
                        stt[1].update(kb.res[kk][1])
            st["i"] += 1
            if st["loaded"] < len(specs) and st["loaded"] <= i + nf:
                load(st["loaded"])
                st["loaded"] += 1
            return True

        return tick

    def rmsnorm_tile(self, xt, ht, gain_col0, sq, rstd, ps_ss, tag, xkeys):
        kb = self.kb
        htag = tag
        tag = tag[0]
        for kc in range(16):
            s = kc % 2
            kb.op("act", lambda e, kc=kc, s=s: e.activation(out=sq[:, s, :], in_=xt[:, kc, :], func=AF.Square),
                  r=[xkeys[kc]], w=[tag + "sq%d" % s])
            kb.op("pe", lambda e, kc=kc, s=s: e.matmul(ps_ss[:], self.ones_b[:], sq[:, s, :], start=(kc == 0), stop=(kc == 15)),
                  r=[tag + "sq%d" % s, "onesb"], w=[tag + "ss"])
        kb.op("act", lambda e: e.activation(out=rstd[:], in_=ps_ss[:], func=AF.Ln, bias=self.eps_t[:, 0:1], scale=1.0 / D),
              r=[tag + "ss", "eps"], w=[tag + "rstd"])
        kb.op("act", lambda e: e.activation(out=rstd[:], in_=rstd[:], func=AF.Exp, scale=-0.5), r=[tag + "rstd"], w=[tag + "rstd"])
        for kc in range(16):
            kb.op("dve", lambda e, kc=kc: e.scalar_tensor_tensor(out=ht[:, kc, :], in0=xt[:, kc, :],
                                                              scalar=self.gains[:, gain_col0 + kc:gain_col0 + kc + 1],
                                                              in1=rstd[:], op0=ALU.mult, op1=ALU.mult),
                  r=[xkeys[kc], tag + "rstd", "gains"], w=[htag + "h%d" % kc])

    def phase_inproj(self, l):
        kb, nc = self.kb, self.nc
        xsrc = self.xT_in if l == 0 else self.xres
        with ExitStack() as es:
            sb = lambda n, s, dt: es.enter_context(self.sbt(n, s, dt))
            xt = sb("a_x", [128, 16, TT], F32)
            ht2 = sb("a_h", [128, 2, 16, TT], BF16)
            sq = sb("a_sq", [128, 2, TT], BF16)
            rstd = sb("a_rstd", [128, TT], F32)
            self.eps_t = sb("a_eps", [128, 1], F32)
            NW = 4
            wt = sb("a_w", [128, NW, 16, 128], BF16)
            NO = 4
            ot = sb("a_o", [128, NO, TT], F32)
            osm = sb("a_osm", [128, 4, 32], F32)
            ps_ss = es.enter_context(self.pst("a_pss", [128, TT], F32))
            ps_o = [es.enter_context(self.pst("a_po%d" % i, [128, TT], F32)) for i in range(4)]
            ps_sm = es.enter_context(self.pst("a_psm", [128, 4, 32], F32))
            kb.op("pool", lambda e: e.memset(self.eps_t[:], EPS), w=["eps"])
            ui = 0

            def prep_tile(t):
                t0 = t * TT
                kb.dma(xt[:], xsrc.ap()[:, t0:t0 + TT].rearrange("(k p) t -> p k t", p=128), "ax", w=["ax"])
                if l == 0:
                    kb.dma(self.xres.ap()[:, t0:t0 + TT].rearrange("(k p) t -> p k t", p=128), xt[:], "axs", r=["ax"], q="pool")
                self.rmsnorm_tile(xt, ht2[:, t % 2], l * 16, sq, rstd, ps_ss, "a%d" % (t % 2), ["ax"] * 16)

            prep_tile(0)
            for t in range(NT):
                t0 = t * TT
                ht = ht2[:, t % 2]
                HK = ["a%dh%d" % (t % 2, k) for k in range(16)]
                for j in range(NJ_IN):
                    if j == 6 and t + 1 < NT:
                        prep_tile(t + 1)
                    ws = ui % NW
                    pb = ui % 4
                    os_ = ui % NO
                    ui += 1
                    kb.dma(wt[:, ws], self.wt_in[l].ap()[j], "aw%d" % ws, w=["aw%d" % ws])
                    M = 128 if j < CH_SM else 30
                    for kc in range(16):
                        kb.op("pe", lambda e, kc=kc, ws=ws, pb=pb, M=M: e.matmul(ps_o[pb][0:M, :], wt[:, ws, kc, 0:M], ht[:, kc, :],
                                                                                 start=(kc == 0), stop=(kc == 15)),
                              r=["aw%d" % ws, HK[kc]], w=["apo%d" % pb])
                    ev = "act" if ui % 2 == 0 else "dve"
                    if ev == "act":
                        kb.op("act", lambda e, pb=pb, os_=os_, M=M: e.copy(out=ot[0:M, os_, :], in_=ps_o[pb][0:M, :]),
                              r=["apo%d" % pb], w=["ao%d" % os_])
                    else:
                        kb.op("dve", lambda e, pb=pb, os_=os_, M=M: e.tensor_copy(out=ot[0:M, os_, :], in_=ps_o[pb][0:M, :]),
                              r=["apo%d" % pb], w=["ao%d" % os_])
                    kb.dma(self.projT.ap()[j * 128:j * 128 + M, t0:t0 + TT], ot[0:M, os_, :], "aos%d" % os_, r=["ao%d" % os_], q="pool")
                    if j == CH_SM:
                        for q4 in range(4):
                            for kc in range(16):
                                kb.op("pe", lambda e, kc=kc, ws=ws, q4=q4: e.matmul(ps_sm[:, q4, 0:30], ht[:, kc, q4 * 128:(q4 + 1) * 128],
                                                                                   wt[:, ws, kc, 0:30], start=(kc == 0), stop=(kc == 15)),
                                      r=["aw%d" % ws, HK[kc]], w=["apsm"])
                        kb.op("dve", lambda e: e.tensor_copy(out=osm[:, :, 0:30], in_=ps_sm[:, :, 0:30]), r=["apsm"], w=["aosm"])
                        kb.dma(self.small_tm.ap()[t0:t0 + TT, 0:30].rearrange("(q p) c -> p q c", p=128), osm[:, :, 0:30], "aosm", r=["aosm"], q="pool")
            kb.barrier()

    def phase_ffn(self, l, last):
        kb, nc = self.kb, self.nc
        with ExitStack() as es:
            sb = lambda n, s, dt: es.enter_context(self.sbt(n, s, dt))
            xt = sb("f_x", [128, 16, TT], F32)
            ht = sb("f_h", [128, 16, TT], BF16)
            at = sb("f_a", [128, 64, TT], BF16)
            sq = sb("f_sq", [128, 2, TT], BF16)
            rstd = sb("f_rstd", [128, TT], F32)
            rl = sb("f_rl", [128, 2, TT], F32)
            self.eps_t = sb("f_eps", [128, 1], F32)
            NW = 3
            wt = sb("f_w", [128, NW, 16, 128], BF16)
            wd = sb("f_wd", [128, 2, 64, 128], BF16)
            ps_ss = es.enter_context(self.pst("f_pss", [128, TT], F32))
            ps_o = [es.enter_context(self.pst("f_po%d" % i, [128, TT], F32)) for i in range(4)]
            kb.op("pool", lambda e: e.memset(self.eps_t[:], EPS), w=["eps"])
            HK = ["fh%d" % k for k in range(16)]
            XK = ["fx%d" % k for k in range(16)]
            ui = 0
            for t in range(NT):
                t0 = t * TT
                kb.dma(ht[:], self.mixT.ap()[:, t0:t0 + TT].rearrange("(k p) t -> p k t", p=128), "fm", w=HK)
                for j in range(16):
                    ws, pb = ui % NW, ui % 4
                    ui += 1
                    kb.dma(wt[:, ws], self.wt_out[l].ap()[j], "fw%d" % ws, w=["fw%d" % ws])
                    kb.dma(xt[:, j, :], self.xres.ap()[j * 128:(j + 1) * 128, t0:t0 + TT], "fxl%d" % (j % 8), w=[XK[j]])
                    for kc in range(16):
                        kb.op("pe", lambda e, kc=kc, ws=ws, pb=pb: e.matmul(ps_o[pb][:], wt[:, ws, kc, :], ht[:, kc, :],
                                                                           start=(kc == 0), stop=(kc == 15)),
                              r=["fw%d" % ws, HK[kc]], w=["fpo%d" % pb])
                    kb.op("dve", lambda e, j=j, pb=pb: e.tensor_tensor(out=xt[:, j, :], in0=xt[:, j, :], in1=ps_o[pb][:], op=ALU.add),
                          r=["fpo%d" % pb, XK[j]], w=[XK[j]])
                self.rmsnorm_tile(xt, ht, DEPTH * 16 + l * 16, sq, rstd, ps_ss, "f", XK)
                for j in range(64):
                    ws, pb = ui % NW, ui % 4
                    ui += 1
                    kb.dma(wt[:, ws], self.wt_up[l].ap()[j], "fw%d" % ws, w=["fw%d" % ws])
                    for kc in range(16):
                        kb.op("pe", lambda e, kc=kc, ws=ws, pb=pb: e.matmul(ps_o[pb][:], wt[:, ws, kc, :], ht[:, kc, :],
                                                                           start=(kc == 0), stop=(kc == 15)),
                              r=["fw%d" % ws, HK[kc]], w=["fpo%d" % pb])
                    s2 = j % 2
                    kb.op("act", lambda e, pb=pb, s2=s2: e.activation(out=rl[:, s2, :], in_=ps_o[pb][:], func=AF.Relu),
                          r=["fpo%d" % pb], w=["frl%d" % s2])
                    kb.op("dve", lambda e, j=j, s2=s2: e.tensor_tensor(out=at[:, j, :], in0=rl[:, s2, :], in1=rl[:, s2, :], op=ALU.mult),
                          r=["frl%d" % s2], w=["fa%d" % j])
                for n in range(16):
                    s2, pb = n % 2, ui % 4
                    ui += 1
                    kb.dma(wd[:, s2], self.wt_dn[l].ap()[n], "fwd%d" % s2, w=["fwd%d" % s2])
                    for f in range(64):
                        kb.op("pe", lambda e, f=f, s2=s2, pb=pb: e.matmul(ps_o[pb][:], wd[:, s2, f, :], at[:, f, :],
                                                                          start=(f == 0), stop=(f == 63)),
                              r=["fwd%d" % s2, "fa%d" % f], w=["fpo%d" % pb])
                    kb.op("dve", lambda e, n=n, pb=pb: e.tensor_tensor(out=xt[:, n, :], in0=xt[:, n, :], in1=ps_o[pb][:], op=ALU.add),
                          r=["fpo%d" % pb, XK[n]], w=[XK[n]])
                    dst = self.yT if last else self.xres
                    kb.dma(dst.ap()[n * 128:(n + 1) * 128, t0:t0 + TT], xt[:, n, :], "fxs%d" % (n % 4), r=[XK[n]], q="pool")
            kb.barrier()

    def phase_mixers(self, l):
        if "gdn" in self.enable:
            self.phase_gdn(l)
            self.kb.barrier()
        if "sgu" in self.enable:
            self.phase_sgu(l)
            self.kb.barrier()
        if "nsa" in self.enable:
            self.phase_nsa(l)
            self.kb.barrier()

    def gelu(self, e_name, out, in_, r, w):
        self.kb.op("act", lambda e: e.activation(out=out, in_=in_, func=AF.Gelu_apprx_tanh), r=r, w=w)

    def bcast_row(self, handle, offset, n):
        return bass.AP(handle, offset, [[0, 128], [1, n]])

    def phase_sgu(self, l):
        kb, nc = self.kb, self.nc
        with ExitStack() as es:
            sb = lambda n, s, dt: es.enter_context(self.sbt(n, s, dt))
            wnat = sb("s_wn", [128, 4, 128], F32)
            wTm = sb("s_wT", [128, 4, 128], F32)
            brow = sb("s_br", [1, 512], F32)
            lng = sb("s_lng", [128, 512], F32)
            lnb = sb("s_lnb", [128, 512], F32)
            ut = sb("s_u", [128, 2, 4, TT], F32)
            vt = sb("s_v", [128, 2, 4, TT], F32)
            vn = sb("s_vn", [128, 2, 512], F32)
            st6 = sb("s_st", [128, 6], F32)
            mv = sb("s_mv", [128, 2], F32)
            rs = sb("s_rs", [128, 1], F32)
            eps_t = sb("s_eps", [128, 1], F32)
            obf = sb("s_o", [128, 2, 4, TT], BF16)
            ps_tr = es.enter_context(self.pst("s_ptr", [128, 512], F32))
            ps_m = es.enter_context(self.pst("s_pm", [128, 4, 128], F32))
            maskT = self.c_f[:, 512:640]
            kb.op("pool", lambda e: e.memset(eps_t[:], EPS), w=["seps"])
            kb.dma(wnat[:], self.sgu_w.ap()[l].rearrange("g t s -> t g s"), "swn", w=["swn"])
            kb.dma(brow[:], self.sgu_b.ap()[l:l + 1].rearrange("a g t -> a (g t)"), "sbr", w=["sbr"])
            kb.dma(lng[:], self.bcast_row(self.sgu_ln_g, l * 512, 512), "slg", w=["slg"])
            kb.dma(lnb[:], self.bcast_row(self.sgu_ln_b, l * 512, 512), "slb", w=["slb"])
            for g in range(4):
                kb.op("pe", lambda e, g=g: e.transpose(ps_m[:, g, :], wnat[:, g, :], self.ident), r=["swn", "cst"], w=["spm"])
            kb.op("dve", lambda e: e.tensor_tensor(out=wTm[:], in0=ps_m[:], in1=maskT.unsqueeze(1).to_broadcast([128, 4, 128]), op=ALU.mult),
                  r=["spm", "cst"], w=["swT"])
            for t in range(NT):
                t0 = t * TT
                s = t % 2
                kb.dma(ut[:, s], self.projT.ap()[CH_UB * 128:(CH_UB + 4) * 128, t0:t0 + TT].rearrange("(g p) t -> p g t", p=128), "su%d" % s, w=["su%d" % s])
                kb.dma(vt[:, s], self.projT.ap()[CH_VB * 128:(CH_VB + 4) * 128, t0:t0 + TT].rearrange("(g p) t -> p g t", p=128), "sv%d" % s, w=["sv%d" % s])
                for g in range(4):
                    self.gelu("act", ut[:, s, g, :], ut[:, s, g, :], ["su%d" % s], ["su%d" % s])
                    self.gelu("act", vt[:, s, g, :], vt[:, s, g, :], ["sv%d" % s], ["sv%d" % s])
                for c4 in range(4):
                    cs = slice(c4 * 128, (c4 + 1) * 128)
                    v2 = c4 % 2
                    for g in range(4):
                        kb.op("pe", lambda e, g=g, cs=cs: e.transpose(ps_tr[:, g * 128:(g + 1) * 128], vt[:, s, g, cs], self.ident),
                              r=["sv%d" % s, "cst"], w=["sptr"])
                    kb.op("dve", lambda e: e.bn_stats(out=st6[:], in_=ps_tr[:]), r=["sptr"], w=["sst"])
                    kb.op("dve", lambda e: e.bn_aggr(out=mv[:], in_=st6[:]), r=["sst"], w=["smv"])
                    kb.op("act", lambda e: e.activation(out=rs[:], in_=mv[:, 1:2], func=AF.Ln, bias=eps_t[:, 0:1], scale=1.0),
                          r=["smv", "seps"], w=["srs"])
                    kb.op("act", lambda e: e.activation(out=rs[:], in_=rs[:], func=AF.Exp, scale=-0.5), r=["srs"], w=["srs"])
                    kb.op("dve", lambda e, v2=v2: e.tensor_scalar(out=vn[:, v2, :], in0=ps_tr[:], scalar1=mv[:, 0:1], scalar2=rs[:, 0:1],
                                                                   op0=ALU.subtract, op1=ALU.mult), r=["sptr", "smv", "srs"], w=["svn%d" % v2])
                    kb.op("dve", lambda e, v2=v2: e.tensor_tensor(out=vn[:, v2, :], in0=vn[:, v2, :], in1=lng[:], op=ALU.mult),
                          r=["svn%d" % v2, "slg"], w=["svn%d" % v2])
                    kb.op("dve", lambda e, v2=v2: e.tensor_tensor(out=vn[:, v2, :], in0=vn[:, v2, :], in1=lnb[:], op=ALU.add),
                          r=["svn%d" % v2, "slb"], w=["svn%d" % v2])
                    for g in range(4):
                        kb.op("pe", lambda e, g=g, v2=v2: e.matmul(ps_m[:, g, :], vn[:, v2, g * 128:(g + 1) * 128], wTm[:, g, :], start=True, stop=False),
                              r=["svn%d" % v2, "swT"], w=["spm"])
                        kb.op("pe", lambda e, g=g: e.matmul(ps_m[:, g, :], self.ones_f[0:1, :], brow[0:1, g * 128:(g + 1) * 128], start=False, stop=True),
                              r=["sbr", "ones"], w=["spm"])
                    kb.op("dve", lambda e, cs=cs: e.tensor_tensor(out=obf[:, s, :, cs], in0=ut[:, s, :, cs], in1=ps_m[:], op=ALU.mult),
                          r=["spm", "su%d" % s], w=["so%d" % s])
                kb.dma(self.mixT.ap()[768:1280, t0:t0 + TT].rearrange("(g p) t -> p g t", p=128), obf[:, s], "sos%d" % s, r=["so%d" % s])
            kb.barrier()

    def phase_gdn(self, l):
        kb, nc = self.kb, self.nc
        H = 6
        with ExitStack() as es:
            sb = lambda n, s, dt: es.enter_context(self.sbt(n, s, dt))
            PS = es.enter_context(self.pst("g_ps", [128, 4096], F32))
            slot = lambda i: PS[:, i * 128:(i + 1) * 128]
            P1 = lambda h: slot(4 * h)
            P2 = lambda h: slot(4 * h + 1)
            P3 = lambda h: slot(4 * h + 2)
            P4 = lambda h: slot(4 * h + 3)
            ps_ss = PS[:, 24 * 128:28 * 128]
            ps_sm = PS[:, 28 * 128:28 * 128 + 6]
            convw = sb("g_cw", [128, 72], F32)
            dtb = sb("g_dtb", [128, 6], F32)
            nega = sb("g_na", [128, 6], F32)
            eps_t = sb("g_eps", [128, 1], F32)
            sm = sb("g_sm", [128, 4, 30], F32)
            beta = sb("g_beta", [128, 4, 6], F32)
            nbeta = sb("g_nbeta", [128, 4, 6], F32)
            gtm = sb("g_g", [128, 4, 6], F32)
            xin = sb("g_xin", [128, 2, 3, 515], F32)
            QKV = sb("g_qkv", [128, 6, 3, 512], F32)
            sq = sb("g_sq", [128, 512], BF16)
            rn = sb("g_rn", [128, 512], F32)
            sz = sb("g_sz", [128, 6, 512], F32)
            obf = sb("g_obf", [128, 6, 512], BF16)
            Sst = sb("g_S", [128, 6, 128], F32)
            names = ["Ktm", "Vb", "rb", "ebc", "E", "Es", "Ei", "NT", "Aqk", "N", "AqkT", "R", "Pa", "Pb", "PTa", "PTb",
                     "Um0", "Um1", "Kw", "WT", "QgT", "Kpp", "Vnew", "Osb", "On"]
            A = {n: sb("g_" + n, [128, 6, 128], F32) for n in names}
            A["tE"] = A["E"]
            bg_tick = None
            if BG_CAST and "cast" in self.phases:
                bsf = sb("g_csf", [128, 2, 4224], F32)
                bsb = sb("g_csb", [128, 1, 4224], BF16)
                bspecs = self.cast_specs(l, ("rest",)) + (self.cast_specs(l + 1, ("in",)) if l + 1 < self.n_layers else [])
                bg_tick = self.cast_runner(bspecs, bsf, bsb, 2, 1, "cg")
            gc = sb("g_gc", [128, 6], F32)
            s1 = sb("g_s1", [128, 6], F32)
            dl = sb("g_dl", [128, 6], F32)
            egl = sb("g_egl", [128, 6, 2], F32)
            ssq = sb("g_ssq", [128, 6], F32)
            rm = sb("g_rm", [128, 4], F32)
            triBD = self.c_f[:, 128:256]
            m_incl = self.c_f[:, 256:384]
            m_strict = self.c_f[:, 384:512]
            GK = lambda n, h: "%s%d" % (n, 3 * (h // 3))
            K = lambda n, h: ("@gb%d" % h) if n in ("P1", "P2", "P3", "P4") else "g%s%d" % (n, h)
            kb.op("pool", lambda e: e.memset(eps_t[:], EPS), w=["geps"])
            kb.op("pool", lambda e: e.memset(rm[:], 0.0), w=["grm"])
            kb.op("pool", lambda e: e.memset(rm[0:64, 0:1], 1.0), w=["grm"])
            kb.op("pool", lambda e: e.memset(rm[64:128, 1:2], 1.0), w=["grm"])
            kb.op("pool", lambda e: e.memset(rm[0:64, 2:3], -1.0), w=["grm"])
            kb.op("pool", lambda e: e.memset(rm[64:128, 3:4], -1.0), w=["grm"])
            kb.op("pool", lambda e: e.memset(Sst[:], 0.0), w=[K("S", h) for h in range(H)])
            kb.dma(convw[:], self.conv_pl.ap()[l], "gcw", w=["gcw"])
            kb.dma(dtb[:], self.bcast_row(self.dt_bias, l * 6, 6), "gdtb", w=["gdtb"])
            kb.dma(nega[:], self.bcast_row(self.a_log, l * 6, 6), "gna", w=["gna"])
            kb.op("act", lambda e: e.activation(out=nega[:], in_=nega[:], func=AF.Exp), r=["gna"], w=["gna"])
            kb.op("dve", lambda e: e.tensor_scalar(out=nega[:], in0=nega[:], scalar1=-1.0, scalar2=None, op0=ALU.mult), r=["gna"], w=["gna"])
            for b in range(NT):
                t0 = b * TT
                kb.dma(sm[:], self.small_tm.ap()[t0:t0 + TT, 0:30].rearrange("(q p) c -> p q c", p=128), "gsm", w=["gsm"])
                kb.dma(sz[:], self.projT.ap()[CH_ZA * 128:(CH_ZA + 6) * 128, t0:t0 + TT].rearrange("(h p) t -> p h t", p=128), "gsz", w=["gsz"])
                kb.op("act", lambda e: e.activation(out=sz[:], in_=sz[:], func=AF.Silu), r=["gsz"], w=["gsz"])
                kb.op("act", lambda e: e.activation(out=beta[:], in_=sm[:, :, 0:6], func=AF.Sigmoid), r=["gsm"], w=["gbeta"])
                kb.op("dve", lambda e: e.tensor_scalar(out=nbeta[:], in0=beta[:], scalar1=-1.0, scalar2=None, op0=ALU.mult), r=["gbeta"], w=["gnbeta"])
                kb.op("dve", lambda e: e.tensor_tensor(out=gtm[:], in0=sm[:, :, 6:12], in1=dtb[:].unsqueeze(1).to_broadcast([128, 4, 6]), op=ALU.add),
                      r=["gsm", "gdtb"], w=["gg"])
                kb.op("act", lambda e: e.activation(out=gtm[:], in_=gtm[:], func=AF.Exp), r=["gg"], w=["gg"])
                kb.op("act", lambda e: e.activation(out=gtm[:], in_=gtm[:], func=AF.Ln, bias=1.0, scale=1.0), r=["gg"], w=["gg"])
                kb.op("dve", lambda e: e.tensor_tensor(out=gtm[:], in0=gtm[:], in1=nega[:].unsqueeze(1).to_broadcast([128, 4, 6]), op=ALU.mult),
                      r=["gg", "gna"], w=["gg"])
                for h in range(H):
                    xs = h % 2
                    for qi, ch in enumerate((CH_QA, CH_KA, CH_VA)):
                        row = (ch + h) * 128
                        if b == 0:
                            kb.op("pool", lambda e, qi=qi: e.memset(xin[:, xs, qi, 0:3], 0.0), w=["gx%d" % xs])
                            kb.dma(xin[:, xs, qi, 3:515], self.projT.ap()[row:row + 128, 0:TT], "gx%d" % xs, w=["gx%d" % xs])
                        else:
                            kb.dma(xin[:, xs, qi, :], self.projT.ap()[row:row + 128, t0 - 3:t0 + TT], "gx%d" % xs, w=["gx%d" % xs])
                    for qi, ch in enumerate((CH_QA, CH_KA, CH_VA)):
                        cw = lambda j: convw[:, (ch + h) * 4 + j:(ch + h) * 4 + j + 1]
                        dst = QKV[:, h, qi, :]
                        kb.op("dve", lambda e, qi=qi, cw=cw, dst=dst: e.tensor_scalar(out=dst, in0=xin[:, xs, qi, 0:512], scalar1=cw(0), scalar2=None, op0=ALU.mult),
                              r=["gx%d" % xs, "gcw"], w=[K("qkv%d" % qi, h)])
                        for j in (1, 2, 3):
                            kb.op("dve", lambda e, qi=qi, cw=cw, dst=dst, j=j: e.scalar_tensor_tensor(out=dst, in0=xin[:, xs, qi, j:j + 512], scalar=cw(j), in1=dst,
                                                                                              op0=ALU.mult, op1=ALU.add),
                                  r=["gx%d" % xs, "gcw", K("qkv%d" % qi, h)], w=[K("qkv%d" % qi, h)])
                        kb.op("act", lambda e, dst=dst: e.activation(out=dst, in_=dst, func=AF.Silu), r=[K("qkv%d" % qi, h)], w=[K("qkv%d" % qi, h)])
                        if qi < 2:
                            kb.op("act", lambda e, dst=dst: e.activation(out=sq[:], in_=dst, func=AF.Square), r=[K("qkv%d" % qi, h)], w=["gsq"])
                            kb.op("pe", lambda e: e.matmul(ps_ss, self.ones_b[:], sq[:], start=True, stop=True), r=["gsq", "onesb"], w=["@gpss"])
                            kb.op("act", lambda e: e.activation(out=rn[:], in_=ps_ss, func=AF.Ln, bias=eps_t[:, 0:1], scale=1.0), r=["@gpss", "geps"], w=["grn"])
                            kb.op("act", lambda e: e.activation(out=rn[:], in_=rn[:], func=AF.Exp, scale=-0.5), r=["grn"], w=["grn"])
                            sc = (128.0 ** -0.5) if qi == 0 else 1.0
                            kb.op("dve", lambda e, dst=dst, sc=sc: e.scalar_tensor_tensor(out=dst, in0=dst, scalar=sc, in1=rn[:], op0=ALU.mult, op1=ALU.mult),
                                  r=["grn", K("qkv%d" % qi, h)], w=[K("qkv%d" % qi, h)])
                def mk_stages(pr):
                    stages = []
                    lev = 0; Pc = PTc = Pn = PTn = None; c = 0; r0 = 0
                    cs = slice(pr * 128, (pr + 1) * 128)
                    Qt = lambda h: QKV[:, h, 0, cs]
                    Kt = lambda h: QKV[:, h, 1, cs]
                    Vt = lambda h: QKV[:, h, 2, cs]
                    a = lambda n, h: A[n][:, h, :]
                    def _pre(heads):
                        h0, n = heads[0], len(heads)
                        psm = PS[:, 28 * 128 + h0:28 * 128 + h0 + n]
                        kb.op("pe", lambda e: e.matmul(psm, triBD, gtm[:, pr, h0:h0 + n], start=True, stop=True), r=["gg", "cst"], w=["@gpsm"])
                        kb.op("dve", lambda e: e.tensor_copy(out=gc[:, h0:h0 + n], in_=psm), r=["@gpsm"], w=["ggc%d" % h0])
                        kb.op("act", lambda e: e.activation(out=s1[:, h0:h0 + n], in_=gc[:, h0:h0 + n], func=AF.Exp), r=["ggc%d" % h0], w=["gs1%d" % h0])
                        kb.op("dve", lambda e: e.tensor_tensor(out=s1[:, h0:h0 + n], in0=s1[:, h0:h0 + n], in1=beta[:, pr, h0:h0 + n], op=ALU.mult), r=["gs1%d" % h0, "gbeta"], w=["gs1%d" % h0])
                    stages.append(_pre)
                    def _st(heads, lev=lev, Pc=Pc, PTc=PTc, Pn=Pn, PTn=PTn, c=c, r0=r0):
                        for h in heads:
                            kb.op("pe", lambda e, h=h: e.transpose(P1(h), Kt(h), self.ident), r=[K("qkv1", h), "cst"], w=[K("P1", h)])
                            kb.op("pe", lambda e, h=h: e.transpose(P2(h), Vt(h), self.ident), r=[K("qkv2", h), "cst"], w=[K("P2", h)])
                            kb.op("dve", lambda e, h=h: e.tensor_scalar(out=a("rb", h), in0=triBD, scalar1=gtm[:, pr, h:h + 1], scalar2=None, op0=ALU.mult),
                                  r=["gg", "cst"], w=[K("rb", h)])
                            kb.op("pe", lambda e, h=h: e.matmul(P3(h), self.ones_f[:], a("rb", h), start=True, stop=True), r=[K("rb", h), "ones"], w=[K("P3", h)])
                            kb.op("act", lambda e, h=h: e.copy(out=a("Ktm", h), in_=P1(h)), r=[K("P1", h)], w=[K("Ktm", h)])
                            kb.op("dve", lambda e, h=h: e.tensor_scalar(out=a("Vb", h), in0=P2(h), scalar1=beta[:, pr, h:h + 1], scalar2=None, op0=ALU.mult),
                                  r=[K("P2", h), "gbeta"], w=[K("Vb", h)])
                    stages.append(_st)
                    def _st(heads, lev=lev, Pc=Pc, PTc=PTc, Pn=Pn, PTn=PTn, c=c, r0=r0):
                        for h in heads:
                            kb.op("dve", lambda e, h=h: e.tensor_scalar(out=a("tE", h), in0=P3(h), scalar1=gc[:, h:h + 1], scalar2=0.0, op0=ALU.subtract, op1=ALU.max),
                                  r=[K("P3", h), GK("ggc", h)], w=[K("E", h)])
                            kb.op("act", lambda e, h=h: e.activation(out=a("E", h), in_=a("tE", h), func=AF.Exp, scale=-1.0), r=[K("E", h)], w=[K("E", h)])
                            kb.op("act", lambda e, h=h: e.activation(out=a("ebc", h), in_=P3(h), func=AF.Exp), r=[K("P3", h)], w=[K("ebc", h)])
                            kb.op("act", lambda e, h=h: e.activation(out=egl[:, h, :], in_=PS[:, (4 * h + 2) * 128 + 63:(4 * h + 2) * 128 + 128:64], func=AF.Exp),
                                  r=[K("P3", h)], w=[K("egl", h)])
                            kb.op("dve", lambda e, h=h: e.tensor_tensor(out=dl[0:64, h:h + 1], in0=PS[0:64, (4 * h + 2) * 128 + 63:(4 * h + 2) * 128 + 64], in1=gc[0:64, h:h + 1], op=ALU.subtract),
                                  r=[K("P3", h), GK("ggc", h)], w=[K("dl", h)])
                            kb.op("dve", lambda e, h=h: e.tensor_tensor(out=dl[64:128, h:h + 1], in0=PS[64:128, (4 * h + 2) * 128 + 127:(4 * h + 2) * 128 + 128], in1=gc[64:128, h:h + 1], op=ALU.subtract),
                                  r=[K("P3", h), GK("ggc", h)], w=[K("dl", h)])
                            kb.op("act", lambda e, h=h: e.activation(out=dl[:, h:h + 1], in_=dl[:, h:h + 1], func=AF.Exp), r=[K("dl", h)], w=[K("dl", h)])
                            kb.op("dve", lambda e, h=h: e.tensor_tensor(out=a("Es", h), in0=a("E", h), in1=m_strict, op=ALU.mult), r=[K("E", h), "cst"], w=[K("Es", h)])
                            kb.op("dve", lambda e, h=h: e.tensor_tensor(out=a("Ei", h), in0=a("E", h), in1=m_incl, op=ALU.mult), r=[K("E", h), "cst"], w=[K("Ei", h)])
                            kb.op("pe", lambda e, h=h: e.matmul(P1(h), Kt(h), Kt(h), start=True, stop=True), r=[K("qkv1", h)], w=[K("P1", h)])
                            kb.op("pe", lambda e, h=h: e.matmul(P2(h), Qt(h), Kt(h), start=True, stop=True), r=[K("qkv0", h), K("qkv1", h)], w=[K("P2", h)])
                            kb.op("dve", lambda e, h=h: e.scalar_tensor_tensor(out=a("NT", h), in0=P1(h), scalar=nbeta[:, pr, h:h + 1], in1=a("Es", h), op0=ALU.mult, op1=ALU.mult),
                                  r=[K("P1", h), "gnbeta", K("Es", h)], w=[K("NT", h)])
                            kb.op("dve", lambda e, h=h: e.tensor_tensor(out=a("Aqk", h), in0=P2(h), in1=a("Ei", h), op=ALU.mult), r=[K("P2", h), K("Ei", h)], w=[K("Aqk", h)])
                            kb.op("act", lambda e, h=h: e.activation(out=a("Kw", h), in_=a("Ktm", h), func=AF.Identity, scale=s1[:, h:h + 1]),
                                  r=[K("Ktm", h), GK("gs1", h)], w=[K("Kw", h)])
                            kb.op("act", lambda e, h=h: e.activation(out=a("Kpp", h), in_=a("Ktm", h), func=AF.Identity, scale=dl[:, h:h + 1]),
                                  r=[K("Ktm", h), K("dl", h)], w=[K("Kpp", h)])
                            kb.op("dve", lambda e, h=h: e.tensor_tensor(out=a("QgT", h), in0=Qt(h), in1=a("ebc", h), op=ALU.mult), r=[K("qkv0", h), K("ebc", h)], w=[K("QgT", h)])
                    stages.append(_st)
                    def _st(heads, lev=lev, Pc=Pc, PTc=PTc, Pn=Pn, PTn=PTn, c=c, r0=r0):
                        for h in heads:
                            kb.op("pe", lambda e, h=h: e.transpose(P3(h), a("NT", h), self.ident), r=[K("NT", h), "cst"], w=[K("P3", h)])
                            kb.op("pe", lambda e, h=h: e.transpose(P4(h), a("Aqk", h), self.ident), r=[K("Aqk", h), "cst"], w=[K("P4", h)])
                            kb.op("act", lambda e, h=h: e.copy(out=a("N", h), in_=P3(h)), r=[K("P3", h)], w=[K("N", h)])
                            kb.op("dve", lambda e, h=h: e.tensor_tensor(out=a("R", h), in0=P3(h), in1=self.ident, op=ALU.add), r=[K("P3", h), "cst"], w=[K("R", h)])
                            kb.op("act", lambda e, h=h: e.copy(out=a("AqkT", h), in_=P4(h)), r=[K("P4", h)], w=[K("AqkT", h)])
                    stages.append(_st)
                    _n0 = len(stages)
                    Pc, PTc = "N", "NT"
                    for lev in range(1, 6):
                        Pn, PTn = ("Pa", "PTa") if lev % 2 == 1 else ("Pb", "PTb")
                        def _st(heads, lev=lev, Pc=Pc, PTc=PTc, Pn=Pn, PTn=PTn, c=c, r0=r0):
                            for h in heads:
                                if lev < 5:
                                    kb.op("pe", lambda e, h=h, Pc=Pc, PTc=PTc: e.matmul(P1(h), a(PTc, h), a(Pc, h), start=True, stop=True),
                                          r=[K(Pc, h), K(PTc, h)], w=[K("P1", h)])
                                kb.op("pe", lambda e, h=h, Pc=Pc, PTc=PTc: e.matmul(P2(h), a(Pc, h), a(PTc, h), start=True, stop=True),
                                      r=[K(Pc, h), K(PTc, h)], w=[K("P2", h)])
                        stages.append(_st)
                        def _st(heads, lev=lev, Pc=Pc, PTc=PTc, Pn=Pn, PTn=PTn, c=c, r0=r0):
                            for h in heads:
                                if lev < 5:
                                    kb.op("act", lambda e, h=h, Pn=Pn: e.copy(out=a(Pn, h), in_=P1(h)), r=[K("P1", h)], w=[K(Pn, h)])
                                kb.op("dve", lambda e, h=h, PTn=PTn: e.tensor_copy(out=a(PTn, h), in_=P2(h)), r=[K("P2", h)], w=[K(PTn, h)])
                        stages.append(_st)
                        def _st(heads, lev=lev, Pc=Pc, PTc=PTc, Pn=Pn, PTn=PTn, c=c, r0=r0):
                            for h in heads:
                                kb.op("pe", lambda e, h=h, PTn=PTn: e.matmul(P3(h), a(PTn, h), a("R", h), start=True, stop=True),
                                      r=[K(PTn, h), K("R", h)], w=[K("P3", h)])
                        stages.append(_st)
                        def _st(heads, lev=lev, Pc=Pc, PTc=PTc, Pn=Pn, PTn=PTn, c=c, r0=r0):
                            for h in heads:
                                kb.op("dve", lambda e, h=h: e.tensor_tensor(out=a("R", h), in0=a("R", h), in1=P3(h), op=ALU.add), r=[K("P3", h), K("R", h)], w=[K("R", h)])
                        stages.append(_st)
                        Pc, PTc = Pn, PTn
                    _lv = stages[_n0:]
                    del stages[_n0:]
                    _sq = [_lv[4 * k] for k in range(5)]; _ev = [_lv[4 * k + 1] for k in range(5)]
                    _rm = [_lv[4 * k + 2] for k in range(5)]; _ra = [_lv[4 * k + 3] for k in range(5)]
                    stages += [_sq[0], _ev[0]]
                    for _k in range(1, 5):
                        stages += [_sq[_k], _rm[_k - 1], _ev[_k], _ra[_k - 1]]
                    stages += [_rm[4], _ra[4]]
                    def _st(heads, lev=lev, Pc=Pc, PTc=PTc, Pn=Pn, PTn=PTn, c=c, r0=r0):
                        for h in heads:
                            kb.op("pe", lambda e, h=h: e.matmul(P1(h), a("R", h), a("Vb", h), start=True, stop=True), r=[K("R", h), K("Vb", h)], w=[K("P1", h)])
                            kb.op("pe", lambda e, h=h: e.matmul(P2(h), a("Kw", h), a("R", h), start=True, stop=True), r=[K("R", h), K("Kw", h)], w=[K("P2", h)])
                            kb.op("act", lambda e, h=h: e.activation(out=a("Um0", h), in_=P1(h), func=AF.Identity, scale=rm[:, 0:1]), r=[K("P1", h), "grm"], w=[K("Um0", h)])
                            kb.op("dve", lambda e, h=h: e.tensor_scalar(out=a("Um1", h), in0=P1(h), scalar1=rm[:, 1:2], scalar2=None, op0=ALU.mult), r=[K("P1", h), "grm"], w=[K("Um1", h)])
                            kb.op("act", lambda e, h=h: e.copy(out=a("WT", h), in_=P2(h)), r=[K("P2", h)], w=[K("WT", h)])
                    stages.append(_st)
                    for c in range(2):
                        r0 = 64 * c
                        def _st(heads, lev=lev, Pc=Pc, PTc=PTc, Pn=Pn, PTn=PTn, c=c, r0=r0):
                            for h in heads:
                                kb.op("pe", lambda e, h=h: e.matmul(P3(h), a("WT", h), Sst[:, h, :], start=True, stop=True), r=[K("WT", h), K("S", h)], w=[K("P3", h)])
                        stages.append(_st)
                        def _st(heads, lev=lev, Pc=Pc, PTc=PTc, Pn=Pn, PTn=PTn, c=c, r0=r0):
                            for h in heads:
                                kb.op("dve", lambda e, h=h, c=c: e.scalar_tensor_tensor(out=a("Vnew", h), in0=P3(h), scalar=rm[:, 2 + c:3 + c], in1=a("Um%d" % c, h),
                                                                                        op0=ALU.mult, op1=ALU.add), r=[K("P3", h), K("Um%d" % c, h), "grm"], w=[K("Vnew", h)])
                        stages.append(_st)
                        def _st(heads, lev=lev, Pc=Pc, PTc=PTc, Pn=Pn, PTn=PTn, c=c, r0=r0):
                            for h in heads:
                                kb.op("pe", lambda e, h=h: e.matmul(P4(h), a("QgT", h), Sst[:, h, :], start=True, stop=False), r=[K("QgT", h), K("S", h)], w=[K("P4", h)])
                                kb.op("pe", lambda e, h=h: e.matmul(P4(h), a("AqkT", h), a("Vnew", h), start=False, stop=True), r=[K("AqkT", h), K("Vnew", h)], w=[K("P4", h)])
                                kb.op("pe", lambda e, h=h: e.matmul(P1(h), a("Kpp", h), a("Vnew", h), start=True, stop=True), r=[K("Kpp", h), K("Vnew", h)], w=[K("P1", h)])
                        stages.append(_st)
                        def _st(heads, lev=lev, Pc=Pc, PTc=PTc, Pn=Pn, PTn=PTn, c=c, r0=r0):
                            for h in heads:
                                kb.op("act", lambda e, h=h, r0=r0: e.copy(out=A["Osb"][r0:r0 + 64, h, :], in_=PS[r0:r0 + 64, (4 * h + 3) * 128:(4 * h + 4) * 128]),
                                      r=[K("P4", h)], w=[K("Osb%d" % c, h)])
                                kb.op("dve", lambda e, h=h, c=c: e.scalar_tensor_tensor(out=Sst[:, h, :], in0=Sst[:, h, :], scalar=egl[:, h, c:c + 1], in1=P1(h),
                                                                                        op0=ALU.mult, op1=ALU.add), r=[K("P1", h), K("egl", h), K("S", h)], w=[K("S", h)])
                        stages.append(_st)
                    def _st(heads, lev=lev, Pc=Pc, PTc=PTc, Pn=Pn, PTn=PTn, c=c, r0=r0):
                        for h in heads:
                            kb.op("act", lambda e, h=h: e.activation(out=a("On", h), in_=a("Osb", h), func=AF.Square, accum_out=ssq[:, h:h + 1]),
                                  r=[K("Osb0", h), K("Osb1", h)], w=[K("On", h), K("ssq", h)])
                            kb.op("act", lambda e, h=h: e.activation(out=ssq[:, h:h + 1], in_=ssq[:, h:h + 1], func=AF.Ln, bias=eps_t[:, 0:1], scale=1.0 / 128),
                                  r=[K("ssq", h), "geps"], w=[K("ssq", h)])
                            kb.op("act", lambda e, h=h: e.activation(out=ssq[:, h:h + 1], in_=ssq[:, h:h + 1], func=AF.Exp, scale=-0.5), r=[K("ssq", h)], w=[K("ssq", h)])
                            kb.op("dve", lambda e, h=h: e.tensor_scalar(out=a("On", h), in0=a("Osb", h), scalar1=ssq[:, h:h + 1], scalar2=None, op0=ALU.mult),
                                  r=[K("ssq", h), K("Osb0", h), K("Osb1", h), K("On", h)], w=[K("On", h)])
                            kb.op("pe", lambda e, h=h: e.transpose(P2(h), a("On", h), self.ident), r=[K("On", h), "cst"], w=[K("P2", h)])
                            kb.op("dve", lambda e, h=h: e.scalar_tensor_tensor(out=obf[:, h, cs], in0=P2(h), scalar=self.gdnn[:, l:l + 1], in1=sz[:, h, cs],
                                                                               op0=ALU.mult, op1=ALU.mult), r=[K("P2", h), "gsz", "gdnn"], w=["gobf%d" % h])
                    stages.append(_st)
                    return stages
                seq = []
                for pr in range(4):
                    seq += mk_stages(pr)
                GA, GB, off = [0, 1, 2], [3, 4, 5], GDN_OFF
                for step in range(len(seq) + off):
                    if step < len(seq):
                        seq[step](GA)
                    if 0 <= step - off < len(seq):
                        seq[step - off](GB)
                    if bg_tick is not None and step % 10 == 5:
                        bg_tick()
                kb.dma(self.mixT.ap()[0:768, t0:t0 + TT].rearrange("(h p) t -> p h t", p=128), obf[:], "gos", r=["gobf%d" % h for h in range(6)])
            if bg_tick is not None:
                while bg_tick():
                    pass
            kb.barrier()


    LF = 8448
    OFF = 4224

    def nsa_tables(self):
        kb, nc = self.kb, self.nc
        LF, OFF = self.LF, self.OFF
        with ExitStack() as es:
            sb = lambda n, s, dt: es.enter_context(self.sbt(n, s, dt))
            Fv = sb("t_fv", [6, LF], F32)
            Fvb = sb("t_fvb", [6, LF], BF16)
            relb = sb("t_rb", [32, 6], F32)
            t31 = sb("t_31", [6, 1], F32)
            ps = es.enter_context(self.pst("t_ps", [128, 512], F32))
            OH = self.c2[0:32, 0:128]
            kb.dma(relb[:], self.rel_bias.ap(), "trb", w=["trb"])
            kb.dma(t31[:], self.rel_bias.ap()[31:32, :].rearrange("a h -> h a"), "t31", w=["t31"], allow_slow_non_contiguous=True)
            kb.dma(self.b31bc[:], self.bcast_row(self.rel_bias, 31 * 6, 6), "tb31", w=["b31"])
            kb.op("pool", lambda e: e.memset(Fv[:, 0:OFF], 0.0), w=["tfv0"])
            kb.op("pe", lambda e: e.matmul(ps[0:6, 0:128], relb[:], OH, start=True, stop=True), r=["trb", "cst2"], w=["@tps"])
            kb.op("act", lambda e: e.activation(out=Fv[:, OFF:OFF + 128], in_=ps[0:6, 0:128], func=AF.Exp), r=["@tps"], w=["tfv1"])
            kb.op("act", lambda e: e.activation(out=Fv[:, OFF + 128:LF], in_=Fv[:, 0:LF - OFF - 128], func=AF.Exp, bias=t31[:, 0:1], scale=0.0),
                  r=["tfv0", "t31"], w=["tfv2"])
            kb.dma(self.fvec.ap()[0:6, :], Fv[:], "tfs", r=["tfv0", "tfv1", "tfv2"])
            kb.op("dve", lambda e: e.tensor_copy(out=Fvb[:], in_=Fv[:]), r=["tfv0", "tfv1", "tfv2"], w=["tfvb"])
            kb.dma(self.fvecb.ap()[0:6, :], Fvb[:], "tfsb", r=["tfvb"])
            kb.op("pool", lambda e: e.memset(Fv[:, OFF + 512:LF], 0.0), r=["tfv2"], w=["tfv2"])
            kb.dma(self.fvec.ap()[6:12, :], Fv[:], "tfs", r=["tfv0", "tfv1", "tfv2"], w=["fvec"])
            kb.op("dve", lambda e: e.tensor_copy(out=Fvb[:], in_=Fv[:]), r=["tfv0", "tfv1", "tfv2"], w=["tfvb"])
            kb.dma(self.fvecb.ap()[6:12, :], Fvb[:], "tfsb", r=["tfvb"])
            kb.barrier()
            for i in range(12):
                kb.dma(self.Btab[i].ap(), bass.AP(self.fvec, i * LF, [[0, 128], [1, LF]]), "tbr")
                kb.dma(self.Btabb[i].ap(), bass.AP(self.fvecb, i * LF, [[0, 128], [1, LF]]), "tbrb")
            kb.barrier()

    def mtile(self, hh, rs, delta, bf=False):
        LF, OFF = self.LF, self.OFF
        return bass.AP((self.Btabb if bf else self.Btab)[hh], delta + OFF, [[LF - rs, 128], [1, TT]])

    def phase_nsa(self, l):
        kb, nc = self.kb, self.nc
        H = 6
        with ExitStack() as es:
            sb = lambda n, s, dt: es.enter_context(self.sbt(n, s, dt))
            pst = lambda n: es.enter_context(self.pst(n, [128, TT], F32))
            ps_s = [pst("n_ps0"), pst("n_ps1")]
            ps_o, ps_d, ps_i, ps_g, ps_x = pst("n_po"), pst("n_pd"), pst("n_pi"), pst("n_pg"), pst("n_px")
            kT = {"s": sb("n_ksT", [128, 2, S], BF16), "w": sb("n_kwT", [128, 2, S], BF16)}
            vtm = {"s": sb("n_vs", [128, 2, 32, 128], BF16), "w": sb("n_vw", [128, 2, 32, 128], BF16)}
            kcT = sb("n_kcT", [128, 2, 256], F32)
            vc = sb("n_vc", [128, 2, 2, 128], F32)
            eps_t = sb("n_eps", [128, 1], F32)
            ones_b = sb("n_1b", [128, 128], BF16)
            eblk = sb("n_eblk", [64, S], BF16)
            gq = sb("n_gq", [128, 1], F32)
            kb.op("pool", lambda e: e.memset(eps_t[:], EPS), w=["neps"])
            kb.op("pool", lambda e: e.memset(ones_b[:], 1.0), w=["n1b"])
            kb.op("pool", lambda e: e.memset(kcT[:], 0.0), w=["nkcT0", "nkcT1"])
            kb.op("pool", lambda e: e.memset(vc[:], 0.0), w=["nvc0", "nvc1"])
            kb.op("dve", lambda e: e.tensor_scalar(out=gq[:], in0=self.nqk[:, l:l + 1], scalar1=128.0 ** -0.5, scalar2=None, op0=ALU.mult), r=["nqk"], w=["ngq"])
            gk = self.nqk[:, DEPTH + l:DEPTH + l + 1]
            with ExitStack() as es2:
                sb2 = lambda n, s, dt: es2.enter_context(self.sbt(n, s, dt))
                stg = sb2("n_stg", [128, 2, S], F32)
                w1 = sb2("n_w1", [128, 32, 128], F32)
                w2 = sb2("n_w2", [128, 128], F32)
                posT = sb2("n_pos", [128, 32], F32)
                c1 = sb2("n_c1", [128, 1], F32)
                g1T = sb2("n_g1T", [128, 256], F32)
                sq = sb2("n_sq", [128, TT], F32)
                rstd = sb2("n_rstd", [128, TT], F32)
                ebf = sb2("n_ebf", [64, S], F32)
                kb.dma(ebf[:], self.eblk_in.ap(), "neb", w=["nebf"])
                kb.op("dve", lambda e: e.tensor_copy(out=eblk[:], in_=ebf[:]), r=["nebf"], w=["neblk"])
                si = 0
                for br, chk, chv in (("c", CH_KCC, CH_VCC), ("s", CH_KSL, CH_VSL), ("w", CH_KWN, CH_VWN)):
                    for g in range(2):
                        for kv, ch in ((0, chk), (1, chv)):
                            s_ = si % 2
                            si += 1
                            sk = "nstg%d" % s_
                            kb.dma(stg[:, s_, :], self.projT.ap()[(ch + g) * 128:(ch + g + 1) * 128, :], sk, w=[sk])
                            if br == "c":
                                kb.dma(w1[:], self.cmp_w1.ap()[l, kv].rearrange("(a d) j -> d a j", d=128), "nw1", w=["nw1"])
                                kb.dma(w2[:], self.cmp_w2.ap()[l, kv], "nw2", w=["nw2"])
                                kb.dma(posT[:], self.cpos_pl.ap()[l, kv], "npos", w=["npos"])
                                for a in range(32):
                                    kb.op("pe", lambda e, a=a: e.matmul(ps_x[:, 0:1], w1[:, a, :], posT[:, a:a + 1], start=(a == 0), stop=(a == 31)),
                                          r=["nw1", "npos"], w=["@npx"])
                                kb.op("dve", lambda e: e.tensor_copy(out=c1[:], in_=ps_x[:, 0:1]), r=["@npx"], w=["nc1"])
                                for a in range(32):
                                    kb.op("pe", lambda e, a=a, s_=s_: e.matmul(ps_o[:, 0:255], w1[:, a, :], stg[:, s_, a:a + 16 * 254 + 1:16], start=(a == 0), stop=(a == 31)),
                                          r=["nw1", sk], w=["@npo"])
                                kb.op("act", lambda e: e.activation(out=g1T[:, 0:255], in_=ps_o[:, 0:255], func=AF.Gelu_apprx_tanh, bias=c1[:, 0:1], scale=1.0),
                                      r=["@npo", "nc1"], w=["ng1T"])
                                if kv == 0:
                                    kb.op("pe", lambda e: e.matmul(ps_d[:, 0:255], w2[:], g1T[:, 0:255], start=True, stop=True), r=["nw2", "ng1T"], w=["@npd"])
                                    kb.op("act", lambda e: e.activation(out=sq[:, 0:255], in_=ps_d[:, 0:255], func=AF.Square), r=["@npd"], w=["nsq"])
                                    kb.op("pe", lambda e: e.matmul(ps_i[:, 0:255], self.ones_f[:], sq[:, 0:255], start=True, stop=True), r=["nsq", "ones"], w=["@npi"])
                                    kb.op("act", lambda e: e.activation(out=rstd[:, 0:255], in_=ps_i[:, 0:255], func=AF.Ln, bias=eps_t[:, 0:1], scale=1.0 / 128),
                                          r=["@npi", "neps"], w=["nrstd"])
                                    kb.op("act", lambda e: e.activation(out=rstd[:, 0:255], in_=rstd[:, 0:255], func=AF.Exp, scale=-0.5), r=["nrstd"], w=["nrstd"])
                                    kb.op("dve", lambda e, g=g: e.scalar_tensor_tensor(out=kcT[:, g, 0:255], in0=ps_d[:, 0:255], scalar=gk, in1=rstd[:, 0:255],
                                                                                      op0=ALU.mult, op1=ALU.mult), r=["@npd", "nrstd", "nqk"], w=["nkcT%d" % g])
                                else:
                                    for c, nn in ((0, 128), (1, 127)):
                                        kb.op("pe", lambda e, c=c, nn=nn: e.matmul(ps_d[0:nn, c * 128:(c + 1) * 128], g1T[:, c * 128:c * 128 + nn], w2[:], start=True, stop=True),
                                              r=["nw2", "ng1T"], w=["@npd"])
                                        kb.op("dve", lambda e, c=c, nn=nn, g=g: e.tensor_copy(out=vc[0:nn, g, c, :], in_=ps_d[0:nn, c * 128:(c + 1) * 128]),
                                              r=["@npd"], w=["nvc%d" % g])
                            elif kv == 0:
                                for t in range(NT):
                                    ts_ = slice(t * TT, (t + 1) * TT)
                                    kb.op("act", lambda e, ts_=ts_, s_=s_: e.activation(out=sq[:], in_=stg[:, s_, ts_], func=AF.Square), r=[sk], w=["nsq"])
                                    kb.op("pe", lambda e: e.matmul(ps_x[:], self.ones_f[:], sq[:], start=True, stop=True), r=["nsq", "ones"], w=["@npx"])
                                    kb.op("act", lambda e: e.activation(out=rstd[:], in_=ps_x[:], func=AF.Ln, bias=eps_t[:, 0:1], scale=1.0 / 128), r=["@npx", "neps"], w=["nrstd"])
                                    kb.op("act", lambda e: e.activation(out=rstd[:], in_=rstd[:], func=AF.Exp, scale=-0.5), r=["nrstd"], w=["nrstd"])
                                    kb.op("dve", lambda e, ts_=ts_, s_=s_, br=br, g=g: e.scalar_tensor_tensor(out=kT[br][:, g, ts_], in0=stg[:, s_, ts_], scalar=gk, in1=rstd[:],
                                                                                                      op0=ALU.mult, op1=ALU.mult), r=[sk, "nrstd", "nqk"], w=["nkT" + br])
                            else:
                                for kt in range(32):
                                    q4 = kt % 4
                                    kb.op("pe", lambda e, kt=kt, q4=q4, s_=s_: e.transpose(ps_i[:, q4 * 128:(q4 + 1) * 128], stg[:, s_, kt * 128:(kt + 1) * 128], self.ident),
                                          r=[sk, "cst"], w=["@npi"])
                                    if q4 == 3:
                                        kb.op("act", lambda e, kt=kt, br=br, g=g: e.copy(out=vtm[br][:, g, kt - 3:kt + 1, :], in_=ps_i[:].rearrange("p (a c) -> p a c", c=128)),
                                              r=["@npi"], w=["nv" + br])
                kb.barrier()
            qraw = sb("n_qraw", [128, 6, TT], F32)
            qn = sb("n_qn", [128, 6, TT], F32)
            qb = sb("n_qb", [128, 6, TT], BF16)
            acc = sb("n_acc", [128, 6, TT], F32)
            obf = sb("n_obf", [128, 6, TT], BF16)
            ec = sb("n_ec", [128, 2, 2, TT], F32)
            ecb = sb("n_ecb", [128, 2, 2, TT], BF16)
            vcb = sb("n_vcb", [128, 2, 2, 128], BF16)
            OvEb = sb("n_oveb", [128, 2, 65], BF16)
            Selb = sb("n_selb", [30, 18, 128], BF16)
            sigb = sb("n_sigb", [30, TT], BF16)
            e32 = sb("n_e32", [128, 2, TT], F32)
            ebt = sb("n_eb", [128, 3, TT], BF16)
            Mt = sb("n_M", [128, 3, TT], F32)
            Mtb = sb("n_Mb", [128, 3, TT], BF16)
            negT = sb("n_negT", [64, 2, TT], BF16)
            selc = sb("n_selc", [128, 3, 4, 64], F32)
            sig = sb("n_sig", [30, TT], F32)
            impg = sb("n_imp", [128, 2, 4, 64], F32)
            imp2 = sb("n_imp2", [128, 4, 64], F32)
            rden = sb("n_rden", [128, 4, 1], F32)
            score = sb("n_score", [128, 64], F32)
            work = sb("n_work", [128, 64], F32)
            mx8 = sb("n_mx8", [128, 16], F32)
            nsel = sb("n_nsel", [128, 64], F32)
            wgt = sb("n_wgt", [128, 2, TT], F32)
            sqd = sb("n_sq2", [128, 2, TT], BF16)
            rstdd = sb("n_rstd2", [128, 2, TT], F32)
            OvE = self.c2[:, 128:258].rearrange("p (c j) -> p c j", c=2)
            Sel = self.c2[0:30, 258:258 + 18 * 128].rearrange("p (k c) -> p k c", k=18)
            cnt = {"e": 0, "m": 0, "s": 0, "mb": 0}
            kb.op("dve", lambda e: e.tensor_copy(out=OvEb[:], in_=OvE), r=["cst2"], w=["noveb"])
            kb.op("dve", lambda e: e.tensor_copy(out=Selb[:], in_=Sel), r=["cst2"], w=["nselb"])
            kb.op("dve", lambda e: e.tensor_copy(out=vcb[:], in_=vc[:]), r=["nvc0", "nvc1"], w=["nvcb"])

            def combine(h, gate_k, first):
                kb.op("pe", lambda e: e.matmul(ps_g[:], Selb[:, gate_k, :], sigb[:], start=True, stop=True), r=["nsigb", "nselb"], w=["@npg"])
                kb.op("dve", lambda e: e.tensor_scalar(out=wgt[:, 0, :], in0=ps_d[:], scalar1=1e-18, scalar2=None, op0=ALU.max), r=["@npd"], w=["nwgt0"])
                kb.op("act", lambda e: e.activation(out=wgt[:, 0, :], in_=wgt[:, 0, :], func=AF.Ln), r=["nwgt0"], w=["nwgt0"])
                kb.op("act", lambda e: e.activation(out=wgt[:, 0, :], in_=wgt[:, 0, :], func=AF.Exp, scale=-1.0), r=["nwgt0"], w=["nwgt0"])
                kb.op("dve", lambda e: e.tensor_tensor(out=wgt[:, 0, :], in0=wgt[:, 0, :], in1=ps_g[:], op=ALU.mult), r=["nwgt0", "@npg"], w=["nwgt0"])
                if first:
                    kb.op("dve", lambda e: e.tensor_tensor(out=acc[:, h, :], in0=ps_o[:], in1=wgt[:, 0, :], op=ALU.mult), r=["@npo", "nwgt0"], w=["nacc%d" % h])
                else:
                    kb.op("dve", lambda e: e.tensor_tensor(out=wgt[:, 1, :], in0=ps_o[:], in1=wgt[:, 0, :], op=ALU.mult), r=["@npo", "nwgt0"], w=["nwgt1"])
                    kb.op("dve", lambda e: e.tensor_tensor(out=acc[:, h, :], in0=acc[:, h, :], in1=wgt[:, 1, :], op=ALU.add), r=["nwgt1", "nacc%d" % h], w=["nacc%d" % h])

            for t in range(NT):
                q0 = t * TT
                kb.dma(qraw[:], self.projT.ap()[CH_QC * 128:(CH_QC + 6) * 128, q0:q0 + TT].rearrange("(h p) t -> p h t", p=128), "nq", w=["nqraw"])
                kb.dma(sig[:], self.projT.ap()[CH_SM * 128:CH_SM * 128 + 30, q0:q0 + TT], "nsg", w=["nsig"])
                kb.op("act", lambda e: e.activation(out=sigb[:], in_=sig[:], func=AF.Sigmoid), r=["nsig"], w=["nsigb"])
                for a_ in range(3):
                    kb.dma(selc[:, a_], self.selc_in.ap()[a_, q0:q0 + TT, :].rearrange("(s p) j -> p s j", p=128), "nselc", w=["nselc"])
                for h in range(H):
                    p2 = h % 2
                    kb.op("act", lambda e, h=h, p2=p2: e.activation(out=sqd[:, p2, :], in_=qraw[:, h, :], func=AF.Square), r=["nqraw"], w=["nsq2_%d" % p2])
                    kb.op("pe", lambda e, p2=p2: e.matmul(ps_x[:], self.ones_b[:], sqd[:, p2, :], start=True, stop=True), r=["nsq2_%d" % p2, "onesb"], w=["@npx"])
                    kb.op("act", lambda e, p2=p2: e.activation(out=rstdd[:, p2, :], in_=ps_x[:], func=AF.Ln, bias=eps_t[:, 0:1], scale=1.0 / 128), r=["@npx", "neps"], w=["nrstd2_%d" % p2])
                    kb.op("act", lambda e, p2=p2: e.activation(out=rstdd[:, p2, :], in_=rstdd[:, p2, :], func=AF.Exp, scale=-0.5), r=["nrstd2_%d" % p2], w=["nrstd2_%d" % p2])
                    kb.op("dve", lambda e, h=h, p2=p2: e.scalar_tensor_tensor(out=qn[:, h, :], in0=qraw[:, h, :], scalar=gq[:, 0:1], in1=rstdd[:, p2, :], op0=ALU.mult, op1=ALU.mult),
                          r=["nqraw", "nrstd2_%d" % p2, "ngq"], w=["nqn%d" % h])
                    kb.op("act", lambda e, h=h: e.copy(out=qb[:, h, :], in_=qn[:, h, :]), r=["nqn%d" % h], w=["nqb%d" % h])
                def cmp_front(h):
                    g = h // 3
                    ep = h % 2
                    for c in range(2):
                        pb = cnt["s"] % 2
                        cnt["s"] += 1
                        ms = cnt["m"] % 3
                        cnt["m"] += 1
                        kb.dma(Mt[:, ms, :], self.mtile(h, 16, q0 - 16 * (c * 128) - 31), "nM%d" % ms, w=["nM%d" % ms])
                        kb.op("pe", lambda e, c=c, pb=pb: e.matmul(ps_s[pb][:], kcT[:, g, c * 128:(c + 1) * 128], qn[:, h, :], start=True, stop=True),
                              r=["nkcT%d" % g, "nqn%d" % h], w=["@nps%d" % pb])
                        kb.op("act", lambda e, c=c, pb=pb: e.activation(out=ec[:, ep, c, :], in_=ps_s[pb][:], func=AF.Exp), r=["@nps%d" % pb], w=["nec%d_%d" % (ep, c)])
                        kb.op("dve", lambda e, c=c, ms=ms: e.tensor_tensor(out=ecb[:, ep, c, :], in0=ec[:, ep, c, :], in1=Mt[:, ms, :], op=ALU.mult),
                              r=["nec%d_%d" % (ep, c), "nM%d" % ms], w=["necb%d_%d" % (ep, c)])

                def cmp_rest(h):
                    g = h // 3
                    r_ = h % 3
                    ep = h % 2
                    EK = ["necb%d_0" % ep, "necb%d_1" % ep]
                    for c in range(2):
                        kb.op("pe", lambda e, c=c: e.matmul(ps_o[:], vcb[:, g, c, :], ecb[:, ep, c, :], start=(c == 0), stop=(c == 1)), r=["nvcb", EK[c]], w=["@npo"])
                    for c in range(2):
                        kb.op("pe", lambda e, c=c: e.matmul(ps_d[:], self.ones_b[:], ecb[:, ep, c, :], start=(c == 0), stop=(c == 1)), r=["onesb", EK[c]], w=["@npd"])
                    for qs in range(4):
                        for c in range(2):
                            kb.op("pe", lambda e, c=c, qs=qs: e.matmul(ps_i[:, qs * 65:(qs + 1) * 65], ecb[:, ep, c, qs * 128:(qs + 1) * 128], OvEb[:, c, :], start=(c == 0), stop=(c == 1)),
                                  r=EK + ["noveb"], w=["@npi"])
                    pi3 = ps_i[:, 0:260].rearrange("p (s j) -> p s j", j=65)
                    kb.op("dve", lambda e: e.tensor_scalar(out=rden[:], in0=pi3[:, :, 64:65], scalar1=1e-30, scalar2=None, op0=ALU.max), r=["@npi"], w=["nrden"])
                    kb.op("dve", lambda e: e.reciprocal(out=rden[:], in_=rden[:]), r=["nrden"], w=["nrden"])
                    if r_ == 0:
                        kb.op("dve", lambda e: e.tensor_tensor(out=impg[:, g], in0=pi3[:, :, 0:64], in1=rden[:].to_broadcast([128, 4, 64]), op=ALU.mult),
                              r=["@npi", "nrden"], w=["nimp%d" % g])
                    else:
                        kb.op("dve", lambda e: e.tensor_tensor(out=imp2[:], in0=pi3[:, :, 0:64], in1=rden[:].to_broadcast([128, 4, 64]), op=ALU.mult),
                              r=["@npi", "nrden"], w=["nimp2"])
                        kb.op("dve", lambda e: e.tensor_tensor(out=impg[:, g], in0=impg[:, g], in1=imp2[:], op=ALU.add), r=["nimp%d" % g, "nimp2"], w=["nimp%d" % g])
                    combine(h, 0 * 6 + h, True)

                cmp_front(0)
                for h in range(H):
                    if h + 1 < H:
                        cmp_front(h + 1)
                    cmp_rest(h)
                for g in range(2):
                    for qs in range(4):
                        kb.op("dve", lambda e, qs=qs: e.tensor_tensor(out=score[:], in0=impg[:, g, qs, :], in1=selc[:, 0, qs, :], op=ALU.mult), r=["nimp%d" % g, "nselc"], w=["nscore"])
                        kb.op("dve", lambda e, qs=qs: e.tensor_tensor(out=score[:], in0=score[:], in1=selc[:, 1, qs, :], op=ALU.add), r=["nscore", "nselc"], w=["nscore"])
                        kb.op("dve", lambda e: e.max(out=mx8[:, 0:8], in_=score[:]), r=["nscore"], w=["nmx8"])
                        kb.op("dve", lambda e: e.match_replace(out=work[:], in_to_replace=mx8[:, 0:8], in_values=score[:], imm_value=-3.0e38), r=["nscore", "nmx8"], w=["nwork"])
                        kb.op("dve", lambda e: e.max(out=mx8[:, 8:16], in_=work[:]), r=["nwork"], w=["nmx8b"])
                        kb.op("dve", lambda e, qs=qs: e.scalar_tensor_tensor(out=nsel[:], in0=score[:], scalar=mx8[:, 15:16], in1=selc[:, 2, qs, :], op0=ALU.is_ge, op1=ALU.mult),
                              r=["nscore", "nmx8b", "nselc"], w=["nnsel"])
                        kb.op("dve", lambda e: e.tensor_scalar(out=nsel[:], in0=nsel[:], scalar1=-1.0, scalar2=30000.0, op0=ALU.add, op1=ALU.mult), r=["nnsel"], w=["nnsel"])
                        kb.op("pe", lambda e: e.transpose(ps_x[0:64, 0:128], nsel[:], self.ident), r=["nnsel", "cst"], w=["@npx"])
                        kb.op("act", lambda e, qs=qs: e.copy(out=negT[:, g, qs * 128:(qs + 1) * 128], in_=ps_x[0:64, 0:128]), r=["@npx"], w=["nnegT%d" % g])
                for g in range(2):
                    tiles = []
                    for r_ in range(3):
                        h = g * 3 + r_
                        for br in ("s", "w"):
                            if br == "s":
                                kts = list(range(0, (q0 + TT) // 128))
                            else:
                                kts = list(range(max(0, (q0 - 512) // 128), (q0 + TT) // 128))
                            for i, kt in enumerate(kts):
                                tiles.append(dict(h=h, br=br, kt=kt, i=i, n=len(kts), delta=q0 - kt * 128))
                    for ti, tl in enumerate(tiles):
                        tl["pb"] = ti % 2
                        tl["es"] = ti % 3
                        tl["e2"] = ti % 2
                        tl["band"] = (tl["br"] == "w") or tl["delta"] < 256

                    def emitS(tl):
                        h, br, kt, pb = tl["h"], tl["br"], tl["kt"], tl["pb"]
                        if tl["band"]:
                            ms = cnt["mb"] % 3
                            cnt["mb"] += 1
                            tl["ms"] = ms
                            kb.dma(Mtb[:, ms, :], self.mtile(h + (6 if br == "w" else 0), 1, tl["delta"], bf=True), "nMb%d" % ms, w=["nMb%d" % ms])
                        kb.op("pe", lambda e: e.matmul(ps_s[pb][:], kT[br][:, g, kt * 128:(kt + 1) * 128], qb[:, h, :], start=True, stop=(br == "w")),
                              r=["nkT" + br, "nqb%d" % h], w=["@nps%d" % pb])
                        if br == "s":
                            kb.op("pe", lambda e: e.matmul(ps_s[pb][:], eblk[:, kt * 128:(kt + 1) * 128], negT[:, g, :], start=False, stop=True),
                                  r=["neblk", "nnegT%d" % g], w=["@nps%d" % pb])

                    def emitE(tl):
                        h, br, kt, pb, es_, e2 = tl["h"], tl["br"], tl["kt"], tl["pb"], tl["es"], tl["e2"]
                        if tl["band"]:
                            ms = tl["ms"]
                            kb.op("act", lambda e: e.activation(out=e32[:, e2, :], in_=ps_s[pb][:], func=AF.Exp), r=["@nps%d" % pb], w=["ne32%d" % e2])
                            kb.op("dve", lambda e: e.tensor_tensor(out=ebt[:, es_, :], in0=e32[:, e2, :], in1=Mtb[:, ms, :], op=ALU.mult),
                                  r=["ne32%d" % e2, "nMb%d" % ms], w=["neb%d" % es_])
                        else:
                            kb.op("act", lambda e: e.activation(out=ebt[:, es_, :], in_=ps_s[pb][:], func=AF.Exp, bias=self.b31bc[:, h:h + 1], scale=1.0),
                                  r=["@nps%d" % pb, "b31"], w=["neb%d" % es_])

                    def emitPV(tl):
                        h, br, kt, es_, i, n = tl["h"], tl["br"], tl["kt"], tl["es"], tl["i"], tl["n"]
                        kb.op("pe", lambda e: e.matmul(ps_o[:], vtm[br][:, g, kt, :], ebt[:, es_, :], start=(i == 0), stop=(i == n - 1)),
                              r=["nv" + br, "neb%d" % es_], w=["@npo"])
                        kb.op("pe", lambda e: e.matmul(ps_d[:], ones_b[:], ebt[:, es_, :], start=(i == 0), stop=(i == n - 1)),
                              r=["n1b", "neb%d" % es_], w=["@npd"])
                        if i == n - 1:
                            combine(h, (1 if br == "s" else 2) * 6 + h, False)

                    emitS(tiles[0])
                    for ti, tl in enumerate(tiles):
                        if ti + 1 < len(tiles):
                            emitS(tiles[ti + 1])
                        emitE(tl)
                        emitPV(tl)
                for h in range(H):
                    kb.op("act", lambda e, h=h: e.copy(out=obf[:, h, :], in_=acc[:, h, :]), r=["nacc%d" % h], w=["nobf"])
                kb.dma(self.mixT.ap()[1280:2048, q0:q0 + TT].rearrange("(h p) t -> p h t", p=128), obf[:], "nos", r=["nobf"])
            kb.barrier()


def make_consts():
    c = np.zeros((128, 1024), np.float32)
    c[:, 0:128] = np.eye(128, dtype=np.float32)
    i = np.arange(128)
    same = (i[:, None] // 64) == (i[None, :] // 64)
    c[:, 128:256] = (same & (i[:, None] <= i[None, :])).astype(np.float32)
    c[:, 256:384] = (same & (i[:, None] >= i[None, :])).astype(np.float32)
    c[:, 384:512] = (same & (i[:, None] > i[None, :])).astype(np.float32)
    c[:, 512:640] = (i[:, None] <= i[None, :]).astype(np.float32)
    return c


def t5_bucket_np(n):
    n = np.maximum(n, 0)
    max_exact = 16
    lr = np.log(np.maximum(n, 1).astype(np.float32) / max_exact) / np.float32(np.log(128 / max_exact))
    large = np.minimum(max_exact + (lr * 16).astype(np.int32), 31)
    return np.where(n < max_exact, n, large)


def make_consts2():
    c = np.zeros((128, 2562), np.float32)
    d = np.arange(128)
    b = t5_bucket_np(d)
    c[b, d] = 1.0
    n = np.arange(256)
    j = np.arange(64)
    ov = ((n[:, None] * 16 < j[None, :] * 64 + 64) & (n[:, None] * 16 + 32 > j[None, :] * 64)).astype(np.float32)
    ov[255] = 0
    ove = np.zeros((256, 65), np.float32)
    ove[:, :64] = ov
    ove[:255, 64] = 1.0
    c[:, 128:258] = ove.reshape(2, 128, 65).transpose(1, 0, 2).reshape(128, 130)
    sel = np.zeros((30, 18, 128), np.float32)
    for k in range(18):
        sel[12 + k, k, :] = 1.0
    c[0:30, 258:258 + 18 * 128] = sel.reshape(30, -1)
    key = np.arange(S)
    eblk = (key[None, :] // 64 == np.arange(64)[:, None]).astype(np.float32)
    q = np.arange(S)
    cur = q // 64
    causal = j[None, :] <= cur[:, None]
    forced = (j[None, :] == 0) | (j[None, :] == cur[:, None]) | (j[None, :] == cur[:, None] - 1)
    a1 = (causal & ~forced).astype(np.float32)
    a2 = np.where(forced, np.float32(1e9), np.where(causal, np.float32(0), np.float32(-1e30))).astype(np.float32)
    selc = np.stack([a1, a2, causal.astype(np.float32)], axis=0)
    return c, eblk, np.ascontiguousarray(selc)


def kernel(**inputs):
    prog = Prog()
    return run_prog(prog, inputs)


def run_prog(prog, inputs, extra=None, cores=8):
    x = np.asarray(inputs["x"], np.float32)
    cst = make_consts()
    pl = lambda a: np.asarray(a, np.float32).reshape(DEPTH, 16, 128).transpose(2, 0, 1).reshape(128, DEPTH * 16)
    gains = np.ascontiguousarray(np.concatenate([pl(inputs["attn_norm"]), pl(inputs["mlp_norm"])], axis=1))
    conv_pl = np.ascontiguousarray(np.asarray(inputs["conv_a"], np.float32).reshape(DEPTH, 4, 18, 128).transpose(0, 3, 2, 1).reshape(DEPTH, 128, 72))
    gdnn = np.ascontiguousarray(np.asarray(inputs["gdn_norm"], np.float32).T)
    cst2, eblk, selc = make_consts2()
    nqk = np.ascontiguousarray(np.concatenate([np.asarray(inputs["nsa_q_norm"], np.float32).T, np.asarray(inputs["nsa_k_norm"], np.float32).T], axis=1))
    cpos_pl = np.ascontiguousarray(np.asarray(inputs["cmp_pos"], np.float32).transpose(0, 1, 3, 2))
    in_maps = []
    for c in range(cores):
        m = {k: np.ascontiguousarray(np.asarray(v, np.float32)) for k, v in inputs.items() if k != "x"}
        m["xT"] = np.ascontiguousarray(x[c].T)
        m["cst"] = cst
        m["gains_in"] = gains
        m["conv_pl"] = conv_pl
        m["cst2"] = cst2
        m["eblk_in"] = eblk
        m["selc_in"] = selc
        m["nqk_in"] = nqk
        m["cpos_pl"] = cpos_pl
        m["gdnn_in"] = gdnn
        if extra:
            m.update(extra)
        in_maps.append(m)
    res = run_bass_kernel_spmd(prog.nc, in_maps, core_ids=list(range(cores)))
    prog.last_results = res.results
    out = np.stack([np.ascontiguousarray(r["yT"].T) for r in res.results], axis=0)
    return out.astype(np.float32)
```

```python
from contextlib import ExitStack
import numpy as np
import concourse.bass as bass
import concourse.mybir as mybir
from concourse.bass_utils import run_bass_kernel_spmd

F32 = mybir.dt.float32
BF16 = mybir.dt.bfloat16
I32 = mybir.dt.int32
AF = mybir.ActivationFunctionType
ALU = mybir.AluOpType
AX = mybir.AxisListType

S = 4096
D = 2048
DEPTH = 4
DFF = 8192
NPROJ = 6430
NJ_IN = 51
TT = 512
NT = S // TT
EPS = 1e-6
GDN_OFF = 1
BG_CAST = True
IN_COLMAP = [(0, 0, 3072), (3084, 3072, 1024), (4108, 4096, 2304), (3072, 6400, 12), (6412, 6412, 18)]
CH_QA, CH_KA, CH_VA, CH_ZA = 0, 6, 12, 18
CH_UB, CH_VB = 24, 28
CH_QC, CH_KCC, CH_VCC, CH_KSL, CH_VSL, CH_KWN, CH_VWN = 32, 38, 40, 42, 44, 46, 48
CH_SM = 50


class KB:
    def __init__(self, nc, es):
        self.nc = nc
        self.es = es
        self.eng = {"pe": nc.tensor, "act": nc.scalar, "dve": nc.vector, "pool": nc.gpsimd, "sp": nc.sync}
        self.sem = {k: es.enter_context(nc.semaphore("s_" + k)) for k in self.eng}
        self.cnt = {k: 0 for k in self.eng}
        self.waited = {k: {} for k in self.eng}
        self.res = {}
        self.dsem = {}
        self.semname = {}

    def _deps(self, e, r, w):
        need = {}

        def add(tok, raw):
            sem, val, te = tok
            if te == e and e == "pe":
                return
            if te == e and not raw:
                return
            k = id(sem)
            if k not in need or need[k][1] < val:
                need[k] = (sem, val)

        r = list(r)
        w = list(w)
        for k in list(r):
            if k.startswith("@"):
                w.append(k)
        for k in r:
            st = self.res.get(k)
            if st and st[0]:
                add(st[0], True)
        for k in w:
            st = self.res.get(k)
            if st:
                if st[0]:
                    add(st[0], False)
                for t in st[1].values():
                    add(t, False)
        wd = self.waited[e]
        for k, (sem, val) in need.items():
            if wd.get(k, 0) < val:
                self.eng[e].wait_ge(sem, val)
                wd[k] = val

    def _upd(self, tok, r, w):
        w = list(w) + [k for k in r if k.startswith("@")]
        r = [k for k in r if not k.startswith("@")]
        for k in r:
            st = self.res.setdefault(k, [None, {}])
            st[1][id(tok[0])] = tok
        for k in w:
            self.res[k] = [tok, {}]

    def op(self, e, fn, r=(), w=()):
        self._deps(e, r, w)
        ins = fn(self.eng[e])
        self.cnt[e] += 1
        ins.then_inc(self.sem[e], 1)
        self._upd((self.sem[e], self.cnt[e], e), r, w)

    def dma(self, out, in_, key, r=(), w=(), q="sp", **kw):
        self._deps(q, r, w)
        if key not in self.dsem:
            self.dsem[key] = [self.es.enter_context(self.nc.semaphore("d%d" % len(self.dsem))), 0]
        ds = self.dsem[key]
        ds[1] += 16
        self.eng[q].dma_start(out=out, in_=in_, **kw).then_inc(ds[0], 16)
        self._upd((ds[0], ds[1], "dma"), r, w)

    def barrier(self):
        for e in self.eng:
            wd = self.waited[e]
            for e2 in self.eng:
                if e2 != e and self.cnt[e2] > wd.get(id(self.sem[e2]), 0):
                    self.eng[e].wait_ge(self.sem[e2], self.cnt[e2])
                    wd[id(self.sem[e2])] = self.cnt[e2]
            for key, (sem, val) in self.dsem.items():
                if val > wd.get(id(sem), 0):
                    self.eng[e].wait_ge(sem, val)
                    wd[id(sem)] = val
        self.res = {}


def dram_ap(handle, offset, pattern):
    return bass.AP(handle, offset, pattern)


class Prog:
    def __init__(self, n_layers=DEPTH, dbg=None, mix_in=False, enable=("gdn", "sgu", "nsa"), phases=("cast", "inproj", "mix", "ffn")):
        self.enable = enable
        self.phases = phases
        self.n_layers = n_layers
        self.dbg = dbg or ()
        self.mix_in = mix_in
        self.nc = bass.Bass("TRN2", target_bir_lowering=False)
        self.build()

    def sbt(self, name, shape, dt):
        self._uid = getattr(self, "_uid", 0) + 1
        return self.nc.sbuf_tensor("%s_%d" % (name, self._uid), shape, dt)

    def pst(self, name, shape, dt):
        self._uid = getattr(self, "_uid", 0) + 1
        return self.nc.psum_tensor("%s_%d" % (name, self._uid), shape, dt)

    def build(self):
        nc = self.nc
        L = DEPTH
        di = lambda n, s: nc.dram_tensor(n, s, F32, kind="ExternalInput")
        self.xT_in = di("xT", [D, S])
        self.attn_norm = di("attn_norm", [L, D])
        self.w_in = di("w_in", [L, D, NPROJ])
        self.conv_a = di("conv_a", [L, 4, 2304])
        self.a_log = di("a_log", [L, 6])
        self.dt_bias = di("dt_bias", [L, 6])
        self.gdn_norm = di("gdn_norm", [L, 128])
        self.sgu_ln_g = di("sgu_ln_g", [L, 512])
        self.sgu_ln_b = di("sgu_ln_b", [L, 512])
        self.sgu_w = di("sgu_w", [L, 4, 128, 128])
        self.sgu_b = di("sgu_b", [L, 4, 128])
        self.nsa_q_norm = di("nsa_q_norm", [L, 128])
        self.nsa_k_norm = di("nsa_k_norm", [L, 128])
        self.cmp_pos = di("cmp_pos", [L, 2, 32, 128])
        self.cmp_w1 = di("cmp_w1", [L, 2, 4096, 128])
        self.cmp_w2 = di("cmp_w2", [L, 2, 128, 128])
        self.rel_bias = di("rel_bias", [32, 6])
        self.w_out = di("w_out", [L, D, D])
        self.mlp_norm = di("mlp_norm", [L, D])
        self.w_up = di("w_up", [L, D, DFF])
        self.w_down = di("w_down", [L, DFF, D])
        self.cst = di("cst", [128, 1024])
        self.gains_in = di("gains_in", [128, 2 * L * 16])
        self.conv_pl = di("conv_pl", [L, 128, 72])
        self.gdnn_in = di("gdnn_in", [128, L])
        self.cst2 = di("cst2", [128, 2562])
        self.nqk_in = di("nqk_in", [128, 2 * L])
        self.cpos_pl = di("cpos_pl", [L, 2, 128, 32])
        self.eblk_in = di("eblk_in", [64, S])
        self.selc_in = di("selc_in", [3, S, 64])
        if self.mix_in:
            self.mix_dbg = di("mix_dbg", [D, S])
        self.yT = nc.dram_tensor("yT", [D, S], F32, kind="ExternalOutput")
        ds = lambda n, s, dt: nc.dram_tensor(n, s, dt, kind="Internal")
        self.xres = ds("xres", [D, S], F32)
        self.projT = ds("projT", [NJ_IN * 128, S], F32)
        self.small_tm = ds("small_tm", [S, 32], F32)
        self.mixT = ds("mixT", [D, S], BF16)
        self.wt_in = [ds("wt_in%d" % l, [NJ_IN, 128, 16, 128], BF16) for l in range(L)]
        self.wt_out = [ds("wt_out%d" % l, [16, 128, 16, 128], BF16) for l in range(L)]
        self.wt_up = [ds("wt_up%d" % l, [64, 128, 16, 128], BF16) for l in range(L)]
        self.wt_dn = [ds("wt_dn%d" % l, [16, 128, 64, 128], BF16) for l in range(L)]
        self.fvec = ds("fvec", [12, self.LF], F32)
        self.Btab = [ds("btab%d" % i, [128, self.LF], F32) for i in range(12)]
        self.fvecb = ds("fvecb", [12, self.LF], BF16)
        self.Btabb = [ds("btabb%d" % i, [128, self.LF], BF16) for i in range(12)]
        self.dbg_out = {}
        if "proj" in self.dbg:
            self.dbg_out["proj"] = nc.dram_tensor("dbg_proj", [NJ_IN * 128, S], F32, kind="ExternalOutput")
            self.dbg_out["small"] = nc.dram_tensor("dbg_small", [S, 32], F32, kind="ExternalOutput")
        if "mix" in self.dbg:
            self.dbg_out["mix"] = nc.dram_tensor("dbg_mix", [D, S], BF16, kind="ExternalOutput")

        with ExitStack() as es:
            self.kb = kb = KB(nc, es)
            sb = lambda n, s, dt: es.enter_context(self.sbt(n, s, dt))
            self.c_f = sb("c_f", [128, 1024], F32)
            self.ones_f = sb("ones_f", [128, 128], F32)
            self.gains = sb("gains", [128, 2 * L * 16], F32)
            kb.dma(self.c_f[:], self.cst.ap(), "cst", w=["cst"])
            kb.op("dve", lambda e: e.memset(self.ones_f[:], 1.0), w=["ones"])
            self.ones_b = sb("ones_b", [128, 128], BF16)
            kb.op("dve", lambda e: e.memset(self.ones_b[:], 1.0), w=["onesb"])
            kb.dma(self.gains[:], self.gains_in.ap(), "g1", w=["gains"])
            self.gdnn = sb("gdnn", [128, L], F32)
            kb.dma(self.gdnn[:], self.gdnn_in.ap(), "g3", w=["gdnn"])
            self.ident = self.c_f[:, 0:128]
            self.c2 = sb("c2", [128, 2562], F32)
            self.nqk = sb("nqk", [128, 2 * L], F32)
            self.b31bc = sb("b31bc", [128, 6], F32)
            kb.dma(self.c2[:], self.cst2.ap(), "cst2", w=["cst2"])
            kb.dma(self.nqk[:], self.nqk_in.ap(), "nqk", w=["nqk"])
            kb.barrier()
            if "nsa" in self.enable and not self.mix_in:
                self.nsa_tables()
            if "cast" in self.phases:
                with self.sbt("cu_f", [128, 2, 4224], F32) as sf0, self.sbt("cu_b", [128, 2, 4224], BF16) as sbf0:
                    specs0 = self.cast_specs(0, ("in",)) if BG_CAST else [s_ for l_ in range(self.n_layers) for s_ in self.cast_specs(l_, ("in", "rest"))]
                    tk = self.cast_runner(specs0, sf0, sbf0, 2, 2, "cu")
                    while tk():
                        pass
                    kb.barrier()
            kb.barrier()
            for l in range(self.n_layers):
                if "inproj" in self.phases:
                    self.phase_inproj(l)
                kb.barrier()
                if "proj" in self.dbg and l == 0:
                    self.copy_dram(self.dbg_out["proj"], self.projT, NJ_IN * 128, S, F32)
                    kb.dma(self.dbg_out["small"].ap(), self.small_tm.ap(), "dbgs")
                    kb.barrier()
                if self.mix_in:
                    self.cast_mix_dbg()
                elif "mix" in self.phases:
                    self.phase_mixers(l)
                kb.barrier()
                if "mix" in self.dbg and l == 0:
                    kb.dma(self.dbg_out["mix"].ap(), self.mixT.ap(), "dbgm")
                    kb.barrier()
                if "ffn" in self.phases:
                    self.phase_ffn(l, last=(l == self.n_layers - 1))
                else:
                    kb.dma(self.yT.ap()[0:128, :], self.xT_in.ap()[0:128, :], "dummyy")
                kb.barrier()

    def copy_dram(self, dst, src, rows, cols, dt):
        kb = self.kb
        for r0 in range(0, rows, 1024):
            n = min(1024, rows - r0)
            kb.dma(dst.ap()[r0:r0 + n, :], src.ap()[r0:r0 + n, :], "cpd")

    def cast_mix_dbg(self):
        kb, nc = self.kb, self.nc
        with self.sbt("mdb_f", [128, S], F32) as tf, self.sbt("mdb_b", [128, S], BF16) as tb:
            for k in range(16):
                kb.dma(tf[:], self.mix_dbg.ap()[k * 128:(k + 1) * 128, :], "mdbl", w=["mdbf"])
                kb.op("dve", lambda e: e.tensor_copy(out=tb[:], in_=tf[:]), r=["mdbf"], w=["mdbb"])
                kb.dma(self.mixT.ap()[k * 128:(k + 1) * 128, :], tb[:], "mdbs", r=["mdbb"])
            kb.barrier()

    def phase_cast(self, l):
        kb, nc = self.kb, self.nc
        with self.sbt("cs_f", [128, 2, 8192], F32) as sf, self.sbt("cs_b", [128, 2, 8192], BF16) as sbf:
            self._cast_i = 0

            def unit(src_ap_list, n_src_cols, colmap, dst_fn, n_dst_cols, pad_from=None):
                i = self._cast_i
                self._cast_i += 1
                s = i % 2
                fk, bk = "csf%d" % s, "csb%d" % s
                for (o, ap, n) in src_ap_list:
                    kb.dma(sf[:, s, o:o + n], ap, "csl%d" % s, w=[fk])
                pieces = []
                for (sc, dc, n) in colmap:
                    step = (n + 3) // 4 if n >= 1536 else n
                    for a in range(0, n, step):
                        pieces.append((sc + a, dc + a, min(step, n - a)))
                engs = ["dve", "act"]
                first = True
                for pi, (sc, dc, n) in enumerate(pieces):
                    e = engs[pi % 2]
                    if e == "act":
                        f = lambda en, sc=sc, dc=dc, n=n: en.copy(out=sbf[:, s, dc:dc + n], in_=sf[:, s, sc:sc + n])
                    else:
                        f = lambda en, sc=sc, dc=dc, n=n: en.tensor_copy(out=sbf[:, s, dc:dc + n], in_=sf[:, s, sc:sc + n])
                    kb.op(e, f, r=[fk], w=[bk + "_%d" % pi])
                if pad_from is not None:
                    kb.op("pool", lambda en: en.memset(sbf[:, s, pad_from:n_dst_cols], 0.0), w=[bk + "_pad"])
                rk = [bk + "_%d" % pi for pi in range(len(pieces))] + ([bk + "_pad"] if pad_from is not None else [])
                for (dst_ap, c0, n) in dst_fn:
                    kb.dma(dst_ap, sbf[:, s, c0:c0 + n].rearrange("p (j c) -> p j c", c=128), "css%d" % s, r=rk, q="pool")

            for kc in range(16):
                src = self.w_in.ap()[l, kc * 128:(kc + 1) * 128, :]
                dst = self.wt_in[l].ap()[:, :, kc, :].rearrange("j p c -> p j c")
                unit([(0, src, NPROJ)], NPROJ, IN_COLMAP, [(dst, 0, NJ_IN * 128)], NJ_IN * 128, pad_from=NPROJ)
            for kc in range(16):
                src = self.w_up.ap()[l, kc * 128:(kc + 1) * 128, :]
                dst = self.wt_up[l].ap()[:, :, kc, :].rearrange("j p c -> p j c")
                unit([(0, src, DFF)], DFF, [(0, 0, DFF)], [(dst, 0, DFF)], DFF)
            for k4 in range(16):
                srcs = [(i * 2048, self.w_down.ap()[l, (k4 * 4 + i) * 128:(k4 * 4 + i + 1) * 128, :], 2048) for i in range(4)]
                dsts = [(self.wt_dn[l].ap()[:, :, k4 * 4 + i, :].rearrange("j p c -> p j c"), i * 2048, 2048) for i in range(4)]
                unit(srcs, 8192, [(0, 0, 8192)], dsts, 8192)
            for k4 in range(4):
                srcs = [(i * 2048, self.w_out.ap()[l, (k4 * 4 + i) * 128:(k4 * 4 + i + 1) * 128, :], 2048) for i in range(4)]
                dsts = [(self.wt_out[l].ap()[:, :, k4 * 4 + i, :].rearrange("j p c -> p j c"), i * 2048, 2048) for i in range(4)]
                unit(srcs, 8192, [(0, 0, 8192)], dsts, 8192)
            kb.barrier()


    def cast_specs(self, l, which):
        specs = []
        if "in" in which:
            for kc in range(16):
                row = self.w_in.ap()[l, kc * 128:(kc + 1) * 128, :]
                dA = self.wt_in[l].ap()[0:32, :, kc, :].rearrange("j p c -> p j c")
                dB = self.wt_in[l].ap()[32:51, :, kc, :].rearrange("j p c -> p j c")
                specs.append(([(0, row[:, 0:4108], 4108)], [(0, 0, 3072), (3084, 3072, 1024)], None, [(dA, 0, 4096)]))
                specs.append(([(0, row[:, 4108:6430], 2322), (2322, row[:, 3072:3084], 12)],
                              [(0, 0, 2304), (2322, 2304, 12), (2304, 2316, 18)], (2334, 2432), [(dB, 0, 2432)]))
        if "rest" in which:
            for kc in range(16):
                for hf in range(2):
                    src = self.w_up.ap()[l, kc * 128:(kc + 1) * 128, hf * 4096:(hf + 1) * 4096]
                    dst = self.wt_up[l].ap()[hf * 32:(hf + 1) * 32, :, kc, :].rearrange("j p c -> p j c")
                    specs.append(([(0, src, 4096)], [(0, 0, 4096)], None, [(dst, 0, 4096)]))
            for k2 in range(32):
                loads = [(i * 2048, self.w_down.ap()[l, (k2 * 2 + i) * 128:(k2 * 2 + i + 1) * 128, :], 2048) for i in range(2)]
                stores = [(self.wt_dn[l].ap()[:, :, k2 * 2 + i, :].rearrange("j p c -> p j c"), i * 2048, 2048) for i in range(2)]
                specs.append((loads, [(0, 0, 4096)], None, stores))
            for k2 in range(8):
                loads = [(i * 2048, self.w_out.ap()[l, (k2 * 2 + i) * 128:(k2 * 2 + i + 1) * 128, :], 2048) for i in range(2)]
                stores = [(self.wt_out[l].ap()[:, :, k2 * 2 + i, :].rearrange("j p c -> p j c"), i * 2048, 2048) for i in range(2)]
                specs.append((loads, [(0, 0, 4096)], None, stores))
        return specs

    def cast_runner(self, specs, sf, sbf, nf, nb, tag):
        kb = self.kb
        st = {"i": 0, "loaded": 0}

        def load(u):
            s = u % nf
            for (o, ap, n) in specs[u][0]:
                kb.dma(sf[:, s, o:o + n], ap, "%sl%d" % (tag, s), w=["%sf%d" % (tag, s)])

        def tick():
            i = st["i"]
            if i >= len(specs):
                return False
            while st["loaded"] <= min(i, len(specs) - 1) or (i == 0 and st["loaded"] <= min(nf - 1, len(specs) - 1)):
                load(st["loaded"])
                st["loaded"] += 1
            s, sb_ = i % nf, i % nb
            fk, bk = "%sf%d" % (tag, s), "%sb%d" % (tag, sb_)
            loads, convs, pad, stores = specs[i]
            keys = []
            pi = 0
            for (sc, dc, n) in convs:
                step = (n + 1) // 2 if n >= 1024 else n
                for a in range(0, n, step):
                    m = min(step, n - a)
                    e = "dve" if pi % 2 == 0 else "act"
                    k_ = "%s_%d" % (bk, pi)
                    if e == "act":
                        kb.op("act", lambda en, sc=sc, dc=dc, a=a, m=m: en.copy(out=sbf[:, sb_, dc + a:dc + a + m], in_=sf[:, s, sc + a:sc + a + m]), r=[fk], w=[k_])
                    else:
                        kb.op("dve", lambda en, sc=sc, dc=dc, a=a, m=m: en.tensor_copy(out=sbf[:, sb_, dc + a:dc + a + m], in_=sf[:, s, sc + a:sc + a + m]), r=[fk], w=[k_])
                    keys.append(k_)
                    pi += 1
            if pad is not None:
                kb.op("pool", lambda en: en.memset(sbf[:, sb_, pad[0]:pad[1]], 0.0), w=[bk + "_pad"])
                keys.append(bk + "_pad")
            allk = ["%s_%d" % (bk, q) for q in range(8)] + [bk + "_pad"]
            for (dst_ap, c0, n) in stores:
                kb.dma(dst_ap, sbf[:, sb_, c0:c0 + n].rearrange("p (j c) -> p j c", c=128), "%ss%d" % (tag, sb_), r=keys, w=[], q="pool")
            for k_ in allk:
                if k_ not in keys:
                    stt = kb.res.setdefault(k_, [None, {}])
                    for kk in keys[:1]:
                        stt[1].update(kb.res[kk][1])
            st["i"] += 1
            if st["loaded"] < len(specs) and st["loaded"] <= i + nf:
                load(st["loaded"])
                st["loaded"] += 1
            return True

        return tick

    def rmsnorm_tile(self, xt, ht, gain_col0, sq, rstd, ps_ss, tag, xkeys):
        kb = self.kb
        htag = tag
        tag = tag[0]
        for kc in range(16):
            s = kc % 2
            kb.op("act", lambda e, kc=kc, s=s: e.activation(out=sq[:, s, :], in_=xt[:, kc, :], func=AF.Square),
                  r=[xkeys[kc]], w=[tag + "sq%d" % s])
            kb.op("pe", lambda e, kc=kc, s=s: e.matmul(ps_ss[:], self.ones_b[:], sq[:, s, :], start=(kc == 0), stop=(kc == 15)),
                  r=[tag + "sq%d" % s, "onesb"], w=[tag + "ss"])
        kb.op("act", lambda e: e.activation(out=rstd[:], in_=ps_ss[:], func=AF.Ln, bias=self.eps_t[:, 0:1], scale=1.0 / D),
              r=[tag + "ss", "eps"], w=[tag + "rstd"])
        kb.op("act", lambda e: e.activation(out=rstd[:], in_=rstd[:], func=AF.Exp, scale=-0.5), r=[tag + "rstd"], w=[tag + "rstd"])
        for kc in range(16):
            kb.op("dve", lambda e, kc=kc: e.scalar_tensor_tensor(out=ht[:, kc, :], in0=xt[:, kc, :],
                                                              scalar=self.gains[:, gain_col0 + kc:gain_col0 + kc + 1],
                                                              in1=rstd[:], op0=ALU.mult, op1=ALU.mult),
                  r=[xkeys[kc], tag + "rstd", "gains"], w=[htag + "h%d" % kc])

    def phase_inproj(self, l):
        kb, nc = self.kb, self.nc
        xsrc = self.xT_in if l == 0 else self.xres
        with ExitStack() as es:
            sb = lambda n, s, dt: es.enter_context(self.sbt(n, s, dt))
            xt = sb("a_x", [128, 16, TT], F32)
            ht2 = sb("a_h", [128, 2, 16, TT], BF16)
            sq = sb("a_sq", [128, 2, TT], BF16)
            rstd = sb("a_rstd", [128, TT], F32)
            self.eps_t = sb("a_eps", [128, 1], F32)
            NW = 4
            wt = sb("a_w", [128, NW, 16, 128], BF16)
            NO = 4
            ot = sb("a_o", [128, NO, TT], F32)
            osm = sb("a_osm", [128, 4, 32], F32)
            ps_ss = es.enter_context(self.pst("a_pss", [128, TT], F32))
            ps_o = [es.enter_context(self.pst("a_po%d" % i, [128, TT], F32)) for i in range(4)]
            ps_sm = es.enter_context(self.pst("a_psm", [128, 4, 32], F32))
            kb.op("pool", lambda e: e.memset(self.eps_t[:], EPS), w=["eps"])
            ui = 0

            def prep_tile(t):
                t0 = t * TT
                kb.dma(xt[:], xsrc.ap()[:, t0:t0 + TT].rearrange("(k p) t -> p k t", p=128), "ax", w=["ax"])
                if l == 0:
                    kb.dma(self.xres.ap()[:, t0:t0 + TT].rearrange("(k p) t -> p k t", p=128), xt[:], "axs", r=["ax"], q="pool")
                self.rmsnorm_tile(xt, ht2[:, t % 2], l * 16, sq, rstd, ps_ss, "a%d" % (t % 2), ["ax"] * 16)

            prep_tile(0)
            for t in range(NT):
                t0 = t * TT
                ht = ht2[:, t % 2]
                HK = ["a%dh%d" % (t % 2, k) for k in range(16)]
                for j in range(NJ_IN):
                    if j == 6 and t + 1 < NT:
                        prep_tile(t + 1)
                    ws = ui % NW
                    pb = ui % 4
                    os_ = ui % NO
                    ui += 1
                    kb.dma(wt[:, ws], self.wt_in[l].ap()[j], "aw%d" % ws, w=["aw%d" % ws])
                    M = 128 if j < CH_SM else 30
                    for kc in range(16):
                        kb.op("pe", lambda e, kc=kc, ws=ws, pb=pb, M=M: e.matmul(ps_o[pb][0:M, :], wt[:, ws, kc, 0:M], ht[:, kc, :],
                                                                                 start=(kc == 0), stop=(kc == 15)),
                              r=["aw%d" % ws, HK[kc]], w=["apo%d" % pb])
                    ev = "act" if ui % 2 == 0 else "dve"
                    if ev == "act":
                        kb.op("act", lambda e, pb=pb, os_=os_, M=M: e.copy(out=ot[0:M, os_, :], in_=ps_o[pb][0:M, :]),
                              r=["apo%d" % pb], w=["ao%d" % os_])
                    else:
                        kb.op("dve", lambda e, pb=pb, os_=os_, M=M: e.tensor_copy(out=ot[0:M, os_, :], in_=ps_o[pb][0:M, :]),
                              r=["apo%d" % pb], w=["ao%d" % os_])
                    kb.dma(self.projT.ap()[j * 128:j * 128 + M, t0:t0 + TT], ot[0:M, os_, :], "aos%d" % os_, r=["ao%d" % os_], q="pool")
                    if j == CH_SM:
                        for q4 in range(4):
                            for kc in range(16):
                                kb.op("pe", lambda e, kc=kc, ws=ws, q4=q4: e.matmul(ps_sm[:, q4, 0:30], ht[:, kc, q4 * 128:(q4 + 1) * 128],
                                                                                   wt[:, ws, kc, 0:30], start=(kc == 0), stop=(kc == 15)),
                                      r=["aw%d" % ws, HK[kc]], w=["apsm"])
                        kb.op("dve", lambda e: e.tensor_copy(out=osm[:, :, 0:30], in_=ps_sm[:, :, 0:30]), r=["apsm"], w=["aosm"])
                        kb.dma(self.small_tm.ap()[t0:t0 + TT, 0:30].rearrange("(q p) c -> p q c", p=128), osm[:, :, 0:30], "aosm", r=["aosm"], q="pool")
            kb.barrier()

    def phase_ffn(self, l, last):
        kb, nc = self.kb, self.nc
        with ExitStack() as es:
            sb = lambda n, s, dt: es.enter_context(self.sbt(n, s, dt))
            xt = sb("f_x", [128, 16, TT], F32)
            ht = sb("f_h", [128, 16, TT], BF16)
            at = sb("f_a", [128, 64, TT], BF16)
            sq = sb("f_sq", [128, 2, TT], BF16)
            rstd = sb("f_rstd", [128, TT], F32)
            rl = sb("f_rl", [128, 2, TT], F32)
            self.eps_t = sb("f_eps", [128, 1], F32)
            NW = 3
            wt = sb("f_w", [128, NW, 16, 128], BF16)
            wd = sb("f_wd", [128, 2, 64, 128], BF16)
            ps_ss = es.enter_context(self.pst("f_pss", [128, TT], F32))
            ps_o = [es.enter_context(self.pst("f_po%d" % i, [128, TT], F32)) for i in range(4)]
            kb.op("pool", lambda e: e.memset(self.eps_t[:], EPS), w=["eps"])
            HK = ["fh%d" % k for k in range(16)]
            XK = ["fx%d" % k for k in range(16)]
            ui = 0
            for t in range(NT):
                t0 = t * TT
                kb.dma(ht[:], self.mixT.ap()[:, t0:t0 + TT].rearrange("(k p) t -> p k t", p=128), "fm", w=HK)
                for j in range(16):
                    ws, pb = ui % NW, ui % 4
                    ui += 1
                    kb.dma(wt[:, ws], self.wt_out[l].ap()[j], "fw%d" % ws, w=["fw%d" % ws])
                    kb.dma(xt[:, j, :], self.xres.ap()[j * 128:(j + 1) * 128, t0:t0 + TT], "fxl%d" % (j % 8), w=[XK[j]])
                    for kc in range(16):
                        kb.op("pe", lambda e, kc=kc, ws=ws, pb=pb: e.matmul(ps_o[pb][:], wt[:, ws, kc, :], ht[:, kc, :],
                                                                           start=(kc == 0), stop=(kc == 15)),
                              r=["fw%d" % ws, HK[kc]], w=["fpo%d" % pb])
                    kb.op("dve", lambda e, j=j, pb=pb: e.tensor_tensor(out=xt[:, j, :], in0=xt[:, j, :], in1=ps_o[pb][:], op=ALU.add),
                          r=["fpo%d" % pb, XK[j]], w=[XK[j]])
                self.rmsnorm_tile(xt, ht, DEPTH * 16 + l * 16, sq, rstd, ps_ss, "f", XK)
                for j in range(64):
                    ws, pb = ui % NW, ui % 4
                    ui += 1
                    kb.dma(wt[:, ws], self.wt_up[l].ap()[j], "fw%d" % ws, w=["fw%d" % ws])
                    for kc in range(16):
                        kb.op("pe", lambda e, kc=kc, ws=ws, pb=pb: e.matmul(ps_o[pb][:], wt[:, ws, kc, :], ht[:, kc, :],
                                                                           start=(kc == 0), stop=(kc == 15)),
                              r=["fw%d" % ws, HK[kc]], w=["fpo%d" % pb])
                    s2 = j % 2
                    kb.op("act", lambda e, pb=pb, s2=s2: e.activation(out=rl[:, s2, :], in_=ps_o[pb][:], func=AF.Relu),
                          r=["fpo%d" % pb], w=["frl%d" % s2])
                    kb.op("dve", lambda e, j=j, s2=s2: e.tensor_tensor(out=at[:, j, :], in0=rl[:, s2, :], in1=rl[:, s2, :], op=ALU.mult),
                          r=["frl%d" % s2], w=["fa%d" % j])
                for n in range(16):
                    s2, pb = n % 2, ui % 4
                    ui += 1
                    kb.dma(wd[:, s2], self.wt_dn[l].ap()[n], "fwd%d" % s2, w=["fwd%d" % s2])
                    for f in range(64):
                        kb.op("pe", lambda e, f=f, s2=s2, pb=pb: e.matmul(ps_o[pb][:], wd[:, s2, f, :], at[:, f, :],
                                                                          start=(f == 0), stop=(f == 63)),
                              r=["fwd%d" % s2, "fa%d" % f], w=["fpo%d" % pb])
                    kb.op("dve", lambda e, n=n, pb=pb: e.tensor_tensor(out=xt[:, n, :], in0=xt[:, n, :], in1=ps_o[pb][:], op=ALU.add),
                          r=["fpo%d" % pb, XK[n]], w=[XK[n]])
                    dst = self.yT if last else self.xres
                    kb.dma(dst.ap()[n * 128:(n + 1) * 128, t0:t0 + TT], xt[:, n, :], "fxs%d" % (n % 4), r=[XK[n]], q="pool")
            kb.barrier()

    def phase_mixers(self, l):
        if "gdn" in self.enable:
            self.phase_gdn(l)
            self.kb.barrier()
        if "sgu" in self.enable:
            self.phase_sgu(l)
            self.kb.barrier()
        if "nsa" in self.enable:
            self.phase_nsa(l)
            self.kb.barrier()

    def gelu(self, e_name, out, in_, r, w):
        self.kb.op("act", lambda e: e.activation(out=out, in_=in_, func=AF.Gelu_apprx_tanh), r=r, w=w)

    def bcast_row(self, handle, offset, n):
        return bass.AP(handle, offset, [[0, 128], [1, n]])

    def phase_sgu(self, l):
        kb, nc = self.kb, self.nc
        with ExitStack() as es:
            sb = lambda n, s, dt: es.enter_context(self.sbt(n, s, dt))
            wnat = sb("s_wn", [128, 4, 128], F32)
            wTm = sb("s_wT", [128, 4, 128], F32)
            brow = sb("s_br", [1, 512], F32)
            lng = sb("s_lng", [128, 512], F32)
            lnb = sb("s_lnb", [128, 512], F32)
            ut = sb("s_u", [128, 2, 4, TT], F32)
            vt = sb("s_v", [128, 2, 4, TT], F32)
            vn = sb("s_vn", [128, 2, 512], F32)
            st6 = sb("s_st", [128, 6], F32)
            mv = sb("s_mv", [128, 2], F32)
            rs = sb("s_rs", [128, 1], F32)
            eps_t = sb("s_eps", [128, 1], F32)
            obf = sb("s_o", [128, 2, 4, TT], BF16)
            ps_tr = es.enter_context(self.pst("s_ptr", [128, 512], F32))
            ps_m = es.enter_context(self.pst("s_pm", [128, 4, 128], F32))
            maskT = self.c_f[:, 512:640]
            kb.op("pool", lambda e: e.memset(eps_t[:], EPS), w=["seps"])
            kb.dma(wnat[:], self.sgu_w.ap()[l].rearrange("g t s -> t g s"), "swn", w=["swn"])
            kb.dma(brow[:], self.sgu_b.ap()[l:l + 1].rearrange("a g t -> a (g t)"), "sbr", w=["sbr"])
            kb.dma(lng[:], self.bcast_row(self.sgu_ln_g, l * 512, 512), "slg", w=["slg"])
            kb.dma(lnb[:], self.bcast_row(self.sgu_ln_b, l * 512, 512), "slb", w=["slb"])
            for g in range(4):
                kb.op("pe", lambda e, g=g: e.transpose(ps_m[:, g, :], wnat[:, g, :], self.ident), r=["swn", "cst"], w=["spm"])
            kb.op("dve", lambda e: e.tensor_tensor(out=wTm[:], in0=ps_m[:], in1=maskT.unsqueeze(1).to_broadcast([128, 4, 128]), op=ALU.mult),
                  r=["spm", "cst"], w=["swT"])
            for t in range(NT):
                t0 = t * TT
                s = t % 2
                kb.dma(ut[:, s], self.projT.ap()[CH_UB * 128:(CH_UB + 4) * 128, t0:t0 + TT].rearrange("(g p) t -> p g t", p=128), "su%d" % s, w=["su%d" % s])
                kb.dma(vt[:, s], self.projT.ap()[CH_VB * 128:(CH_VB + 4) * 128, t0:t0 + TT].rearrange("(g p) t -> p g t", p=128), "sv%d" % s, w=["sv%d" % s])
                for g in range(4):
                    self.gelu("act", ut[:, s, g, :], ut[:, s, g, :], ["su%d" % s], ["su%d" % s])
                    self.gelu("act", vt[:, s, g, :], vt[:, s, g, :], ["sv%d" % s], ["sv%d" % s])
                for c4 in range(4):
                    cs = slice(c4 * 128, (c4 + 1) * 128)
                    v2 = c4 % 2
                    for g in range(4):
                        kb.op("pe", lambda e, g=g, cs=cs: e.transpose(ps_tr[:, g * 128:(g + 1) * 128], vt[:, s, g, cs], self.ident),
                              r=["sv%d" % s, "cst"], w=["sptr"])
                    kb.op("dve", lambda e: e.bn_stats(out=st6[:], in_=ps_tr[:]), r=["sptr"], w=["sst"])
                    kb.op("dve", lambda e: e.bn_aggr(out=mv[:], in_=st6[:]), r=["sst"], w=["smv"])
                    kb.op("act", lambda e: e.activation(out=rs[:], in_=mv[:, 1:2], func=AF.Ln, bias=eps_t[:, 0:1], scale=1.0),
                          r=["smv", "seps"], w=["srs"])
                    kb.op("act", lambda e: e.activation(out=rs[:], in_=rs[:], func=AF.Exp, scale=-0.5), r=["srs"], w=["srs"])
                    kb.op("dve", lambda e, v2=v2: e.tensor_scalar(out=vn[:, v2, :], in0=ps_tr[:], scalar1=mv[:, 0:1], scalar2=rs[:, 0:1],
                                                                   op0=ALU.subtract, op1=ALU.mult), r=["sptr", "smv", "srs"], w=["svn%d" % v2])
                    kb.op("dve", lambda e, v2=v2: e.tensor_tensor(out=vn[:, v2, :], in0=vn[:, v2, :], in1=lng[:], op=ALU.mult),
                          r=["svn%d" % v2, "slg"], w=["svn%d" % v2])
                    kb.op("dve", lambda e, v2=v2: e.tensor_tensor(out=vn[:, v2, :], in0=vn[:, v2, :], in1=lnb[:], op=ALU.add),
                          r=["svn%d" % v2, "slb"], w=["svn%d" % v2])
                    for g in range(4):
                        kb.op("pe", lambda e, g=g, v2=v2: e.matmul(ps_m[:, g, :], vn[:, v2, g * 128:(g + 1) * 128], wTm[:, g, :], start=True, stop=False),
                              r=["svn%d" % v2, "swT"], w=["spm"])
                        kb.op("pe", lambda e, g=g: e.matmul(ps_m[:, g, :], self.ones_f[0:1, :], brow[0:1, g * 128:(g + 1) * 128], start=False, stop=True),
                              r=["sbr", "ones"], w=["spm"])
                    kb.op("dve", lambda e, cs=cs: e.tensor_tensor(out=obf[:, s, :, cs], in0=ut[:, s, :, cs], in1=ps_m[:], op=ALU.mult),
                          r=["spm", "su%d" % s], w=["so%d" % s])
                kb.dma(self.mixT.ap()[768:1280, t0:t0 + TT].rearrange("(g p) t -> p g t", p=128), obf[:, s], "sos%d" % s, r=["so%d" % s])
            kb.barrier()

    def phase_gdn(self, l):
        kb, nc = self.kb, self.nc
        H = 6
        with ExitStack() as es:
            sb = lambda n, s, dt: es.enter_context(self.sbt(n, s, dt))
            PS = es.enter_context(self.pst("g_ps", [128, 4096], F32))
            slot = lambda i: PS[:, i * 128:(i + 1) * 128]
            P1 = lambda h: slot(4 * h)
            P2 = lambda h: slot(4 * h + 1)
            P3 = lambda h: slot(4 * h + 2)
            P4 = lambda h: slot(4 * h + 3)
            ps_ss = PS[:, 24 * 128:28 * 128]
            ps_sm = PS[:, 28 * 128:28 * 128 + 6]
            convw = sb("g_cw", [128, 72], F32)
            dtb = sb("g_dtb", [128, 6], F32)
            nega = sb("g_na", [128, 6], F32)
            eps_t = sb("g_eps", [128, 1], F32)
            sm = sb("g_sm", [128, 4, 30], F32)
            beta = sb("g_beta", [128, 4, 6], F32)
            nbeta = sb("g_nbeta", [128, 4, 6], F32)
            gtm = sb("g_g", [128, 4, 6], F32)
            xin = sb("g_xin", [128, 2, 3, 515], F32)
            QKV = sb("g_qkv", [128, 6, 3, 512], F32)
            sq = sb("g_sq", [128, 512], BF16)
            rn = sb("g_rn", [128, 512], F32)
            sz = sb("g_sz", [128, 6, 512], F32)
            obf = sb("g_obf", [128, 6, 512], BF16)
            Sst = sb("g_S", [128, 6, 128], F32)
            names = ["Ktm", "Vb", "rb", "ebc", "E", "Es", "Ei", "NT", "Aqk", "N", "AqkT", "R", "Pa", "Pb", "PTa", "PTb",
                     "Um0", "Um1", "Kw", "WT", "QgT", "Kpp", "Vnew", "Osb", "On"]
            A = {n: sb("g_" + n, [128, 6, 128], F32) for n in names}
            A["tE"] = A["E"]
            bg_tick = None
            if BG_CAST and "cast" in self.phases:
                bsf = sb("g_csf", [128, 2, 4224], F32)
                bsb = sb("g_csb", [128, 1, 4224], BF16)
                bspecs = self.cast_specs(l, ("rest",)) + (self.cast_specs(l + 1, ("in",)) if l + 1 < self.n_layers else [])
                bg_tick = self.cast_runner(bspecs, bsf, bsb, 2, 1, "cg")
            gc = sb("g_gc", [128, 6], F32)
            s1 = sb("g_s1", [128, 6], F32)
            dl = sb("g_dl", [128, 6], F32)
            egl = sb("g_egl", [128, 6, 2], F32)
            ssq = sb("g_ssq", [128, 6], F32)
            rm = sb("g_rm", [128, 4], F32)
            triBD = self.c_f[:, 128:256]
            m_incl = self.c_f[:, 256:384]
            m_strict = self.c_f[:, 384:512]
            GK = lambda n, h: "%s%d" % (n, 3 * (h // 3))
            K = lambda n, h: ("@gb%d" % h) if n in ("P1", "P2", "P3", "P4") else "g%s%d" % (n, h)
            kb.op("pool", lambda e: e.memset(eps_t[:], EPS), w=["geps"])
            kb.op("pool", lambda e: e.memset(rm[:], 0.0), w=["grm"])
            kb.op("pool", lambda e: e.memset(rm[0:64, 0:1], 1.0), w=["grm"])
            kb.op("pool", lambda e: e.memset(rm[64:128, 1:2], 1.0), w=["grm"])
            kb.op("pool", lambda e: e.memset(rm[0:64, 2:3], -1.0), w=["grm"])
            kb.op("pool", lambda e: e.memset(rm[64:128, 3:4], -1.0), w=["grm"])
            kb.op("pool", lambda e: e.memset(Sst[:], 0.0), w=[K("S", h) for h in range(H)])
            kb.dma(convw[:], self.conv_pl.ap()[l], "gcw", w=["gcw"])
            kb.dma(dtb[:], self.bcast_row(self.dt_bias, l * 6, 6), "gdtb", w=["gdtb"])
            kb.dma(nega[:], self.bcast_row(self.a_log, l * 6, 6), "gna", w=["gna"])
            kb.op("act", lambda e: e.activation(out=nega[:], in_=nega[:], func=AF.Exp), r=["gna"], w=["gna"])
            kb.op("dve", lambda e: e.tensor_scalar(out=nega[:], in0=nega[:], scalar1=-1.0, scalar2=None, op0=ALU.mult), r=["gna"], w=["gna"])
            for b in range(NT):
                t0 = b * TT
                kb.dma(sm[:], self.small_tm.ap()[t0:t0 + TT, 0:30].rearrange("(q p) c -> p q c", p=128), "gsm", w=["gsm"])
                kb.dma(sz[:], self.projT.ap()[CH_ZA * 128:(CH_ZA + 6) * 128, t0:t0 + TT].rearrange("(h p) t -> p h t", p=128), "gsz", w=["gsz"])
                kb.op("act", lambda e: e.activation(out=sz[:], in_=sz[:], func=AF.Silu), r=["gsz"], w=["gsz"])
                kb.op("act", lambda e: e.activation(out=beta[:], in_=sm[:, :, 0:6], func=AF.Sigmoid), r=["gsm"], w=["gbeta"])
                kb.op("dve", lambda e: e.tensor_scalar(out=nbeta[:], in0=beta[:], scalar1=-1.0, scalar2=None, op0=ALU.mult), r=["gbeta"], w=["gnbeta"])
                kb.op("dve", lambda e: e.tensor_tensor(out=gtm[:], in0=sm[:, :, 6:12], in1=dtb[:].unsqueeze(1).to_broadcast([128, 4, 6]), op=ALU.add),
                      r=["gsm", "gdtb"], w=["gg"])
                kb.op("act", lambda e: e.activation(out=gtm[:], in_=gtm[:], func=AF.Exp), r=["gg"], w=["gg"])
                kb.op("act", lambda e: e.activation(out=gtm[:], in_=gtm[:], func=AF.Ln, bias=1.0, scale=1.0), r=["gg"], w=["gg"])
                kb.op("dve", lambda e: e.tensor_tensor(out=gtm[:], in0=gtm[:], in1=nega[:].unsqueeze(1).to_broadcast([128, 4, 6]), op=ALU.mult),
                      r=["gg", "gna"], w=["gg"])
                for h in range(H):
                    xs = h % 2
                    for qi, ch in enumerate((CH_QA, CH_KA, CH_VA)):
                        row = (ch + h) * 128
                        if b == 0:
                            kb.op("pool", lambda e, qi=qi: e.memset(xin[:, xs, qi, 0:3], 0.0), w=["gx%d" % xs])
                            kb.dma(xin[:, xs, qi, 3:515], self.projT.ap()[row:row + 128, 0:TT], "gx%d" % xs, w=["gx%d" % xs])
                        else:
                            kb.dma(xin[:, xs, qi, :], self.projT.ap()[row:row + 128, t0 - 3:t0 + TT], "gx%d" % xs, w=["gx%d" % xs])
                    for qi, ch in enumerate((CH_QA, CH_KA, CH_VA)):
                        cw = lambda j: convw[:, (ch + h) * 4 + j:(ch + h) * 4 + j + 1]
                        dst = QKV[:, h, qi, :]
                        kb.op("dve", lambda e, qi=qi, cw=cw, dst=dst: e.tensor_scalar(out=dst, in0=xin[:, xs, qi, 0:512], scalar1=cw(0), scalar2=None, op0=ALU.mult),
                              r=["gx%d" % xs, "gcw"], w=[K("qkv%d" % qi, h)])
                        for j in (1, 2, 3):
                            kb.op("dve", lambda e, qi=qi, cw=cw, dst=dst, j=j: e.scalar_tensor_tensor(out=dst, in0=xin[:, xs, qi, j:j + 512], scalar=cw(j), in1=dst,
                                                                                              op0=ALU.mult, op1=ALU.add),
                                  r=["gx%d" % xs, "gcw", K("qkv%d" % qi, h)], w=[K("qkv%d" % qi, h)])
                        kb.op("act", lambda e, dst=dst: e.activation(out=dst, in_=dst, func=AF.Silu), r=[K("qkv%d" % qi, h)], w=[K("qkv%d" % qi, h)])
                        if qi < 2:
                            kb.op("act", lambda e, dst=dst: e.activation(out=sq[:], in_=dst, func=AF.Square), r=[K("qkv%d" % qi, h)], w=["gsq"])
                            kb.op("pe", lambda e: e.matmul(ps_ss, self.ones_b[:], sq[:], start=True, stop=True), r=["gsq", "onesb"], w=["@gpss"])
                            kb.op("act", lambda e: e.activation(out=rn[:], in_=ps_ss, func=AF.Ln, bias=eps_t[:, 0:1], scale=1.0), r=["@gpss", "geps"], w=["grn"])
                            kb.op("act", lambda e: e.activation(out=rn[:], in_=rn[:], func=AF.Exp, scale=-0.5), r=["grn"], w=["grn"])
                            sc = (128.0 ** -0.5) if qi == 0 else 1.0
                            kb.op("dve", lambda e, dst=dst, sc=sc: e.scalar_tensor_tensor(out=dst, in0=dst, scalar=sc, in1=rn[:], op0=ALU.mult, op1=ALU.mult),
                                  r=["grn", K("qkv%d" % qi, h)], w=[K("qkv%d" % qi, h)])
                def mk_stages(pr):
                    stages = []
                    lev = 0; Pc = PTc = Pn = PTn = None; c = 0; r0 = 0
                    cs = slice(pr * 128, (pr + 1) * 128)
                    Qt = lambda h: QKV[:, h, 0, cs]
                    Kt = lambda h: QKV[:, h, 1, cs]
                    Vt = lambda h: QKV[:, h, 2, cs]
                    a = lambda n, h: A[n][:, h, :]
                    def _pre(heads):
                        h0, n = heads[0], len(heads)
                        psm = PS[:, 28 * 128 + h0:28 * 128 + h0 + n]
                        kb.op("pe", lambda e: e.matmul(psm, triBD, gtm[:, pr, h0:h0 + n], start=True, stop=True), r=["gg", "cst"], w=["@gpsm"])
                        kb.op("dve", lambda e: e.tensor_copy(out=gc[:, h0:h0 + n], in_=psm), r=["@gpsm"], w=["ggc%d" % h0])
                        kb.op("act", lambda e: e.activation(out=s1[:, h0:h0 + n], in_=gc[:, h0:h0 + n], func=AF.Exp), r=["ggc%d" % h0], w=["gs1%d" % h0])
                        kb.op("dve", lambda e: e.tensor_tensor(out=s1[:, h0:h0 + n], in0=s1[:, h0:h0 + n], in1=beta[:, pr, h0:h0 + n], op=ALU.mult), r=["gs1%d" % h0, "gbeta"], w=["gs1%d" % h0])
                    stages.append(_pre)
                    def _st(heads, lev=lev, Pc=Pc, PTc=PTc, Pn=Pn, PTn=PTn, c=c, r0=r0):
                        for h in heads:
                            kb.op("pe", lambda e, h=h: e.transpose(P1(h), Kt(h), self.ident), r=[K("qkv1", h), "cst"], w=[K("P1", h)])
                            kb.op("pe", lambda e, h=h: e.transpose(P2(h), Vt(h), self.ident), r=[K("qkv2", h), "cst"], w=[K("P2", h)])
                            kb.op("dve", lambda e, h=h: e.tensor_scalar(out=a("rb", h), in0=triBD, scalar1=gtm[:, pr, h:h + 1], scalar2=None, op0=ALU.mult),
                                  r=["gg", "cst"], w=[K("rb", h)])
                            kb.op("pe", lambda e, h=h: e.matmul(P3(h), self.ones_f[:], a("rb", h), start=True, stop=True), r=[K("rb", h), "ones"], w=[K("P3", h)])
                            kb.op("act", lambda e, h=h: e.copy(out=a("Ktm", h), in_=P1(h)), r=[K("P1", h)], w=[K("Ktm", h)])
                            kb.op("dve", lambda e, h=h: e.tensor_scalar(out=a("Vb", h), in0=P2(h), scalar1=beta[:, pr, h:h + 1], scalar2=None, op0=ALU.mult),
                                  r=[K("P2", h), "gbeta"], w=[K("Vb", h)])
                    stages.append(_st)
                    def _st(heads, lev=lev, Pc=Pc, PTc=PTc, Pn=Pn, PTn=PTn, c=c, r0=r0):
                        for h in heads:
                            kb.op("dve", lambda e, h=h: e.tensor_scalar(out=a("tE", h), in0=P3(h), scalar1=gc[:, h:h + 1], scalar2=0.0, op0=ALU.subtract, op1=ALU.max),
                                  r=[K("P3", h), GK("ggc", h)], w=[K("E", h)])
                            kb.op("act", lambda e, h=h: e.activation(out=a("E", h), in_=a("tE", h), func=AF.Exp, scale=-1.0), r=[K("E", h)], w=[K("E", h)])
                            kb.op("act", lambda e, h=h: e.activation(out=a("ebc", h), in_=P3(h), func=AF.Exp), r=[K("P3", h)], w=[K("ebc", h)])
                            kb.op("act", lambda e, h=h: e.activation(out=egl[:, h, :], in_=PS[:, (4 * h + 2) * 128 + 63:(4 * h + 2) * 128 + 128:64], func=AF.Exp),
                                  r=[K("P3", h)], w=[K("egl", h)])
                            kb.op("dve", lambda e, h=h: e.tensor_tensor(out=dl[0:64, h:h + 1], in0=PS[0:64, (4 * h + 2) * 128 + 63:(4 * h + 2) * 128 + 64], in1=gc[0:64, h:h + 1], op=ALU.subtract),
                                  r=[K("P3", h), GK("ggc", h)], w=[K("dl", h)])
                            kb.op("dve", lambda e, h=h: e.tensor_tensor(out=dl[64:128, h:h + 1], in0=PS[64:128, (4 * h + 2) * 128 + 127:(4 * h + 2) * 128 + 128], in1=gc[64:128, h:h + 1], op=ALU.subtract),
                                  r=[K("P3", h), GK("ggc", h)], w=[K("dl", h)])
                            kb.op("act", lambda e, h=h: e.activation(out=dl[:, h:h + 1], in_=dl[:, h:h + 1], func=AF.Exp), r=[K("dl", h)], w=[K("dl", h)])
                            kb.op("dve", lambda e, h=h: e.tensor_tensor(out=a("Es", h), in0=a("E", h), in1=m_strict, op=ALU.mult), r=[K("E", h), "cst"], w=[K("Es", h)])
                            kb.op("dve", lambda e, h=h: e.tensor_tensor(out=a("Ei", h), in0=a("E", h), in1=m_incl, op=ALU.mult), r=[K("E", h), "cst"], w=[K("Ei", h)])
                            kb.op("pe", lambda e, h=h: e.matmul(P1(h), Kt(h), Kt(h), start=True, stop=True), r=[K("qkv1", h)], w=[K("P1", h)])
                            kb.op("pe", lambda e, h=h: e.matmul(P2(h), Qt(h), Kt(h), start=True, stop=True), r=[K("qkv0", h), K("qkv1", h)], w=[K("P2", h)])
                            kb.op("dve", lambda e, h=h: e.scalar_tensor_tensor(out=a("NT", h), in0=P1(h), scalar=nbeta[:, pr, h:h + 1], in1=a("Es", h), op0=ALU.mult, op1=ALU.mult),
                                  r=[K("P1", h), "gnbeta", K("Es", h)], w=[K("NT", h)])
                            kb.op("dve", lambda e, h=h: e.tensor_tensor(out=a("Aqk", h), in0=P2(h), in1=a("Ei", h), op=ALU.mult), r=[K("P2", h), K("Ei", h)], w=[K("Aqk", h)])
                            kb.op("act", lambda e, h=h: e.activation(out=a("Kw", h), in_=a("Ktm", h), func=AF.Identity, scale=s1[:, h:h + 1]),
                                  r=[K("Ktm", h), GK("gs1", h)], w=[K("Kw", h)])
                            kb.op("act", lambda e, h=h: e.activation(out=a("Kpp", h), in_=a("Ktm", h), func=AF.Identity, scale=dl[:, h:h + 1]),
                                  r=[K("Ktm", h), K("dl", h)], w=[K("Kpp", h)])
                            kb.op("dve", lambda e, h=h: e.tensor_tensor(out=a("QgT", h), in0=Qt(h), in1=a("ebc", h), op=ALU.mult), r=[K("qkv0", h), K("ebc", h)], w=[K("QgT", h)])
                    stages.append(_st)
                    def _st(heads, lev=lev, Pc=Pc, PTc=PTc, Pn=Pn, PTn=PTn, c=c, r0=r0):
                        for h in heads:
                            kb.op("pe", lambda e, h=h: e.transpose(P3(h), a("NT", h), self.ident), r=[K("NT", h), "cst"], w=[K("P3", h)])
                            kb.op("pe", lambda e, h=h: e.transpose(P4(h), a("Aqk", h), self.ident), r=[K("Aqk", h), "cst"], w=[K("P4", h)])
                            kb.op("act", lambda e, h=h: e.copy(out=a("N", h), in_=P3(h)), r=[K("P3", h)], w=[K("N", h)])
                            kb.op("dve", lambda e, h=h: e.tensor_tensor(out=a("R", h), in0=P3(h), in1=self.ident, op=ALU.add), r=[K("P3", h), "cst"], w=[K("R", h)])
                            kb.op("act", lambda e, h=h: e.copy(out=a("AqkT", h), in_=P4(h)), r=[K("P4", h)], w=[K("AqkT", h)])
                    stages.append(_st)
                    _n0 = len(stages)
                    Pc, PTc = "N", "NT"
                    for lev in range(1, 6):
                        Pn, PTn = ("Pa", "PTa") if lev % 2 == 1 else ("Pb", "PTb")
                        def _st(heads, lev=lev, Pc=Pc, PTc=PTc, Pn=Pn, PTn=PTn, c=c, r0=r0):
                            for h in heads:
                                if lev < 5:
                                    kb.op("pe", lambda e, h=h, Pc=Pc, PTc=PTc: e.matmul(P1(h), a(PTc, h), a(Pc, h), start=True, stop=True),
                                          r=[K(Pc, h), K(PTc, h)], w=[K("P1", h)])
                                kb.op("pe", lambda e, h=h, Pc=Pc, PTc=PTc: e.matmul(P2(h), a(Pc, h), a(PTc, h), start=True, stop=True),
                                      r=[K(Pc, h), K(PTc, h)], w=[K("P2", h)])
                        stages.append(_st)
                        def _st(heads, lev=lev, Pc=Pc, PTc=PTc, Pn=Pn, PTn=PTn, c=c, r0=r0):
                            for h in heads:
                                if lev < 5:
                                    kb.op("act", lambda e, h=h, Pn=Pn: e.copy(out=a(Pn, h), in_=P1(h)), r=[K("P1", h)], w=[K(Pn, h)])
                                kb.op("dve", lambda e, h=h, PTn=PTn: e.tensor_copy(out=a(PTn, h), in_=P2(h)), r=[K("P2", h)], w=[K(PTn, h)])
                        stages.append(_st)
                        def _st(heads, lev=lev, Pc=Pc, PTc=PTc, Pn=Pn, PTn=PTn, c=c, r0=r0):
                            for h in heads:
                                kb.op("pe", lambda e, h=h, PTn=PTn: e.matmul(P3(h), a(PTn, h), a("R", h), start=True, stop=True),
                                      r=[K(PTn, h), K("R", h)], w=[K("P3", h)])
                        stages.append(_st)
                        def _st(heads, lev=lev, Pc=Pc, PTc=PTc, Pn=Pn, PTn=PTn, c=c, r0=r0):
                            for h in heads:
                                kb.op("dve", lambda e, h=h: e.tensor_tensor(out=a("R", h), in0=a("R", h), in1=P3(h), op=ALU.add), r=[K("P3", h), K("R", h)], w=[K("R", h)])
                        stages.append(_st)
                        Pc, PTc = Pn, PTn
                    _lv = stages[_n0:]
                    del stages[_n0:]
                    _sq = [_lv[4 * k] for k in range(5)]; _ev = [_lv[4 * k + 1] for k in range(5)]
                    _rm = [_lv[4 * k + 2] for k in range(5)]; _ra = [_lv[4 * k + 3] for k in range(5)]
                    stages += [_sq[0], _ev[0]]
                    for _k in range(1, 5):
                        stages += [_sq[_k], _rm[_k - 1], _ev[_k], _ra[_k - 1]]
                    stages += [_rm[4], _ra[4]]
                    def _st(heads, lev=lev, Pc=Pc, PTc=PTc, Pn=Pn, PTn=PTn, c=c, r0=r0):
                        for h in heads:
                            kb.op("pe", lambda e, h=h: e.matmul(P1(h), a("R", h), a("Vb", h), start=True, stop=True), r=[K("R", h), K("Vb", h)], w=[K("P1", h)])
                            kb.op("pe", lambda e, h=h: e.matmul(P2(h), a("Kw", h), a("R", h), start=True, stop=True), r=[K("R", h), K("Kw", h)], w=[K("P2", h)])
                            kb.op("act", lambda e, h=h: e.activation(out=a("Um0", h), in_=P1(h), func=AF.Identity, scale=rm[:, 0:1]), r=[K("P1", h), "grm"], w=[K("Um0", h)])
                            kb.op("dve", lambda e, h=h: e.tensor_scalar(out=a("Um1", h), in0=P1(h), scalar1=rm[:, 1:2], scalar2=None, op0=ALU.mult), r=[K("P1", h), "grm"], w=[K("Um1", h)])
                            kb.op("act", lambda e, h=h: e.copy(out=a("WT", h), in_=P2(h)), r=[K("P2", h)], w=[K("WT", h)])
                    stages.append(_st)
                    for c in range(2):
                        r0 = 64 * c
                        def _st(heads, lev=lev, Pc=Pc, PTc=PTc, Pn=Pn, PTn=PTn, c=c, r0=r0):
                            for h in heads:
                                kb.op("pe", lambda e, h=h: e.matmul(P3(h), a("WT", h), Sst[:, h, :], start=True, stop=True), r=[K("WT", h), K("S", h)], w=[K("P3", h)])
                        stages.append(_st)
                        def _st(heads, lev=lev, Pc=Pc, PTc=PTc, Pn=Pn, PTn=PTn, c=c, r0=r0):
                            for h in heads:
                                kb.op("dve", lambda e, h=h, c=c: e.scalar_tensor_tensor(out=a("Vnew", h), in0=P3(h), scalar=rm[:, 2 + c:3 + c], in1=a("Um%d" % c, h),
                                                                                        op0=ALU.mult, op1=ALU.add), r=[K("P3", h), K("Um%d" % c, h), "grm"], w=[K("Vnew", h)])
                        stages.append(_st)
                        def _st(heads, lev=lev, Pc=Pc, PTc=PTc, Pn=Pn, PTn=PTn, c=c, r0=r0):
                            for h in heads:
                                kb.op("pe", lambda e, h=h: e.matmul(P4(h), a("QgT", h), Sst[:, h, :], start=True, stop=False), r=[K("QgT", h), K("S", h)], w=[K("P4", h)])
                                kb.op("pe", lambda e, h=h: e.matmul(P4(h), a("AqkT", h), a("Vnew", h), start=False, stop=True), r=[K("AqkT", h), K("Vnew", h)], w=[K("P4", h)])
                                kb.op("pe", lambda e, h=h: e.matmul(P1(h), a("Kpp", h), a("Vnew", h), start=True, stop=True), r=[K("Kpp", h), K("Vnew", h)], w=[K("P1", h)])
                        stages.append(_st)
                        def _st(heads, lev=lev, Pc=Pc, PTc=PTc, Pn=Pn, PTn=PTn, c=c, r0=r0):
                            for h in heads:
                                kb.op("act", lambda e, h=h, r0=r0: e.copy(out=A["Osb"][r0:r0 + 64, h, :], in_=PS[r0:r0 + 64, (4 * h + 3) * 128:(4 * h + 4) * 128]),
                                      r=[K("P4", h)], w=[K("Osb%d" % c, h)])
                                kb.op("dve", lambda e, h=h, c=c: e.scalar_tensor_tensor(out=Sst[:, h, :], in0=Sst[:, h, :], scalar=egl[:, h, c:c + 1], in1=P1(h),
                                                                                        op0=ALU.mult, op1=ALU.add), r=[K("P1", h), K("egl", h), K("S", h)], w=[K("S", h)])
                        stages.append(_st)
                    def _st(heads, lev=lev, Pc=Pc, PTc=PTc, Pn=Pn, PTn=PTn, c=c, r0=r0):
                        for h in heads:
                            kb.op("act", lambda e, h=h: e.activation(out=a("On", h), in_=a("Osb", h), func=AF.Square, accum_out=ssq[:, h:h + 1]),
                                  r=[K("Osb0", h), K("Osb1", h)], w=[K("On", h), K("ssq", h)])
                            kb.op("act", lambda e, h=h: e.activation(out=ssq[:, h:h + 1], in_=ssq[:, h:h + 1], func=AF.Ln, bias=eps_t[:, 0:1], scale=1.0 / 128),
                                  r=[K("ssq", h), "geps"], w=[K("ssq", h)])
                            kb.op("act", lambda e, h=h: e.activation(out=ssq[:, h:h + 1], in_=ssq[:, h:h + 1], func=AF.Exp, scale=-0.5), r=[K("ssq", h)], w=[K("ssq", h)])
                            kb.op("dve", lambda e, h=h: e.tensor_scalar(out=a("On", h), in0=a("Osb", h), scalar1=ssq[:, h:h + 1], scalar2=None, op0=ALU.mult),
                                  r=[K("ssq", h), K("Osb0", h), K("Osb1", h), K("On", h)], w=[K("On", h)])
                            kb.op("pe", lambda e, h=h: e.transpose(P2(h), a("On", h), self.ident), r=[K("On", h), "cst"], w=[K("P2", h)])
                            kb.op("dve", lambda e, h=h: e.scalar_tensor_tensor(out=obf[:, h, cs], in0=P2(h), scalar=self.gdnn[:, l:l + 1], in1=sz[:, h, cs],
                                                                               op0=ALU.mult, op1=ALU.mult), r=[K("P2", h), "gsz", "gdnn"], w=["gobf%d" % h])
                    stages.append(_st)
                    return stages
                seq = []
                for pr in range(4):
                    seq += mk_stages(pr)
                GA, GB, off = [0, 1, 2], [3, 4, 5], GDN_OFF
                for step in range(len(seq) + off):
                    if step < len(seq):
                        seq[step](GA)
                    if 0 <= step - off < len(seq):
                        seq[step - off](GB)
                    if bg_tick is not None and step % 10 == 5:
                        bg_tick()
                kb.dma(self.mixT.ap()[0:768, t0:t0 + TT].rearrange("(h p) t -> p h t", p=128), obf[:], "gos", r=["gobf%d" % h for h in range(6)])
            if bg_tick is not None:
                while bg_tick():
                    pass
            kb.barrier()


    LF = 8448
    OFF = 4224

    def nsa_tables(self):
        kb, nc = self.kb, self.nc
        LF, OFF = self.LF, self.OFF
        with ExitStack() as es:
            sb = lambda n, s, dt: es.enter_context(self.sbt(n, s, dt))
            Fv = sb("t_fv", [6, LF], F32)
            Fvb = sb("t_fvb", [6, LF], BF16)
            relb = sb("t_rb", [32, 6], F32)
            t31 = sb("t_31", [6, 1], F32)
            ps = es.enter_context(self.pst("t_ps", [128, 512], F32))
            OH = self.c2[0:32, 0:128]
            kb.dma(relb[:], self.rel_bias.ap(), "trb", w=["trb"])
            kb.dma(t31[:], self.rel_bias.ap()[31:32, :].rearrange("a h -> h a"), "t31", w=["t31"], allow_slow_non_contiguous=True)
            kb.dma(self.b31bc[:], self.bcast_row(self.rel_bias, 31 * 6, 6), "tb31", w=["b31"])
            kb.op("pool", lambda e: e.memset(Fv[:, 0:OFF], 0.0), w=["tfv0"])
            kb.op("pe", lambda e: e.matmul(ps[0:6, 0:128], relb[:], OH, start=True, stop=True), r=["trb", "cst2"], w=["@tps"])
            kb.op("act", lambda e: e.activation(out=Fv[:, OFF:OFF + 128], in_=ps[0:6, 0:128], func=AF.Exp), r=["@tps"], w=["tfv1"])
            kb.op("act", lambda e: e.activation(out=Fv[:, OFF + 128:LF], in_=Fv[:, 0:LF - OFF - 128], func=AF.Exp, bias=t31[:, 0:1], scale=0.0),
                  r=["tfv0", "t31"], w=["tfv2"])
            kb.dma(self.fvec.ap()[0:6, :], Fv[:], "tfs", r=["tfv0", "tfv1", "tfv2"])
            kb.op("dve", lambda e: e.tensor_copy(out=Fvb[:], in_=Fv[:]), r=["tfv0", "tfv1", "tfv2"], w=["tfvb"])
            kb.dma(self.fvecb.ap()[0:6, :], Fvb[:], "tfsb", r=["tfvb"])
            kb.op("pool", lambda e: e.memset(Fv[:, OFF + 512:LF], 0.0), r=["tfv2"], w=["tfv2"])
            kb.dma(self.fvec.ap()[6:12, :], Fv[:], "tfs", r=["tfv0", "tfv1", "tfv2"], w=["fvec"])
            kb.op("dve", lambda e: e.tensor_copy(out=Fvb[:], in_=Fv[:]), r=["tfv0", "tfv1", "tfv2"], w=["tfvb"])
            kb.dma(self.fvecb.ap()[6:12, :], Fvb[:], "tfsb", r=["tfvb"])
            kb.barrier()
            for i in range(12):
                kb.dma(self.Btab[i].ap(), bass.AP(self.fvec, i * LF, [[0, 128], [1, LF]]), "tbr")
                kb.dma(self.Btabb[i].ap(), bass.AP(self.fvecb, i * LF, [[0, 128], [1, LF]]), "tbrb")
            kb.barrier()

    def mtile(self, hh, rs, delta, bf=False):
        LF, OFF = self.LF, self.OFF
        return bass.AP((self.Btabb if bf else self.Btab)[hh], delta + OFF, [[LF - rs, 128], [1, TT]])

    def phase_nsa(self, l):
        kb, nc = self.kb, self.nc
        H = 6
        with ExitStack() as es:
            sb = lambda n, s, dt: es.enter_context(self.sbt(n, s, dt))
            pst = lambda n: es.enter_context(self.pst(n, [128, TT], F32))
            ps_s = [pst("n_ps0"), pst("n_ps1")]
            ps_o, ps_d, ps_i, ps_g = pst("n_po"), pst("n_pd"), pst("n_pi"), pst("n_pg")
            ps_x = ps_i
            ps_o2, ps_d2 = pst("n_po2"), pst("n_pd2")
            OD = [(ps_o, ps_d, "@npo", "@npd"), (ps_o2, ps_d2, "@npo2", "@npd2")]
            kT = {"s": sb("n_ksT", [128, 2, S], BF16), "w": sb("n_kwT", [128, 2, S], BF16)}
            vtm = {"s": sb("n_vs", [128, 2, 32, 128], BF16), "w": sb("n_vw", [128, 2, 32, 128], BF16)}
            kcT = sb("n_kcT", [128, 2, 256], F32)
            vc = sb("n_vc", [128, 2, 2, 128], F32)
            eps_t = sb("n_eps", [128, 1], F32)
            ones_b = sb("n_1b", [128, 128], BF16)
            eblk = sb("n_eblk", [64, S], BF16)
            gq = sb("n_gq", [128, 1], F32)
            kb.op("pool", lambda e: e.memset(eps_t[:], EPS), w=["neps"])
            kb.op("pool", lambda e: e.memset(ones_b[:], 1.0), w=["n1b"])
            kb.op("pool", lambda e: e.memset(kcT[:], 0.0), w=["nkcT0", "nkcT1"])
            kb.op("pool", lambda e: e.memset(vc[:], 0.0), w=["nvc0", "nvc1"])
            kb.op("dve", lambda e: e.tensor_scalar(out=gq[:], in0=self.nqk[:, l:l + 1], scalar1=128.0 ** -0.5, scalar2=None, op0=ALU.mult), r=["nqk"], w=["ngq"])
            gk = self.nqk[:, DEPTH + l:DEPTH + l + 1]
            with ExitStack() as es2:
                sb2 = lambda n, s, dt: es2.enter_context(self.sbt(n, s, dt))
                stg = sb2("n_stg", [128, 2, S], F32)
                w1 = sb2("n_w1", [128, 32, 128], F32)
                w2 = sb2("n_w2", [128, 128], F32)
                posT = sb2("n_pos", [128, 32], F32)
                c1 = sb2("n_c1", [128, 1], F32)
                g1T = sb2("n_g1T", [128, 256], F32)
                sq = sb2("n_sq", [128, TT], F32)
                rstd = sb2("n_rstd", [128, TT], F32)
                ebf = sb2("n_ebf", [64, S], F32)
                kb.dma(ebf[:], self.eblk_in.ap(), "neb", w=["nebf"])
                kb.op("dve", lambda e: e.tensor_copy(out=eblk[:], in_=ebf[:]), r=["nebf"], w=["neblk"])
                si = 0
                for br, chk, chv in (("c", CH_KCC, CH_VCC), ("s", CH_KSL, CH_VSL), ("w", CH_KWN, CH_VWN)):
                    for g in range(2):
                        for kv, ch in ((0, chk), (1, chv)):
                            s_ = si % 2
                            si += 1
                            sk = "nstg%d" % s_
                            kb.dma(stg[:, s_, :], self.projT.ap()[(ch + g) * 128:(ch + g + 1) * 128, :], sk, w=[sk])
                            if br == "c":
                                kb.dma(w1[:], self.cmp_w1.ap()[l, kv].rearrange("(a d) j -> d a j", d=128), "nw1", w=["nw1"])
                                kb.dma(w2[:], self.cmp_w2.ap()[l, kv], "nw2", w=["nw2"])
                                kb.dma(posT[:], self.cpos_pl.ap()[l, kv], "npos", w=["npos"])
                                for a in range(32):
                                    kb.op("pe", lambda e, a=a: e.matmul(ps_x[:, 0:1], w1[:, a, :], posT[:, a:a + 1], start=(a == 0), stop=(a == 31)),
                                          r=["nw1", "npos"], w=["@npi"])
                                kb.op("dve", lambda e: e.tensor_copy(out=c1[:], in_=ps_x[:, 0:1]), r=["@npi"], w=["nc1"])
                                for a in range(32):
                                    kb.op("pe", lambda e, a=a, s_=s_: e.matmul(ps_o[:, 0:255], w1[:, a, :], stg[:, s_, a:a + 16 * 254 + 1:16], start=(a == 0), stop=(a == 31)),
                                          r=["nw1", sk], w=["@npo"])
                                kb.op("act", lambda e: e.activation(out=g1T[:, 0:255], in_=ps_o[:, 0:255], func=AF.Gelu_apprx_tanh, bias=c1[:, 0:1], scale=1.0),
                                      r=["@npo", "nc1"], w=["ng1T"])
                                if kv == 0:
                                    kb.op("pe", lambda e: e.matmul(ps_d[:, 0:255], w2[:], g1T[:, 0:255], start=True, stop=True), r=["nw2", "ng1T"], w=["@npd"])
                                    kb.op("act", lambda e: e.activation(out=sq[:, 0:255], in_=ps_d[:, 0:255], func=AF.Square), r=["@npd"], w=["nsq"])
                                    kb.op("pe", lambda e: e.matmul(ps_i[:, 0:255], self.ones_f[:], sq[:, 0:255], start=True, stop=True), r=["nsq", "ones"], w=["@npi"])
                                    kb.op("act", lambda e: e.activation(out=rstd[:, 0:255], in_=ps_i[:, 0:255], func=AF.Ln, bias=eps_t[:, 0:1], scale=1.0 / 128),
                                          r=["@npi", "neps"], w=["nrstd"])
                                    kb.op("act", lambda e: e.activation(out=rstd[:, 0:255], in_=rstd[:, 0:255], func=AF.Exp, scale=-0.5), r=["nrstd"], w=["nrstd"])
                                    kb.op("dve", lambda e, g=g: e.scalar_tensor_tensor(out=kcT[:, g, 0:255], in0=ps_d[:, 0:255], scalar=gk, in1=rstd[:, 0:255],
                                                                                      op0=ALU.mult, op1=ALU.mult), r=["@npd", "nrstd", "nqk"], w=["nkcT%d" % g])
                                else:
                                    for c, nn in ((0, 128), (1, 127)):
                                        kb.op("pe", lambda e, c=c, nn=nn: e.matmul(ps_d[0:nn, c * 128:(c + 1) * 128], g1T[:, c * 128:c * 128 + nn], w2[:], start=True, stop=True),
                                              r=["nw2", "ng1T"], w=["@npd"])
                                        kb.op("dve", lambda e, c=c, nn=nn, g=g: e.tensor_copy(out=vc[0:nn, g, c, :], in_=ps_d[0:nn, c * 128:(c + 1) * 128]),
                                              r=["@npd"], w=["nvc%d" % g])
                            elif kv == 0:
                                for t in range(NT):
                                    ts_ = slice(t * TT, (t + 1) * TT)
                                    kb.op("act", lambda e, ts_=ts_, s_=s_: e.activation(out=sq[:], in_=stg[:, s_, ts_], func=AF.Square), r=[sk], w=["nsq"])
                                    kb.op("pe", lambda e: e.matmul(ps_x[:], self.ones_f[:], sq[:], start=True, stop=True), r=["nsq", "ones"], w=["@npi"])
                                    kb.op("act", lambda e: e.activation(out=rstd[:], in_=ps_x[:], func=AF.Ln, bias=eps_t[:, 0:1], scale=1.0 / 128), r=["@npi", "neps"], w=["nrstd"])
                                    kb.op("act", lambda e: e.activation(out=rstd[:], in_=rstd[:], func=AF.Exp, scale=-0.5), r=["nrstd"], w=["nrstd"])
                                    kb.op("dve", lambda e, ts_=ts_, s_=s_, br=br, g=g: e.scalar_tensor_tensor(out=kT[br][:, g, ts_], in0=stg[:, s_, ts_], scalar=gk, in1=rstd[:],
                                                                                                      op0=ALU.mult, op1=ALU.mult), r=[sk, "nrstd", "nqk"], w=["nkT" + br])
                            else:
                                for kt in range(32):
                                    q4 = kt % 4
                                    kb.op("pe", lambda e, kt=kt, q4=q4, s_=s_: e.transpose(ps_i[:, q4 * 128:(q4 + 1) * 128], stg[:, s_, kt * 128:(kt + 1) * 128], self.ident),
                                          r=[sk, "cst"], w=["@npi"])
                                    if q4 == 3:
                                        kb.op("act", lambda e, kt=kt, br=br, g=g: e.copy(out=vtm[br][:, g, kt - 3:kt + 1, :], in_=ps_i[:].rearrange("p (a c) -> p a c", c=128)),
                                              r=["@npi"], w=["nv" + br])
                kb.barrier()
            qraw = sb("n_qraw", [128, 6, TT], F32)
            qn = sb("n_qn", [128, 6, TT], F32)
            qb = sb("n_qb", [128, 6, TT], BF16)
            acc = sb("n_acc", [128, 6, TT], F32)
            obf = sb("n_obf", [128, 6, TT], BF16)
            ec = sb("n_ec", [128, 2, 2, TT], F32)
            e32 = sb("n_e32", [128, 2, TT], F32)
            ebt = sb("n_eb", [128, 3, TT], BF16)
            Mt = sb("n_M", [128, 3, TT], F32)
            Mtb = sb("n_Mb", [128, 3, TT], BF16)
            negT = sb("n_negT", [64, 2, TT], BF16)
            selc = sb("n_selc", [128, 3, 4, 64], F32)
            sig = sb("n_sig", [30, TT], F32)
            impg = sb("n_imp", [128, 2, 4, 64], F32)
            imp2 = sb("n_imp2", [128, 4, 64], F32)
            rden = sb("n_rden", [128, 4, 1], F32)
            score = sb("n_score", [128, 64], F32)
            work = sb("n_work", [128, 64], F32)
            mx8 = sb("n_mx8", [128, 16], F32)
            nsel = sb("n_nsel", [128, 64], F32)
            wgt = sb("n_wgt", [128, 2, TT], F32)
            sqd = sb("n_sq2", [128, 2, TT], BF16)
            rstdd = sb("n_rstd2", [128, 2, TT], F32)
            OvE = self.c2[:, 128:258].rearrange("p (c j) -> p c j", c=2)
            Sel = self.c2[0:30, 258:258 + 18 * 128].rearrange("p (k c) -> p k c", k=18)
            cnt = {"e": 0, "m": 0, "s": 0, "mb": 0}

            def combine(h, gate_k, first, od=0):
                ps_o, ps_d, ko, kd = OD[od]
                kb.op("pe", lambda e: e.matmul(ps_g[:], Sel[:, gate_k, :], sig[:], start=True, stop=True), r=["nsig", "cst2"], w=["@npg"])
                kb.op("dve", lambda e: e.tensor_scalar(out=wgt[:, 0, :], in0=ps_d[:], scalar1=1e-18, scalar2=None, op0=ALU.max), r=[kd], w=["nwgt0"])
                kb.op("act", lambda e: e.activation(out=wgt[:, 0, :], in_=wgt[:, 0, :], func=AF.Ln), r=["nwgt0"], w=["nwgt0"])
                kb.op("act", lambda e: e.activation(out=wgt[:, 0, :], in_=wgt[:, 0, :], func=AF.Exp, scale=-1.0), r=["nwgt0"], w=["nwgt0"])
                kb.op("dve", lambda e: e.tensor_tensor(out=wgt[:, 0, :], in0=wgt[:, 0, :], in1=ps_g[:], op=ALU.mult), r=["nwgt0", "@npg"], w=["nwgt0"])
                if first:
                    kb.op("dve", lambda e: e.tensor_tensor(out=acc[:, h, :], in0=ps_o[:], in1=wgt[:, 0, :], op=ALU.mult), r=[ko, "nwgt0"], w=["nacc%d" % h])
                else:
                    kb.op("dve", lambda e: e.tensor_tensor(out=wgt[:, 1, :], in0=ps_o[:], in1=wgt[:, 0, :], op=ALU.mult), r=[ko, "nwgt0"], w=["nwgt1"])
                    kb.op("dve", lambda e: e.tensor_tensor(out=acc[:, h, :], in0=acc[:, h, :], in1=wgt[:, 1, :], op=ALU.add), r=["nwgt1", "nacc%d" % h], w=["nacc%d" % h])

            for t in range(NT):
                q0 = t * TT
                kb.dma(qraw[:], self.projT.ap()[CH_QC * 128:(CH_QC + 6) * 128, q0:q0 + TT].rearrange("(h p) t -> p h t", p=128), "nq", w=["nqraw"])
                kb.dma(sig[:], self.projT.ap()[CH_SM * 128:CH_SM * 128 + 30, q0:q0 + TT], "nsg", w=["nsig"])
                kb.op("act", lambda e: e.activation(out=sig[:], in_=sig[:], func=AF.Sigmoid), r=["nsig"], w=["nsig"])
                for a_ in range(3):
                    kb.dma(selc[:, a_], self.selc_in.ap()[a_, q0:q0 + TT, :].rearrange("(s p) j -> p s j", p=128), "nselc", w=["nselc"])
                for h in range(H):
                    p2 = h % 2
                    kb.op("act", lambda e, h=h, p2=p2: e.activation(out=sqd[:, p2, :], in_=qraw[:, h, :], func=AF.Square), r=["nqraw"], w=["nsq2_%d" % p2])
                    kb.op("pe", lambda e, p2=p2: e.matmul(ps_x[:], self.ones_b[:], sqd[:, p2, :], start=True, stop=True), r=["nsq2_%d" % p2, "onesb"], w=["@npi"])
                    kb.op("act", lambda e, p2=p2: e.activation(out=rstdd[:, p2, :], in_=ps_x[:], func=AF.Ln, bias=eps_t[:, 0:1], scale=1.0 / 128), r=["@npi", "neps"], w=["nrstd2_%d" % p2])
                    kb.op("act", lambda e, p2=p2: e.activation(out=rstdd[:, p2, :], in_=rstdd[:, p2, :], func=AF.Exp, scale=-0.5), r=["nrstd2_%d" % p2], w=["nrstd2_%d" % p2])
                    kb.op("dve", lambda e, h=h, p2=p2: e.scalar_tensor_tensor(out=qn[:, h, :], in0=qraw[:, h, :], scalar=gq[:, 0:1], in1=rstdd[:, p2, :], op0=ALU.mult, op1=ALU.mult),
                          r=["nqraw", "nrstd2_%d" % p2, "ngq"], w=["nqn%d" % h])
                    kb.op("act", lambda e, h=h: e.copy(out=qb[:, h, :], in_=qn[:, h, :]), r=["nqn%d" % h], w=["nqb%d" % h])
                def cmp_front(h):
                    g = h // 3
                    ep = h % 2
                    for c in range(2):
                        pb = cnt["s"] % 2
                        cnt["s"] += 1
                        ms = cnt["m"] % 3
                        cnt["m"] += 1
                        kb.dma(Mt[:, ms, :], self.mtile(h, 16, q0 - 16 * (c * 128) - 31), "nM%d" % ms, w=["nM%d" % ms])
                        kb.op("pe", lambda e, c=c, pb=pb: e.matmul(ps_s[pb][:], kcT[:, g, c * 128:(c + 1) * 128], qn[:, h, :], start=True, stop=True),
                              r=["nkcT%d" % g, "nqn%d" % h], w=["@nps%d" % pb])
                        kb.op("act", lambda e, c=c, pb=pb: e.activation(out=ec[:, ep, c, :], in_=ps_s[pb][:], func=AF.Exp), r=["@nps%d" % pb], w=["nec%d_%d" % (ep, c)])
                        kb.op("dve", lambda e, c=c, ms=ms: e.tensor_tensor(out=ec[:, ep, c, :], in0=ec[:, ep, c, :], in1=Mt[:, ms, :], op=ALU.mult),
                              r=["nec%d_%d" % (ep, c), "nM%d" % ms], w=["nec%d_%d" % (ep, c)])

                def cmp_rest(h):
                    g = h // 3
                    r_ = h % 3
                    ep = h % 2
                    EK = ["nec%d_0" % ep, "nec%d_1" % ep]
                    for c in range(2):
                        kb.op("pe", lambda e, c=c: e.matmul(ps_o[:], vc[:, g, c, :], ec[:, ep, c, :], start=(c == 0), stop=(c == 1)), r=["nvc%d" % g, EK[c]], w=["@npo"])
                    for c in range(2):
                        kb.op("pe", lambda e, c=c: e.matmul(ps_d[:], self.ones_f[:], ec[:, ep, c, :], start=(c == 0), stop=(c == 1)), r=["ones", EK[c]], w=["@npd"])
                    for qs in range(4):
                        for c in range(2):
                            kb.op("pe", lambda e, c=c, qs=qs: e.matmul(ps_i[:, qs * 65:(qs + 1) * 65], ec[:, ep, c, qs * 128:(qs + 1) * 128], OvE[:, c, :], start=(c == 0), stop=(c == 1)),
                                  r=EK + ["cst2"], w=["@npi"])
                    pi3 = ps_i[:, 0:260].rearrange("p (s j) -> p s j", j=65)
                    kb.op("dve", lambda e: e.tensor_scalar(out=rden[:], in0=pi3[:, :, 64:65], scalar1=1e-30, scalar2=None, op0=ALU.max), r=["@npi"], w=["nrden"])
                    kb.op("dve", lambda e: e.reciprocal(out=rden[:], in_=rden[:]), r=["nrden"], w=["nrden"])
                    if r_ == 0:
                        kb.op("dve", lambda e: e.tensor_tensor(out=impg[:, g], in0=pi3[:, :, 0:64], in1=rden[:].to_broadcast([128, 4, 64]), op=ALU.mult),
                              r=["@npi", "nrden"], w=["nimp%d" % g])
                    else:
                        kb.op("dve", lambda e: e.tensor_tensor(out=imp2[:], in0=pi3[:, :, 0:64], in1=rden[:].to_broadcast([128, 4, 64]), op=ALU.mult),
                              r=["@npi", "nrden"], w=["nimp2"])
                        kb.op("dve", lambda e: e.tensor_tensor(out=impg[:, g], in0=impg[:, g], in1=imp2[:], op=ALU.add), r=["nimp%d" % g, "nimp2"], w=["nimp%d" % g])
                    combine(h, 0 * 6 + h, True)

                cmp_front(0)
                for h in range(H):
                    if h + 1 < H:
                        cmp_front(h + 1)
                    cmp_rest(h)
                for g in range(2):
                    for qs in range(4):
                        kb.op("dve", lambda e, qs=qs: e.tensor_tensor(out=score[:], in0=impg[:, g, qs, :], in1=selc[:, 0, qs, :], op=ALU.mult), r=["nimp%d" % g, "nselc"], w=["nscore"])
                        kb.op("dve", lambda e, qs=qs: e.tensor_tensor(out=score[:], in0=score[:], in1=selc[:, 1, qs, :], op=ALU.add), r=["nscore", "nselc"], w=["nscore"])
                        kb.op("dve", lambda e: e.max(out=mx8[:, 0:8], in_=score[:]), r=["nscore"], w=["nmx8"])
                        kb.op("dve", lambda e: e.match_replace(out=work[:], in_to_replace=mx8[:, 0:8], in_values=score[:], imm_value=-3.0e38), r=["nscore", "nmx8"], w=["nwork"])
                        kb.op("dve", lambda e: e.max(out=mx8[:, 8:16], in_=work[:]), r=["nwork"], w=["nmx8b"])
                        kb.op("dve", lambda e, qs=qs: e.scalar_tensor_tensor(out=nsel[:], in0=score[:], scalar=mx8[:, 15:16], in1=selc[:, 2, qs, :], op0=ALU.is_ge, op1=ALU.mult),
                              r=["nscore", "nmx8b", "nselc"], w=["nnsel"])
                        kb.op("dve", lambda e: e.tensor_scalar(out=nsel[:], in0=nsel[:], scalar1=-1.0, scalar2=30000.0, op0=ALU.add, op1=ALU.mult), r=["nnsel"], w=["nnsel"])
                        kb.op("pe", lambda e: e.transpose(ps_x[0:64, 0:128], nsel[:], self.ident), r=["nnsel", "cst"], w=["@npi"])
                        kb.op("act", lambda e, qs=qs: e.copy(out=negT[:, g, qs * 128:(qs + 1) * 128], in_=ps_x[0:64, 0:128]), r=["@npi"], w=["nnegT%d" % g])
                for g in range(2):
                    tiles = []
                    for r_ in range(3):
                        h = g * 3 + r_
                        for br in ("s", "w"):
                            if br == "s":
                                kts = list(range(0, (q0 + TT) // 128))
                            else:
                                kts = list(range(max(0, (q0 - 512) // 128), (q0 + TT) // 128))
                            for i, kt in enumerate(kts):
                                tiles.append(dict(h=h, br=br, kt=kt, i=i, n=len(kts), delta=q0 - kt * 128, od=(r_ * 2 + (0 if br == "s" else 1)) % 2))
                    for ti, tl in enumerate(tiles):
                        tl["pb"] = ti % 2
                        tl["es"] = ti % 3
                        tl["e2"] = ti % 2
                        tl["band"] = (tl["br"] == "w") or tl["delta"] < 256

                    def emitS(tl):
                        h, br, kt, pb = tl["h"], tl["br"], tl["kt"], tl["pb"]
                        if tl["band"]:
                            ms = cnt["mb"] % 3
                            cnt["mb"] += 1
                            tl["ms"] = ms
                            kb.dma(Mtb[:, ms, :], self.mtile(h + (6 if br == "w" else 0), 1, tl["delta"], bf=True), "nMb%d" % ms, w=["nMb%d" % ms])
                        kb.op("pe", lambda e: e.matmul(ps_s[pb][:], kT[br][:, g, kt * 128:(kt + 1) * 128], qb[:, h, :], start=True, stop=(br == "w")),
                              r=["nkT" + br, "nqb%d" % h], w=["@nps%d" % pb])
                        if br == "s":
                            kb.op("pe", lambda e: e.matmul(ps_s[pb][:], eblk[:, kt * 128:(kt + 1) * 128], negT[:, g, :], start=False, stop=True),
                                  r=["neblk", "nnegT%d" % g], w=["@nps%d" % pb])

                    def emitE(tl):
                        h, br, kt, pb, es_, e2 = tl["h"], tl["br"], tl["kt"], tl["pb"], tl["es"], tl["e2"]
                        if tl["band"]:
                            ms = tl["ms"]
                            kb.op("act", lambda e: e.activation(out=e32[:, e2, :], in_=ps_s[pb][:], func=AF.Exp), r=["@nps%d" % pb], w=["ne32%d" % e2])
                            kb.op("dve", lambda e: e.tensor_tensor(out=ebt[:, es_, :], in0=e32[:, e2, :], in1=Mtb[:, ms, :], op=ALU.mult),
                                  r=["ne32%d" % e2, "nMb%d" % ms], w=["neb%d" % es_])
                        else:
                            kb.op("act", lambda e: e.activation(out=ebt[:, es_, :], in_=ps_s[pb][:], func=AF.Exp, bias=self.b31bc[:, h:h + 1], scale=1.0),
                                  r=["@nps%d" % pb, "b31"], w=["neb%d" % es_])

                    def emitPV(tl):
                        h, br, kt, es_, i, n = tl["h"], tl["br"], tl["kt"], tl["es"], tl["i"], tl["n"]
                        po_, pd_, ko_, kd_ = OD[tl["od"]]
                        kb.op("pe", lambda e: e.matmul(po_[:], vtm[br][:, g, kt, :], ebt[:, es_, :], start=(i == 0), stop=(i == n - 1)),
                              r=["nv" + br, "neb%d" % es_], w=[ko_])
                        kb.op("pe", lambda e: e.matmul(pd_[:], ones_b[:], ebt[:, es_, :], start=(i == 0), stop=(i == n - 1)),
                              r=["n1b", "neb%d" % es_], w=[kd_])
                        if i == n - 1:
                            combine(h, (1 if br == "s" else 2) * 6 + h, False, od=tl["od"])

                    emitS(tiles[0])
                    for ti, tl in enumerate(tiles):
                        if ti + 1 < len(tiles):
                            emitS(tiles[ti + 1])
                        emitE(tl)
                        emitPV(tl)
                for h in range(H):
                    kb.op("act", lambda e, h=h: e.copy(out=obf[:, h, :], in_=acc[:, h, :]), r=["nacc%d" % h], w=["nobf"])
                kb.dma(self.mixT.ap()[1280:2048, q0:q0 + TT].rearrange("(h p) t -> p h t", p=128), obf[:], "nos", r=["nobf"])
            kb.barrier()


def make_consts():
    c = np.zeros((128, 1024), np.float32)
    c[:, 0:128] = np.eye(128, dtype=np.float32)
    i = np.arange(128)
    same = (i[:, None] // 64) == (i[None, :] // 64)
    c[:, 128:256] = (same & (i[:, None] <= i[None, :])).astype(np.float32)
    c[:, 256:384] = (same & (i[:, None] >= i[None, :])).astype(np.float32)
    c[:, 384:512] = (same & (i[:, None] > i[None, :])).astype(np.float32)
    c[:, 512:640] = (i[:, None] <= i[None, :]).astype(np.float32)
    return c


def t5_bucket_np(n):
    n = np.maximum(n, 0)
    max_exact = 16
    lr = np.log(np.maximum(n, 1).astype(np.float32) / max_exact) / np.float32(np.log(128 / max_exact))
    large = np.minimum(max_exact + (lr * 16).astype(np.int32), 31)
    return np.where(n < max_exact, n, large)


def make_consts2():
    c = np.zeros((128, 2562), np.float32)
    d = np.arange(128)
    b = t5_bucket_np(d)
    c[b, d] = 1.0
    n = np.arange(256)
    j = np.arange(64)
    ov = ((n[:, None] * 16 < j[None, :] * 64 + 64) & (n[:, None] * 16 + 32 > j[None, :] * 64)).astype(np.float32)
    ov[255] = 0
    ove = np.zeros((256, 65), np.float32)
    ove[:, :64] = ov
    ove[:255, 64] = 1.0
    c[:, 128:258] = ove.reshape(2, 128, 65).transpose(1, 0, 2).reshape(128, 130)
    sel = np.zeros((30, 18, 128), np.float32)
    for k in range(18):
        sel[12 + k, k, :] = 1.0
    c[0:30, 258:258 + 18 * 128] = sel.reshape(30, -1)
    key = np.arange(S)
    eblk = (key[None, :] // 64 == np.arange(64)[:, None]).astype(np.float32)
    q = np.arange(S)
    cur = q // 64
    causal = j[None, :] <= cur[:, None]
    forced = (j[None, :] == 0) | (j[None, :] == cur[:, None]) | (j[None, :] == cur[:, None] - 1)
    a1 = (causal & ~forced).astype(np.float32)
    a2 = np.where(forced, np.float32(1e9), np.where(causal, np.float32(0), np.float32(-1e30))).astype(np.float32)
    selc = np.stack([a1, a2, causal.astype(np.float32)], axis=0)
    return c, eblk, np.ascontiguousarray(selc)


def kernel(**inputs):
    prog = Prog()
    return run_prog(prog, inputs)


def run_prog(prog, inputs, extra=None, cores=8):
    x = np.asarray(inputs["x"], np.float32)
    cst = make_consts()
    pl = lambda a: np.asarray(a, np.float32).reshape(DEPTH, 16, 128).transpose(2, 0, 1).reshape(128, DEPTH * 16)
    gains = np.ascontiguousarray(np.concatenate([pl(inputs["attn_norm"]), pl(inputs["mlp_norm"])], axis=1))
    conv_pl = np.ascontiguousarray(np.asarray(inputs["conv_a"], np.float32).reshape(DEPTH, 4, 18, 128).transpose(0, 3, 2, 1).reshape(DEPTH, 128, 72))
    gdnn = np.ascontiguousarray(np.asarray(inputs["gdn_norm"], np.float32).T)
    cst2, eblk, selc = make_consts2()
    nqk = np.ascontiguousarray(np.concatenate([np.asarray(inputs["nsa_q_norm"], np.float32).T, np.asarray(inputs["nsa_k_norm"], np.float32).T], axis=1))
    cpos_pl = np.ascontiguousarray(np.asarray(inputs["cmp_pos"], np.float32).transpose(0, 1, 3, 2))
    in_maps = []
    for c in range(cores):
        m = {k: np.ascontiguousarray(np.asarray(v, np.float32)) for k, v in inputs.items() if k != "x"}
        m["xT"] = np.ascontiguousarray(x[c].T)
        m["cst"] = cst
        m["gains_in"] = gains
        m["conv_pl"] = conv_pl
        m["cst2"] = cst2
        m["eblk_in"] = eblk
        m["selc_in"] = selc
        m["nqk_in"] = nqk
        m["cpos_pl"] = cpos_pl
        m["gdnn_in"] = gdnn
        if extra:
            m.update(extra)
        in_maps.append(m)
    res = run_bass_kernel_spmd(prog.nc, in_maps, core_ids=list(range(cores)))
    prog.last_results = res.results
    out = np.stack([np.ascontiguousarray(r["yT"].T) for r in res.results], axis=0)
    return out.astype(np.float32)
```
